# Optimizing a Trainium2 kernel written in Bass

```python
import jax
import jax.numpy as jnp
from jax import lax
import numpy as np

D_MODEL = 4096
BATCH = 2
SEQ = 4096
DEPTH = 2

MEM_LEN = 256
CONV_W = 4
RMS_EPS = 1e-6
W_A = D_MODEL // 2
NB_A = 8
BS_A = W_A // NB_A
LRU_C = 8.0
W_B = D_MODEL // 2
N_B = 64
H_B = W_B // N_B
R_W = 96
R_A = 96
RWKV_GN_EPS = 64e-5
B_SHIFT_W = 3 * W_B + R_W + R_A
W_C = D_MODEL // 2
H_C = 4
DV_C = W_C // H_C
DQK_C = DV_C // 2
QK_W = H_C * DQK_C
CHUNK = 64
MLSTM_GN_EPS = 1e-6
H_X = 4
DH_X = 128
W_X = H_X * DH_X
N_BRANCH = 4
IN_SIZES = (W_A, W_A, B_SHIFT_W, W_B, 2 * QK_W, W_C, W_C, W_C, 2 * H_C, W_X, W_X, N_BRANCH * D_MODEL)
C_IN = sum(IN_SIZES)

kernel_name = 'hybrid_rglru_rwkv7_mlstm_memxattn'

F32 = jnp.float32


def rmsnorm(x, g, eps=RMS_EPS):
    xf = x.astype(F32)
    y = xf * lax.rsqrt(jnp.mean(xf * xf, axis=-1, keepdims=True) + eps)
    return (y * g.astype(F32)).astype(x.dtype)


def head_layernorm(y, w, b, eps):
    mu = jnp.mean(y, axis=-1, keepdims=True)
    yc = y - mu
    var = jnp.mean(yc * yc, axis=-1, keepdims=True)
    out = (yc * lax.rsqrt(var + eps)).reshape(y.shape[0], y.shape[1], -1) * w.astype(F32)
    if b is not None:
        out = out + b.astype(F32)
    return out


def causal_dwconv(u, w, b):
    k_w = w.shape[0]
    s = u.shape[1]
    up = jnp.pad(u, ((0, 0), (k_w - 1, 0), (0, 0)))
    return sum(up[:, j:j + s] * w[j] for j in range(k_w)) + b


def token_shift(p, mu):
    prev = jnp.pad(p, ((0, 0), (1, 0), (0, 0)))[:, :-1]
    return p + (prev - p) * mu


def _lin_combine(left, right):
    a1, b1 = left
    a2, b2 = right
    return a1 * a2, a2 * b1 + b2


def rg_lru_branch(u_in, gate, conv_w, conv_b, w_a, b_a, w_x, b_x, lam):
    bsz, s, _ = u_in.shape
    u = causal_dwconv(u_in, conv_w, conv_b)
    ub = u.reshape(bsz, s, NB_A, BS_A)
    r = jax.nn.sigmoid(jnp.einsum('bsnc,ncd->bsnd', ub, w_a).reshape(bsz, s, W_A) + b_a)
    i = jax.nn.sigmoid(jnp.einsum('bsnc,ncd->bsnd', ub, w_x).reshape(bsz, s, W_A) + b_x)
    log_a = (-LRU_C * r.astype(F32) * jax.nn.softplus(-lam.astype(F32)))
    a = jnp.exp(log_a)
    mult = jnp.sqrt(-jnp.expm1(2.0 * log_a))
    bterm = mult * (i * u).astype(F32)
    _, h = lax.associative_scan(_lin_combine, (a, bterm), axis=1)
    return (h * jax.nn.silu(gate.astype(F32))).astype(u_in.dtype)


def rwkv7_branch(p, gate, mu, w0, w_up, a0, a_up, k_k, k_a, r_k, gn_w, gn_b):
    bsz, s, _ = p.shape
    q = token_shift(p, mu)
    r, k, v, wd, ad = jnp.split(q, [W_B, 2 * W_B, 3 * W_B, 3 * W_B + R_W], axis=-1)
    w_log = -jax.nn.softplus(-(w0 + jnp.tanh(wd) @ w_up)) - 0.5
    decay = jnp.exp(-jnp.exp(w_log.astype(F32)))
    a = jax.nn.sigmoid(a0 + ad @ a_up)

    def heads(t):
        return t.astype(F32).reshape(bsz, s, H_B, N_B)

    r, k, v, decay, a = heads(r), heads(k), heads(v), heads(decay), heads(a)
    kk = k * k_k.astype(F32).reshape(H_B, N_B)
    kk = kk / jnp.maximum(jnp.sqrt(jnp.sum(kk * kk, axis=-1, keepdims=True)), 1e-12)
    k = k * (1.0 + (a - 1.0) * k_a.astype(F32).reshape(H_B, N_B))
    kka = kk * a

    def step(state, inp):
        r_t, w_t, k_t, v_t, kk_t, kka_t = inp
        sa = jnp.einsum('bhij,bhj->bhi', state, kk_t)
        state = (state * w_t[:, :, None, :] - sa[..., None] * kka_t[:, :, None, :]
                 + v_t[..., None] * k_t[:, :, None, :])
        return state, jnp.einsum('bhij,bhj->bhi', state, r_t)

    xs = tuple(jnp.moveaxis(t, 1, 0) for t in (r, decay, k, v, kk, kka))
    state0 = jnp.zeros((bsz, H_B, N_B, N_B), F32)
    _, y = lax.scan(step, state0, xs)
    y = head_layernorm(jnp.moveaxis(y, 0, 1), gn_w, gn_b, RWKV_GN_EPS)
    bonus = (jnp.sum(r * k * r_k.astype(F32), axis=-1, keepdims=True) * v).reshape(bsz, s, W_B)
    return ((y + bonus) * jax.nn.silu(gate.astype(F32))).astype(p.dtype)


def mlstm_branch(qk_in, v_in, o_in, gate, if_in, conv_w, conv_b, b_i, b_f, gn_w):
    bsz, s, _ = v_in.shape
    n_chunks = s // CHUNK
    qk = jax.nn.silu(causal_dwconv(qk_in, conv_w, conv_b))
    q, k = jnp.split(qk, 2, axis=-1)
    q = q.reshape(bsz, s, H_C, DQK_C)
    k = k.reshape(bsz, s, H_C, DQK_C) * (DQK_C ** -0.5)
    v = v_in.reshape(bsz, s, H_C, DV_C)
    i_pre, f_pre = jnp.split(if_in, 2, axis=-1)
    log_i = (i_pre + b_i).astype(F32)
    log_f = jax.nn.log_sigmoid((f_pre + b_f).astype(F32))

    def to_chunks(t):
        t = t.astype(F32).reshape(bsz, n_chunks, CHUNK, H_C, *t.shape[3:])
        return jnp.moveaxis(t, (1, 3), (0, 2))

    causal = jnp.tril(jnp.ones((CHUNK, CHUNK), dtype=bool))

    def chunk_step(carry, inp):
        c_st, n_st, m_st = carry
        qc, kc, vc, li, lf = inp
        bcum = jnp.cumsum(lf, axis=-1)
        g_tot = bcum[..., -1]
        dmat = bcum[..., :, None] - bcum[..., None, :] + li[..., None, :]
        dmat = jnp.where(causal, dmat, -jnp.inf)
        inter = bcum + m_st[..., None]
        m_t = jnp.maximum(inter, jnp.max(dmat, axis=-1))
        w_intra = jnp.exp(dmat - m_t[..., None])
        w_inter = jnp.exp(inter - m_t)
        sc = jnp.einsum('bhld,bhsd->bhls', qc, kc) * w_intra
        num = (w_inter[..., None] * jnp.einsum('bhld,bhde->bhle', qc, c_st)
               + jnp.einsum('bhls,bhse->bhle', sc, vc))
        den = w_inter * jnp.einsum('bhld,bhd->bhl', qc, n_st) + jnp.sum(sc, axis=-1)
        h = num / jnp.maximum(jnp.abs(den), jnp.exp(-m_t))[..., None]
        le = g_tot[..., None] - bcum + li
        m_new = jnp.maximum(g_tot + m_st, jnp.max(le, axis=-1))
        keep = jnp.exp(g_tot + m_st - m_new)
        wk = jnp.exp(le - m_new[..., None])
        c_st = keep[..., None, None] * c_st + jnp.einsum('bhs,bhsd,bhse->bhde', wk, kc, vc)
        n_st = keep[..., None] * n_st + jnp.einsum('bhs,bhsd->bhd', wk, kc)
        return (c_st, n_st, m_new), h

    c0 = jnp.zeros((bsz, H_C, DQK_C, DV_C), F32)
    n0 = jnp.zeros((bsz, H_C, DQK_C), F32)
    m0 = jnp.full((bsz, H_C), -jnp.inf, F32)
    _, h = lax.scan(chunk_step, (c0, n0, m0), tuple(to_chunks(t) for t in (q, k, v, log_i, log_f)))
    h = jnp.moveaxis(h, (0, 2), (1, 3)).reshape(bsz, s, H_C, DV_C)
    o = jax.nn.sigmoid(o_in.astype(F32)).reshape(bsz, s, H_C, DV_C)
    hn = head_layernorm(h * o, gn_w, None, MLSTM_GN_EPS)
    return (hn * jax.nn.silu(gate.astype(F32))).astype(v_in.dtype)


def memory_xattn_branch(q_in, gate, mem_n, w_kv):
    bsz, s, _ = q_in.shape
    m_len = mem_n.shape[1]
    k, v = jnp.split(mem_n @ w_kv, 2, axis=-1)
    k = k.reshape(bsz, m_len, H_X, DH_X)
    v = v.reshape(bsz, m_len, H_X, DH_X)
    q = q_in.reshape(bsz, s, H_X, DH_X)
    logits = jnp.einsum('bshd,bmhd->bhsm', q, k).astype(F32) * (DH_X ** -0.5)
    probs = jax.nn.softmax(logits, axis=-1).astype(v.dtype)
    o = jnp.einsum('bhsm,bmhd->bshd', probs, v).reshape(bsz, s, W_X)
    return (o * jax.nn.silu(gate)).astype(q_in.dtype)


def hybrid_layer(x, mem, norm_g, mem_norm_g, w_in,
                 lru_conv_w, lru_conv_b, lru_wa, lru_ba, lru_wx, lru_bx, lru_lambda,
                 rwkv_mu, rwkv_w0, rwkv_w_up, rwkv_a0, rwkv_a_up, rwkv_k_k, rwkv_k_a, rwkv_r_k,
                 rwkv_gn_w, rwkv_gn_b,
                 mlstm_conv_w, mlstm_conv_b, mlstm_b_i, mlstm_b_f, mlstm_gn_w,
                 xattn_w_kv, w_branch_a, w_branch_b, w_branch_c, w_branch_x, w_out):
    bsz, s, d = x.shape
    h = rmsnorm(x, norm_g)
    proj = h @ w_in
    cuts = [int(c) for c in np.cumsum(IN_SIZES)[:-1]]
    (a_x, a_g, b_s, b_g, c_qk, c_v, c_o, c_g, c_if, x_q, x_g,
     gate_logits) = jnp.split(proj, cuts, axis=-1)
    y_a = rg_lru_branch(a_x, a_g, lru_conv_w, lru_conv_b, lru_wa, lru_ba, lru_wx, lru_bx, lru_lambda)
    y_b = rwkv7_branch(b_s, b_g, rwkv_mu, rwkv_w0, rwkv_w_up, rwkv_a0, rwkv_a_up,
                       rwkv_k_k, rwkv_k_a, rwkv_r_k, rwkv_gn_w, rwkv_gn_b)
    y_c = mlstm_branch(c_qk, c_v, c_o, c_g, c_if, mlstm_conv_w, mlstm_conv_b,
                       mlstm_b_i, mlstm_b_f, mlstm_gn_w)
    y_x = memory_xattn_branch(x_q, x_g, rmsnorm(mem, mem_norm_g), xattn_w_kv)
    gates = jax.nn.sigmoid(gate_logits.reshape(bsz, s, N_BRANCH, d))
    merged = (gates[:, :, 0] * (y_a @ w_branch_a) + gates[:, :, 1] * (y_b @ w_branch_b)
              + gates[:, :, 2] * (y_c @ w_branch_c) + gates[:, :, 3] * (y_x @ w_branch_x))
    return x + merged @ w_out


def setup_inputs(seed: int = 0) -> dict:
    key = jax.random.key(seed)
    ks = iter(jax.random.split(key, 40))
    L, D = DEPTH, D_MODEL

    def nrm(shape, scale):
        return scale * jax.random.normal(next(ks), shape, F32)

    def uni(shape, lo, hi):
        return jax.random.uniform(next(ks), shape, F32, lo, hi)

    u_lru = uni((L, W_A), 0.9, 0.999) ** (1.0 / LRU_C)
    lru_lambda = jnp.log(u_lru) - jnp.log1p(-u_lru)
    return {
        'x': nrm((BATCH, SEQ, D), 1.0),
        'mem': nrm((BATCH, MEM_LEN, D), 1.0),
        'norm_g': 1.0 + nrm((L, D), 0.01),
        'mem_norm_g': 1.0 + nrm((L, D), 0.01),
        'w_in': nrm((L, D, C_IN), D ** -0.5),
        'lru_conv_w': nrm((L, CONV_W, W_A), CONV_W ** -0.5),
        'lru_conv_b': nrm((L, W_A), 0.01),
        'lru_wa': nrm((L, NB_A, BS_A, BS_A), BS_A ** -0.5),
        'lru_ba': nrm((L, W_A), 0.01),
        'lru_wx': nrm((L, NB_A, BS_A, BS_A), BS_A ** -0.5),
        'lru_bx': nrm((L, W_A), 0.01),
        'lru_lambda': lru_lambda,
        'rwkv_mu': uni((L, B_SHIFT_W), 0.0, 1.0),
        'rwkv_w0': uni((L, W_B), -6.0, -1.0),
        'rwkv_w_up': nrm((L, R_W, W_B), 0.5 * R_W ** -0.5),
        'rwkv_a0': nrm((L, W_B), 0.1),
        'rwkv_a_up': nrm((L, R_A, W_B), R_A ** -0.5),
        'rwkv_k_k': 0.85 + nrm((L, W_B), 0.05),
        'rwkv_k_a': 1.0 + nrm((L, W_B), 0.05),
        'rwkv_r_k': nrm((L, H_B, N_B), 0.1),
        'rwkv_gn_w': 1.0 + nrm((L, W_B), 0.01),
        'rwkv_gn_b': nrm((L, W_B), 0.01),
        'mlstm_conv_w': nrm((L, CONV_W, 2 * QK_W), CONV_W ** -0.5),
        'mlstm_conv_b': nrm((L, 2 * QK_W), 0.01),
        'mlstm_b_i': nrm((L, H_C), 0.1),
        'mlstm_b_f': uni((L, H_C), 3.0, 6.0),
        'mlstm_gn_w': 1.0 + nrm((L, W_C), 0.01),
        'xattn_w_kv': nrm((L, D, 2 * W_X), D ** -0.5),
        'w_branch_a': nrm((L, W_A, D), W_A ** -0.5),
        'w_branch_b': nrm((L, W_B, D), W_B ** -0.5),
        'w_branch_c': nrm((L, W_C, D), W_C ** -0.5),
        'w_branch_x': nrm((L, W_X, D), W_X ** -0.5),
        'w_out': nrm((L, D, D), D ** -0.5),
        'final_norm_g': 1.0 + nrm((D,), 0.01),
    }


def reference(x, mem, norm_g, mem_norm_g, w_in,
              lru_conv_w, lru_conv_b, lru_wa, lru_ba, lru_wx, lru_bx, lru_lambda,
              rwkv_mu, rwkv_w0, rwkv_w_up, rwkv_a0, rwkv_a_up, rwkv_k_k, rwkv_k_a, rwkv_r_k,
              rwkv_gn_w, rwkv_gn_b,
              mlstm_conv_w, mlstm_conv_b, mlstm_b_i, mlstm_b_f, mlstm_gn_w,
              xattn_w_kv, w_branch_a, w_branch_b, w_branch_c, w_branch_x, w_out, final_norm_g):
    for l in range(DEPTH):
        x = hybrid_layer(x, mem, norm_g[l], mem_norm_g[l], w_in[l],
                         lru_conv_w[l], lru_conv_b[l], lru_wa[l], lru_ba[l], lru_wx[l], lru_bx[l],
                         lru_lambda[l],
                         rwkv_mu[l], rwkv_w0[l], rwkv_w_up[l], rwkv_a0[l], rwkv_a_up[l],
                         rwkv_k_k[l], rwkv_k_a[l], rwkv_r_k[l], rwkv_gn_w[l], rwkv_gn_b[l],
                         mlstm_conv_w[l], mlstm_conv_b[l], mlstm_b_i[l], mlstm_b_f[l], mlstm_gn_w[l],
                         xattn_w_kv[l], w_branch_a[l], w_branch_b[l], w_branch_c[l], w_branch_x[l],
                         w_out[l])
    return rmsnorm(x, final_norm_g)
```

```python
import contextlib
import numpy as np
import concourse.bass as bass
import concourse.mybir as mybir
from concourse.bass_utils import run_bass_kernel_spmd

F32 = mybir.dt.float32
BF16 = mybir.dt.bfloat16
AF = mybir.ActivationFunctionType
ALU = mybir.AluOpType
AX = mybir.AxisListType


class Buf:
    __slots__ = ("name", "w", "r")

    def __init__(self, name=""):
        self.name = name
        self.w = set()
        self.r = set()


class Sched:
    EPOCH = 4000
    NDMA = 12

    def __init__(self, nc, stack):
        self.nc = nc
        self.stack = stack
        self.eng = {"pe": nc.tensor, "act": nc.scalar, "dve": nc.vector,
                    "pool": nc.gpsimd, "sp": nc.sync}
        self.sem = {}
        self.cnt = {}
        self.pending = {e: False for e in self.eng}
        self.nsem = 0
        for e in self.eng:
            self._new_sem(e)
        self.dsem = {}
        for q in ("sp", "act", "pool"):
            self.dsem[q] = [[self._alloc(f"d{q}{i}"), 0] for i in range(self.NDMA)]
        self.dnext = {q: 0 for q in self.dsem}
        self.seen = {e: {} for e in self.eng}
        self.all_tokens = {}
        self.ninst = 0
        self.per_eng = {}

    def _alloc(self, name):
        self.nsem += 1
        return self.stack.enter_context(self.nc.semaphore(f"{name}_{self.nsem}"))

    def _new_sem(self, e):
        self.sem[e] = self._alloc(f"s{e}")
        self.cnt[e] = 0

    def _wait(self, e, tok):
        sem, val = tok
        k = id(sem)
        if self.seen[e].get(k, 0) >= val:
            return
        self.eng[e].wait_ge(sem, val)
        self.seen[e][k] = val

    def _deps(self, e, reads, writes):
        deps = set()
        for b in reads:
            deps |= b.w
        for b in writes:
            deps |= b.w
            deps |= b.r
        for tok in deps:
            if e == "pe" and tok[0] is self.sem["pe"]:
                continue
            self._wait(e, tok)

    def _record(self, tok, reads, writes):
        self.all_tokens[id(tok[0])] = tok
        for b in reads:
            b.r.add(tok)
        for b in writes:
            b.w = {tok}
            b.r = set()

    def op(self, e, fn, reads=(), writes=(), inc=True):
        self._deps(e, reads, writes)
        inst = fn(self.eng[e])
        self.ninst += 1
        self.per_eng[e] = self.per_eng.get(e, 0) + 1
        if inc:
            if self.cnt[e] >= self.EPOCH and not self.pending[e]:
                self._new_sem(e)
            self.cnt[e] += 1
            inst.then_inc(self.sem[e], 1)
            tok = (self.sem[e], self.cnt[e])
            self.pending[e] = False
        else:
            assert e == "pe"
            tok = (self.sem[e], self.cnt[e] + 1)
            self.pending[e] = True
        self._record(tok, reads, writes)
        return inst

    def dma(self, q, out, in_, reads=(), writes=(), **kw):
        slot = self.dsem[q][self.dnext[q]]
        self.dnext[q] = (self.dnext[q] + 1) % self.NDMA
        sem, val = slot
        if val > 0:
            self._wait(q, (sem, val))
        self._deps(q, reads, writes)
        inst = self.eng[q].dma_start(out=out, in_=in_, **kw)
        self.ninst += 1
        self.per_eng['dma_' + q] = self.per_eng.get('dma_' + q, 0) + 1
        slot[1] = val + 16
        inst.then_inc(sem, 16)
        tok = (sem, slot[1])
        self._record(tok, reads, writes)
        return inst

    def barrier(self):
        toks = list(self.all_tokens.values())
        for e in self.eng:
            for tok in toks:
                self._wait(e, tok)


class Ctx:
    def __init__(self, nc, stack):
        self.nc = nc
        self.s = Sched(nc, stack)
        self.n = 0

    def sb(self, st, shape, dt, name="t"):
        self.n += 1
        return st.enter_context(self.nc.sbuf_tensor(f"{name}{self.n}", list(shape), dt))

    def ps(self, st, shape, dt=F32, name="p"):
        self.n += 1
        return st.enter_context(self.nc.psum_tensor(f"{name}{self.n}", list(shape), dt))

    def dram(self, name, shape, dt):
        return self.nc.dram_tensor(name, list(shape), dt, kind="Internal").ap()


def dma_rows(s, q, sb3, dram2, nchunks, reads=(), writes=(), to_dram=False, step=8):
    for c0 in range(0, nchunks, step):
        c1 = min(nchunks, c0 + step)
        d = dram2[c0 * 128:c1 * 128, :].rearrange("(c p) t -> p c t", p=128)
        if to_dram:
            s.dma(q, d, sb3[:, c0:c1, :], reads=reads, writes=writes)
        else:
            s.dma(q, sb3[:, c0:c1, :], d, reads=reads, writes=writes)


def bufs(n):
    return [Buf() for _ in range(n)]


NT = 512


def stage_norm(k, xT, g_dram, out, D, T, eps, out_dt):
    NT = min(512, T)
    s = k.s
    KC = D // 128
    with contextlib.ExitStack() as st:
        ones = k.sb(st, [128, 128], F32)
        gcol = k.sb(st, [128, KC], F32)
        xt = k.sb(st, [128, KC, NT], F32)
        ht = k.sb(st, [128, KC, NT], out_dt)
        sq = [k.sb(st, [128, NT], F32) for _ in range(2)]
        rs = k.sb(st, [128, NT], F32)
        rstd = k.sb(st, [128, NT], F32)
        pss = k.ps(st, [128, NT])
        b_ones, b_g, b_rs, b_rstd, b_ps = bufs(5)
        b_xt, b_ht, b_sq = bufs(KC), bufs(KC), bufs(2)
        s.op("pool", lambda e: e.memset(ones[:], 1.0), writes=[b_ones])
        s.dma("sp", gcol[:], g_dram, writes=[b_g])
        for t0 in range(0, T, NT):
            for c in range(KC):
                s.dma("sp", xt[:, c, :], xT[c * 128:(c + 1) * 128, t0:t0 + NT], writes=[b_xt[c]])
                s.op("act", lambda e: e.activation(out=sq[c % 2][:], in_=xt[:, c, :], func=AF.Square),
                     reads=[b_xt[c]], writes=[b_sq[c % 2]])
                s.op("pe", lambda e: e.matmul(pss[:], lhsT=ones[:], rhs=sq[c % 2][:],
                                             start=(c == 0), stop=(c == KC - 1)),
                     reads=[b_ones, b_sq[c % 2]], writes=[b_ps], inc=True)
            s.op("act", lambda e: e.activation(out=rs[:], in_=pss[:], func=AF.Sqrt, scale=1.0 / D, bias=eps_ap(k, eps)),
                 reads=[b_ps], writes=[b_rs])
            s.op("dve", lambda e: e.reciprocal(out=rstd[:], in_=rs[:]), reads=[b_rs], writes=[b_rstd])
            for c in range(KC):
                s.op("dve", lambda e: e.scalar_tensor_tensor(out=ht[:, c, :], in0=xt[:, c, :], scalar=gcol[:, c:c + 1],
                                                             in1=rstd[:], op0=ALU.mult, op1=ALU.mult),
                     reads=[b_xt[c], b_g, b_rstd], writes=[b_ht[c]])
            dma_rows(s, "sp", ht, out[:, t0:t0 + NT], KC, reads=b_ht, to_dram=True)
        s.barrier()


_EPS = {}


def eps_ap(k, val):
    return float(val)


def stage_proj(k, hT, W, PT, D, T, NB):
    NT = min(512, T)
    s = k.s
    KC = D // 128
    GRP = 8
    with contextlib.ExitStack() as st:
        wf = [k.sb(st, [128, KC, 128], F32) for _ in range(2)]
        wb = [k.sb(st, [128, KC, 128], BF16) for _ in range(GRP)]
        ht = [k.sb(st, [128, KC, NT], BF16) for _ in range(2)]
        ot = [k.sb(st, [128, NT], F32) for _ in range(4)]
        ps = [k.ps(st, [128, NT]) for _ in range(8)]
        b_wf, b_wb, b_ht, b_ot, b_ps = bufs(2), bufs(GRP), bufs(2), bufs(4), bufs(8)
        nld = 0
        nht = 0
        no = 0
        for g0 in range(0, NB, GRP):
            nb = min(GRP, NB - g0)
            for b in range(nb):
                i = nld % 2
                nld += 1
                s.dma("sp", wf[i][:], W[g0 + b], writes=[b_wf[i]])
                ce = "act" if b % 2 == 0 else "pool"
                if ce == "act":
                    s.op("act", lambda e: e.copy(out=wb[b][:], in_=wf[i][:]), reads=[b_wf[i]], writes=[b_wb[b]])
                else:
                    s.op("pool", lambda e: e.tensor_copy(out=wb[b][:], in_=wf[i][:]), reads=[b_wf[i]], writes=[b_wb[b]])
            for t0 in range(0, T, NT):
                h = nht % 2
                nht += 1
                dma_rows(s, "act", ht[h], hT[:, t0:t0 + NT], KC, writes=[b_ht[h]])
                for b in range(nb):
                    p = ps[b]
                    for c in range(KC):
                        s.op("pe", lambda e: e.matmul(p[:], lhsT=wb[b][:, c, :], rhs=ht[h][:, c, :],
                                                     start=(c == 0), stop=(c == KC - 1)),
                             reads=[b_wb[b], b_ht[h]], writes=[b_ps[b]], inc=(c == KC - 1))
                    o = no % 4
                    no += 1
                    if b % 2 == 0:
                        s.op("dve", lambda e: e.tensor_copy(out=ot[o][:], in_=p[:]), reads=[b_ps[b]], writes=[b_ot[o]])
                    else:
                        s.op("act", lambda e: e.copy(out=ot[o][:], in_=p[:]), reads=[b_ps[b]], writes=[b_ot[o]])
                    pt_t, pt_r = PT(g0 + b)
                    s.dma("sp", pt_t[pt_r:pt_r + 128, t0:t0 + NT], ot[o][:], reads=[b_ot[o]])
        s.barrier()


def cast_dram(k, dst, src, n_elems, q="pool", width=1024):
    s = k.s
    ROW = width
    assert n_elems % ROW == 0
    rows = n_elems // ROW
    CH = 2048
    for r0 in range(0, rows, CH):
        r1 = min(rows, r0 + CH)
        s.dma(q, dst[r0:r1, :], src[r0:r1, :])


def stage_merge(k, hT, yT, br_kc, Wg, Wbr, Wo, xT, xnT, D, T):
    s = k.s
    KC = D // 128
    OB = D // 128
    YC = sum(br_kc)
    yoff = [sum(br_kc[:i]) for i in range(len(br_kc))]
    NBR = len(br_kc)
    with contextlib.ExitStack() as st:
        ht = k.sb(st, [128, KC, NT], BF16)
        yt = k.sb(st, [128, YC, NT], BF16)
        mg = k.sb(st, [128, KC, NT], BF16)
        wg = [k.sb(st, [128, KC, 128], BF16) for _ in range(3)]
        wbr = [k.sb(st, [128, max(br_kc), 128], BF16) for _ in range(3)]
        gs = [k.sb(st, [128, NT], F32) for _ in range(2)]
        acc = [k.sb(st, [128, NT], F32) for _ in range(2)]
        tmp = [k.sb(st, [128, NT], F32) for _ in range(2)]
        xt = [k.sb(st, [128, NT], F32) for _ in range(2)]
        xn = [k.sb(st, [128, NT], F32) for _ in range(2)]
        psg = [k.ps(st, [128, NT]) for _ in range(2)]
        psp = [k.ps(st, [128, NT]) for _ in range(2)]
        pso = [k.ps(st, [128, NT]) for _ in range(2)]
        b_ht, b_yt = Buf(), Buf()
        b_mg = bufs(KC)
        b_wg, b_wbr, b_gs, b_acc, b_tmp, b_xt, b_xn = bufs(3), bufs(3), bufs(2), bufs(2), bufs(2), bufs(2), bufs(2)
        b_psg, b_psp, b_pso = bufs(2), bufs(2), bufs(2)
        nw = 0
        ng = 0
        na = 0
        for t0 in range(0, T, NT):
            dma_rows(s, "act", ht, hT[:, t0:t0 + NT], KC, writes=[b_ht])
            dma_rows(s, "act", yt, yT[:, t0:t0 + NT], YC, writes=[b_yt])
            for ob in range(OB):
                a = na % 2
                na += 1
                for br in range(NBR):
                    w = nw % 3
                    nw += 1
                    g = ng % 2
                    ng += 1
                    kcb = br_kc[br]
                    s.dma("sp", wg[w][:], Wg[br, ob], writes=[b_wg[w]])
                    s.dma("sp", wbr[w][:, 0:kcb, :], Wbr[br][ob], writes=[b_wbr[w]])
                    for c in range(KC):
                        s.op("pe", lambda e: e.matmul(psg[g][:], lhsT=wg[w][:, c, :], rhs=ht[:, c, :],
                                                     start=(c == 0), stop=(c == KC - 1)),
                             reads=[b_wg[w], b_ht], writes=[b_psg[g]], inc=(c == KC - 1))
                    for c in range(kcb):
                        s.op("pe", lambda e: e.matmul(psp[g][:], lhsT=wbr[w][:, c, :], rhs=yt[:, yoff[br] + c, :],
                                                     start=(c == 0), stop=(c == kcb - 1)),
                             reads=[b_wbr[w], b_yt], writes=[b_psp[g]], inc=(c == kcb - 1))
                    s.op("act", lambda e: e.activation(out=gs[g][:], in_=psg[g][:], func=AF.Sigmoid),
                         reads=[b_psg[g]], writes=[b_gs[g]])
                    last = (br == NBR - 1)
                    if br == 0:
                        dst, bdst = (mg[:, ob, :], b_mg[ob]) if last else (acc[a][:], b_acc[a])
                        s.op("dve", lambda e: e.tensor_tensor(out=dst, in0=psp[g][:], in1=gs[g][:], op=ALU.mult),
                             reads=[b_psp[g], b_gs[g]], writes=[bdst])
                    else:
                        s.op("dve", lambda e: e.tensor_tensor(out=tmp[g][:], in0=psp[g][:], in1=gs[g][:], op=ALU.mult),
                             reads=[b_psp[g], b_gs[g]], writes=[b_tmp[g]])
                        if last:
                            s.op("pool", lambda e: e.tensor_tensor(out=mg[:, ob, :], in0=acc[a][:], in1=tmp[g][:], op=ALU.add),
                                 reads=[b_acc[a], b_tmp[g]], writes=[b_mg[ob]])
                        else:
                            s.op("pool", lambda e: e.tensor_tensor(out=acc[a][:], in0=acc[a][:], in1=tmp[g][:], op=ALU.add),
                                 reads=[b_acc[a], b_tmp[g]], writes=[b_acc[a]])
            for ob in range(OB):
                w = nw % 3
                nw += 1
                g = ng % 2
                ng += 1
                s.dma("sp", wg[w][:], Wo[ob], writes=[b_wg[w]])
                s.dma("act", xt[g][:], xT[ob * 128:(ob + 1) * 128, t0:t0 + NT], writes=[b_xt[g]])
                for c in range(KC):
                    s.op("pe", lambda e: e.matmul(pso[g][:], lhsT=wg[w][:, c, :], rhs=mg[:, c, :],
                                                 start=(c == 0), stop=(c == KC - 1)),
                         reads=[b_wg[w], b_mg[c]], writes=[b_pso[g]], inc=(c == KC - 1))
                s.op("dve", lambda e: e.tensor_tensor(out=xn[g][:], in0=pso[g][:], in1=xt[g][:], op=ALU.add),
                     reads=[b_pso[g], b_xt[g]], writes=[b_xn[g]])
                s.dma("sp", xnT[ob * 128:(ob + 1) * 128, t0:t0 + NT], xn[g][:], reads=[b_xn[g]])
        s.barrier()


def stage_lru(k, PT, row_ax, row_ag, yT, row_y, prm, T, NBLK):
    s = k.s
    NCH = NBLK * 2
    with contextlib.ExitStack() as st:
        cw = k.sb(st, [128, NCH, 4], F32)
        cb, ba, bx, lam, c1, c2, tt = [k.sb(st, [128, NCH], F32) for _ in range(7)]
        b_prm = Buf()
        for dst, nm in ((cw, "conv_w"), (cb, "conv_b"), (ba, "ba"), (bx, "bx"), (lam, "lam")):
            s.dma("sp", dst[:], prm[nm], writes=[b_prm])
        s.op("act", lambda e: e.activation(out=tt[:], in_=lam[:], func=AF.Exp, scale=-1.0), reads=[b_prm], writes=[b_prm])
        s.op("act", lambda e: e.activation(out=tt[:], in_=tt[:], func=AF.Ln, bias=1.0), reads=[b_prm], writes=[b_prm])
        s.op("dve", lambda e: e.tensor_scalar(out=c1[:], in0=tt[:], scalar1=-8.0, scalar2=None, op0=ALU.mult), reads=[b_prm], writes=[b_prm])
        s.op("dve", lambda e: e.tensor_scalar(out=c2[:], in0=tt[:], scalar1=-16.0, scalar2=None, op0=ALU.mult), reads=[b_prm], writes=[b_prm])
        waf = k.sb(st, [128, 2, 2, 128], F32)
        wab = [k.sb(st, [128, 2, 2, 128], BF16) for _ in range(2)]
        xin = [k.sb(st, [128, 3 + NT], F32) for _ in range(2)]
        u = [k.sb(st, [128, NT], F32) for _ in range(2)]
        ub = [k.sb(st, [128, NT], BF16) for _ in range(2)]
        rr, ii, aa, a2, mm, bt, gt, sg = [k.sb(st, [128, NT], F32) for _ in range(8)]
        hs = [[k.sb(st, [128, NT], F32) for _ in range(2)] for _ in range(2)]
        yo = [k.sb(st, [128, NT], BF16) for _ in range(2)]
        psr, psi = k.ps(st, [128, NT]), k.ps(st, [128, NT])
        b_waf, b_psr, b_psi, b_rr, b_ii, b_aa, b_a2, b_mm, b_bt, b_gt, b_sg = bufs(11)
        b_wab, b_xin, b_u, b_ub, b_yo = bufs(2), bufs(2), bufs(2), bufs(2), bufs(2)
        b_hs = [bufs(2), bufs(2)]
        for nb in range(NBLK):
            for wi, nm in enumerate(("wa", "wx")):
                s.dma("sp", waf[:], prm[nm][nb].rearrange("o p c m -> p o c m"), writes=[b_waf])
                s.op("act", lambda e: e.copy(out=wab[wi][:], in_=waf[:]), reads=[b_waf], writes=[b_wab[wi]])
            for ti, t0 in enumerate(range(0, T, NT)):
                par = ti % 2
                for kc in range(2):
                    ch = nb * 2 + kc
                    if ti == 0:
                        s.op("pool", lambda e: e.memset(xin[kc][:, 0:3], 0.0), writes=[b_xin[kc]])
                    else:
                        s.op("pool", lambda e: e.tensor_copy(out=xin[kc][:, 0:3], in_=xin[kc][:, NT:NT + 3]),
                             reads=[b_xin[kc]], writes=[b_xin[kc]])
                    s.dma("sp", xin[kc][:, 3:3 + NT], PT[row_ax + ch * 128: row_ax + (ch + 1) * 128, t0:t0 + NT],
                          reads=[b_xin[kc]], writes=[b_xin[kc]])
                    s.op("dve", lambda e: e.tensor_scalar(out=u[kc][:], in0=xin[kc][:, 3:3 + NT], scalar1=cw[:, ch, 3:4],
                                                          scalar2=cb[:, ch:ch + 1], op0=ALU.mult, op1=ALU.add),
                         reads=[b_xin[kc], b_prm], writes=[b_u[kc]])
                    for j in (2, 1, 0):
                        s.op("dve", lambda e: e.scalar_tensor_tensor(out=u[kc][:], in0=xin[kc][:, j:j + NT], scalar=cw[:, ch, j:j + 1],
                                                                     in1=u[kc][:], op0=ALU.mult, op1=ALU.add),
                             reads=[b_xin[kc], b_prm, b_u[kc]], writes=[b_u[kc]])
                    s.op("act", lambda e: e.copy(out=ub[kc][:], in_=u[kc][:]), reads=[b_u[kc]], writes=[b_ub[kc]])
                for oc in range(2):
                    ch = nb * 2 + oc
                    for kc in range(2):
                        s.op("pe", lambda e: e.matmul(psr[:], lhsT=wab[0][:, oc, kc, :], rhs=ub[kc][:], start=(kc == 0), stop=(kc == 1)),
                             reads=[b_wab[0], b_ub[kc]], writes=[b_psr], inc=(kc == 1))
                    for kc in range(2):
                        s.op("pe", lambda e: e.matmul(psi[:], lhsT=wab[1][:, oc, kc, :], rhs=ub[kc][:], start=(kc == 0), stop=(kc == 1)),
                             reads=[b_wab[1], b_ub[kc]], writes=[b_psi], inc=(kc == 1))
                    s.op("act", lambda e: e.activation(out=rr[:], in_=psr[:], func=AF.Sigmoid, bias=ba[:, ch:ch + 1]),
                         reads=[b_psr, b_prm], writes=[b_rr])
                    s.op("act", lambda e: e.activation(out=ii[:], in_=psi[:], func=AF.Sigmoid, bias=bx[:, ch:ch + 1]),
                         reads=[b_psi, b_prm], writes=[b_ii])
                    s.op("act", lambda e: e.activation(out=aa[:], in_=rr[:], func=AF.Exp, scale=c1[:, ch:ch + 1]),
                         reads=[b_rr, b_prm], writes=[b_aa])
                    s.op("act", lambda e: e.activation(out=a2[:], in_=rr[:], func=AF.Exp, scale=c2[:, ch:ch + 1]),
                         reads=[b_rr, b_prm], writes=[b_a2])
                    s.op("act", lambda e: e.activation(out=mm[:], in_=a2[:], func=AF.Sqrt, scale=-1.0, bias=1.0),
                         reads=[b_a2], writes=[b_mm])
                    s.op("dve", lambda e: e.tensor_tensor(out=bt[:], in0=mm[:], in1=ii[:], op=ALU.mult),
                         reads=[b_mm, b_ii], writes=[b_bt])
                    s.op("dve", lambda e: e.tensor_tensor(out=bt[:], in0=bt[:], in1=u[oc][:], op=ALU.mult),
                         reads=[b_bt, b_u[oc]], writes=[b_bt])
                    init = 0.0 if ti == 0 else hs[oc][1 - par][:, NT - 1:NT]
                    s.op("dve", lambda e: e.tensor_tensor_scan(out=hs[oc][par][:], data0=aa[:], data1=bt[:], initial=init,
                                                               op0=ALU.mult, op1=ALU.add),
                         reads=[b_aa, b_bt] + ([] if ti == 0 else [b_hs[oc][1 - par]]), writes=[b_hs[oc][par]])
                    s.dma("act", gt[:], PT[row_ag + ch * 128: row_ag + (ch + 1) * 128, t0:t0 + NT], writes=[b_gt])
                    s.op("act", lambda e: e.activation(out=sg[:], in_=gt[:], func=AF.Silu), reads=[b_gt], writes=[b_sg])
                    s.op("dve", lambda e: e.tensor_tensor(out=yo[oc][:], in0=hs[oc][par][:], in1=sg[:], op=ALU.mult),
                         reads=[b_hs[oc][par], b_sg], writes=[b_yo[oc]])
                    s.dma("sp", yT[row_y + ch * 128: row_y + (ch + 1) * 128, t0:t0 + NT], yo[oc][:], reads=[b_yo[oc]])
        s.barrier()


def stage_xattn(k, PT, row_q, row_g, yT, row_y, memnT, Wk, Wv, ident_d, T, D, M, H):
    s = k.s
    KC = D // 128
    MC = M // 128
    sc = 128 ** -0.5
    with contextlib.ExitStack() as st:
        ident = k.sb(st, [128, 128], F32)
        memn = k.sb(st, [128, KC, M], BF16)
        kT = k.sb(st, [128, H, M], BF16)
        vt = k.sb(st, [128, MC, H * 128], BF16)
        b_id, b_memn, b_kT, b_vt = bufs(4)
        s.dma("sp", ident[:], ident_d, writes=[b_id])
        dma_rows(s, "sp", memn, memnT, KC, writes=[b_memn])
        with contextlib.ExitStack() as st1:
            wf = k.sb(st1, [128, KC, 128], F32)
            wb = k.sb(st1, [128, KC, 128], BF16)
            vf = [k.sb(st1, [128, H * 128], F32) for _ in range(2)]
            vb = [k.sb(st1, [128, H * 128], BF16) for _ in range(2)]
            psk = k.ps(st1, [128, M])
            psv = [k.ps(st1, [128, H * 128]) for _ in range(MC)]
            b_wf, b_wb, b_psk = bufs(3)
            b_vf, b_vb, b_psv = bufs(2), bufs(2), bufs(MC)
            for hd in range(H):
                s.dma("sp", wf[:], Wk[hd], writes=[b_wf])
                s.op("act", lambda e: e.copy(out=wb[:], in_=wf[:]), reads=[b_wf], writes=[b_wb])
                for c in range(KC):
                    s.op("pe", lambda e: e.matmul(psk[:], lhsT=wb[:, c, :], rhs=memn[:, c, :], start=(c == 0), stop=(c == KC - 1)),
                         reads=[b_wb, b_memn], writes=[b_psk], inc=(c == KC - 1))
                s.op("dve", lambda e: e.tensor_copy(out=kT[:, hd, :], in_=psk[:]), reads=[b_psk], writes=[b_kT])
            for c in range(KC):
                i = c % 2
                s.dma("sp", vf[i][:], Wv[c], writes=[b_vf[i]])
                s.op("act", lambda e: e.copy(out=vb[i][:], in_=vf[i][:]), reads=[b_vf[i]], writes=[b_vb[i]])
                for mc in range(MC):
                    s.op("pe", lambda e: e.matmul(psv[mc][:], lhsT=memn[:, c, mc * 128:(mc + 1) * 128], rhs=vb[i][:],
                                                 start=(c == 0), stop=(c == KC - 1)),
                         reads=[b_vb[i], b_memn], writes=[b_psv[mc]], inc=True)
            for mc in range(MC):
                s.op("dve", lambda e: e.tensor_copy(out=vt[:, mc, :], in_=psv[mc][:]), reads=[b_psv[mc]], writes=[b_vt])
            s.barrier()
        qf = k.sb(st, [128, H, NT], F32)
        qb = k.sb(st, [128, H, NT], BF16)
        gf = k.sb(st, [128, H, NT], F32)
        sg = k.sb(st, [128, H, NT], F32)
        pf = [k.sb(st, [128, M], F32) for _ in range(2)]
        pn = [k.sb(st, [128, M], F32) for _ in range(2)]
        pT = [k.sb(st, [128, MC, 128], BF16) for _ in range(2)]
        mx, nmx, rsum, rinv = [[k.sb(st, [128, 1], F32) for _ in range(2)] for _ in range(4)]
        yo = [k.sb(st, [128, NT], BF16) for _ in range(2)]
        pss = [k.ps(st, [128, M]) for _ in range(2)]
        pst = [k.ps(st, [128, MC, 128]) for _ in range(2)]
        pso = [k.ps(st, [128, 128]) for _ in range(2)]
        b_qf, b_qb, b_gf, b_sg = bufs(4)
        b_pf, b_pn, b_pT, b_mx, b_nmx, b_rsum, b_rinv, b_yo, b_pss, b_pst, b_pso = [bufs(2) for _ in range(11)]
        it = 0
        for t0 in range(0, T, NT):
            s.dma("sp", qf[:], PT[row_q:row_q + H * 128, t0:t0 + NT].rearrange("(h p) t -> p h t", p=128), writes=[b_qf])
            s.dma("act", gf[:], PT[row_g:row_g + H * 128, t0:t0 + NT].rearrange("(h p) t -> p h t", p=128), writes=[b_gf])
            s.op("act", lambda e: e.copy(out=qb[:], in_=qf[:]), reads=[b_qf], writes=[b_qb])
            s.op("act", lambda e: e.activation(out=sg[:], in_=gf[:], func=AF.Silu), reads=[b_gf], writes=[b_sg])
            for hd in range(H):
                yi = hd % 2
                for tb in range(NT // 128):
                    i = it % 2
                    it += 1
                    tsl = slice(tb * 128, (tb + 1) * 128)
                    s.op("pe", lambda e: e.matmul(pss[i][:], lhsT=qb[:, hd, tsl], rhs=kT[:, hd, :], start=True, stop=True),
                         reads=[b_qb, b_kT], writes=[b_pss[i]])
                    s.op("dve", lambda e: e.tensor_reduce(out=mx[i][:], in_=pss[i][:], axis=AX.X, op=ALU.max),
                         reads=[b_pss[i]], writes=[b_mx[i]])
                    s.op("dve", lambda e: e.tensor_scalar(out=nmx[i][:], in0=mx[i][:], scalar1=-sc, scalar2=None, op0=ALU.mult),
                         reads=[b_mx[i]], writes=[b_nmx[i]])
                    s.op("act", lambda e: e.activation(out=pf[i][:], in_=pss[i][:], func=AF.Exp, scale=sc, bias=nmx[i][:],
                                                       accum_out=rsum[i][:]),
                         reads=[b_pss[i], b_nmx[i]], writes=[b_pf[i], b_rsum[i]])
                    s.op("dve", lambda e: e.reciprocal(out=rinv[i][:], in_=rsum[i][:]), reads=[b_rsum[i]], writes=[b_rinv[i]])
                    s.op("dve", lambda e: e.tensor_scalar(out=pn[i][:], in0=pf[i][:], scalar1=rinv[i][:], scalar2=None, op0=ALU.mult),
                         reads=[b_pf[i], b_rinv[i]], writes=[b_pn[i]])
                    for mc in range(MC):
                        s.op("pe", lambda e: e.transpose(pst[i][:, mc, :], pn[i][:, mc * 128:(mc + 1) * 128], ident[:]),
                             reads=[b_pn[i], b_id], writes=[b_pst[i]], inc=True)
                    s.op("act", lambda e: e.copy(out=pT[i][:], in_=pst[i][:]), reads=[b_pst[i]], writes=[b_pT[i]])
                    for mc in range(MC):
                        s.op("pe", lambda e: e.matmul(pso[i][:], lhsT=vt[:, mc, hd * 128:(hd + 1) * 128], rhs=pT[i][:, mc, :],
                                                     start=(mc == 0), stop=(mc == MC - 1)),
                             reads=[b_vt, b_pT[i]], writes=[b_pso[i]], inc=(mc == MC - 1))
                    s.op("dve", lambda e: e.tensor_tensor(out=yo[yi][:, tsl], in0=pso[i][:], in1=sg[:, hd, tsl], op=ALU.mult),
                         reads=[b_pso[i], b_sg], writes=[b_yo[yi]])
                s.dma("sp", yT[row_y + hd * 128: row_y + (hd + 1) * 128, t0:t0 + NT], yo[yi][:], reads=[b_yo[yi]])
        s.barrier()


LCH = 64
NEG = -1.0e30
RWKV_G = 2
RWKV_L = 128


def stage_mlstm(k, PT, rows, yT, row_y, prm, cst, T, HC):
    s = k.s
    NCK = T // LCH
    QC = 2 * HC
    with contextlib.ExitStack() as st:
        ident = k.sb(st, [128, 128], F32)
        sel = k.sb(st, [HC, HC, 128], F32)
        i4 = k.sb(st, [HC, HC], F32)
        negmask = k.sb(st, [64, 64], F32)
        ones = k.sb(st, [128, 128], F32)
        cw = k.sb(st, [128, 2 * QC, 4], F32)
        cb = k.sb(st, [128, 2 * QC], F32)
        gnw = k.sb(st, [128, HC * 4], F32)
        bi = k.sb(st, [HC, 1], F32)
        bfn = k.sb(st, [HC, 1], F32)
        b_c = Buf()
        for dst, src in ((ident, cst["ident"]), (sel, cst["sel"]), (i4, cst["i4"]), (negmask, cst["negmask"]),
                         (cw, prm["conv_w"]), (cb, prm["conv_b"]), (gnw, prm["gn_w"]), (bi, prm["b_i"]), (bfn, prm["b_f"])):
            s.dma("sp", dst[:], src, writes=[b_c])
        s.op("pool", lambda e: e.memset(ones[:], 1.0), writes=[b_c])
        s.op("dve", lambda e: e.tensor_scalar(out=bfn[:], in0=bfn[:], scalar1=-1.0, scalar2=None, op0=ALU.mult), reads=[b_c], writes=[b_c])
        gi = k.sb(st, [HC, T], F32)
        gf = k.sb(st, [HC, T], F32)
        gm = k.sb(st, [HC, T], F32)
        ge = k.sb(st, [HC, T], F32)
        b_g = Buf()
        s.dma("sp", gi[:], PT[rows["ifg"]:rows["ifg"] + HC, :], writes=[b_g])
        s.dma("sp", gf[:], PT[rows["ifg"] + HC:rows["ifg"] + 2 * HC, :], writes=[b_g])
        s.op("act", lambda e: e.activation(out=gf[:], in_=gf[:], func=AF.Exp, scale=-1.0, bias=bfn[:]), reads=[b_g, b_c], writes=[b_g])
        s.op("act", lambda e: e.activation(out=gf[:], in_=gf[:], func=AF.Ln, bias=1.0), reads=[b_g], writes=[b_g])
        s.op("dve", lambda e: e.tensor_scalar(out=gf[:], in0=gf[:], scalar1=-1.0, scalar2=None, op0=ALU.mult), reads=[b_g], writes=[b_g])
        s.op("pool", lambda e: e.memset(gm[:], 1.0), writes=[b_g])
        s.op("dve", lambda e: e.tensor_tensor_scan(out=gf[:], data0=gm[:], data1=gf[:], initial=0.0, op0=ALU.mult, op1=ALU.add),
             reads=[b_g], writes=[b_g])
        s.op("dve", lambda e: e.scalar_tensor_tensor(out=gi[:], in0=gi[:], scalar=bi[:], in1=gf[:], op0=ALU.add, op1=ALU.subtract),
             reads=[b_g, b_c], writes=[b_g])
        s.op("dve", lambda e: e.tensor_tensor_scan(out=gm[:], data0=gi[:], data1=gi[:], initial=NEG, op0=ALU.max, op1=ALU.max),
             reads=[b_g], writes=[b_g])
        s.op("dve", lambda e: e.tensor_tensor(out=ge[:], in0=gf[:], in1=gm[:], op=ALU.add), reads=[b_g], writes=[b_g])
        cols = k.sb(st, [64, NCK, 2 * HC], F32)
        b_cols = Buf()
        with contextlib.ExitStack() as st1:
            psc = [k.ps(st1, [64, 8, 2 * HC]) for _ in range(2)]
            b_psc = bufs(2)
            for c8 in range(0, NCK, 8):
                i = (c8 // 8) % 2
                for cc in range(8):
                    c = c8 + cc
                    s.op("pe", lambda e: e.matmul(psc[i][:, cc, 0:HC], lhsT=gi[:, c * 64:(c + 1) * 64], rhs=i4[:, 0:HC], start=True, stop=True),
                         reads=[b_g, b_c], writes=[b_psc[i]])
                    s.op("pe", lambda e: e.matmul(psc[i][:, cc, HC:2 * HC], lhsT=ge[:, c * 64:(c + 1) * 64], rhs=i4[:, 0:HC], start=True, stop=True),
                         reads=[b_g, b_c], writes=[b_psc[i]])
                s.op("dve", lambda e: e.tensor_copy(out=cols[:, c8:c8 + 8, :], in_=psc[i][:]), reads=[b_psc[i]], writes=[b_cols])
            s.op("act", lambda e: e.activation(out=cols[:, :, HC:2 * HC], in_=cols[:, :, HC:2 * HC], func=AF.Exp, scale=-1.0),
                 reads=[b_cols], writes=[b_cols])
            s.barrier()
        xin = k.sb(st, [128, 4, 3 + NT], F32)
        qk = k.sb(st, [128, 4, NT], F32)
        vT = k.sb(st, [128, 4, NT], F32)
        oT = k.sb(st, [128, 4, NT], F32)
        gT = k.sb(st, [128, 4, NT], F32)
        negMb = k.sb(st, [128, NT], F32)
        ms = k.sb(st, [128, NT // LCH + 1], F32)
        keep = k.sb(st, [128, 1], F32)
        ho = k.sb(st, [128, 4, NT], F32)
        Cst = k.sb(st, [128, 2, 512], F32)
        nst = k.sb(st, [128, 2], F32)
        ktok = k.sb(st, [64, 256], F32)
        khat = k.sb(st, [64, 256], F32)
        vtok = k.sb(st, [64, 512], F32)
        wk = k.sb(st, [64, 1], F32)
        argE = k.sb(st, [64, 64], F32)
        ET = k.sb(st, [64, 64], F32)
        scT = k.sb(st, [64, 64], F32)
        wint = k.sb(st, [128, 64], F32)
        qt = k.sb(st, [128, 2, 64], F32)
        den = k.sb(st, [64, 1], F32)
        rden = k.sb(st, [64, 1], F32)
        htok = k.sb(st, [64, 512], F32)
        mean = k.sb(st, [128, NT], F32)
        yc = k.sb(st, [128, 4, NT], F32)
        ysq = k.sb(st, [128, NT], F32)
        rstd = k.sb(st, [128, NT], F32)
        yo = [k.sb(st, [128, NT], BF16) for _ in range(2)]
        ps_b = k.ps(st, [128, NT])
        ps_t = k.ps(st, [64, 512])
        ps_s = k.ps(st, [64, 64])
        ps_n = k.ps(st, [64, 512])
        ps_d = k.ps(st, [64, 8])
        ps_c = k.ps(st, [128, 512])
        ps_h = k.ps(st, [128, 4, 64])
        ps_m = k.ps(st, [128, 8])
        (b_xin, b_qk, b_vT, b_oT, b_gT, b_negMb, b_ms, b_keep, b_ho, b_C, b_n, b_ktok, b_khat, b_vtok, b_wk, b_argE, b_ET, b_scT,
         b_wint, b_qt, b_den, b_rden, b_htok, b_mean, b_yc, b_ysq, b_rstd, b_psb, b_pst, b_pss, b_psn, b_psd, b_psc2, b_psh, b_psm) = bufs(35)
        b_yo = bufs(2)
        for h in range(HC):
            qrows = [rows["q"] + h * 256, rows["q"] + h * 256 + 128, rows["k"] + h * 256, rows["k"] + h * 256 + 128]
            cidx = [h * 2, h * 2 + 1, QC + h * 2, QC + h * 2 + 1]
            s.op("pool", lambda e: e.memset(Cst[:], 0.0), reads=[b_C], writes=[b_C])
            s.op("pool", lambda e: e.memset(nst[:], 0.0), reads=[b_n], writes=[b_n])
            s.op("pool", lambda e: e.memset(ms[:, 0:1], NEG), reads=[b_ms], writes=[b_ms])
            for ti, t0 in enumerate(range(0, T, NT)):
                for j in range(4):
                    if ti == 0:
                        s.op("pool", lambda e: e.memset(xin[:, j, 0:3], 0.0), reads=[b_xin], writes=[b_xin])
                    else:
                        s.op("pool", lambda e: e.tensor_copy(out=xin[:, j, 0:3], in_=xin[:, j, NT:NT + 3]), reads=[b_xin], writes=[b_xin])
                for j in range(4):
                    s.dma("sp", xin[:, j, 3:3 + NT], PT[qrows[j]:qrows[j] + 128, t0:t0 + NT], reads=[b_xin], writes=[b_xin])
                s.dma("act", vT[:], PT[rows["v"] + h * 512: rows["v"] + (h + 1) * 512, t0:t0 + NT].rearrange("(c p) t -> p c t", p=128), writes=[b_vT])
                s.dma("act", oT[:], PT[rows["o"] + h * 512: rows["o"] + (h + 1) * 512, t0:t0 + NT].rearrange("(c p) t -> p c t", p=128), writes=[b_oT])
                s.dma("act", gT[:], PT[rows["g"] + h * 512: rows["g"] + (h + 1) * 512, t0:t0 + NT].rearrange("(c p) t -> p c t", p=128), writes=[b_gT])
                for j in range(4):
                    ci = cidx[j]
                    s.op("dve", lambda e: e.tensor_scalar(out=qk[:, j, :], in0=xin[:, j, 3:3 + NT], scalar1=cw[:, ci, 3:4],
                                                          scalar2=cb[:, ci:ci + 1], op0=ALU.mult, op1=ALU.add),
                         reads=[b_xin, b_c, b_qk], writes=[b_qk])
                    for jj in (2, 1, 0):
                        s.op("dve", lambda e: e.scalar_tensor_tensor(out=qk[:, j, :], in0=xin[:, j, jj:jj + NT], scalar=cw[:, ci, jj:jj + 1],
                                                                     in1=qk[:, j, :], op0=ALU.mult, op1=ALU.add),
                             reads=[b_xin, b_c, b_qk], writes=[b_qk])
                s.op("act", lambda e: e.activation(out=qk[:], in_=qk[:], func=AF.Silu), reads=[b_qk], writes=[b_qk])
                s.op("dve", lambda e: e.tensor_scalar(out=qk[:, 2:4, :], in0=qk[:, 2:4, :], scalar1=256 ** -0.5, scalar2=None, op0=ALU.mult),
                     reads=[b_qk], writes=[b_qk])
                s.op("act", lambda e: e.activation(out=oT[:], in_=oT[:], func=AF.Sigmoid), reads=[b_oT], writes=[b_oT])
                s.op("act", lambda e: e.activation(out=gT[:], in_=gT[:], func=AF.Silu), reads=[b_gT], writes=[b_gT])
                s.op("pe", lambda e: e.matmul(ps_b[:], lhsT=sel[:, h, :], rhs=gm[:, t0:t0 + NT], start=True, stop=True),
                     reads=[b_c, b_g, b_psb], writes=[b_psb])
                if ti > 0:
                    s.op("dve", lambda e: e.tensor_scalar(out=ms[:, 0:1], in0=negMb[:, NT - 1:NT], scalar1=-1.0, scalar2=None, op0=ALU.mult),
                         reads=[b_negMb, b_ms], writes=[b_ms])
                s.op("act", lambda e: e.activation(out=negMb[:], in_=ps_b[:], func=AF.Copy, scale=-1.0), reads=[b_psb, b_negMb], writes=[b_negMb])
                nck = NT // LCH
                s.op("dve", lambda e: e.tensor_scalar(out=ms[:, 1:nck + 1], in0=negMb[:, LCH - 1:NT:LCH], scalar1=-1.0, scalar2=None, op0=ALU.mult),
                     reads=[b_negMb, b_ms], writes=[b_ms])
                for cc in range(nck):
                    c = ti * nck + cc
                    csl = slice(cc * LCH, (cc + 1) * LCH)
                    for j in range(2):
                        s.op("pe", lambda e: e.transpose(ps_t[:, j * 128:(j + 1) * 128], qk[:, 2 + j, csl], ident[:]),
                             reads=[b_qk, b_c, b_pst], writes=[b_pst])
                    s.op("act", lambda e: e.copy(out=ktok[:], in_=ps_t[:, 0:256]), reads=[b_pst, b_ktok], writes=[b_ktok])
                    for j in range(4):
                        s.op("pe", lambda e: e.transpose(ps_t[:, j * 128:(j + 1) * 128], vT[:, j, csl], ident[:]),
                             reads=[b_vT, b_c, b_pst], writes=[b_pst])
                    s.op("act", lambda e: e.copy(out=vtok[:], in_=ps_t[:]), reads=[b_pst, b_vtok], writes=[b_vtok])
                    for j in range(2):
                        s.op("pe", lambda e: e.matmul(ps_s[:], lhsT=qk[:, 2 + j, csl], rhs=qk[:, j, csl], start=(j == 0), stop=(j == 1)),
                             reads=[b_qk, b_pss], writes=[b_pss])
                    s.op("dve", lambda e: e.tensor_tensor(out=argE[:], in0=negMb[0:64, csl], in1=negmask[:], op=ALU.add),
                         reads=[b_negMb, b_c, b_argE], writes=[b_argE])
                    s.op("act", lambda e: e.activation(out=ET[:], in_=argE[:], func=AF.Exp, bias=cols[:, c, h:h + 1]),
                         reads=[b_argE, b_cols, b_ET], writes=[b_ET])
                    s.op("dve", lambda e: e.tensor_tensor(out=scT[:], in0=ps_s[:], in1=ET[:], op=ALU.mult),
                         reads=[b_pss, b_ET, b_scT], writes=[b_scT])
                    s.op("act", lambda e: e.activation(out=wint[:], in_=negMb[:, csl], func=AF.Exp, bias=ms[:, cc:cc + 1]),
                         reads=[b_negMb, b_ms, b_wint], writes=[b_wint])
                    for j in range(2):
                        s.op("dve", lambda e: e.tensor_tensor(out=qt[:, j, :], in0=qk[:, j, csl], in1=wint[:], op=ALU.mult),
                             reads=[b_qk, b_wint, b_qt], writes=[b_qt])
                    for j in range(2):
                        s.op("pe", lambda e: e.matmul(ps_n[:], lhsT=qt[:, j, :], rhs=Cst[:, j, :], start=(j == 0), stop=False),
                             reads=[b_qt, b_C, b_psn], writes=[b_psn], inc=False)
                    s.op("pe", lambda e: e.matmul(ps_n[:], lhsT=scT[:], rhs=vtok[:], start=False, stop=True),
                         reads=[b_scT, b_vtok, b_psn], writes=[b_psn])
                    for j in range(2):
                        s.op("pe", lambda e: e.matmul(ps_d[:, 0:1], lhsT=qt[:, j, :], rhs=nst[:, j:j + 1], start=(j == 0), stop=False),
                             reads=[b_qt, b_n, b_psd], writes=[b_psd], inc=False)
                    s.op("pe", lambda e: e.matmul(ps_d[:, 0:1], lhsT=scT[:], rhs=ones[0:64, 0:1], start=False, stop=True),
                         reads=[b_scT, b_c, b_psd], writes=[b_psd])
                    s.op("act", lambda e: e.activation(out=den[:], in_=ps_d[:, 0:1], func=AF.Abs),
                         reads=[b_psd, b_den], writes=[b_den])
                    s.op("dve", lambda e: e.tensor_tensor(out=den[:], in0=den[:], in1=cols[:, c, HC + h:HC + h + 1], op=ALU.max),
                         reads=[b_den, b_cols], writes=[b_den])
                    s.op("dve", lambda e: e.reciprocal(out=rden[:], in_=den[:]), reads=[b_den, b_rden], writes=[b_rden])
                    s.op("act", lambda e: e.activation(out=htok[:], in_=ps_n[:], func=AF.Copy, scale=rden[:]),
                         reads=[b_psn, b_rden, b_htok], writes=[b_htok])
                    for j in range(4):
                        s.op("pe", lambda e: e.transpose(ps_h[:, j, :], htok[:, j * 128:(j + 1) * 128], ident[0:64, 0:64]),
                             reads=[b_htok, b_c, b_psh], writes=[b_psh])
                    s.op("dve", lambda e: e.tensor_tensor(out=ho[:, :, csl], in0=ps_h[:], in1=oT[:, :, csl], op=ALU.mult),
                         reads=[b_psh, b_oT, b_ho], writes=[b_ho])
                    s.op("act", lambda e: e.activation(out=keep[:], in_=negMb[:, (cc + 1) * LCH - 1:(cc + 1) * LCH], func=AF.Exp, bias=ms[:, cc:cc + 1]),
                         reads=[b_negMb, b_ms, b_keep], writes=[b_keep])
                    s.op("act", lambda e: e.activation(out=wk[:], in_=negMb[0:64, (cc + 1) * LCH - 1:(cc + 1) * LCH], func=AF.Exp, bias=cols[:, c, h:h + 1]),
                         reads=[b_negMb, b_cols, b_wk], writes=[b_wk])
                    s.op("dve", lambda e: e.tensor_scalar(out=khat[:], in0=ktok[:], scalar1=wk[:], scalar2=None, op0=ALU.mult),
                         reads=[b_ktok, b_wk, b_khat], writes=[b_khat])
                    for j in range(2):
                        s.op("pe", lambda e: e.matmul(ps_c[:], lhsT=khat[:, j * 128:(j + 1) * 128], rhs=vtok[:], start=True, stop=True),
                             reads=[b_khat, b_vtok, b_psc2], writes=[b_psc2])
                        s.op("dve", lambda e: e.scalar_tensor_tensor(out=Cst[:, j, :], in0=Cst[:, j, :], scalar=keep[:], in1=ps_c[:],
                                                                     op0=ALU.mult, op1=ALU.add),
                             reads=[b_C, b_keep, b_psc2], writes=[b_C])
                        s.op("pe", lambda e: e.matmul(ps_m[:, 0:1], lhsT=khat[:, j * 128:(j + 1) * 128], rhs=ones[0:64, 0:1], start=True, stop=True),
                             reads=[b_khat, b_c, b_psm], writes=[b_psm])
                        s.op("dve", lambda e: e.scalar_tensor_tensor(out=nst[:, j:j + 1], in0=nst[:, j:j + 1], scalar=keep[:], in1=ps_m[:, 0:1],
                                                                     op0=ALU.mult, op1=ALU.add),
                             reads=[b_n, b_keep, b_psm], writes=[b_n])
                for j in range(4):
                    s.op("pe", lambda e: e.matmul(ps_b[:], lhsT=ones[:], rhs=ho[:, j, :], start=(j == 0), stop=(j == 3)),
                         reads=[b_c, b_ho, b_psb], writes=[b_psb], inc=(j == 3))
                s.op("act", lambda e: e.activation(out=mean[:], in_=ps_b[:], func=AF.Copy, scale=1.0 / 512), reads=[b_psb, b_mean], writes=[b_mean])
                for j in range(4):
                    s.op("dve", lambda e: e.tensor_tensor(out=yc[:, j, :], in0=ho[:, j, :], in1=mean[:], op=ALU.subtract),
                         reads=[b_ho, b_mean, b_yc], writes=[b_yc])
                for j in range(4):
                    s.op("act", lambda e: e.activation(out=ysq[:], in_=yc[:, j, :], func=AF.Square), reads=[b_yc, b_ysq], writes=[b_ysq])
                    s.op("pe", lambda e: e.matmul(ps_b[:], lhsT=ones[:], rhs=ysq[:], start=(j == 0), stop=(j == 3)),
                         reads=[b_c, b_ysq, b_psb], writes=[b_psb])
                s.op("act", lambda e: e.activation(out=rstd[:], in_=ps_b[:], func=AF.Sqrt, scale=1.0 / 512, bias=1e-6),
                     reads=[b_psb, b_rstd], writes=[b_rstd])
                s.op("dve", lambda e: e.reciprocal(out=rstd[:], in_=rstd[:]), reads=[b_rstd], writes=[b_rstd])
                for j in range(4):
                    yi = j % 2
                    s.op("dve", lambda e: e.scalar_tensor_tensor(out=yc[:, j, :], in0=yc[:, j, :], scalar=gnw[:, h * 4 + j:h * 4 + j + 1],
                                                                 in1=rstd[:], op0=ALU.mult, op1=ALU.mult),
                         reads=[b_yc, b_c, b_rstd], writes=[b_yc])
                    s.op("dve", lambda e: e.tensor_tensor(out=yo[yi][:], in0=yc[:, j, :], in1=gT[:, j, :], op=ALU.mult),
                         reads=[b_yc, b_gT, b_yo[yi]], writes=[b_yo[yi]])
                    r0 = row_y + h * 512 + j * 128
                    s.dma("sp", yT[r0:r0 + 128, t0:t0 + NT], yo[yi][:], reads=[b_yo[yi]])
        s.barrier()


def stage_rwkv(k, PT, rows, yT, row_y, prm, cst, T, NH, GN_EPS):
    s = k.s
    G = RWKV_G
    assert G == 2
    NLVL = {64: 6, 128: 7}[RWKV_L]
    RL = RWKV_L
    NCK = NT // RL
    assert NH % G == 0 and T % NT == 0
    with contextlib.ExitStack() as st:
        ident = k.sb(st, [128, 128], F32)
        ones64 = k.sb(st, [64, 64], F32)
        rmask = k.sb(st, [64, NT], F32)
        maskA = k.sb(st, [RL, G, 2 * RL], F32)
        maskN = k.sb(st, [RL, G, RL], F32)
        P = {}
        for nm in ("mu_r", "mu_k", "mu_v", "w0", "a0", "k_k", "k_a", "r_k", "gn_w", "gn_b"):
            P[nm] = k.sb(st, [64, NH], F32)
        om = {nm: k.sb(st, [64, NH], F32) for nm in ("mu_r", "mu_k", "mu_v", "k_a")}
        nw0 = k.sb(st, [64, NH], F32)
        mud = {nm: k.sb(st, [96, 1], F32) for nm in ("mu_wd", "mu_ad")}
        omd = {nm: k.sb(st, [96, 1], F32) for nm in ("mu_wd", "mu_ad")}
        b_c = Buf()
        for dst, src in ((ident, cst["ident"]), (rmask, cst["resetmask"]), (maskA, cst["maskA"]), (maskN, cst["maskN"])):
            s.dma("sp", dst[:], src, writes=[b_c])
        for nm in P:
            s.dma("sp", P[nm][:], prm[nm], writes=[b_c])
        for nm in mud:
            s.dma("sp", mud[nm][:], prm[nm], writes=[b_c])
        s.op("pool", lambda e: e.memset(ones64[:], 1.0), writes=[b_c])
        for nm in om:
            s.op("dve", lambda e: e.tensor_scalar(out=om[nm][:], in0=P[nm][:], scalar1=-1.0, scalar2=1.0, op0=ALU.mult, op1=ALU.add),
                 reads=[b_c], writes=[b_c])
        for nm in omd:
            s.op("dve", lambda e: e.tensor_scalar(out=omd[nm][:], in0=mud[nm][:], scalar1=-1.0, scalar2=1.0, op0=ALU.mult, op1=ALU.add),
                 reads=[b_c], writes=[b_c])
        s.op("dve", lambda e: e.tensor_scalar(out=nw0[:], in0=P["w0"][:], scalar1=-1.0, scalar2=None, op0=ALU.mult), reads=[b_c], writes=[b_c])

        wup = k.sb(st, [96, G * 64], F32)
        aup = k.sb(st, [96, G * 64], F32)
        rawd = k.sb(st, [96, 1 + NT], F32)
        rawa = k.sb(st, [96, 1 + NT], F32)
        twd = k.sb(st, [96, NT], F32)
        adq = k.sb(st, [96, NT], F32)
        raw = {nm: k.sb(st, [64, G, 1 + NT], F32) for nm in ("r", "k", "v")}
        gate = k.sb(st, [64, G, NT], F32)
        (r_, k_, e2, c_, eP, eN, a_, kk, kkn, kmod, kka, t1, t2) = [[k.sb(st, [64, NT], F32) for _ in range(G)] for _ in range(13)]
        Vt = k.sb(st, [64, G, NT], F32)
        AR = k.sb(st, [64, G, NCK, 2, RL], F32)
        Bt = k.sb(st, [64, G, NT], F32)
        Kt = k.sb(st, [64, G, NT], F32)
        Bh = k.sb(st, [64, G, NT], F32)
        Kh = k.sb(st, [64, G, NT], F32)
        bonus = k.sb(st, [64, G, NT], F32)
        yfm = k.sb(st, [64, G, NT], F32)
        gL = k.sb(st, [64, G, NCK], F32)
        tokb = [k.sb(st, [RL, 3, G, 64], F32) for _ in range(2)]
        GAb = [k.sb(st, [RL, G, 2 * RL], F32) for _ in range(2)]
        GKb = [k.sb(st, [RL, G, 2 * RL], F32) for _ in range(2)]
        Pnb = [k.sb(st, [RL, G, RL], F32) for _ in range(2)]
        PPb = [[k.sb(st, [RL, G, 2, RL], F32) for _ in range(NLVL - 1)] for _ in range(2)]
        b_tokb, b_GAb, b_GKb, b_Pnb = bufs(2), bufs(2), bufs(2), bufs(2)
        b_PPb = [bufs(NLVL - 1), bufs(NLVL - 1)]
        U = k.sb(st, [RL, G, 64], F32)
        Ysb = k.sb(st, [RL, G, 64], F32)
        ST = k.sb(st, [64, G, 64], F32)
        yo = [k.sb(st, [64, NT], BF16) for _ in range(2)]
        bank = [k.ps(st, [128, 512]) for _ in range(8)]
        b_bank = bufs(8)
        ps_A, ps_B = bank[0], bank[1]
        ps_ga = bank[0][0:RL, 0:G * 2 * RL].rearrange("p (g x) -> p g x", g=G)
        ps_gk = bank[1][0:RL, 0:G * 2 * RL].rearrange("p (g x) -> p g x", g=G)
        ps_x1 = bank[2][0:RL, 0:2 * G * 64].rearrange("p (a g x) -> p a g x", a=2, g=G)
        ps_x2k = bank[3][0:RL, 0:G * 64].rearrange("p (g x) -> p g x", g=G)
        ps_gn = bank[3][0:RL, G * 64:G * 64 + G * RL].rearrange("p (g x) -> p g x", g=G)
        ps_z = bank[4][0:RL, 0:2 * G * 64].rearrange("p (a g x) -> p a g x", a=2, g=G)
        ps_pp = bank[5][0:RL, 0:G * 2 * RL].rearrange("p (g a x) -> p g a x", g=G, a=2)
        ps_y = bank[6][0:RL, 0:G * 64].rearrange("p (g x) -> p g x", g=G)
        ps_yT = bank[6][0:64, G * 64:G * 64 + G * RL].rearrange("p (g x) -> p g x", g=G)
        ps_s = bank[7][0:64, 0:G * 64].rearrange("p (g x) -> p g x", g=G)
        ps_ln = bank[7]
        (b_wup, b_rawd, b_rawa, b_twd, b_adq, b_gate,
         b_Vt, b_AR, b_Bt, b_Kt, b_Bh, b_Kh, b_bonus, b_yfm, b_gL, b_tok, b_GAs, b_GKs, b_Pn, b_U, b_Ysb, b_ST) = bufs(22)
        (b_r, b_k, b_e2, b_cc, b_eP, b_eN, b_a, b_kk, b_kkn, b_kmod, b_kka, b_t1, b_t2) = [bufs(G) for _ in range(13)]
        b_raw = {nm: Buf() for nm in raw}
        b_PP = bufs(2)
        b_yo = bufs(2)
        i64 = ident[0:64, 0:64]

        def shift(dst, src3, g, mu_t, om_t, H, reads, bdst):
            s.op("dve", lambda e: e.tensor_scalar(out=dst, in0=src3[:, g, 1:1 + NT], scalar1=om_t[:, H:H + 1], scalar2=None, op0=ALU.mult),
                 reads=reads + [b_c, bdst], writes=[bdst])
            s.op("dve", lambda e: e.scalar_tensor_tensor(out=dst, in0=src3[:, g, 0:NT], scalar=mu_t[:, H:H + 1], in1=dst,
                                                         op0=ALU.mult, op1=ALU.add),
                 reads=reads + [b_c, bdst], writes=[bdst])

        for h0 in range(0, NH, G):
            s.dma("sp", wup[:], prm["w_up"][:, h0 * 64:(h0 + G) * 64], reads=[b_wup], writes=[b_wup])
            s.dma("sp", aup[:], prm["a_up"][:, h0 * 64:(h0 + G) * 64], reads=[b_wup], writes=[b_wup])
            s.op("pool", lambda e: e.memset(ST[:], 0.0), reads=[b_ST], writes=[b_ST])
            for ti, t0 in enumerate(range(0, T, NT)):
                for rw, brw, nm, rowk in ((rawd, b_rawd, "mu_wd", "wd"), (rawa, b_rawa, "mu_ad", "ad")):
                    if ti == 0:
                        s.op("pool", lambda e: e.memset(rw[:, 0:1], 0.0), reads=[brw], writes=[brw])
                    else:
                        s.op("pool", lambda e: e.tensor_copy(out=rw[:, 0:1], in_=rw[:, NT:NT + 1]), reads=[brw], writes=[brw])
                    s.dma("sp", rw[:, 1:1 + NT], PT[rows[rowk]:rows[rowk] + 96, t0:t0 + NT], reads=[brw], writes=[brw])
                dsts = ((twd, b_twd, rawd, b_rawd, "mu_wd"), (adq, b_adq, rawa, b_rawa, "mu_ad"))
                for dst, bdst, rw, brw, nm in dsts:
                    s.op("dve", lambda e: e.tensor_scalar(out=dst[:], in0=rw[:, 1:1 + NT], scalar1=omd[nm][:], scalar2=None, op0=ALU.mult),
                         reads=[brw, b_c, bdst], writes=[bdst])
                    s.op("dve", lambda e: e.scalar_tensor_tensor(out=dst[:], in0=rw[:, 0:NT], scalar=mud[nm][:], in1=dst[:],
                                                                 op0=ALU.mult, op1=ALU.add),
                         reads=[brw, b_c, bdst], writes=[bdst])
                s.op("act", lambda e: e.activation(out=twd[:], in_=twd[:], func=AF.Tanh), reads=[b_twd], writes=[b_twd])
                for nm in ("r", "k", "v"):
                    if ti == 0:
                        s.op("pool", lambda e: e.memset(raw[nm][:, :, 0:1], 0.0), reads=[b_raw[nm]], writes=[b_raw[nm]])
                    else:
                        s.op("pool", lambda e: e.tensor_copy(out=raw[nm][:, :, 0:1], in_=raw[nm][:, :, NT:NT + 1]),
                             reads=[b_raw[nm]], writes=[b_raw[nm]])
                    r0 = rows[nm] + h0 * 64
                    s.dma("sp", raw[nm][:, :, 1:1 + NT], PT[r0:r0 + G * 64, t0:t0 + NT].rearrange("(h j) t -> j h t", j=64),
                          reads=[b_raw[nm]], writes=[b_raw[nm]])
                r0 = rows["g"] + h0 * 64
                s.dma("act", gate[:], PT[r0:r0 + G * 64, t0:t0 + NT].rearrange("(h j) t -> j h t", j=64), reads=[b_gate], writes=[b_gate])
                s.op("act", lambda e: e.activation(out=gate[:], in_=gate[:], func=AF.Silu), reads=[b_gate], writes=[b_gate])
                def head_gen(g):
                        H = h0 + g
                        hc = slice(g * 64, (g + 1) * 64)
                        shift(r_[g][:], raw["r"], g, P["mu_r"], om["mu_r"], H, [b_raw["r"]], b_r[g])
                        yield
                        shift(k_[g][:], raw["k"], g, P["mu_k"], om["mu_k"], H, [b_raw["k"]], b_k[g])
                        yield
                        shift(Vt[:, g, :], raw["v"], g, P["mu_v"], om["mu_v"], H, [b_raw["v"]], b_Vt)
                        yield
                        s.op("pe", lambda e: e.matmul(bank[g][0:64, :], lhsT=wup[:, hc], rhs=twd[:], start=True, stop=True),
                             reads=[b_wup, b_twd, b_bank[g]], writes=[b_bank[g]])
                        yield
                        s.op("act", lambda e: e.activation(out=t1[g][:], in_=bank[g][0:64, :], func=AF.Exp, scale=-1.0, bias=nw0[:, H:H + 1]),
                             reads=[b_bank[g], b_c, b_t1[g]], writes=[b_t1[g]])
                        yield
                        s.op("act", lambda e: e.activation(out=t1[g][:], in_=t1[g][:], func=AF.Ln, bias=1.0), reads=[b_t1[g]], writes=[b_t1[g]])
                        yield
                        s.op("act", lambda e: e.activation(out=e2[g][:], in_=t1[g][:], func=AF.Exp, scale=-1.0, bias=-0.5), reads=[b_t1[g], b_e2[g]], writes=[b_e2[g]])
                        yield
                        s.op("dve", lambda e: e.tensor_tensor_scan(out=c_[g][:], data0=rmask[:], data1=e2[g][:], initial=0.0, op0=ALU.mult, op1=ALU.add),
                             reads=[b_c, b_e2[g], b_cc[g]], writes=[b_cc[g]])
                        yield
                        s.op("act", lambda e: e.activation(out=eP[g][:], in_=c_[g][:], func=AF.Exp, scale=-1.0), reads=[b_cc[g], b_eP[g]], writes=[b_eP[g]])
                        yield
                        s.op("act", lambda e: e.activation(out=eN[g][:], in_=c_[g][:], func=AF.Exp), reads=[b_cc[g], b_eN[g]], writes=[b_eN[g]])
                        yield
                        s.op("pool", lambda e: e.tensor_copy(out=gL[:, g, :], in_=eP[g][:, RL - 1:NT:RL]), reads=[b_eP[g], b_gL], writes=[b_gL])
                        yield
                        yield
                        s.op("pe", lambda e: e.matmul(bank[g][0:64, :], lhsT=aup[:, hc], rhs=adq[:], start=True, stop=True),
                             reads=[b_wup, b_adq, b_bank[g]], writes=[b_bank[g]])
                        yield
                        s.op("act", lambda e: e.activation(out=a_[g][:], in_=bank[g][0:64, :], func=AF.Sigmoid, bias=P["a0"][:, H:H + 1]),
                             reads=[b_bank[g], b_c, b_a[g]], writes=[b_a[g]])
                        yield
                        s.op("dve", lambda e: e.tensor_scalar(out=kk[g][:], in0=k_[g][:], scalar1=P["k_k"][:, H:H + 1], scalar2=None, op0=ALU.mult),
                             reads=[b_k[g], b_c, b_kk[g]], writes=[b_kk[g]])
                        yield
                        s.op("act", lambda e: e.activation(out=t2[g][:], in_=kk[g][:], func=AF.Square), reads=[b_kk[g], b_t2[g]], writes=[b_t2[g]])
                        yield
                        s.op("pe", lambda e: e.matmul(bank[g][0:64, :], lhsT=ones64[:], rhs=t2[g][:], start=True, stop=True),
                             reads=[b_c, b_t2[g], b_bank[g]], writes=[b_bank[g]])
                        yield
                        s.op("act", lambda e: e.activation(out=t2[g][:], in_=bank[g][0:64, :], func=AF.Sqrt), reads=[b_bank[g], b_t2[g]], writes=[b_t2[g]])
                        yield
                        s.op("dve", lambda e: e.tensor_scalar(out=t2[g][:], in0=t2[g][:], scalar1=1e-12, scalar2=None, op0=ALU.max), reads=[b_t2[g]], writes=[b_t2[g]])
                        yield
                        s.op("dve", lambda e: e.reciprocal(out=t2[g][:], in_=t2[g][:]), reads=[b_t2[g]], writes=[b_t2[g]])
                        yield
                        s.op("dve", lambda e: e.tensor_tensor(out=kkn[g][:], in0=kk[g][:], in1=t2[g][:], op=ALU.mult), reads=[b_kk[g], b_t2[g], b_kkn[g]], writes=[b_kkn[g]])
                        yield
                        s.op("dve", lambda e: e.tensor_scalar(out=t1[g][:], in0=a_[g][:], scalar1=P["k_a"][:, H:H + 1], scalar2=om["k_a"][:, H:H + 1],
                                                              op0=ALU.mult, op1=ALU.add), reads=[b_a[g], b_c, b_t1[g]], writes=[b_t1[g]])
                        yield
                        s.op("dve", lambda e: e.tensor_tensor(out=kmod[g][:], in0=k_[g][:], in1=t1[g][:], op=ALU.mult), reads=[b_k[g], b_t1[g], b_kmod[g]], writes=[b_kmod[g]])
                        yield
                        s.op("dve", lambda e: e.tensor_tensor(out=kka[g][:], in0=kkn[g][:], in1=a_[g][:], op=ALU.mult), reads=[b_kkn[g], b_a[g], b_kka[g]], writes=[b_kka[g]])
                        yield
                        s.op("dve", lambda e: e.scalar_tensor_tensor(out=t2[g][:], in0=r_[g][:], scalar=P["r_k"][:, H:H + 1], in1=kmod[g][:],
                                                                     op0=ALU.mult, op1=ALU.mult), reads=[b_r[g], b_c, b_kmod[g], b_t2[g]], writes=[b_t2[g]])
                        yield
                        s.op("pe", lambda e: e.matmul(bank[g][0:64, :], lhsT=ones64[:], rhs=t2[g][:], start=True, stop=True),
                             reads=[b_c, b_t2[g], b_bank[g]], writes=[b_bank[g]])
                        yield
                        s.op("dve", lambda e: e.tensor_tensor(out=bonus[:, g, :], in0=bank[g][0:64, :], in1=Vt[:, g, :], op=ALU.mult),
                             reads=[b_bank[g], b_Vt, b_bonus], writes=[b_bonus])
                        yield
                        s.op("dve", lambda e: e.tensor_tensor(out=t1[g][:], in0=e2[g][:], in1=c_[g][:], op=ALU.subtract), reads=[b_e2[g], b_cc[g], b_t1[g]], writes=[b_t1[g]])
                        yield
                        s.op("act", lambda e: e.activation(out=t1[g][:], in_=t1[g][:], func=AF.Exp), reads=[b_t1[g]], writes=[b_t1[g]])
                        yield
                        v3 = lambda ap: ap.rearrange("p (n l) -> p n l", l=RL)
                        s.op("dve", lambda e: e.scalar_tensor_tensor(out=AR[:, g, :, 0, :], in0=v3(kkn[g][:]), scalar=-1.0, in1=v3(t1[g][:]),
                                                                     op0=ALU.mult, op1=ALU.mult), reads=[b_kkn[g], b_t1[g], b_AR], writes=[b_AR])
                        yield
                        s.op("dve", lambda e: e.tensor_tensor(out=AR[:, g, :, 1, :], in0=v3(r_[g][:]), in1=v3(eP[g][:]), op=ALU.mult),
                             reads=[b_r[g], b_eP[g], b_AR], writes=[b_AR])
                        yield
                        s.op("dve", lambda e: e.tensor_tensor(out=Bt[:, g, :], in0=kka[g][:], in1=eN[g][:], op=ALU.mult), reads=[b_kka[g], b_eN[g], b_Bt], writes=[b_Bt])
                        yield
                        s.op("dve", lambda e: e.tensor_tensor(out=Kt[:, g, :], in0=kmod[g][:], in1=eN[g][:], op=ALU.mult), reads=[b_kmod[g], b_eN[g], b_Kt], writes=[b_Kt])
                        yield
                        gbc = gL[:, g, :].unsqueeze(2).broadcast_to([64, NCK, RL])
                        s.op("dve", lambda e: e.tensor_tensor(out=v3(Bh[:, g, :]), in0=v3(Bt[:, g, :]), in1=gbc, op=ALU.mult),
                             reads=[b_Bt, b_gL, b_Bh], writes=[b_Bh])
                        yield
                        s.op("dve", lambda e: e.tensor_tensor(out=v3(Kh[:, g, :]), in0=v3(Kt[:, g, :]), in1=gbc, op=ALU.mult),
                             reads=[b_Kt, b_gL, b_Kh], writes=[b_Kh])
                        yield

                gens = [head_gen(g) for g in range(G)]
                alive = list(gens)
                while alive:
                    for gg in list(alive):
                        try:
                            next(gg)
                        except StopIteration:
                            alive.remove(gg)
                def indep_steps(cc):
                    par = cc % 2
                    cs = slice(cc * RL, (cc + 1) * RL)
                    tk, ga_s, gk_s, pn_s, ppl = tokb[par], GAb[par], GKb[par], Pnb[par], PPb[par]
                    b_tk, b_ga, b_gk, b_pn, b_ppl = b_tokb[par], b_GAb[par], b_GKb[par], b_Pnb[par], b_PPb[par]

                    def tr():
                        for g in range(G):
                            s.op("pe", lambda e: e.transpose(ps_x1[:, 0, g, :], Vt[:, g, cs], i64), reads=[b_Vt, b_c, b_bank[2]], writes=[b_bank[2]])
                            s.op("pe", lambda e: e.transpose(ps_x1[:, 1, g, :], Bh[:, g, cs], i64), reads=[b_Bh, b_c, b_bank[2]], writes=[b_bank[2]])
                            s.op("pe", lambda e: e.transpose(ps_x2k[:, g, :], Kh[:, g, cs], i64), reads=[b_Kh, b_c, b_bank[3]], writes=[b_bank[3]])
                        s.op("act", lambda e: e.copy(out=tk[:, 0:2], in_=ps_x1), reads=[b_bank[2], b_tk], writes=[b_tk])
                        s.op("act", lambda e: e.copy(out=tk[:, 2], in_=ps_x2k), reads=[b_bank[3], b_tk], writes=[b_tk])

                    def gm():
                        for g in range(G):
                            arc = AR[:, g, cc].rearrange("p a l -> p (a l)")
                            s.op("pe", lambda e: e.matmul(ps_ga[:, g, :], lhsT=Bt[:, g, cs], rhs=arc, start=True, stop=True),
                                 reads=[b_Bt, b_AR, b_bank[0]], writes=[b_bank[0]])
                            s.op("pe", lambda e: e.matmul(ps_gk[:, g, :], lhsT=Kt[:, g, cs], rhs=arc, start=True, stop=True),
                                 reads=[b_Kt, b_AR, b_bank[1]], writes=[b_bank[1]])
                            s.op("pe", lambda e: e.matmul(ps_gn[:, g, :], lhsT=AR[:, g, cc, 0, :], rhs=Bt[:, g, cs], start=True, stop=True),
                                 reads=[b_Bt, b_AR, b_bank[3]], writes=[b_bank[3]])
                        s.op("dve", lambda e: e.tensor_tensor(out=ga_s[:], in0=ps_ga, in1=maskA[:], op=ALU.mult),
                             reads=[b_bank[0], b_c, b_ga], writes=[b_ga])
                        s.op("dve", lambda e: e.tensor_tensor(out=gk_s[:], in0=ps_gk, in1=maskA[:], op=ALU.mult),
                             reads=[b_bank[1], b_c, b_gk], writes=[b_gk])
                        s.op("dve", lambda e: e.tensor_tensor(out=pn_s[:], in0=ps_gn, in1=maskN[:], op=ALU.mult),
                             reads=[b_bank[3], b_c, b_pn], writes=[b_pn])

                    def sq(lvl):
                        def f():
                            if lvl == 0:
                                Pl = lambda g: pn_s[:, g, :]
                                PTl = lambda g: ga_s[:, g, 0:RL]
                                rd = [b_pn, b_ga]
                            else:
                                Pl = lambda g: ppl[lvl - 1][:, g, 0, :]
                                PTl = lambda g: ppl[lvl - 1][:, g, 1, :]
                                rd = [b_ppl[lvl - 1]]
                            for g in range(G):
                                s.op("pe", lambda e: e.matmul(ps_pp[:, g, 0, :], lhsT=PTl(g), rhs=Pl(g), start=True, stop=True),
                                     reads=rd + [b_bank[5]], writes=[b_bank[5]])
                                s.op("pe", lambda e: e.matmul(ps_pp[:, g, 1, :], lhsT=Pl(g), rhs=PTl(g), start=True, stop=True),
                                     reads=rd + [b_bank[5]], writes=[b_bank[5]])
                            s.op("act", lambda e: e.copy(out=ppl[lvl][:], in_=ps_pp), reads=[b_bank[5], b_ppl[lvl]], writes=[b_ppl[lvl]])
                        return f
                    return [tr, gm] + [sq(l) for l in range(NLVL - 1)]

                def dep_steps(cc):
                    par = cc % 2
                    cs = slice(cc * RL, (cc + 1) * RL)
                    tk, ga_s, gk_s, pn_s, ppl = tokb[par], GAb[par], GKb[par], Pnb[par], PPb[par]
                    b_tk, b_ga, b_gk, b_pn, b_ppl = b_tokb[par], b_GAb[par], b_GKb[par], b_Pnb[par], b_PPb[par]

                    def zz():
                        for g in range(G):
                            s.op("pe", lambda e: e.matmul(ps_z[:, 0, g, :], lhsT=AR[:, g, cc, 0, :], rhs=ST[:, g, :], start=True, stop=False),
                                 reads=[b_AR, b_ST, b_bank[4]], writes=[b_bank[4]], inc=False)
                            s.op("pe", lambda e: e.matmul(ps_z[:, 0, g, :], lhsT=gk_s[:, g, 0:RL], rhs=tk[:, 0, g, :], start=False, stop=True),
                                 reads=[b_gk, b_tk, b_bank[4]], writes=[b_bank[4]])
                        s.op("act", lambda e: e.copy(out=U[:], in_=ps_z[:, 0]), reads=[b_bank[4], b_U], writes=[b_U])

                    def app(lvl):
                        def f():
                            if lvl == 0:
                                PTl = lambda g: ga_s[:, g, 0:RL]
                                rd = [b_ga]
                            else:
                                PTl = lambda g: ppl[lvl - 1][:, g, 1, :]
                                rd = [b_ppl[lvl - 1]]
                            for g in range(G):
                                s.op("pe", lambda e: e.matmul(ps_z[:, 1, g, :], lhsT=PTl(g), rhs=U[:, g, :], start=True, stop=True),
                                     reads=rd + [b_U, b_bank[4]], writes=[b_bank[4]])
                            s.op("dve", lambda e: e.tensor_tensor(out=U[:], in0=ps_z[:, 1], in1=U[:], op=ALU.add),
                                 reads=[b_bank[4], b_U], writes=[b_U])
                        return f

                    def yy():
                        for g in range(G):
                            s.op("pe", lambda e: e.matmul(ps_y[:, g, :], lhsT=AR[:, g, cc, 1, :], rhs=ST[:, g, :], start=True, stop=False),
                                 reads=[b_AR, b_ST, b_bank[6]], writes=[b_bank[6]], inc=False)
                            s.op("pe", lambda e: e.matmul(ps_y[:, g, :], lhsT=ga_s[:, g, RL:2 * RL], rhs=U[:, g, :], start=False, stop=False),
                                 reads=[b_ga, b_U, b_bank[6]], writes=[b_bank[6]], inc=False)
                            s.op("pe", lambda e: e.matmul(ps_y[:, g, :], lhsT=gk_s[:, g, RL:2 * RL], rhs=tk[:, 0, g, :], start=False, stop=True),
                                 reads=[b_gk, b_tk, b_bank[6]], writes=[b_bank[6]])
                        s.op("act", lambda e: e.copy(out=Ysb[:], in_=ps_y), reads=[b_bank[6], b_Ysb], writes=[b_Ysb])

                    def ss():
                        for g in range(G):
                            s.op("pe", lambda e: e.matmul(ps_s[:, g, :], lhsT=tk[:, 1, g, :], rhs=U[:, g, :], start=True, stop=False),
                                 reads=[b_tk, b_U, b_bank[7]], writes=[b_bank[7]], inc=False)
                            s.op("pe", lambda e: e.matmul(ps_s[:, g, :], lhsT=tk[:, 2, g, :], rhs=tk[:, 0, g, :], start=False, stop=True),
                                 reads=[b_tk, b_bank[7]], writes=[b_bank[7]])
                        for g in range(G):
                            s.op("dve", lambda e: e.scalar_tensor_tensor(out=ST[:, g, :], in0=ST[:, g, :], scalar=gL[:, g, cc:cc + 1], in1=ps_s[:, g, :],
                                                                         op0=ALU.mult, op1=ALU.add),
                                 reads=[b_ST, b_gL, b_bank[7]], writes=[b_ST])

                    def yt():
                        for g in range(G):
                            s.op("pe", lambda e: e.transpose(ps_yT[:, g, :], Ysb[:, g, :], ident[0:RL, 0:RL]), reads=[b_Ysb, b_c, b_bank[6]], writes=[b_bank[6]])
                        s.op("dve", lambda e: e.tensor_copy(out=yfm[:, :, cs], in_=ps_yT), reads=[b_bank[6], b_yfm], writes=[b_yfm])
                    return [zz] + [app(l) for l in range(NLVL)] + [yy, ss, yt]

                for f in indep_steps(0):
                    f()
                for cc in range(NCK):
                    dsteps = dep_steps(cc)
                    isteps = indep_steps(cc + 1) if cc + 1 < NCK else []
                    for j in range(max(len(dsteps), len(isteps))):
                        if j < len(dsteps):
                            dsteps[j]()
                        if j < len(isteps):
                            isteps[j]()
                def ph4(g):
                    H = h0 + g
                    yi = g % 2
                    s.op("pe", lambda e: e.matmul(bank[(7, 4)[g]][0:64, :], lhsT=ones64[:], rhs=yfm[:, g, :], start=True, stop=True),
                         reads=[b_c, b_yfm, b_bank[(7, 4)[g]]], writes=[b_bank[(7, 4)[g]]])
                    yield
                    s.op("act", lambda e: e.activation(out=t1[g][:], in_=bank[(7, 4)[g]][0:64, :], func=AF.Copy, scale=1.0 / 64), reads=[b_bank[(7, 4)[g]], b_t1[g]], writes=[b_t1[g]])
                    yield
                    s.op("dve", lambda e: e.tensor_tensor(out=t1[g][:], in0=yfm[:, g, :], in1=t1[g][:], op=ALU.subtract), reads=[b_yfm, b_t1[g]], writes=[b_t1[g]])
                    yield
                    s.op("act", lambda e: e.activation(out=t2[g][:], in_=t1[g][:], func=AF.Square), reads=[b_t1[g], b_t2[g]], writes=[b_t2[g]])
                    yield
                    s.op("pe", lambda e: e.matmul(bank[(7, 4)[g]][0:64, :], lhsT=ones64[:], rhs=t2[g][:], start=True, stop=True),
                         reads=[b_c, b_t2[g], b_bank[(7, 4)[g]]], writes=[b_bank[(7, 4)[g]]])
                    yield
                    s.op("act", lambda e: e.activation(out=t2[g][:], in_=bank[(7, 4)[g]][0:64, :], func=AF.Sqrt, scale=1.0 / 64, bias=GN_EPS),
                         reads=[b_bank[(7, 4)[g]], b_t2[g]], writes=[b_t2[g]])
                    yield
                    s.op("dve", lambda e: e.reciprocal(out=t2[g][:], in_=t2[g][:]), reads=[b_t2[g]], writes=[b_t2[g]])
                    yield
                    s.op("dve", lambda e: e.tensor_tensor(out=t1[g][:], in0=t1[g][:], in1=t2[g][:], op=ALU.mult), reads=[b_t1[g], b_t2[g]], writes=[b_t1[g]])
                    yield
                    s.op("dve", lambda e: e.tensor_scalar(out=t1[g][:], in0=t1[g][:], scalar1=P["gn_w"][:, H:H + 1], scalar2=P["gn_b"][:, H:H + 1],
                                                          op0=ALU.mult, op1=ALU.add), reads=[b_t1[g], b_c], writes=[b_t1[g]])
                    yield
                    s.op("dve", lambda e: e.tensor_tensor(out=t1[g][:], in0=t1[g][:], in1=bonus[:, g, :], op=ALU.add), reads=[b_t1[g], b_bonus], writes=[b_t1[g]])
                    yield
                    s.op("dve", lambda e: e.tensor_tensor(out=yo[yi][:], in0=t1[g][:], in1=gate[:, g, :], op=ALU.mult),
                         reads=[b_t1[g], b_gate, b_yo[yi]], writes=[b_yo[yi]])
                    yield
                    r0 = row_y + H * 64
                    s.dma("sp", yT[r0:r0 + 64, t0:t0 + NT], yo[yi][:], reads=[b_yo[yi]])
                    yield
                alive = [ph4(g) for g in range(G)]
                while alive:
                    for gg in list(alive):
                        try:
                            next(gg)
                        except StopIteration:
                            alive.remove(gg)
        s.barrier()


class Cfg:
    def __init__(self, D=4096, T=4096, NBLK_A=8, NH_B=32, HC=4, HX=4, M=256, DEPTH=2):
        self.D, self.T, self.NBLK_A, self.NH_B, self.HC, self.HX, self.M, self.DEPTH = D, T, NBLK_A, NH_B, HC, HX, M, DEPTH
        self.KC = D // 128
        self.WA = NBLK_A * 256
        self.WB = NH_B * 64
        self.QKW = HC * 256
        self.WC = HC * 512
        self.WX = HX * 128
        self.in_sizes = (self.WA, self.WA, 3 * self.WB + 192, self.WB, 2 * self.QKW, self.WC, self.WC, self.WC, 2 * HC,
                         self.WX, self.WX, 4 * D)
        self.c_in = sum(self.in_sizes)
        off = np.concatenate([[0], np.cumsum(self.in_sizes)])
        self.col = dict(a_x=off[0], a_g=off[1], b_s=off[2], b_g=off[3], c_qk=off[4], c_v=off[5], c_o=off[6], c_g=off[7],
                        c_if=off[8], x_q=off[9], x_g=off[10], gates=off[11])
        segs = [("a_x", self.col["a_x"], self.WA), ("a_g", self.col["a_g"], self.WA),
                ("r", self.col["b_s"], self.WB), ("k", self.col["b_s"] + self.WB, self.WB), ("v", self.col["b_s"] + 2 * self.WB, self.WB),
                ("wd", self.col["b_s"] + 3 * self.WB, 96), ("ad", self.col["b_s"] + 3 * self.WB + 96, 96),
                ("b_g", self.col["b_g"], self.WB),
                ("c_q", self.col["c_qk"], self.QKW), ("c_k", self.col["c_qk"] + self.QKW, self.QKW),
                ("c_v", self.col["c_v"], self.WC), ("c_o", self.col["c_o"], self.WC), ("c_g", self.col["c_g"], self.WC),
                ("c_if", self.col["c_if"], 2 * HC), ("x_q", self.col["x_q"], self.WX), ("x_g", self.col["x_g"], self.WX)]
        self.segs = segs
        self.row = {}
        r = 0
        for nm, c0, w in segs:
            self.row[nm] = r
            r += ((w + 127) // 128) * 128
        self.NB1 = r // 128
        self.grp_first = {"A": "a_x", "B": "r", "C": "c_q", "X": "x_q"}
        order = ["A", "B", "C", "X"]
        starts = [self.row[self.grp_first[g]] for g in order] + [r]
        self.grp_rows = {g: (starts[i], starts[i + 1]) for i, g in enumerate(order)}
        self.lrow = {}
        for nm, c0, w in segs:
            for g in order:
                lo, hi = self.grp_rows[g]
                if lo <= self.row[nm] < hi:
                    self.lrow[nm] = (g, self.row[nm] - lo)
        self.br_kc = [self.WA // 128, self.WB // 128, self.WC // 128, self.WX // 128]
        self.FY = sum(self.br_kc) * 128


RMS_EPS = 1e-6
RWKV_GN_EPS = 64e-5


def tile_layout(W):
    Kd, M = W.shape
    return np.ascontiguousarray(W.reshape(Kd // 128, 128, M // 128, 128).transpose(2, 1, 0, 3))


def chunk_cols(v):
    return np.ascontiguousarray(v.reshape(-1, 128).T)


def head_cols(v):
    return np.ascontiguousarray(v.reshape(-1, 64).T)


def const_inputs(cfg):
    HC = cfg.HC
    sel = np.zeros((HC, HC, 128), np.float32)
    for h in range(HC):
        sel[h, h, :] = 1
    a_, b_ = np.meshgrid(np.arange(RWKV_L), np.arange(RWKV_L), indexing="ij")
    mA = np.concatenate([(a_ < b_), (a_ <= b_)], 1).astype(np.float32)
    mN = (b_ < a_).astype(np.float32)
    rm = np.ones((64, NT), np.float32)
    rm[:, ::RWKV_L] = 0
    a_, b_ = np.meshgrid(np.arange(64), np.arange(64), indexing="ij")
    return {"c_ident": np.eye(128, dtype=np.float32), "c_sel": sel, "c_i4": np.eye(HC, dtype=np.float32),
            "c_negmask": np.where(a_ <= b_, 0.0, NEG).astype(np.float32), "c_resetmask": rm,
            "c_maskA": np.ascontiguousarray(np.broadcast_to(mA[:, None, :], (RWKV_L, RWKV_G, 2 * RWKV_L))),
            "c_maskN": np.ascontiguousarray(np.broadcast_to(mN[:, None, :], (RWKV_L, RWKV_G, RWKV_L)))}


def layer_inputs(cfg, inp, l):
    D, KC, HC = cfg.D, cfg.KC, cfg.HC
    w_in = inp["w_in"][l]
    W1 = np.zeros((D, cfg.NB1 * 128), np.float32)
    for nm, c0, w in cfg.segs:
        W1[:, cfg.row[nm]:cfg.row[nm] + w] = w_in[:, c0:c0 + w]
    o = {}
    o["W1"] = tile_layout(W1)
    g0 = cfg.col["gates"]
    o["Wg"] = np.stack([tile_layout(w_in[:, g0 + i * D: g0 + (i + 1) * D]) for i in range(4)])
    o["Wbr0"] = tile_layout(inp["w_branch_a"][l])
    o["Wbr1"] = tile_layout(inp["w_branch_b"][l])
    o["Wbr2"] = tile_layout(inp["w_branch_c"][l])
    o["Wbr3"] = tile_layout(inp["w_branch_x"][l])
    o["Wo"] = tile_layout(inp["w_out"][l])
    o["norm_g"] = chunk_cols(inp["norm_g"][l])
    o["mem_norm_g"] = chunk_cols(inp["mem_norm_g"][l])
    NCH = cfg.NBLK_A * 2
    o["lru_conv_w"] = np.ascontiguousarray(inp["lru_conv_w"][l].reshape(4, NCH, 128).transpose(2, 1, 0))
    for nm in ("lru_conv_b", "lru_ba", "lru_bx", "lru_lambda"):
        o[nm] = chunk_cols(inp[nm][l])
    for nm in ("lru_wa", "lru_wx"):
        o[nm] = np.ascontiguousarray(inp[nm][l].reshape(cfg.NBLK_A, 2, 128, 2, 128).transpose(0, 3, 2, 1, 4))
    WB = cfg.WB
    mu = inp["rwkv_mu"][l]
    o["rw_mu_r"], o["rw_mu_k"], o["rw_mu_v"] = head_cols(mu[:WB]), head_cols(mu[WB:2 * WB]), head_cols(mu[2 * WB:3 * WB])
    o["rw_mu_wd"] = np.ascontiguousarray(mu[3 * WB:3 * WB + 96].reshape(96, 1))
    o["rw_mu_ad"] = np.ascontiguousarray(mu[3 * WB + 96:].reshape(96, 1))
    for nm, src in (("w0", "rwkv_w0"), ("a0", "rwkv_a0"), ("k_k", "rwkv_k_k"), ("k_a", "rwkv_k_a"), ("gn_w", "rwkv_gn_w"), ("gn_b", "rwkv_gn_b")):
        o["rw_" + nm] = head_cols(inp[src][l])
    o["rw_r_k"] = head_cols(inp["rwkv_r_k"][l].reshape(-1))
    o["rw_w_up"] = np.ascontiguousarray(inp["rwkv_w_up"][l])
    o["rw_a_up"] = np.ascontiguousarray(inp["rwkv_a_up"][l])
    o["ml_conv_w"] = np.ascontiguousarray(inp["mlstm_conv_w"][l].reshape(4, 4 * HC, 128).transpose(2, 1, 0))
    o["ml_conv_b"] = chunk_cols(inp["mlstm_conv_b"][l])
    o["ml_gn_w"] = chunk_cols(inp["mlstm_gn_w"][l])
    o["ml_b_i"] = np.ascontiguousarray(inp["mlstm_b_i"][l].reshape(HC, 1))
    o["ml_b_f"] = np.ascontiguousarray(inp["mlstm_b_f"][l].reshape(HC, 1))
    wkv = inp["xattn_w_kv"][l]
    o["xa_Wk"] = tile_layout(wkv[:, :cfg.WX])
    o["xa_Wv"] = np.ascontiguousarray(wkv[:, cfg.WX:].reshape(KC, 128, cfg.WX))
    return {f"L{l}_{k_}": np.ascontiguousarray(v, dtype=np.float32) for k_, v in o.items()}


def flat2d(ap, ndim, width):
    names = "abcdefgh"[:ndim]
    f = ap.rearrange(f"{' '.join(names)} -> ({' '.join(names)})")
    return f.rearrange("(r c) -> r c", c=width)


def build_program(cfg, shapes):
    nc = bass.Bass("TRN2", target_bir_lowering=False)
    D, T, KC = cfg.D, cfg.T, cfg.KC
    ins = {nm: nc.dram_tensor(nm, list(sh), F32, kind="ExternalInput").ap() for nm, sh in shapes.items()}
    outT = nc.dram_tensor("outT", [D, T], F32, kind="ExternalOutput").ap()
    with contextlib.ExitStack() as st:
        k = Ctx(nc, st)
        hT = k.dram("hT", [D, T], BF16)
        PTs = {g: k.dram(f"PT{g}", [hi - lo, T], F32) for g, (lo, hi) in cfg.grp_rows.items()}

        def pt_block(b):
            r = b * 128
            for g, (lo, hi) in cfg.grp_rows.items():
                if lo <= r < hi:
                    return PTs[g], r - lo
            raise AssertionError
        lr = lambda nm: cfg.lrow[nm][1]
        yT = k.dram("yT", [cfg.FY, T], BF16)
        memnT = k.dram("memnT", [D, cfg.M], BF16)
        xs = [ins["xT"]] + [k.dram(f"x{l + 1}T", [D, T], F32) for l in range(cfg.DEPTH)]
        cst = {nm[2:]: ins[nm] for nm in ins if nm.startswith("c_")}
        OB = D // 128
        for l in range(cfg.DEPTH):
            L = lambda nm: ins[f"L{l}_{nm}"]
            Wg = k.dram(f"Wg{l}", [4, OB, 128, KC, 128], BF16)
            Wo = k.dram(f"Wo{l}", [OB, 128, KC, 128], BF16)
            Wbr = [k.dram(f"Wbr{l}_{i}", [OB, 128, cfg.br_kc[i], 128], BF16) for i in range(4)]
            for dst, src, nd in [(Wg, L("Wg"), 5), (Wo, L("Wo"), 4)] + [(Wbr[i], L(f"Wbr{i}"), 4) for i in range(4)]:
                n = int(np.prod(dst.shape))
                wdt = 1024 if n % 1024 == 0 else 512
                cast_dram(k, flat2d(dst, nd, wdt), flat2d(src, nd, wdt), n, width=wdt)
            stage_norm(k, xs[l], L("norm_g"), hT, D, T, RMS_EPS, BF16)
            stage_norm(k, ins["memT"], L("mem_norm_g"), memnT, D, cfg.M, RMS_EPS, BF16)
            stage_proj(k, hT, L("W1"), pt_block, D, T, cfg.NB1)
            lru_prm = {"conv_w": L("lru_conv_w"), "conv_b": L("lru_conv_b"), "ba": L("lru_ba"), "bx": L("lru_bx"), "lam": L("lru_lambda"),
                       "wa": L("lru_wa"), "wx": L("lru_wx")}
            stage_lru(k, PTs["A"], lr("a_x"), lr("a_g"), yT, 0, lru_prm, T, cfg.NBLK_A)
            rw_prm = {nm: L("rw_" + nm) for nm in ("mu_r", "mu_k", "mu_v", "w0", "a0", "k_k", "k_a", "r_k", "gn_w", "gn_b", "mu_wd", "mu_ad", "w_up", "a_up")}
            rw_rows = {"r": lr("r"), "k": lr("k"), "v": lr("v"), "wd": lr("wd"), "ad": lr("ad"), "g": lr("b_g")}
            stage_rwkv(k, PTs["B"], rw_rows, yT, cfg.WA, rw_prm, cst, T, cfg.NH_B, RWKV_GN_EPS)
            ml_prm = {"conv_w": L("ml_conv_w"), "conv_b": L("ml_conv_b"), "gn_w": L("ml_gn_w"), "b_i": L("ml_b_i"), "b_f": L("ml_b_f")}
            ml_rows = {"q": lr("c_q"), "k": lr("c_k"), "v": lr("c_v"), "o": lr("c_o"), "g": lr("c_g"), "ifg": lr("c_if")}
            stage_mlstm(k, PTs["C"], ml_rows, yT, cfg.WA + cfg.WB, ml_prm, cst, T, cfg.HC)
            stage_xattn(k, PTs["X"], lr("x_q"), lr("x_g"), yT, cfg.WA + cfg.WB + cfg.WC, memnT, L("xa_Wk"), L("xa_Wv"), cst["ident"],
                        T, D, cfg.M, cfg.HX)
            stage_merge(k, hT, yT, cfg.br_kc, Wg, Wbr, Wo, xs[l], xs[l + 1], D, T)
        stage_norm(k, xs[cfg.DEPTH], ins["final_g"], outT, D, T, RMS_EPS, F32)
        k.s.barrier()
        build_program.ninst = k.s.ninst
        build_program.per_eng = dict(k.s.per_eng)
        build_program.nsem = k.s.nsem
    return nc


def run_module(cfg, inputs):
    B = inputs["x"].shape[0]
    shared = const_inputs(cfg)
    for l in range(cfg.DEPTH):
        shared.update(layer_inputs(cfg, inputs, l))
    shared["final_g"] = chunk_cols(np.asarray(inputs["final_norm_g"], dtype=np.float32))
    in_maps = []
    for b in range(B):
        m = dict(shared)
        m["xT"] = np.ascontiguousarray(np.asarray(inputs["x"][b], dtype=np.float32).T)
        m["memT"] = np.ascontiguousarray(np.asarray(inputs["mem"][b], dtype=np.float32).T)
        in_maps.append(m)
    shapes = {nm: v.shape for nm, v in in_maps[0].items()}
    nc = build_program(cfg, shapes)
    res = run_bass_kernel_spmd(nc, in_maps, core_ids=list(range(B)))
    out = np.stack([np.ascontiguousarray(res.results[b]["outT"].T) for b in range(B)])
    return out.astype(np.float32)


def kernel(**inputs):
    inputs = {k_: np.asarray(v) for k_, v in inputs.items()}
    return run_module(Cfg(), inputs)
```

```python
import contextlib
import numpy as np
import concourse.bass as bass
import concourse.mybir as mybir
from concourse.bass_utils import run_bass_kernel_spmd

F32 = mybir.dt.float32
BF16 = mybir.dt.bfloat16
AF = mybir.ActivationFunctionType
ALU = mybir.AluOpType
AX = mybir.AxisListType


class Buf:
    __slots__ = ("name", "w", "r")

    def __init__(self, name=""):
        self.name = name
        self.w = set()
        self.r = set()


class Sched:
    EPOCH = 4000
    NDMA = 12

    def __init__(self, nc, stack):
        self.nc = nc
        self.stack = stack
        self.eng = {"pe": nc.tensor, "act": nc.scalar, "dve": nc.vector,
                    "pool": nc.gpsimd, "sp": nc.sync}
        self.sem = {}
        self.cnt = {}
        self.pending = {e: False for e in self.eng}
        self.nsem = 0
        for e in self.eng:
            self._new_sem(e)
        self.dsem = {}
        for q in ("sp", "act", "pool"):
            self.dsem[q] = [[self._alloc(f"d{q}{i}"), 0] for i in range(self.NDMA)]
        self.dnext = {q: 0 for q in self.dsem}
        self.seen = {e: {} for e in self.eng}
        self.all_tokens = {}
        self.ninst = 0
        self.per_eng = {}

    def _alloc(self, name):
        self.nsem += 1
        return self.stack.enter_context(self.nc.semaphore(f"{name}_{self.nsem}"))

    def _new_sem(self, e):
        self.sem[e] = self._alloc(f"s{e}")
        self.cnt[e] = 0

    def _wait(self, e, tok):
        sem, val = tok
        k = id(sem)
        if self.seen[e].get(k, 0) >= val:
            return
        self.eng[e].wait_ge(sem, val)
        self.seen[e][k] = val

    def _deps(self, e, reads, writes):
        deps = set()
        for b in reads:
            deps |= b.w
        for b in writes:
            deps |= b.w
            deps |= b.r
        for tok in deps:
            if e == "pe" and tok[0] is self.sem["pe"]:
                continue
            self._wait(e, tok)

    def _record(self, tok, reads, writes):
        self.all_tokens[id(tok[0])] = tok
        for b in reads:
            b.r.add(tok)
        for b in writes:
            b.w = {tok}
            b.r = set()

    def op(self, e, fn, reads=(), writes=(), inc=True):
        self._deps(e, reads, writes)
        inst = fn(self.eng[e])
        self.ninst += 1
        self.per_eng[e] = self.per_eng.get(e, 0) + 1
        if inc:
            if self.cnt[e] >= self.EPOCH and not self.pending[e]:
                self._new_sem(e)
            self.cnt[e] += 1
            inst.then_inc(self.sem[e], 1)
            tok = (self.sem[e], self.cnt[e])
            self.pending[e] = False
        else:
            assert e == "pe"
            tok = (self.sem[e], self.cnt[e] + 1)
            self.pending[e] = True
        self._record(tok, reads, writes)
        return inst

    def dma(self, q, out, in_, reads=(), writes=(), **kw):
        slot = self.dsem[q][self.dnext[q]]
        self.dnext[q] = (self.dnext[q] + 1) % self.NDMA
        sem, val = slot
        if val > 0:
            self._wait(q, (sem, val))
        self._deps(q, reads, writes)
        inst = self.eng[q].dma_start(out=out, in_=in_, **kw)
        self.ninst += 1
        self.per_eng['dma_' + q] = self.per_eng.get('dma_' + q, 0) + 1
        slot[1] = val + 16
        inst.then_inc(sem, 16)
        tok = (sem, slot[1])
        self._record(tok, reads, writes)
        return inst

    def barrier(self):
        toks = list(self.all_tokens.values())
        for e in self.eng:
            for tok in toks:
                self._wait(e, tok)


class Ctx:
    def __init__(self, nc, stack):
        self.nc = nc
        self.s = Sched(nc, stack)
        self.n = 0

    def sb(self, st, shape, dt, name="t"):
        self.n += 1
        return st.enter_context(self.nc.sbuf_tensor(f"{name}{self.n}", list(shape), dt))

    def ps(self, st, shape, dt=F32, name="p"):
        self.n += 1
        return st.enter_context(self.nc.psum_tensor(f"{name}{self.n}", list(shape), dt))

    def dram(self, name, shape, dt):
        return self.nc.dram_tensor(name, list(shape), dt, kind="Internal").ap()


def dma_rows(s, q, sb3, dram2, nchunks, reads=(), writes=(), to_dram=False, step=8):
    for c0 in range(0, nchunks, step):
        c1 = min(nchunks, c0 + step)
        d = dram2[c0 * 128:c1 * 128, :].rearrange("(c p) t -> p c t", p=128)
        if to_dram:
            s.dma(q, d, sb3[:, c0:c1, :], reads=reads, writes=writes)
        else:
            s.dma(q, sb3[:, c0:c1, :], d, reads=reads, writes=writes)


def bufs(n):
    return [Buf() for _ in range(n)]


NT = 512


def stage_norm(k, xT, g_dram, out, D, T, eps, out_dt):
    NT = min(512, T)
    s = k.s
    KC = D // 128
    with contextlib.ExitStack() as st:
        ones = k.sb(st, [128, 128], F32)
        gcol = k.sb(st, [128, KC], F32)
        xt = k.sb(st, [128, KC, NT], F32)
        ht = k.sb(st, [128, KC, NT], out_dt)
        sq = [k.sb(st, [128, NT], F32) for _ in range(2)]
        rs = k.sb(st, [128, NT], F32)
        rstd = k.sb(st, [128, NT], F32)
        pss = k.ps(st, [128, NT])
        b_ones, b_g, b_rs, b_rstd, b_ps = bufs(5)
        b_xt, b_ht, b_sq = bufs(KC), bufs(KC), bufs(2)
        s.op("pool", lambda e: e.memset(ones[:], 1.0), writes=[b_ones])
        s.dma("sp", gcol[:], g_dram, writes=[b_g])
        for t0 in range(0, T, NT):
            for c in range(KC):
                s.dma("sp", xt[:, c, :], xT[c * 128:(c + 1) * 128, t0:t0 + NT], writes=[b_xt[c]])
                s.op("act", lambda e: e.activation(out=sq[c % 2][:], in_=xt[:, c, :], func=AF.Square),
                     reads=[b_xt[c]], writes=[b_sq[c % 2]])
                s.op("pe", lambda e: e.matmul(pss[:], lhsT=ones[:], rhs=sq[c % 2][:],
                                             start=(c == 0), stop=(c == KC - 1)),
                     reads=[b_ones, b_sq[c % 2]], writes=[b_ps], inc=True)
            s.op("act", lambda e: e.activation(out=rs[:], in_=pss[:], func=AF.Sqrt, scale=1.0 / D, bias=eps_ap(k, eps)),
                 reads=[b_ps], writes=[b_rs])
            s.op("dve", lambda e: e.reciprocal(out=rstd[:], in_=rs[:]), reads=[b_rs], writes=[b_rstd])
            for c in range(KC):
                s.op("dve", lambda e: e.scalar_tensor_tensor(out=ht[:, c, :], in0=xt[:, c, :], scalar=gcol[:, c:c + 1],
                                                             in1=rstd[:], op0=ALU.mult, op1=ALU.mult),
                     reads=[b_xt[c], b_g, b_rstd], writes=[b_ht[c]])
            dma_rows(s, "sp", ht, out[:, t0:t0 + NT], KC, reads=b_ht, to_dram=True)
        s.barrier()


_EPS = {}


def eps_ap(k, val):
    return float(val)


def stage_proj(k, hT, W, PT, D, T, NB):
    NT = min(512, T)
    s = k.s
    KC = D // 128
    GRP = 8
    with contextlib.ExitStack() as st:
        wf = [k.sb(st, [128, KC, 128], F32) for _ in range(2)]
        wb = [k.sb(st, [128, KC, 128], BF16) for _ in range(GRP)]
        ht = [k.sb(st, [128, KC, NT], BF16) for _ in range(2)]
        ot = [k.sb(st, [128, NT], F32) for _ in range(4)]
        ps = [k.ps(st, [128, NT]) for _ in range(8)]
        b_wf, b_wb, b_ht, b_ot, b_ps = bufs(2), bufs(GRP), bufs(2), bufs(4), bufs(8)
        nld = 0
        nht = 0
        no = 0
        for g0 in range(0, NB, GRP):
            nb = min(GRP, NB - g0)
            for b in range(nb):
                i = nld % 2
                nld += 1
                s.dma("sp", wf[i][:], W[g0 + b], writes=[b_wf[i]])
                ce = "act" if b % 2 == 0 else "pool"
                if ce == "act":
                    s.op("act", lambda e: e.copy(out=wb[b][:], in_=wf[i][:]), reads=[b_wf[i]], writes=[b_wb[b]])
                else:
                    s.op("pool", lambda e: e.tensor_copy(out=wb[b][:], in_=wf[i][:]), reads=[b_wf[i]], writes=[b_wb[b]])
            for t0 in range(0, T, NT):
                h = nht % 2
                nht += 1
                dma_rows(s, "act", ht[h], hT[:, t0:t0 + NT], KC, writes=[b_ht[h]])
                for b in range(nb):
                    p = ps[b]
                    for c in range(KC):
                        s.op("pe", lambda e: e.matmul(p[:], lhsT=wb[b][:, c, :], rhs=ht[h][:, c, :],
                                                     start=(c == 0), stop=(c == KC - 1)),
                             reads=[b_wb[b], b_ht[h]], writes=[b_ps[b]], inc=(c == KC - 1))
                    o = no % 4
                    no += 1
                    if b % 2 == 0:
                        s.op("dve", lambda e: e.tensor_copy(out=ot[o][:], in_=p[:]), reads=[b_ps[b]], writes=[b_ot[o]])
                    else:
                        s.op("act", lambda e: e.copy(out=ot[o][:], in_=p[:]), reads=[b_ps[b]], writes=[b_ot[o]])
                    pt_t, pt_r = PT(g0 + b)
                    s.dma("sp", pt_t[pt_r:pt_r + 128, t0:t0 + NT], ot[o][:], reads=[b_ot[o]])
        s.barrier()


def cast_dram(k, dst, src, n_elems, q="pool", width=1024):
    s = k.s
    ROW = width
    assert n_elems % ROW == 0
    rows = n_elems // ROW
    CH = 2048
    for r0 in range(0, rows, CH):
        r1 = min(rows, r0 + CH)
        s.dma(q, dst[r0:r1, :], src[r0:r1, :])


def stage_merge(k, hT, yT, br_kc, Wg, Wbr, Wo, xT, xnT, D, T):
    s = k.s
    KC = D // 128
    OB = D // 128
    YC = sum(br_kc)
    yoff = [sum(br_kc[:i]) for i in range(len(br_kc))]
    NBR = len(br_kc)
    with contextlib.ExitStack() as st:
        ht = k.sb(st, [128, KC, NT], BF16)
        yt = k.sb(st, [128, YC, NT], BF16)
        mg = k.sb(st, [128, KC, NT], BF16)
        wg = [k.sb(st, [128, KC, 128], BF16) for _ in range(3)]
        wbr = [k.sb(st, [128, max(br_kc), 128], BF16) for _ in range(3)]
        gs = [k.sb(st, [128, NT], F32) for _ in range(2)]
        acc = [k.sb(st, [128, NT], F32) for _ in range(2)]
        tmp = [k.sb(st, [128, NT], F32) for _ in range(2)]
        xt = [k.sb(st, [128, NT], F32) for _ in range(2)]
        xn = [k.sb(st, [128, NT], F32) for _ in range(2)]
        psg = [k.ps(st, [128, NT]) for _ in range(2)]
        psp = [k.ps(st, [128, NT]) for _ in range(2)]
        pso = [k.ps(st, [128, NT]) for _ in range(2)]
        b_ht, b_yt = Buf(), Buf()
        b_mg = bufs(KC)
        b_wg, b_wbr, b_gs, b_acc, b_tmp, b_xt, b_xn = bufs(3), bufs(3), bufs(2), bufs(2), bufs(2), bufs(2), bufs(2)
        b_psg, b_psp, b_pso = bufs(2), bufs(2), bufs(2)
        nw = 0
        ng = 0
        na = 0
        for t0 in range(0, T, NT):
            dma_rows(s, "act", ht, hT[:, t0:t0 + NT], KC, writes=[b_ht])
            dma_rows(s, "act", yt, yT[:, t0:t0 + NT], YC, writes=[b_yt])
            for ob in range(OB):
                a = na % 2
                na += 1
                for br in range(NBR):
                    w = nw % 3
                    nw += 1
                    g = ng % 2
                    ng += 1
                    kcb = br_kc[br]
                    s.dma("sp", wg[w][:], Wg[br, ob], writes=[b_wg[w]])
                    s.dma("sp", wbr[w][:, 0:kcb, :], Wbr[br][ob], writes=[b_wbr[w]])
                    for c in range(KC):
                        s.op("pe", lambda e: e.matmul(psg[g][:], lhsT=wg[w][:, c, :], rhs=ht[:, c, :],
                                                     start=(c == 0), stop=(c == KC - 1)),
                             reads=[b_wg[w], b_ht], writes=[b_psg[g]], inc=(c == KC - 1))
                    for c in range(kcb):
                        s.op("pe", lambda e: e.matmul(psp[g][:], lhsT=wbr[w][:, c, :], rhs=yt[:, yoff[br] + c, :],
                                                     start=(c == 0), stop=(c == kcb - 1)),
                             reads=[b_wbr[w], b_yt], writes=[b_psp[g]], inc=(c == kcb - 1))
                    s.op("act", lambda e: e.activation(out=gs[g][:], in_=psg[g][:], func=AF.Sigmoid),
                         reads=[b_psg[g]], writes=[b_gs[g]])
                    last = (br == NBR - 1)
                    if br == 0:
                        dst, bdst = (mg[:, ob, :], b_mg[ob]) if last else (acc[a][:], b_acc[a])
                        s.op("dve", lambda e: e.tensor_tensor(out=dst, in0=psp[g][:], in1=gs[g][:], op=ALU.mult),
                             reads=[b_psp[g], b_gs[g]], writes=[bdst])
                    else:
                        s.op("dve", lambda e: e.tensor_tensor(out=tmp[g][:], in0=psp[g][:], in1=gs[g][:], op=ALU.mult),
                             reads=[b_psp[g], b_gs[g]], writes=[b_tmp[g]])
                        if last:
                            s.op("pool", lambda e: e.tensor_tensor(out=mg[:, ob, :], in0=acc[a][:], in1=tmp[g][:], op=ALU.add),
                                 reads=[b_acc[a], b_tmp[g]], writes=[b_mg[ob]])
                        else:
                            s.op("pool", lambda e: e.tensor_tensor(out=acc[a][:], in0=acc[a][:], in1=tmp[g][:], op=ALU.add),
                                 reads=[b_acc[a], b_tmp[g]], writes=[b_acc[a]])
            for ob in range(OB):
                w = nw % 3
                nw += 1
                g = ng % 2
                ng += 1
                s.dma("sp", wg[w][:], Wo[ob], writes=[b_wg[w]])
                s.dma("act", xt[g][:], xT[ob * 128:(ob + 1) * 128, t0:t0 + NT], writes=[b_xt[g]])
                for c in range(KC):
                    s.op("pe", lambda e: e.matmul(pso[g][:], lhsT=wg[w][:, c, :], rhs=mg[:, c, :],
                                                 start=(c == 0), stop=(c == KC - 1)),
                         reads=[b_wg[w], b_mg[c]], writes=[b_pso[g]], inc=(c == KC - 1))
                s.op("dve", lambda e: e.tensor_tensor(out=xn[g][:], in0=pso[g][:], in1=xt[g][:], op=ALU.add),
                     reads=[b_pso[g], b_xt[g]], writes=[b_xn[g]])
                s.dma("sp", xnT[ob * 128:(ob + 1) * 128, t0:t0 + NT], xn[g][:], reads=[b_xn[g]])
        s.barrier()


def stage_lru(k, PT, row_ax, row_ag, yT, row_y, prm, T, NBLK):
    s = k.s
    NCH = NBLK * 2
    with contextlib.ExitStack() as st:
        cw = k.sb(st, [128, NCH, 4], F32)
        cb, ba, bx, lam, c1, c2, tt = [k.sb(st, [128, NCH], F32) for _ in range(7)]
        b_prm = Buf()
        for dst, nm in ((cw, "conv_w"), (cb, "conv_b"), (ba, "ba"), (bx, "bx"), (lam, "lam")):
            s.dma("sp", dst[:], prm[nm], writes=[b_prm])
        s.op("act", lambda e: e.activation(out=tt[:], in_=lam[:], func=AF.Exp, scale=-1.0), reads=[b_prm], writes=[b_prm])
        s.op("act", lambda e: e.activation(out=tt[:], in_=tt[:], func=AF.Ln, bias=1.0), reads=[b_prm], writes=[b_prm])
        s.op("dve", lambda e: e.tensor_scalar(out=c1[:], in0=tt[:], scalar1=-8.0, scalar2=None, op0=ALU.mult), reads=[b_prm], writes=[b_prm])
        s.op("dve", lambda e: e.tensor_scalar(out=c2[:], in0=tt[:], scalar1=-16.0, scalar2=None, op0=ALU.mult), reads=[b_prm], writes=[b_prm])
        waf = k.sb(st, [128, 2, 2, 128], F32)
        wab = [k.sb(st, [128, 2, 2, 128], BF16) for _ in range(2)]
        xin = [k.sb(st, [128, 3 + NT], F32) for _ in range(2)]
        u = [k.sb(st, [128, NT], F32) for _ in range(2)]
        ub = [k.sb(st, [128, NT], BF16) for _ in range(2)]
        rr, ii, aa, a2, mm, bt, gt, sg = [k.sb(st, [128, NT], F32) for _ in range(8)]
        hs = [[k.sb(st, [128, NT], F32) for _ in range(2)] for _ in range(2)]
        yo = [k.sb(st, [128, NT], BF16) for _ in range(2)]
        psr, psi = k.ps(st, [128, NT]), k.ps(st, [128, NT])
        b_waf, b_psr, b_psi, b_rr, b_ii, b_aa, b_a2, b_mm, b_bt, b_gt, b_sg = bufs(11)
        b_wab, b_xin, b_u, b_ub, b_yo = bufs(2), bufs(2), bufs(2), bufs(2), bufs(2)
        b_hs = [bufs(2), bufs(2)]
        for nb in range(NBLK):
            for wi, nm in enumerate(("wa", "wx")):
                s.dma("sp", waf[:], prm[nm][nb].rearrange("o p c m -> p o c m"), writes=[b_waf])
                s.op("act", lambda e: e.copy(out=wab[wi][:], in_=waf[:]), reads=[b_waf], writes=[b_wab[wi]])
            for ti, t0 in enumerate(range(0, T, NT)):
                par = ti % 2
                for kc in range(2):
                    ch = nb * 2 + kc
                    if ti == 0:
                        s.op("pool", lambda e: e.memset(xin[kc][:, 0:3], 0.0), writes=[b_xin[kc]])
                    else:
                        s.op("pool", lambda e: e.tensor_copy(out=xin[kc][:, 0:3], in_=xin[kc][:, NT:NT + 3]),
                             reads=[b_xin[kc]], writes=[b_xin[kc]])
                    s.dma("sp", xin[kc][:, 3:3 + NT], PT[row_ax + ch * 128: row_ax + (ch + 1) * 128, t0:t0 + NT],
                          reads=[b_xin[kc]], writes=[b_xin[kc]])
                    s.op("dve", lambda e: e.tensor_scalar(out=u[kc][:], in0=xin[kc][:, 3:3 + NT], scalar1=cw[:, ch, 3:4],
                                                          scalar2=cb[:, ch:ch + 1], op0=ALU.mult, op1=ALU.add),
                         reads=[b_xin[kc], b_prm], writes=[b_u[kc]])
                    for j in (2, 1, 0):
                        s.op("dve", lambda e: e.scalar_tensor_tensor(out=u[kc][:], in0=xin[kc][:, j:j + NT], scalar=cw[:, ch, j:j + 1],
                                                                     in1=u[kc][:], op0=ALU.mult, op1=ALU.add),
                             reads=[b_xin[kc], b_prm, b_u[kc]], writes=[b_u[kc]])
                    s.op("act", lambda e: e.copy(out=ub[kc][:], in_=u[kc][:]), reads=[b_u[kc]], writes=[b_ub[kc]])
                for oc in range(2):
                    ch = nb * 2 + oc
                    for kc in range(2):
                        s.op("pe", lambda e: e.matmul(psr[:], lhsT=wab[0][:, oc, kc, :], rhs=ub[kc][:], start=(kc == 0), stop=(kc == 1)),
                             reads=[b_wab[0], b_ub[kc]], writes=[b_psr], inc=(kc == 1))
                    for kc in range(2):
                        s.op("pe", lambda e: e.matmul(psi[:], lhsT=wab[1][:, oc, kc, :], rhs=ub[kc][:], start=(kc == 0), stop=(kc == 1)),
                             reads=[b_wab[1], b_ub[kc]], writes=[b_psi], inc=(kc == 1))
                    s.op("act", lambda e: e.activation(out=rr[:], in_=psr[:], func=AF.Sigmoid, bias=ba[:, ch:ch + 1]),
                         reads=[b_psr, b_prm], writes=[b_rr])
                    s.op("act", lambda e: e.activation(out=ii[:], in_=psi[:], func=AF.Sigmoid, bias=bx[:, ch:ch + 1]),
                         reads=[b_psi, b_prm], writes=[b_ii])
                    s.op("act", lambda e: e.activation(out=aa[:], in_=rr[:], func=AF.Exp, scale=c1[:, ch:ch + 1]),
                         reads=[b_rr, b_prm], writes=[b_aa])
                    s.op("act", lambda e: e.activation(out=a2[:], in_=rr[:], func=AF.Exp, scale=c2[:, ch:ch + 1]),
                         reads=[b_rr, b_prm], writes=[b_a2])
                    s.op("act", lambda e: e.activation(out=mm[:], in_=a2[:], func=AF.Sqrt, scale=-1.0, bias=1.0),
                         reads=[b_a2], writes=[b_mm])
                    s.op("dve", lambda e: e.tensor_tensor(out=bt[:], in0=mm[:], in1=ii[:], op=ALU.mult),
                         reads=[b_mm, b_ii], writes=[b_bt])
                    s.op("dve", lambda e: e.tensor_tensor(out=bt[:], in0=bt[:], in1=u[oc][:], op=ALU.mult),
                         reads=[b_bt, b_u[oc]], writes=[b_bt])
                    init = 0.0 if ti == 0 else hs[oc][1 - par][:, NT - 1:NT]
                    s.op("dve", lambda e: e.tensor_tensor_scan(out=hs[oc][par][:], data0=aa[:], data1=bt[:], initial=init,
                                                               op0=ALU.mult, op1=ALU.add),
                         reads=[b_aa, b_bt] + ([] if ti == 0 else [b_hs[oc][1 - par]]), writes=[b_hs[oc][par]])
                    s.dma("act", gt[:], PT[row_ag + ch * 128: row_ag + (ch + 1) * 128, t0:t0 + NT], writes=[b_gt])
                    s.op("act", lambda e: e.activation(out=sg[:], in_=gt[:], func=AF.Silu), reads=[b_gt], writes=[b_sg])
                    s.op("dve", lambda e: e.tensor_tensor(out=yo[oc][:], in0=hs[oc][par][:], in1=sg[:], op=ALU.mult),
                         reads=[b_hs[oc][par], b_sg], writes=[b_yo[oc]])
                    s.dma("sp", yT[row_y + ch * 128: row_y + (ch + 1) * 128, t0:t0 + NT], yo[oc][:], reads=[b_yo[oc]])
        s.barrier()


def stage_xattn(k, PT, row_q, row_g, yT, row_y, memnT, Wk, Wv, ident_d, T, D, M, H):
    s = k.s
    KC = D // 128
    MC = M // 128
    sc = 128 ** -0.5
    with contextlib.ExitStack() as st:
        ident = k.sb(st, [128, 128], F32)
        memn = k.sb(st, [128, KC, M], BF16)
        kT = k.sb(st, [128, H, M], BF16)
        vt = k.sb(st, [128, MC, H * 128], BF16)
        b_id, b_memn, b_kT, b_vt = bufs(4)
        s.dma("sp", ident[:], ident_d, writes=[b_id])
        dma_rows(s, "sp", memn, memnT, KC, writes=[b_memn])
        with contextlib.ExitStack() as st1:
            wf = k.sb(st1, [128, KC, 128], F32)
            wb = k.sb(st1, [128, KC, 128], BF16)
            vf = [k.sb(st1, [128, H * 128], F32) for _ in range(2)]
            vb = [k.sb(st1, [128, H * 128], BF16) for _ in range(2)]
            psk = k.ps(st1, [128, M])
            psv = [k.ps(st1, [128, H * 128]) for _ in range(MC)]
            b_wf, b_wb, b_psk = bufs(3)
            b_vf, b_vb, b_psv = bufs(2), bufs(2), bufs(MC)
            for hd in range(H):
                s.dma("sp", wf[:], Wk[hd], writes=[b_wf])
                s.op("act", lambda e: e.copy(out=wb[:], in_=wf[:]), reads=[b_wf], writes=[b_wb])
                for c in range(KC):
                    s.op("pe", lambda e: e.matmul(psk[:], lhsT=wb[:, c, :], rhs=memn[:, c, :], start=(c == 0), stop=(c == KC - 1)),
                         reads=[b_wb, b_memn], writes=[b_psk], inc=(c == KC - 1))
                s.op("dve", lambda e: e.tensor_copy(out=kT[:, hd, :], in_=psk[:]), reads=[b_psk], writes=[b_kT])
            for c in range(KC):
                i = c % 2
                s.dma("sp", vf[i][:], Wv[c], writes=[b_vf[i]])
                s.op("act", lambda e: e.copy(out=vb[i][:], in_=vf[i][:]), reads=[b_vf[i]], writes=[b_vb[i]])
                for mc in range(MC):
                    s.op("pe", lambda e: e.matmul(psv[mc][:], lhsT=memn[:, c, mc * 128:(mc + 1) * 128], rhs=vb[i][:],
                                                 start=(c == 0), stop=(c == KC - 1)),
                         reads=[b_vb[i], b_memn], writes=[b_psv[mc]], inc=True)
            for mc in range(MC):
                s.op("dve", lambda e: e.tensor_copy(out=vt[:, mc, :], in_=psv[mc][:]), reads=[b_psv[mc]], writes=[b_vt])
            s.barrier()
        qf = k.sb(st, [128, H, NT], F32)
        qb = k.sb(st, [128, H, NT], BF16)
        gf = k.sb(st, [128, H, NT], F32)
        sg = k.sb(st, [128, H, NT], F32)
        pf = [k.sb(st, [128, M], F32) for _ in range(2)]
        pn = [k.sb(st, [128, M], F32) for _ in range(2)]
        pT = [k.sb(st, [128, MC, 128], BF16) for _ in range(2)]
        mx, nmx, rsum, rinv = [[k.sb(st, [128, 1], F32) for _ in range(2)] for _ in range(4)]
        yo = [k.sb(st, [128, NT], BF16) for _ in range(2)]
        pss = [k.ps(st, [128, M]) for _ in range(2)]
        pst = [k.ps(st, [128, MC, 128]) for _ in range(2)]
        pso = [k.ps(st, [128, 128]) for _ in range(2)]
        b_qf, b_qb, b_gf, b_sg = bufs(4)
        b_pf, b_pn, b_pT, b_mx, b_nmx, b_rsum, b_rinv, b_yo, b_pss, b_pst, b_pso = [bufs(2) for _ in range(11)]
        it = 0
        for t0 in range(0, T, NT):
            s.dma("sp", qf[:], PT[row_q:row_q + H * 128, t0:t0 + NT].rearrange("(h p) t -> p h t", p=128), writes=[b_qf])
            s.dma("act", gf[:], PT[row_g:row_g + H * 128, t0:t0 + NT].rearrange("(h p) t -> p h t", p=128), writes=[b_gf])
            s.op("act", lambda e: e.copy(out=qb[:], in_=qf[:]), reads=[b_qf], writes=[b_qb])
            s.op("act", lambda e: e.activation(out=sg[:], in_=gf[:], func=AF.Silu), reads=[b_gf], writes=[b_sg])
            for hd in range(H):
                yi = hd % 2
                for tb in range(NT // 128):
                    i = it % 2
                    it += 1
                    tsl = slice(tb * 128, (tb + 1) * 128)
                    s.op("pe", lambda e: e.matmul(pss[i][:], lhsT=qb[:, hd, tsl], rhs=kT[:, hd, :], start=True, stop=True),
                         reads=[b_qb, b_kT], writes=[b_pss[i]])
                    s.op("dve", lambda e: e.tensor_reduce(out=mx[i][:], in_=pss[i][:], axis=AX.X, op=ALU.max),
                         reads=[b_pss[i]], writes=[b_mx[i]])
                    s.op("dve", lambda e: e.tensor_scalar(out=nmx[i][:], in0=mx[i][:], scalar1=-sc, scalar2=None, op0=ALU.mult),
                         reads=[b_mx[i]], writes=[b_nmx[i]])
                    s.op("act", lambda e: e.activation(out=pf[i][:], in_=pss[i][:], func=AF.Exp, scale=sc, bias=nmx[i][:],
                                                       accum_out=rsum[i][:]),
                         reads=[b_pss[i], b_nmx[i]], writes=[b_pf[i], b_rsum[i]])
                    s.op("dve", lambda e: e.reciprocal(out=rinv[i][:], in_=rsum[i][:]), reads=[b_rsum[i]], writes=[b_rinv[i]])
                    s.op("dve", lambda e: e.tensor_scalar(out=pn[i][:], in0=pf[i][:], scalar1=rinv[i][:], scalar2=None, op0=ALU.mult),
                         reads=[b_pf[i], b_rinv[i]], writes=[b_pn[i]])
                    for mc in range(MC):
                        s.op("pe", lambda e: e.transpose(pst[i][:, mc, :], pn[i][:, mc * 128:(mc + 1) * 128], ident[:]),
                             reads=[b_pn[i], b_id], writes=[b_pst[i]], inc=True)
                    s.op("act", lambda e: e.copy(out=pT[i][:], in_=pst[i][:]), reads=[b_pst[i]], writes=[b_pT[i]])
                    for mc in range(MC):
                        s.op("pe", lambda e: e.matmul(pso[i][:], lhsT=vt[:, mc, hd * 128:(hd + 1) * 128], rhs=pT[i][:, mc, :],
                                                     start=(mc == 0), stop=(mc == MC - 1)),
                             reads=[b_vt, b_pT[i]], writes=[b_pso[i]], inc=(mc == MC - 1))
                    s.op("dve", lambda e: e.tensor_tensor(out=yo[yi][:, tsl], in0=pso[i][:], in1=sg[:, hd, tsl], op=ALU.mult),
                         reads=[b_pso[i], b_sg], writes=[b_yo[yi]])
                s.dma("sp", yT[row_y + hd * 128: row_y + (hd + 1) * 128, t0:t0 + NT], yo[yi][:], reads=[b_yo[yi]])
        s.barrier()


LCH = 64
NEG = -1.0e30
RWKV_G = 2
RWKV_L = 128


def stage_mlstm(k, PT, rows, yT, row_y, prm, cst, T, HC):
    s = k.s
    NCK = T // LCH
    QC = 2 * HC
    with contextlib.ExitStack() as st:
        ident = k.sb(st, [128, 128], F32)
        sel = k.sb(st, [HC, HC, 128], F32)
        i4 = k.sb(st, [HC, HC], F32)
        negmask = k.sb(st, [64, 64], F32)
        ones = k.sb(st, [128, 128], F32)
        cw = k.sb(st, [128, 2 * QC, 4], F32)
        cb = k.sb(st, [128, 2 * QC], F32)
        gnw = k.sb(st, [128, HC * 4], F32)
        bi = k.sb(st, [HC, 1], F32)
        bfn = k.sb(st, [HC, 1], F32)
        b_c = Buf()
        for dst, src in ((ident, cst["ident"]), (sel, cst["sel"]), (i4, cst["i4"]), (negmask, cst["negmask"]),
                         (cw, prm["conv_w"]), (cb, prm["conv_b"]), (gnw, prm["gn_w"]), (bi, prm["b_i"]), (bfn, prm["b_f"])):
            s.dma("sp", dst[:], src, writes=[b_c])
        s.op("pool", lambda e: e.memset(ones[:], 1.0), writes=[b_c])
        s.op("dve", lambda e: e.tensor_scalar(out=bfn[:], in0=bfn[:], scalar1=-1.0, scalar2=None, op0=ALU.mult), reads=[b_c], writes=[b_c])
        gi = k.sb(st, [HC, T], F32)
        gf = k.sb(st, [HC, T], F32)
        gm = k.sb(st, [HC, T], F32)
        ge = k.sb(st, [HC, T], F32)
        b_g = Buf()
        s.dma("sp", gi[:], PT[rows["ifg"]:rows["ifg"] + HC, :], writes=[b_g])
        s.dma("sp", gf[:], PT[rows["ifg"] + HC:rows["ifg"] + 2 * HC, :], writes=[b_g])
        s.op("act", lambda e: e.activation(out=gf[:], in_=gf[:], func=AF.Exp, scale=-1.0, bias=bfn[:]), reads=[b_g, b_c], writes=[b_g])
        s.op("act", lambda e: e.activation(out=gf[:], in_=gf[:], func=AF.Ln, bias=1.0), reads=[b_g], writes=[b_g])
        s.op("dve", lambda e: e.tensor_scalar(out=gf[:], in0=gf[:], scalar1=-1.0, scalar2=None, op0=ALU.mult), reads=[b_g], writes=[b_g])
        s.op("pool", lambda e: e.memset(gm[:], 1.0), writes=[b_g])
        s.op("dve", lambda e: e.tensor_tensor_scan(out=gf[:], data0=gm[:], data1=gf[:], initial=0.0, op0=ALU.mult, op1=ALU.add),
             reads=[b_g], writes=[b_g])
        s.op("dve", lambda e: e.scalar_tensor_tensor(out=gi[:], in0=gi[:], scalar=bi[:], in1=gf[:], op0=ALU.add, op1=ALU.subtract),
             reads=[b_g, b_c], writes=[b_g])
        s.op("dve", lambda e: e.tensor_tensor_scan(out=gm[:], data0=gi[:], data1=gi[:], initial=NEG, op0=ALU.max, op1=ALU.max),
             reads=[b_g], writes=[b_g])
        s.op("dve", lambda e: e.tensor_tensor(out=ge[:], in0=gf[:], in1=gm[:], op=ALU.add), reads=[b_g], writes=[b_g])
        cols = k.sb(st, [64, NCK, 2 * HC], F32)
        b_cols = Buf()
        with contextlib.ExitStack() as st1:
            psc = [k.ps(st1, [64, 8, 2 * HC]) for _ in range(2)]
            b_psc = bufs(2)
            for c8 in range(0, NCK, 8):
                i = (c8 // 8) % 2
                for cc in range(8):
                    c = c8 + cc
                    s.op("pe", lambda e: e.matmul(psc[i][:, cc, 0:HC], lhsT=gi[:, c * 64:(c + 1) * 64], rhs=i4[:, 0:HC], start=True, stop=True),
                         reads=[b_g, b_c], writes=[b_psc[i]])
                    s.op("pe", lambda e: e.matmul(psc[i][:, cc, HC:2 * HC], lhsT=ge[:, c * 64:(c + 1) * 64], rhs=i4[:, 0:HC], start=True, stop=True),
                         reads=[b_g, b_c], writes=[b_psc[i]])
                s.op("dve", lambda e: e.tensor_copy(out=cols[:, c8:c8 + 8, :], in_=psc[i][:]), reads=[b_psc[i]], writes=[b_cols])
            s.op("act", lambda e: e.activation(out=cols[:, :, HC:2 * HC], in_=cols[:, :, HC:2 * HC], func=AF.Exp, scale=-1.0),
                 reads=[b_cols], writes=[b_cols])
            s.barrier()
        xin = k.sb(st, [128, 4, 3 + NT], F32)
        qk = k.sb(st, [128, 4, NT], F32)
        vT = k.sb(st, [128, 4, NT], F32)
        oT = k.sb(st, [128, 4, NT], F32)
        gT = k.sb(st, [128, 4, NT], F32)
        negMb = k.sb(st, [128, NT], F32)
        ms = k.sb(st, [128, NT // LCH + 1], F32)
        keep = k.sb(st, [128, 1], F32)
        ho = k.sb(st, [128, 4, NT], F32)
        Cst = k.sb(st, [128, 2, 512], F32)
        nst = k.sb(st, [128, 2], F32)
        ktok = k.sb(st, [64, 256], F32)
        khat = k.sb(st, [64, 256], F32)
        vtok = k.sb(st, [64, 512], F32)
        wk = k.sb(st, [64, 1], F32)
        argE = k.sb(st, [64, 64], F32)
        ET = k.sb(st, [64, 64], F32)
        scT = k.sb(st, [64, 64], F32)
        wint = k.sb(st, [128, 64], F32)
        qt = k.sb(st, [128, 2, 64], F32)
        den = k.sb(st, [64, 1], F32)
        rden = k.sb(st, [64, 1], F32)
        htok = k.sb(st, [64, 512], F32)
        mean = k.sb(st, [128, NT], F32)
        yc = k.sb(st, [128, 4, NT], F32)
        ysq = k.sb(st, [128, NT], F32)
        rstd = k.sb(st, [128, NT], F32)
        yo = [k.sb(st, [128, NT], BF16) for _ in range(2)]
        ps_b = k.ps(st, [128, NT])
        ps_t = k.ps(st, [64, 512])
        ps_s = k.ps(st, [64, 64])
        ps_n = k.ps(st, [64, 512])
        ps_d = k.ps(st, [64, 8])
        ps_c = k.ps(st, [128, 512])
        ps_h = k.ps(st, [128, 4, 64])
        ps_m = k.ps(st, [128, 8])
        (b_xin, b_qk, b_vT, b_oT, b_gT, b_negMb, b_ms, b_keep, b_ho, b_C, b_n, b_ktok, b_khat, b_vtok, b_wk, b_argE, b_ET, b_scT,
         b_wint, b_qt, b_den, b_rden, b_htok, b_mean, b_yc, b_ysq, b_rstd, b_psb, b_pst, b_pss, b_psn, b_psd, b_psc2, b_psh, b_psm) = bufs(35)
        b_yo = bufs(2)
        for h in range(HC):
            qrows = [rows["q"] + h * 256, rows["q"] + h * 256 + 128, rows["k"] + h * 256, rows["k"] + h * 256 + 128]
            cidx = [h * 2, h * 2 + 1, QC + h * 2, QC + h * 2 + 1]
            s.op("pool", lambda e: e.memset(Cst[:], 0.0), reads=[b_C], writes=[b_C])
            s.op("pool", lambda e: e.memset(nst[:], 0.0), reads=[b_n], writes=[b_n])
            s.op("pool", lambda e: e.memset(ms[:, 0:1], NEG), reads=[b_ms], writes=[b_ms])
            for ti, t0 in enumerate(range(0, T, NT)):
                for j in range(4):
                    if ti == 0:
                        s.op("pool", lambda e: e.memset(xin[:, j, 0:3], 0.0), reads=[b_xin], writes=[b_xin])
                    else:
                        s.op("pool", lambda e: e.tensor_copy(out=xin[:, j, 0:3], in_=xin[:, j, NT:NT + 3]), reads=[b_xin], writes=[b_xin])
                for j in range(4):
                    s.dma("sp", xin[:, j, 3:3 + NT], PT[qrows[j]:qrows[j] + 128, t0:t0 + NT], reads=[b_xin], writes=[b_xin])
                s.dma("act", vT[:], PT[rows["v"] + h * 512: rows["v"] + (h + 1) * 512, t0:t0 + NT].rearrange("(c p) t -> p c t", p=128), writes=[b_vT])
                s.dma("act", oT[:], PT[rows["o"] + h * 512: rows["o"] + (h + 1) * 512, t0:t0 + NT].rearrange("(c p) t -> p c t", p=128), writes=[b_oT])
                s.dma("act", gT[:], PT[rows["g"] + h * 512: rows["g"] + (h + 1) * 512, t0:t0 + NT].rearrange("(c p) t -> p c t", p=128), writes=[b_gT])
                for j in range(4):
                    ci = cidx[j]
                    s.op("dve", lambda e: e.tensor_scalar(out=qk[:, j, :], in0=xin[:, j, 3:3 + NT], scalar1=cw[:, ci, 3:4],
                                                          scalar2=cb[:, ci:ci + 1], op0=ALU.mult, op1=ALU.add),
                         reads=[b_xin, b_c, b_qk], writes=[b_qk])
                    for jj in (2, 1, 0):
                        s.op("dve", lambda e: e.scalar_tensor_tensor(out=qk[:, j, :], in0=xin[:, j, jj:jj + NT], scalar=cw[:, ci, jj:jj + 1],
                                                                     in1=qk[:, j, :], op0=ALU.mult, op1=ALU.add),
                             reads=[b_xin, b_c, b_qk], writes=[b_qk])
                s.op("act", lambda e: e.activation(out=qk[:], in_=qk[:], func=AF.Silu), reads=[b_qk], writes=[b_qk])
                s.op("dve", lambda e: e.tensor_scalar(out=qk[:, 2:4, :], in0=qk[:, 2:4, :], scalar1=256 ** -0.5, scalar2=None, op0=ALU.mult),
                     reads=[b_qk], writes=[b_qk])
                s.op("act", lambda e: e.activation(out=oT[:], in_=oT[:], func=AF.Sigmoid), reads=[b_oT], writes=[b_oT])
                s.op("act", lambda e: e.activation(out=gT[:], in_=gT[:], func=AF.Silu), reads=[b_gT], writes=[b_gT])
                s.op("pe", lambda e: e.matmul(ps_b[:], lhsT=sel[:, h, :], rhs=gm[:, t0:t0 + NT], start=True, stop=True),
                     reads=[b_c, b_g, b_psb], writes=[b_psb])
                if ti > 0:
                    s.op("dve", lambda e: e.tensor_scalar(out=ms[:, 0:1], in0=negMb[:, NT - 1:NT], scalar1=-1.0, scalar2=None, op0=ALU.mult),
                         reads=[b_negMb, b_ms], writes=[b_ms])
                s.op("act", lambda e: e.activation(out=negMb[:], in_=ps_b[:], func=AF.Copy, scale=-1.0), reads=[b_psb, b_negMb], writes=[b_negMb])
                nck = NT // LCH
                s.op("dve", lambda e: e.tensor_scalar(out=ms[:, 1:nck + 1], in0=negMb[:, LCH - 1:NT:LCH], scalar1=-1.0, scalar2=None, op0=ALU.mult),
                     reads=[b_negMb, b_ms], writes=[b_ms])
                for cc in range(nck):
                    c = ti * nck + cc
                    csl = slice(cc * LCH, (cc + 1) * LCH)
                    for j in range(2):
                        s.op("pe", lambda e: e.transpose(ps_t[:, j * 128:(j + 1) * 128], qk[:, 2 + j, csl], ident[:]),
                             reads=[b_qk, b_c, b_pst], writes=[b_pst])
                    s.op("act", lambda e: e.copy(out=ktok[:], in_=ps_t[:, 0:256]), reads=[b_pst, b_ktok], writes=[b_ktok])
                    for j in range(4):
                        s.op("pe", lambda e: e.transpose(ps_t[:, j * 128:(j + 1) * 128], vT[:, j, csl], ident[:]),
                             reads=[b_vT, b_c, b_pst], writes=[b_pst])
                    s.op("act", lambda e: e.copy(out=vtok[:], in_=ps_t[:]), reads=[b_pst, b_vtok], writes=[b_vtok])
                    for j in range(2):
                        s.op("pe", lambda e: e.matmul(ps_s[:], lhsT=qk[:, 2 + j, csl], rhs=qk[:, j, csl], start=(j == 0), stop=(j == 1)),
                             reads=[b_qk, b_pss], writes=[b_pss])
                    s.op("dve", lambda e: e.tensor_tensor(out=argE[:], in0=negMb[0:64, csl], in1=negmask[:], op=ALU.add),
                         reads=[b_negMb, b_c, b_argE], writes=[b_argE])
                    s.op("act", lambda e: e.activation(out=ET[:], in_=argE[:], func=AF.Exp, bias=cols[:, c, h:h + 1]),
                         reads=[b_argE, b_cols, b_ET], writes=[b_ET])
                    s.op("dve", lambda e: e.tensor_tensor(out=scT[:], in0=ps_s[:], in1=ET[:], op=ALU.mult),
                         reads=[b_pss, b_ET, b_scT], writes=[b_scT])
                    s.op("act", lambda e: e.activation(out=wint[:], in_=negMb[:, csl], func=AF.Exp, bias=ms[:, cc:cc + 1]),
                         reads=[b_negMb, b_ms, b_wint], writes=[b_wint])
                    for j in range(2):
                        s.op("dve", lambda e: e.tensor_tensor(out=qt[:, j, :], in0=qk[:, j, csl], in1=wint[:], op=ALU.mult),
                             reads=[b_qk, b_wint, b_qt], writes=[b_qt])
                    for j in range(2):
                        s.op("pe", lambda e: e.matmul(ps_n[:], lhsT=qt[:, j, :], rhs=Cst[:, j, :], start=(j == 0), stop=False),
                             reads=[b_qt, b_C, b_psn], writes=[b_psn], inc=False)
                    s.op("pe", lambda e: e.matmul(ps_n[:], lhsT=scT[:], rhs=vtok[:], start=False, stop=True),
                         reads=[b_scT, b_vtok, b_psn], writes=[b_psn])
                    for j in range(2):
                        s.op("pe", lambda e: e.matmul(ps_d[:, 0:1], lhsT=qt[:, j, :], rhs=nst[:, j:j + 1], start=(j == 0), stop=False),
                             reads=[b_qt, b_n, b_psd], writes=[b_psd], inc=False)
                    s.op("pe", lambda e: e.matmul(ps_d[:, 0:1], lhsT=scT[:], rhs=ones[0:64, 0:1], start=False, stop=True),
                         reads=[b_scT, b_c, b_psd], writes=[b_psd])
                    s.op("act", lambda e: e.activation(out=den[:], in_=ps_d[:, 0:1], func=AF.Abs),
                         reads=[b_psd, b_den], writes=[b_den])
                    s.op("dve", lambda e: e.tensor_tensor(out=den[:], in0=den[:], in1=cols[:, c, HC + h:HC + h + 1], op=ALU.max),
                         reads=[b_den, b_cols], writes=[b_den])
                    s.op("dve", lambda e: e.reciprocal(out=rden[:], in_=den[:]), reads=[b_den, b_rden], writes=[b_rden])
                    s.op("act", lambda e: e.activation(out=htok[:], in_=ps_n[:], func=AF.Copy, scale=rden[:]),
                         reads=[b_psn, b_rden, b_htok], writes=[b_htok])
                    for j in range(4):
                        s.op("pe", lambda e: e.transpose(ps_h[:, j, :], htok[:, j * 128:(j + 1) * 128], ident[0:64, 0:64]),
                             reads=[b_htok, b_c, b_psh], writes=[b_psh])
                    s.op("dve", lambda e: e.tensor_tensor(out=ho[:, :, csl], in0=ps_h[:], in1=oT[:, :, csl], op=ALU.mult),
                         reads=[b_psh, b_oT, b_ho], writes=[b_ho])
                    s.op("act", lambda e: e.activation(out=keep[:], in_=negMb[:, (cc + 1) * LCH - 1:(cc + 1) * LCH], func=AF.Exp, bias=ms[:, cc:cc + 1]),
                         reads=[b_negMb, b_ms, b_keep], writes=[b_keep])
                    s.op("act", lambda e: e.activation(out=wk[:], in_=negMb[0:64, (cc + 1) * LCH - 1:(cc + 1) * LCH], func=AF.Exp, bias=cols[:, c, h:h + 1]),
                         reads=[b_negMb, b_cols, b_wk], writes=[b_wk])
                    s.op("dve", lambda e: e.tensor_scalar(out=khat[:], in0=ktok[:], scalar1=wk[:], scalar2=None, op0=ALU.mult),
                         reads=[b_ktok, b_wk, b_khat], writes=[b_khat])
                    for j in range(2):
                        s.op("pe", lambda e: e.matmul(ps_c[:], lhsT=khat[:, j * 128:(j + 1) * 128], rhs=vtok[:], start=True, stop=True),
                             reads=[b_khat, b_vtok, b_psc2], writes=[b_psc2])
                        s.op("dve", lambda e: e.scalar_tensor_tensor(out=Cst[:, j, :], in0=Cst[:, j, :], scalar=keep[:], in1=ps_c[:],
                                                                     op0=ALU.mult, op1=ALU.add),
                             reads=[b_C, b_keep, b_psc2], writes=[b_C])
                        s.op("pe", lambda e: e.matmul(ps_m[:, 0:1], lhsT=khat[:, j * 128:(j + 1) * 128], rhs=ones[0:64, 0:1], start=True, stop=True),
                             reads=[b_khat, b_c, b_psm], writes=[b_psm])
                        s.op("dve", lambda e: e.scalar_tensor_tensor(out=nst[:, j:j + 1], in0=nst[:, j:j + 1], scalar=keep[:], in1=ps_m[:, 0:1],
                                                                     op0=ALU.mult, op1=ALU.add),
                             reads=[b_n, b_keep, b_psm], writes=[b_n])
                for j in range(4):
                    s.op("pe", lambda e: e.matmul(ps_b[:], lhsT=ones[:], rhs=ho[:, j, :], start=(j == 0), stop=(j == 3)),
                         reads=[b_c, b_ho, b_psb], writes=[b_psb], inc=(j == 3))
                s.op("act", lambda e: e.activation(out=mean[:], in_=ps_b[:], func=AF.Copy, scale=1.0 / 512), reads=[b_psb, b_mean], writes=[b_mean])
                for j in range(4):
                    s.op("dve", lambda e: e.tensor_tensor(out=yc[:, j, :], in0=ho[:, j, :], in1=mean[:], op=ALU.subtract),
                         reads=[b_ho, b_mean, b_yc], writes=[b_yc])
                for j in range(4):
                    s.op("act", lambda e: e.activation(out=ysq[:], in_=yc[:, j, :], func=AF.Square), reads=[b_yc, b_ysq], writes=[b_ysq])
                    s.op("pe", lambda e: e.matmul(ps_b[:], lhsT=ones[:], rhs=ysq[:], start=(j == 0), stop=(j == 3)),
                         reads=[b_c, b_ysq, b_psb], writes=[b_psb])
                s.op("act", lambda e: e.activation(out=rstd[:], in_=ps_b[:], func=AF.Sqrt, scale=1.0 / 512, bias=1e-6),
                     reads=[b_psb, b_rstd], writes=[b_rstd])
                s.op("dve", lambda e: e.reciprocal(out=rstd[:], in_=rstd[:]), reads=[b_rstd], writes=[b_rstd])
                for j in range(4):
                    yi = j % 2
                    s.op("dve", lambda e: e.scalar_tensor_tensor(out=yc[:, j, :], in0=yc[:, j, :], scalar=gnw[:, h * 4 + j:h * 4 + j + 1],
                                                                 in1=rstd[:], op0=ALU.mult, op1=ALU.mult),
                         reads=[b_yc, b_c, b_rstd], writes=[b_yc])
                    s.op("dve", lambda e: e.tensor_tensor(out=yo[yi][:], in0=yc[:, j, :], in1=gT[:, j, :], op=ALU.mult),
                         reads=[b_yc, b_gT, b_yo[yi]], writes=[b_yo[yi]])
                    r0 = row_y + h * 512 + j * 128
                    s.dma("sp", yT[r0:r0 + 128, t0:t0 + NT], yo[yi][:], reads=[b_yo[yi]])
        s.barrier()


def stage_rwkv(k, PT, rows, yT, row_y, prm, cst, T, NH, GN_EPS):
    s = k.s
    G = RWKV_G
    assert G == 2
    NLVL = {64: 6, 128: 7}[RWKV_L]
    RL = RWKV_L
    NCK = NT // RL
    assert NH % G == 0 and T % NT == 0
    with contextlib.ExitStack() as st:
        ident = k.sb(st, [128, 128], F32)
        ones64 = k.sb(st, [64, 64], F32)
        rmask = k.sb(st, [64, NT], F32)
        maskA = k.sb(st, [RL, G, 2 * RL], F32)
        maskN = k.sb(st, [RL, G, RL], F32)
        P = {}
        for nm in ("mu_r", "mu_k", "mu_v", "w0", "a0", "k_k", "k_a", "r_k", "gn_w", "gn_b"):
            P[nm] = k.sb(st, [64, NH], F32)
        om = {nm: k.sb(st, [64, NH], F32) for nm in ("mu_r", "mu_k", "mu_v", "k_a")}
        nw0 = k.sb(st, [64, NH], F32)
        mud = {nm: k.sb(st, [96, 1], F32) for nm in ("mu_wd", "mu_ad")}
        omd = {nm: k.sb(st, [96, 1], F32) for nm in ("mu_wd", "mu_ad")}
        b_c = Buf()
        for dst, src in ((ident, cst["ident"]), (rmask, cst["resetmask"]), (maskA, cst["maskA"]), (maskN, cst["maskN"])):
            s.dma("sp", dst[:], src, writes=[b_c])
        for nm in P:
            s.dma("sp", P[nm][:], prm[nm], writes=[b_c])
        for nm in mud:
            s.dma("sp", mud[nm][:], prm[nm], writes=[b_c])
        s.op("pool", lambda e: e.memset(ones64[:], 1.0), writes=[b_c])
        for nm in om:
            s.op("dve", lambda e: e.tensor_scalar(out=om[nm][:], in0=P[nm][:], scalar1=-1.0, scalar2=1.0, op0=ALU.mult, op1=ALU.add),
                 reads=[b_c], writes=[b_c])
        for nm in omd:
            s.op("dve", lambda e: e.tensor_scalar(out=omd[nm][:], in0=mud[nm][:], scalar1=-1.0, scalar2=1.0, op0=ALU.mult, op1=ALU.add),
                 reads=[b_c], writes=[b_c])
        s.op("dve", lambda e: e.tensor_scalar(out=nw0[:], in0=P["w0"][:], scalar1=-1.0, scalar2=None, op0=ALU.mult), reads=[b_c], writes=[b_c])

        wup = k.sb(st, [96, G * 64], F32)
        aup = k.sb(st, [96, G * 64], F32)
        rawd = k.sb(st, [96, 1 + NT], F32)
        rawa = k.sb(st, [96, 1 + NT], F32)
        twd = k.sb(st, [96, NT], F32)
        adq = k.sb(st, [96, NT], F32)
        raw = {nm: k.sb(st, [64, G, 1 + NT], F32) for nm in ("r", "k", "v")}
        gate = None
        (r_, k_, e2, c_, eP, eN, a_, kk, kkn, kmod, kka, t1, t2) = [[k.sb(st, [64, NT], F32) for _ in range(G)] for _ in range(13)]
        yfm = k.sb(st, [64, G, NT], F32)
        dbl_tiles = [[k.sb(st, shp, F32) for _ in range(2)] for shp in
                     ([64, G, NT], [64, G, NCK, 2, RL], [64, G, NT], [64, G, NT], [64, G, NT], [64, G, NT], [64, G, NT], [64, G, NT], [64, G, NCK])]
        dbl_bufs = [bufs(2) for _ in range(9)]
        P1_PER_STEP = 2
        tokb = [k.sb(st, [RL, 3, G, 64], F32) for _ in range(2)]
        GAb = [k.sb(st, [RL, G, 2 * RL], F32) for _ in range(2)]
        GKb = [k.sb(st, [RL, G, 2 * RL], F32) for _ in range(2)]
        Pnb = [k.sb(st, [RL, G, RL], F32) for _ in range(2)]
        PPb = [[k.sb(st, [RL, G, 2, RL], F32) for _ in range(NLVL - 1)] for _ in range(2)]
        b_tokb, b_GAb, b_GKb, b_Pnb = bufs(2), bufs(2), bufs(2), bufs(2)
        b_PPb = [bufs(NLVL - 1), bufs(NLVL - 1)]
        U = k.sb(st, [RL, G, 64], F32)
        Ysb = k.sb(st, [RL, G, 64], F32)
        ST = k.sb(st, [64, G, 64], F32)
        yo = [k.sb(st, [64, NT], BF16) for _ in range(2)]
        bank = [k.ps(st, [128, 512]) for _ in range(8)]
        b_bank = bufs(8)
        ps_A, ps_B = bank[0], bank[1]
        ps_ga = bank[0][0:RL, 0:G * 2 * RL].rearrange("p (g x) -> p g x", g=G)
        ps_gk = bank[1][0:RL, 0:G * 2 * RL].rearrange("p (g x) -> p g x", g=G)
        ps_x1 = bank[2][0:RL, 0:2 * G * 64].rearrange("p (a g x) -> p a g x", a=2, g=G)
        ps_x2k = bank[3][0:RL, 0:G * 64].rearrange("p (g x) -> p g x", g=G)
        ps_gn = bank[3][0:RL, G * 64:G * 64 + G * RL].rearrange("p (g x) -> p g x", g=G)
        ps_z = bank[4][0:RL, 0:2 * G * 64].rearrange("p (a g x) -> p a g x", a=2, g=G)
        ps_pp = bank[5][0:RL, 0:G * 2 * RL].rearrange("p (g a x) -> p g a x", g=G, a=2)
        ps_y = bank[6][0:RL, 0:G * 64].rearrange("p (g x) -> p g x", g=G)
        ps_yT = bank[6][0:64, G * 64:G * 64 + G * RL].rearrange("p (g x) -> p g x", g=G)
        ps_s = bank[7][0:64, 0:G * 64].rearrange("p (g x) -> p g x", g=G)
        ps_ln = bank[7]
        (b_wup, b_rawd, b_rawa, b_twd, b_adq, b_gate,
         b_Vt, b_AR, b_Bt, b_Kt, b_Bh, b_Kh, b_bonus, b_yfm, b_gL, b_tok, b_GAs, b_GKs, b_Pn, b_U, b_Ysb, b_ST) = bufs(22)
        (b_r, b_k, b_e2, b_cc, b_eP, b_eN, b_a, b_kk, b_kkn, b_kmod, b_kka, b_t1, b_t2) = [bufs(G) for _ in range(13)]
        b_raw = {nm: Buf() for nm in raw}
        b_PP = bufs(2)
        b_yo = bufs(2)
        i64 = ident[0:64, 0:64]

        def shift(dst, src3, g, mu_t, om_t, H, reads, bdst):
            s.op("dve", lambda e: e.tensor_scalar(out=dst, in0=src3[:, g, 1:1 + NT], scalar1=om_t[:, H:H + 1], scalar2=None, op0=ALU.mult),
                 reads=reads + [b_c, bdst], writes=[bdst])
            s.op("dve", lambda e: e.scalar_tensor_tensor(out=dst, in0=src3[:, g, 0:NT], scalar=mu_t[:, H:H + 1], in1=dst,
                                                         op0=ALU.mult, op1=ALU.add),
                 reads=reads + [b_c, bdst], writes=[bdst])

        for h0 in range(0, NH, G):
            s.dma("sp", wup[:], prm["w_up"][:, h0 * 64:(h0 + G) * 64], reads=[b_wup], writes=[b_wup])
            s.dma("sp", aup[:], prm["a_up"][:, h0 * 64:(h0 + G) * 64], reads=[b_wup], writes=[b_wup])
            s.op("pool", lambda e: e.memset(ST[:], 0.0), reads=[b_ST], writes=[b_ST])
            def make_tile(ti, t0):
                tp = ti % 2
                Vt, AR, Bt, Kt, Bh, Kh, bonus, gate, gL = (x[tp] for x in dbl_tiles)
                b_Vt, b_AR, b_Bt, b_Kt, b_Bh, b_Kh, b_bonus, b_gate, b_gL = (x[tp] for x in dbl_bufs)
                def head_gen(g):
                        H = h0 + g
                        hc = slice(g * 64, (g + 1) * 64)
                        shift(r_[g][:], raw["r"], g, P["mu_r"], om["mu_r"], H, [b_raw["r"]], b_r[g])
                        yield
                        shift(k_[g][:], raw["k"], g, P["mu_k"], om["mu_k"], H, [b_raw["k"]], b_k[g])
                        yield
                        shift(Vt[:, g, :], raw["v"], g, P["mu_v"], om["mu_v"], H, [b_raw["v"]], b_Vt)
                        yield
                        s.op("pe", lambda e: e.matmul(bank[g][0:64, :], lhsT=wup[:, hc], rhs=twd[:], start=True, stop=True),
                             reads=[b_wup, b_twd, b_bank[g]], writes=[b_bank[g]])
                        yield
                        s.op("act", lambda e: e.activation(out=t1[g][:], in_=bank[g][0:64, :], func=AF.Exp, scale=-1.0, bias=nw0[:, H:H + 1]),
                             reads=[b_bank[g], b_c, b_t1[g]], writes=[b_t1[g]])
                        yield
                        s.op("act", lambda e: e.activation(out=t1[g][:], in_=t1[g][:], func=AF.Ln, bias=1.0), reads=[b_t1[g]], writes=[b_t1[g]])
                        yield
                        s.op("act", lambda e: e.activation(out=e2[g][:], in_=t1[g][:], func=AF.Exp, scale=-1.0, bias=-0.5), reads=[b_t1[g], b_e2[g]], writes=[b_e2[g]])
                        yield
                        s.op("dve", lambda e: e.tensor_tensor_scan(out=c_[g][:], data0=rmask[:], data1=e2[g][:], initial=0.0, op0=ALU.mult, op1=ALU.add),
                             reads=[b_c, b_e2[g], b_cc[g]], writes=[b_cc[g]])
                        yield
                        s.op("act", lambda e: e.activation(out=eP[g][:], in_=c_[g][:], func=AF.Exp, scale=-1.0), reads=[b_cc[g], b_eP[g]], writes=[b_eP[g]])
                        yield
                        s.op("act", lambda e: e.activation(out=eN[g][:], in_=c_[g][:], func=AF.Exp), reads=[b_cc[g], b_eN[g]], writes=[b_eN[g]])
                        yield
                        s.op("pool", lambda e: e.tensor_copy(out=gL[:, g, :], in_=eP[g][:, RL - 1:NT:RL]), reads=[b_eP[g], b_gL], writes=[b_gL])
                        yield
                        yield
                        s.op("pe", lambda e: e.matmul(bank[g][0:64, :], lhsT=aup[:, hc], rhs=adq[:], start=True, stop=True),
                             reads=[b_wup, b_adq, b_bank[g]], writes=[b_bank[g]])
                        yield
                        s.op("act", lambda e: e.activation(out=a_[g][:], in_=bank[g][0:64, :], func=AF.Sigmoid, bias=P["a0"][:, H:H + 1]),
                             reads=[b_bank[g], b_c, b_a[g]], writes=[b_a[g]])
                        yield
                        s.op("dve", lambda e: e.tensor_scalar(out=kk[g][:], in0=k_[g][:], scalar1=P["k_k"][:, H:H + 1], scalar2=None, op0=ALU.mult),
                             reads=[b_k[g], b_c, b_kk[g]], writes=[b_kk[g]])
                        yield
                        s.op("act", lambda e: e.activation(out=t2[g][:], in_=kk[g][:], func=AF.Square), reads=[b_kk[g], b_t2[g]], writes=[b_t2[g]])
                        yield
                        s.op("pe", lambda e: e.matmul(bank[g][0:64, :], lhsT=ones64[:], rhs=t2[g][:], start=True, stop=True),
                             reads=[b_c, b_t2[g], b_bank[g]], writes=[b_bank[g]])
                        yield
                        s.op("act", lambda e: e.activation(out=t2[g][:], in_=bank[g][0:64, :], func=AF.Sqrt), reads=[b_bank[g], b_t2[g]], writes=[b_t2[g]])
                        yield
                        s.op("dve", lambda e: e.tensor_scalar(out=t2[g][:], in0=t2[g][:], scalar1=1e-12, scalar2=None, op0=ALU.max), reads=[b_t2[g]], writes=[b_t2[g]])
                        yield
                        s.op("dve", lambda e: e.reciprocal(out=t2[g][:], in_=t2[g][:]), reads=[b_t2[g]], writes=[b_t2[g]])
                        yield
                        s.op("dve", lambda e: e.tensor_tensor(out=kkn[g][:], in0=kk[g][:], in1=t2[g][:], op=ALU.mult), reads=[b_kk[g], b_t2[g], b_kkn[g]], writes=[b_kkn[g]])
                        yield
                        s.op("dve", lambda e: e.tensor_scalar(out=t1[g][:], in0=a_[g][:], scalar1=P["k_a"][:, H:H + 1], scalar2=om["k_a"][:, H:H + 1],
                                                              op0=ALU.mult, op1=ALU.add), reads=[b_a[g], b_c, b_t1[g]], writes=[b_t1[g]])
                        yield
                        s.op("dve", lambda e: e.tensor_tensor(out=kmod[g][:], in0=k_[g][:], in1=t1[g][:], op=ALU.mult), reads=[b_k[g], b_t1[g], b_kmod[g]], writes=[b_kmod[g]])
                        yield
                        s.op("dve", lambda e: e.tensor_tensor(out=kka[g][:], in0=kkn[g][:], in1=a_[g][:], op=ALU.mult), reads=[b_kkn[g], b_a[g], b_kka[g]], writes=[b_kka[g]])
                        yield
                        s.op("dve", lambda e: e.scalar_tensor_tensor(out=t2[g][:], in0=r_[g][:], scalar=P["r_k"][:, H:H + 1], in1=kmod[g][:],
                                                                     op0=ALU.mult, op1=ALU.mult), reads=[b_r[g], b_c, b_kmod[g], b_t2[g]], writes=[b_t2[g]])
                        yield
                        s.op("pe", lambda e: e.matmul(bank[g][0:64, :], lhsT=ones64[:], rhs=t2[g][:], start=True, stop=True),
                             reads=[b_c, b_t2[g], b_bank[g]], writes=[b_bank[g]])
                        yield
                        s.op("dve", lambda e: e.tensor_tensor(out=bonus[:, g, :], in0=bank[g][0:64, :], in1=Vt[:, g, :], op=ALU.mult),
                             reads=[b_bank[g], b_Vt, b_bonus], writes=[b_bonus])
                        yield
                        s.op("dve", lambda e: e.tensor_tensor(out=t1[g][:], in0=e2[g][:], in1=c_[g][:], op=ALU.subtract), reads=[b_e2[g], b_cc[g], b_t1[g]], writes=[b_t1[g]])
                        yield
                        s.op("act", lambda e: e.activation(out=t1[g][:], in_=t1[g][:], func=AF.Exp), reads=[b_t1[g]], writes=[b_t1[g]])
                        yield
                        v3 = lambda ap: ap.rearrange("p (n l) -> p n l", l=RL)
                        s.op("dve", lambda e: e.scalar_tensor_tensor(out=AR[:, g, :, 0, :], in0=v3(kkn[g][:]), scalar=-1.0, in1=v3(t1[g][:]),
                                                                     op0=ALU.mult, op1=ALU.mult), reads=[b_kkn[g], b_t1[g], b_AR], writes=[b_AR])
                        yield
                        s.op("dve", lambda e: e.tensor_tensor(out=AR[:, g, :, 1, :], in0=v3(r_[g][:]), in1=v3(eP[g][:]), op=ALU.mult),
                             reads=[b_r[g], b_eP[g], b_AR], writes=[b_AR])
                        yield
                        s.op("dve", lambda e: e.tensor_tensor(out=Bt[:, g, :], in0=kka[g][:], in1=eN[g][:], op=ALU.mult), reads=[b_kka[g], b_eN[g], b_Bt], writes=[b_Bt])
                        yield
                        s.op("dve", lambda e: e.tensor_tensor(out=Kt[:, g, :], in0=kmod[g][:], in1=eN[g][:], op=ALU.mult), reads=[b_kmod[g], b_eN[g], b_Kt], writes=[b_Kt])
                        yield
                        gbc = gL[:, g, :].unsqueeze(2).broadcast_to([64, NCK, RL])
                        s.op("dve", lambda e: e.tensor_tensor(out=v3(Bh[:, g, :]), in0=v3(Bt[:, g, :]), in1=gbc, op=ALU.mult),
                             reads=[b_Bt, b_gL, b_Bh], writes=[b_Bh])
                        yield
                        s.op("dve", lambda e: e.tensor_tensor(out=v3(Kh[:, g, :]), in0=v3(Kt[:, g, :]), in1=gbc, op=ALU.mult),
                             reads=[b_Kt, b_gL, b_Kh], writes=[b_Kh])
                        yield


                def phase1():
                    for rw, brw, nm, rowk in ((rawd, b_rawd, "mu_wd", "wd"), (rawa, b_rawa, "mu_ad", "ad")):
                        if ti == 0:
                            s.op("pool", lambda e: e.memset(rw[:, 0:1], 0.0), reads=[brw], writes=[brw])
                        else:
                            s.op("pool", lambda e: e.tensor_copy(out=rw[:, 0:1], in_=rw[:, NT:NT + 1]), reads=[brw], writes=[brw])
                        s.dma("sp", rw[:, 1:1 + NT], PT[rows[rowk]:rows[rowk] + 96, t0:t0 + NT], reads=[brw], writes=[brw])
                    dsts = ((twd, b_twd, rawd, b_rawd, "mu_wd"), (adq, b_adq, rawa, b_rawa, "mu_ad"))
                    for dst, bdst, rw, brw, nm in dsts:
                        s.op("dve", lambda e: e.tensor_scalar(out=dst[:], in0=rw[:, 1:1 + NT], scalar1=omd[nm][:], scalar2=None, op0=ALU.mult),
                             reads=[brw, b_c, bdst], writes=[bdst])
                        s.op("dve", lambda e: e.scalar_tensor_tensor(out=dst[:], in0=rw[:, 0:NT], scalar=mud[nm][:], in1=dst[:],
                                                                     op0=ALU.mult, op1=ALU.add),
                             reads=[brw, b_c, bdst], writes=[bdst])
                    s.op("act", lambda e: e.activation(out=twd[:], in_=twd[:], func=AF.Tanh), reads=[b_twd], writes=[b_twd])
                    for nm in ("r", "k", "v"):
                        if ti == 0:
                            s.op("pool", lambda e: e.memset(raw[nm][:, :, 0:1], 0.0), reads=[b_raw[nm]], writes=[b_raw[nm]])
                        else:
                            s.op("pool", lambda e: e.tensor_copy(out=raw[nm][:, :, 0:1], in_=raw[nm][:, :, NT:NT + 1]),
                                 reads=[b_raw[nm]], writes=[b_raw[nm]])
                        r0 = rows[nm] + h0 * 64
                        s.dma("sp", raw[nm][:, :, 1:1 + NT], PT[r0:r0 + G * 64, t0:t0 + NT].rearrange("(h j) t -> j h t", j=64),
                              reads=[b_raw[nm]], writes=[b_raw[nm]])
                    r0 = rows["g"] + h0 * 64
                    s.dma("act", gate[:], PT[r0:r0 + G * 64, t0:t0 + NT].rearrange("(h j) t -> j h t", j=64), reads=[b_gate], writes=[b_gate])
                    s.op("act", lambda e: e.activation(out=gate[:], in_=gate[:], func=AF.Silu), reads=[b_gate], writes=[b_gate])
                    alive = [head_gen(g) for g in range(G)]
                    while alive:
                        for gg in list(alive):
                            try:
                                next(gg)
                            except StopIteration:
                                alive.remove(gg)
                        yield
                def indep_steps(cc):
                    par = cc % 2
                    cs = slice(cc * RL, (cc + 1) * RL)
                    tk, ga_s, gk_s, pn_s, ppl = tokb[par], GAb[par], GKb[par], Pnb[par], PPb[par]
                    b_tk, b_ga, b_gk, b_pn, b_ppl = b_tokb[par], b_GAb[par], b_GKb[par], b_Pnb[par], b_PPb[par]

                    def tr():
                        for g in range(G):
                            s.op("pe", lambda e: e.transpose(ps_x1[:, 0, g, :], Vt[:, g, cs], i64), reads=[b_Vt, b_c, b_bank[2]], writes=[b_bank[2]])
                            s.op("pe", lambda e: e.transpose(ps_x1[:, 1, g, :], Bh[:, g, cs], i64), reads=[b_Bh, b_c, b_bank[2]], writes=[b_bank[2]])
                            s.op("pe", lambda e: e.transpose(ps_x2k[:, g, :], Kh[:, g, cs], i64), reads=[b_Kh, b_c, b_bank[3]], writes=[b_bank[3]])
                        s.op("act", lambda e: e.copy(out=tk[:, 0:2], in_=ps_x1), reads=[b_bank[2], b_tk], writes=[b_tk])
                        s.op("act", lambda e: e.copy(out=tk[:, 2], in_=ps_x2k), reads=[b_bank[3], b_tk], writes=[b_tk])

                    def gm():
                        for g in range(G):
                            arc = AR[:, g, cc].rearrange("p a l -> p (a l)")
                            s.op("pe", lambda e: e.matmul(ps_ga[:, g, :], lhsT=Bt[:, g, cs], rhs=arc, start=True, stop=True),
                                 reads=[b_Bt, b_AR, b_bank[0]], writes=[b_bank[0]])
                            s.op("pe", lambda e: e.matmul(ps_gk[:, g, :], lhsT=Kt[:, g, cs], rhs=arc, start=True, stop=True),
                                 reads=[b_Kt, b_AR, b_bank[1]], writes=[b_bank[1]])
                            s.op("pe", lambda e: e.matmul(ps_gn[:, g, :], lhsT=AR[:, g, cc, 0, :], rhs=Bt[:, g, cs], start=True, stop=True),
                                 reads=[b_Bt, b_AR, b_bank[3]], writes=[b_bank[3]])
                        s.op("dve", lambda e: e.tensor_tensor(out=ga_s[:], in0=ps_ga, in1=maskA[:], op=ALU.mult),
                             reads=[b_bank[0], b_c, b_ga], writes=[b_ga])
                        s.op("dve", lambda e: e.tensor_tensor(out=gk_s[:], in0=ps_gk, in1=maskA[:], op=ALU.mult),
                             reads=[b_bank[1], b_c, b_gk], writes=[b_gk])
                        s.op("dve", lambda e: e.tensor_tensor(out=pn_s[:], in0=ps_gn, in1=maskN[:], op=ALU.mult),
                             reads=[b_bank[3], b_c, b_pn], writes=[b_pn])

                    def sq(lvl):
                        def f():
                            if lvl == 0:
                                Pl = lambda g: pn_s[:, g, :]
                                PTl = lambda g: ga_s[:, g, 0:RL]
                                rd = [b_pn, b_ga]
                            else:
                                Pl = lambda g: ppl[lvl - 1][:, g, 0, :]
                                PTl = lambda g: ppl[lvl - 1][:, g, 1, :]
                                rd = [b_ppl[lvl - 1]]
                            for g in range(G):
                                s.op("pe", lambda e: e.matmul(ps_pp[:, g, 0, :], lhsT=PTl(g), rhs=Pl(g), start=True, stop=True),
                                     reads=rd + [b_bank[5]], writes=[b_bank[5]])
                                s.op("pe", lambda e: e.matmul(ps_pp[:, g, 1, :], lhsT=Pl(g), rhs=PTl(g), start=True, stop=True),
                                     reads=rd + [b_bank[5]], writes=[b_bank[5]])
                            s.op("act", lambda e: e.copy(out=ppl[lvl][:], in_=ps_pp), reads=[b_bank[5], b_ppl[lvl]], writes=[b_ppl[lvl]])
                        return f
                    return [tr, gm] + [sq(l) for l in range(NLVL - 1)]

                def dep_steps(cc):
                    par = cc % 2
                    cs = slice(cc * RL, (cc + 1) * RL)
                    tk, ga_s, gk_s, pn_s, ppl = tokb[par], GAb[par], GKb[par], Pnb[par], PPb[par]
                    b_tk, b_ga, b_gk, b_pn, b_ppl = b_tokb[par], b_GAb[par], b_GKb[par], b_Pnb[par], b_PPb[par]

                    def zz():
                        for g in range(G):
                            s.op("pe", lambda e: e.matmul(ps_z[:, 0, g, :], lhsT=AR[:, g, cc, 0, :], rhs=ST[:, g, :], start=True, stop=False),
                                 reads=[b_AR, b_ST, b_bank[4]], writes=[b_bank[4]], inc=False)
                            s.op("pe", lambda e: e.matmul(ps_z[:, 0, g, :], lhsT=gk_s[:, g, 0:RL], rhs=tk[:, 0, g, :], start=False, stop=True),
                                 reads=[b_gk, b_tk, b_bank[4]], writes=[b_bank[4]])
                        s.op("act", lambda e: e.copy(out=U[:], in_=ps_z[:, 0]), reads=[b_bank[4], b_U], writes=[b_U])

                    def app(lvl):
                        def f():
                            if lvl == 0:
                                PTl = lambda g: ga_s[:, g, 0:RL]
                                rd = [b_ga]
                            else:
                                PTl = lambda g: ppl[lvl - 1][:, g, 1, :]
                                rd = [b_ppl[lvl - 1]]
                            for g in range(G):
                                s.op("pe", lambda e: e.matmul(ps_z[:, 1, g, :], lhsT=PTl(g), rhs=U[:, g, :], start=True, stop=True),
                                     reads=rd + [b_U, b_bank[4]], writes=[b_bank[4]])
                            s.op("dve", lambda e: e.tensor_tensor(out=U[:], in0=ps_z[:, 1], in1=U[:], op=ALU.add),
                                 reads=[b_bank[4], b_U], writes=[b_U])
                        return f

                    def yy():
                        for g in range(G):
                            s.op("pe", lambda e: e.matmul(ps_y[:, g, :], lhsT=AR[:, g, cc, 1, :], rhs=ST[:, g, :], start=True, stop=False),
                                 reads=[b_AR, b_ST, b_bank[6]], writes=[b_bank[6]], inc=False)
                            s.op("pe", lambda e: e.matmul(ps_y[:, g, :], lhsT=ga_s[:, g, RL:2 * RL], rhs=U[:, g, :], start=False, stop=False),
                                 reads=[b_ga, b_U, b_bank[6]], writes=[b_bank[6]], inc=False)
                            s.op("pe", lambda e: e.matmul(ps_y[:, g, :], lhsT=gk_s[:, g, RL:2 * RL], rhs=tk[:, 0, g, :], start=False, stop=True),
                                 reads=[b_gk, b_tk, b_bank[6]], writes=[b_bank[6]])
                        s.op("act", lambda e: e.copy(out=Ysb[:], in_=ps_y), reads=[b_bank[6], b_Ysb], writes=[b_Ysb])

                    def ss():
                        for g in range(G):
                            s.op("pe", lambda e: e.matmul(ps_s[:, g, :], lhsT=tk[:, 1, g, :], rhs=U[:, g, :], start=True, stop=False),
                                 reads=[b_tk, b_U, b_bank[7]], writes=[b_bank[7]], inc=False)
                            s.op("pe", lambda e: e.matmul(ps_s[:, g, :], lhsT=tk[:, 2, g, :], rhs=tk[:, 0, g, :], start=False, stop=True),
                                 reads=[b_tk, b_bank[7]], writes=[b_bank[7]])
                        for g in range(G):
                            s.op("dve", lambda e: e.scalar_tensor_tensor(out=ST[:, g, :], in0=ST[:, g, :], scalar=gL[:, g, cc:cc + 1], in1=ps_s[:, g, :],
                                                                         op0=ALU.mult, op1=ALU.add),
                                 reads=[b_ST, b_gL, b_bank[7]], writes=[b_ST])

                    def yt():
                        for g in range(G):
                            s.op("pe", lambda e: e.transpose(ps_yT[:, g, :], Ysb[:, g, :], ident[0:RL, 0:RL]), reads=[b_Ysb, b_c, b_bank[6]], writes=[b_bank[6]])
                        s.op("dve", lambda e: e.tensor_copy(out=yfm[:, :, cs], in_=ps_yT), reads=[b_bank[6], b_yfm], writes=[b_yfm])
                    return [zz] + [app(l) for l in range(NLVL)] + [yy, ss, yt]


                def chunks():
                    for f in indep_steps(0):
                        f()
                        yield
                    for cc in range(NCK):
                        dsteps = dep_steps(cc)
                        isteps = indep_steps(cc + 1) if cc + 1 < NCK else []
                        for j in range(max(len(dsteps), len(isteps))):
                            if j < len(dsteps):
                                dsteps[j]()
                            if j < len(isteps):
                                isteps[j]()
                            yield
                def ph4(g):
                    H = h0 + g
                    yi = g % 2
                    s.op("pe", lambda e: e.matmul(bank[(7, 4)[g]][0:64, :], lhsT=ones64[:], rhs=yfm[:, g, :], start=True, stop=True),
                         reads=[b_c, b_yfm, b_bank[(7, 4)[g]]], writes=[b_bank[(7, 4)[g]]])
                    yield
                    s.op("act", lambda e: e.activation(out=t1[g][:], in_=bank[(7, 4)[g]][0:64, :], func=AF.Copy, scale=1.0 / 64), reads=[b_bank[(7, 4)[g]], b_t1[g]], writes=[b_t1[g]])
                    yield
                    s.op("dve", lambda e: e.tensor_tensor(out=t1[g][:], in0=yfm[:, g, :], in1=t1[g][:], op=ALU.subtract), reads=[b_yfm, b_t1[g]], writes=[b_t1[g]])
                    yield
                    s.op("act", lambda e: e.activation(out=t2[g][:], in_=t1[g][:], func=AF.Square), reads=[b_t1[g], b_t2[g]], writes=[b_t2[g]])
                    yield
                    s.op("pe", lambda e: e.matmul(bank[(7, 4)[g]][0:64, :], lhsT=ones64[:], rhs=t2[g][:], start=True, stop=True),
                         reads=[b_c, b_t2[g], b_bank[(7, 4)[g]]], writes=[b_bank[(7, 4)[g]]])
                    yield
                    s.op("act", lambda e: e.activation(out=t2[g][:], in_=bank[(7, 4)[g]][0:64, :], func=AF.Sqrt, scale=1.0 / 64, bias=GN_EPS),
                         reads=[b_bank[(7, 4)[g]], b_t2[g]], writes=[b_t2[g]])
                    yield
                    s.op("dve", lambda e: e.reciprocal(out=t2[g][:], in_=t2[g][:]), reads=[b_t2[g]], writes=[b_t2[g]])
                    yield
                    s.op("dve", lambda e: e.tensor_tensor(out=t1[g][:], in0=t1[g][:], in1=t2[g][:], op=ALU.mult), reads=[b_t1[g], b_t2[g]], writes=[b_t1[g]])
                    yield
                    s.op("dve", lambda e: e.tensor_scalar(out=t1[g][:], in0=t1[g][:], scalar1=P["gn_w"][:, H:H + 1], scalar2=P["gn_b"][:, H:H + 1],
                                                          op0=ALU.mult, op1=ALU.add), reads=[b_t1[g], b_c], writes=[b_t1[g]])
                    yield
                    s.op("dve", lambda e: e.tensor_tensor(out=t1[g][:], in0=t1[g][:], in1=bonus[:, g, :], op=ALU.add), reads=[b_t1[g], b_bonus], writes=[b_t1[g]])
                    yield
                    s.op("dve", lambda e: e.tensor_tensor(out=yo[yi][:], in0=t1[g][:], in1=gate[:, g, :], op=ALU.mult),
                         reads=[b_t1[g], b_gate, b_yo[yi]], writes=[b_yo[yi]])
                    yield
                    r0 = row_y + H * 64
                    s.dma("sp", yT[r0:r0 + 64, t0:t0 + NT], yo[yi][:], reads=[b_yo[yi]])
                    yield

                def phase4():
                    alive = [ph4(g) for g in range(G)]
                    while alive:
                        for gg in list(alive):
                            try:
                                next(gg)
                            except StopIteration:
                                alive.remove(gg)
                        yield
                return phase1, chunks, phase4

            tiles = [make_tile(ti, t0) for ti, t0 in enumerate(range(0, T, NT))]
            for _ in tiles[0][0]():
                pass
            for ti in range(len(tiles)):
                cg = tiles[ti][1]()
                pg = tiles[ti + 1][0]() if ti + 1 < len(tiles) else iter(())
                c_alive, p_alive = True, True
                while c_alive or p_alive:
                    if c_alive:
                        try:
                            next(cg)
                        except StopIteration:
                            c_alive = False
                    for _ in range(P1_PER_STEP):
                        if p_alive:
                            try:
                                next(pg)
                            except StopIteration:
                                p_alive = False
                for _ in tiles[ti][2]():
                    pass
        s.barrier()


class Cfg:
    def __init__(self, D=4096, T=4096, NBLK_A=8, NH_B=32, HC=4, HX=4, M=256, DEPTH=2):
        self.D, self.T, self.NBLK_A, self.NH_B, self.HC, self.HX, self.M, self.DEPTH = D, T, NBLK_A, NH_B, HC, HX, M, DEPTH
        self.KC = D // 128
        self.WA = NBLK_A * 256
        self.WB = NH_B * 64
        self.QKW = HC * 256
        self.WC = HC * 512
        self.WX = HX * 128
        self.in_sizes = (self.WA, self.WA, 3 * self.WB + 192, self.WB, 2 * self.QKW, self.WC, self.WC, self.WC, 2 * HC,
                         self.WX, self.WX, 4 * D)
        self.c_in = sum(self.in_sizes)
        off = np.concatenate([[0], np.cumsum(self.in_sizes)])
        self.col = dict(a_x=off[0], a_g=off[1], b_s=off[2], b_g=off[3], c_qk=off[4], c_v=off[5], c_o=off[6], c_g=off[7],
                        c_if=off[8], x_q=off[9], x_g=off[10], gates=off[11])
        segs = [("a_x", self.col["a_x"], self.WA), ("a_g", self.col["a_g"], self.WA),
                ("r", self.col["b_s"], self.WB), ("k", self.col["b_s"] + self.WB, self.WB), ("v", self.col["b_s"] + 2 * self.WB, self.WB),
                ("wd", self.col["b_s"] + 3 * self.WB, 96), ("ad", self.col["b_s"] + 3 * self.WB + 96, 96),
                ("b_g", self.col["b_g"], self.WB),
                ("c_q", self.col["c_qk"], self.QKW), ("c_k", self.col["c_qk"] + self.QKW, self.QKW),
                ("c_v", self.col["c_v"], self.WC), ("c_o", self.col["c_o"], self.WC), ("c_g", self.col["c_g"], self.WC),
                ("c_if", self.col["c_if"], 2 * HC), ("x_q", self.col["x_q"], self.WX), ("x_g", self.col["x_g"], self.WX)]
        self.segs = segs
        self.row = {}
        r = 0
        for nm, c0, w in segs:
            self.row[nm] = r
            r += ((w + 127) // 128) * 128
        self.NB1 = r // 128
        self.grp_first = {"A": "a_x", "B": "r", "C": "c_q", "X": "x_q"}
        order = ["A", "B", "C", "X"]
        starts = [self.row[self.grp_first[g]] for g in order] + [r]
        self.grp_rows = {g: (starts[i], starts[i + 1]) for i, g in enumerate(order)}
        self.lrow = {}
        for nm, c0, w in segs:
            for g in order:
                lo, hi = self.grp_rows[g]
                if lo <= self.row[nm] < hi:
                    self.lrow[nm] = (g, self.row[nm] - lo)
        self.br_kc = [self.WA // 128, self.WB // 128, self.WC // 128, self.WX // 128]
        self.FY = sum(self.br_kc) * 128


RMS_EPS = 1e-6
RWKV_GN_EPS = 64e-5


def tile_layout(W):
    Kd, M = W.shape
    return np.ascontiguousarray(W.reshape(Kd // 128, 128, M // 128, 128).transpose(2, 1, 0, 3))


def chunk_cols(v):
    return np.ascontiguousarray(v.reshape(-1, 128).T)


def head_cols(v):
    return np.ascontiguousarray(v.reshape(-1, 64).T)


def const_inputs(cfg):
    HC = cfg.HC
    sel = np.zeros((HC, HC, 128), np.float32)
    for h in range(HC):
        sel[h, h, :] = 1
    a_, b_ = np.meshgrid(np.arange(RWKV_L), np.arange(RWKV_L), indexing="ij")
    mA = np.concatenate([(a_ < b_), (a_ <= b_)], 1).astype(np.float32)
    mN = (b_ < a_).astype(np.float32)
    rm = np.ones((64, NT), np.float32)
    rm[:, ::RWKV_L] = 0
    a_, b_ = np.meshgrid(np.arange(64), np.arange(64), indexing="ij")
    return {"c_ident": np.eye(128, dtype=np.float32), "c_sel": sel, "c_i4": np.eye(HC, dtype=np.float32),
            "c_negmask": np.where(a_ <= b_, 0.0, NEG).astype(np.float32), "c_resetmask": rm,
            "c_maskA": np.ascontiguousarray(np.broadcast_to(mA[:, None, :], (RWKV_L, RWKV_G, 2 * RWKV_L))),
            "c_maskN": np.ascontiguousarray(np.broadcast_to(mN[:, None, :], (RWKV_L, RWKV_G, RWKV_L)))}


def layer_inputs(cfg, inp, l):
    D, KC, HC = cfg.D, cfg.KC, cfg.HC
    w_in = inp["w_in"][l]
    W1 = np.zeros((D, cfg.NB1 * 128), np.float32)
    for nm, c0, w in cfg.segs:
        W1[:, cfg.row[nm]:cfg.row[nm] + w] = w_in[:, c0:c0 + w]
    o = {}
    o["W1"] = tile_layout(W1)
    g0 = cfg.col["gates"]
    o["Wg"] = np.stack([tile_layout(w_in[:, g0 + i * D: g0 + (i + 1) * D]) for i in range(4)])
    o["Wbr0"] = tile_layout(inp["w_branch_a"][l])
    o["Wbr1"] = tile_layout(inp["w_branch_b"][l])
    o["Wbr2"] = tile_layout(inp["w_branch_c"][l])
    o["Wbr3"] = tile_layout(inp["w_branch_x"][l])
    o["Wo"] = tile_layout(inp["w_out"][l])
    o["norm_g"] = chunk_cols(inp["norm_g"][l])
    o["mem_norm_g"] = chunk_cols(inp["mem_norm_g"][l])
    NCH = cfg.NBLK_A * 2
    o["lru_conv_w"] = np.ascontiguousarray(inp["lru_conv_w"][l].reshape(4, NCH, 128).transpose(2, 1, 0))
    for nm in ("lru_conv_b", "lru_ba", "lru_bx", "lru_lambda"):
        o[nm] = chunk_cols(inp[nm][l])
    for nm in ("lru_wa", "lru_wx"):
        o[nm] = np.ascontiguousarray(inp[nm][l].reshape(cfg.NBLK_A, 2, 128, 2, 128).transpose(0, 3, 2, 1, 4))
    WB = cfg.WB
    mu = inp["rwkv_mu"][l]
    o["rw_mu_r"], o["rw_mu_k"], o["rw_mu_v"] = head_cols(mu[:WB]), head_cols(mu[WB:2 * WB]), head_cols(mu[2 * WB:3 * WB])
    o["rw_mu_wd"] = np.ascontiguousarray(mu[3 * WB:3 * WB + 96].reshape(96, 1))
    o["rw_mu_ad"] = np.ascontiguousarray(mu[3 * WB + 96:].reshape(96, 1))
    for nm, src in (("w0", "rwkv_w0"), ("a0", "rwkv_a0"), ("k_k", "rwkv_k_k"), ("k_a", "rwkv_k_a"), ("gn_w", "rwkv_gn_w"), ("gn_b", "rwkv_gn_b")):
        o["rw_" + nm] = head_cols(inp[src][l])
    o["rw_r_k"] = head_cols(inp["rwkv_r_k"][l].reshape(-1))
    o["rw_w_up"] = np.ascontiguousarray(inp["rwkv_w_up"][l])
    o["rw_a_up"] = np.ascontiguousarray(inp["rwkv_a_up"][l])
    o["ml_conv_w"] = np.ascontiguousarray(inp["mlstm_conv_w"][l].reshape(4, 4 * HC, 128).transpose(2, 1, 0))
    o["ml_conv_b"] = chunk_cols(inp["mlstm_conv_b"][l])
    o["ml_gn_w"] = chunk_cols(inp["mlstm_gn_w"][l])
    o["ml_b_i"] = np.ascontiguousarray(inp["mlstm_b_i"][l].reshape(HC, 1))
    o["ml_b_f"] = np.ascontiguousarray(inp["mlstm_b_f"][l].reshape(HC, 1))
    wkv = inp["xattn_w_kv"][l]
    o["xa_Wk"] = tile_layout(wkv[:, :cfg.WX])
    o["xa_Wv"] = np.ascontiguousarray(wkv[:, cfg.WX:].reshape(KC, 128, cfg.WX))
    return {f"L{l}_{k_}": np.ascontiguousarray(v, dtype=np.float32) for k_, v in o.items()}


def flat2d(ap, ndim, width):
    names = "abcdefgh"[:ndim]
    f = ap.rearrange(f"{' '.join(names)} -> ({' '.join(names)})")
    return f.rearrange("(r c) -> r c", c=width)


def build_program(cfg, shapes):
    nc = bass.Bass("TRN2", target_bir_lowering=False)
    D, T, KC = cfg.D, cfg.T, cfg.KC
    ins = {nm: nc.dram_tensor(nm, list(sh), F32, kind="ExternalInput").ap() for nm, sh in shapes.items()}
    outT = nc.dram_tensor("outT", [D, T], F32, kind="ExternalOutput").ap()
    with contextlib.ExitStack() as st:
        k = Ctx(nc, st)
        hT = k.dram("hT", [D, T], BF16)
        PTs = {g: k.dram(f"PT{g}", [hi - lo, T], F32) for g, (lo, hi) in cfg.grp_rows.items()}

        def pt_block(b):
            r = b * 128
            for g, (lo, hi) in cfg.grp_rows.items():
                if lo <= r < hi:
                    return PTs[g], r - lo
            raise AssertionError
        lr = lambda nm: cfg.lrow[nm][1]
        yT = k.dram("yT", [cfg.FY, T], BF16)
        memnT = k.dram("memnT", [D, cfg.M], BF16)
        xs = [ins["xT"]] + [k.dram(f"x{l + 1}T", [D, T], F32) for l in range(cfg.DEPTH)]
        cst = {nm[2:]: ins[nm] for nm in ins if nm.startswith("c_")}
        OB = D // 128
        for l in range(cfg.DEPTH):
            L = lambda nm: ins[f"L{l}_{nm}"]
            Wg = k.dram(f"Wg{l}", [4, OB, 128, KC, 128], BF16)
            Wo = k.dram(f"Wo{l}", [OB, 128, KC, 128], BF16)
            Wbr = [k.dram(f"Wbr{l}_{i}", [OB, 128, cfg.br_kc[i], 128], BF16) for i in range(4)]
            for dst, src, nd in [(Wg, L("Wg"), 5), (Wo, L("Wo"), 4)] + [(Wbr[i], L(f"Wbr{i}"), 4) for i in range(4)]:
                n = int(np.prod(dst.shape))
                wdt = 1024 if n % 1024 == 0 else 512
                cast_dram(k, flat2d(dst, nd, wdt), flat2d(src, nd, wdt), n, width=wdt)
            stage_norm(k, xs[l], L("norm_g"), hT, D, T, RMS_EPS, BF16)
            stage_norm(k, ins["memT"], L("mem_norm_g"), memnT, D, cfg.M, RMS_EPS, BF16)
            stage_proj(k, hT, L("W1"), pt_block, D, T, cfg.NB1)
            lru_prm = {"conv_w": L("lru_conv_w"), "conv_b": L("lru_conv_b"), "ba": L("lru_ba"), "bx": L("lru_bx"), "lam": L("lru_lambda"),
                       "wa": L("lru_wa"), "wx": L("lru_wx")}
            stage_lru(k, PTs["A"], lr("a_x"), lr("a_g"), yT, 0, lru_prm, T, cfg.NBLK_A)
            rw_prm = {nm: L("rw_" + nm) for nm in ("mu_r", "mu_k", "mu_v", "w0", "a0", "k_k", "k_a", "r_k", "gn_w", "gn_b", "mu_wd", "mu_ad", "w_up", "a_up")}
            rw_rows = {"r": lr("r"), "k": lr("k"), "v": lr("v"), "wd": lr("wd"), "ad": lr("ad"), "g": lr("b_g")}
            stage_rwkv(k, PTs["B"], rw_rows, yT, cfg.WA, rw_prm, cst, T, cfg.NH_B, RWKV_GN_EPS)
            ml_prm = {"conv_w": L("ml_conv_w"), "conv_b": L("ml_conv_b"), "gn_w": L("ml_gn_w"), "b_i": L("ml_b_i"), "b_f": L("ml_b_f")}
            ml_rows = {"q": lr("c_q"), "k": lr("c_k"), "v": lr("c_v"), "o": lr("c_o"), "g": lr("c_g"), "ifg": lr("c_if")}
            stage_mlstm(k, PTs["C"], ml_rows, yT, cfg.WA + cfg.WB, ml_prm, cst, T, cfg.HC)
            stage_xattn(k, PTs["X"], lr("x_q"), lr("x_g"), yT, cfg.WA + cfg.WB + cfg.WC, memnT, L("xa_Wk"), L("xa_Wv"), cst["ident"],
                        T, D, cfg.M, cfg.HX)
            stage_merge(k, hT, yT, cfg.br_kc, Wg, Wbr, Wo, xs[l], xs[l + 1], D, T)
        stage_norm(k, xs[cfg.DEPTH], ins["final_g"], outT, D, T, RMS_EPS, F32)
        k.s.barrier()
        build_program.ninst = k.s.ninst
        build_program.per_eng = dict(k.s.per_eng)
        build_program.nsem = k.s.nsem
    return nc


def run_module(cfg, inputs):
    B = inputs["x"].shape[0]
    shared = const_inputs(cfg)
    for l in range(cfg.DEPTH):
        shared.update(layer_inputs(cfg, inputs, l))
    shared["final_g"] = chunk_cols(np.asarray(inputs["final_norm_g"], dtype=np.float32))
    in_maps = []
    for b in range(B):
        m = dict(shared)
        m["xT"] = np.ascontiguousarray(np.asarray(inputs["x"][b], dtype=np.float32).T)
        m["memT"] = np.ascontiguousarray(np.asarray(inputs["mem"][b], dtype=np.float32).T)
        in_maps.append(m)
    shapes = {nm: v.shape for nm, v in in_maps[0].items()}
    nc = build_program(cfg, shapes)
    res = run_bass_kernel_spmd(nc, in_maps, core_ids=list(range(B)))
    out = np.stack([np.ascontiguousarray(res.results[b]["outT"].T) for b in range(B)])
    return out.astype(np.float32)


def kernel(**inputs):
    inputs = {k_: np.asarray(v) for k_, v in inputs.items()}
    return run_module(Cfg(), inputs)
```

```python
import contextlib
import numpy as np
import concourse.bass as bass
import concourse.mybir as mybir
from concourse.bass_utils import run_bass_kernel_spmd

F32 = mybir.dt.float32
BF16 = mybir.dt.bfloat16
AF = mybir.ActivationFunctionType
ALU = mybir.AluOpType
AX = mybir.AxisListType


class Buf:
    __slots__ = ("name", "w", "r")

    def __init__(self, name=""):
        self.name = name
        self.w = set()
        self.r = set()


SKIP_SAME_ENGINE = False


class Sched:
    EPOCH = 4000
    NDMA = 12

    def __init__(self, nc, stack):
        self.nc = nc
        self.stack = stack
        self.eng = {"pe": nc.tensor, "act": nc.scalar, "dve": nc.vector,
                    "pool": nc.gpsimd, "sp": nc.sync}
        self.sem = {}
        self.cnt = {}
        self.pending = {e: False for e in self.eng}
        self.nsem = 0
        for e in self.eng:
            self._new_sem(e)
        self.dsem = {}
        for q in ("sp", "act", "pool"):
            self.dsem[q] = [[self._alloc(f"d{q}{i}"), 0] for i in range(self.NDMA)]
        self.dnext = {q: 0 for q in self.dsem}
        self.seen = {e: {} for e in self.eng}
        self.all_tokens = {}
        self.ninst = 0
        self.per_eng = {}

    def _alloc(self, name):
        self.nsem += 1
        return self.stack.enter_context(self.nc.semaphore(f"{name}_{self.nsem}"))

    def _new_sem(self, e):
        self.sem[e] = self._alloc(f"s{e}")
        self.cnt[e] = 0

    def _wait(self, e, tok):
        sem, val = tok
        k = id(sem)
        if self.seen[e].get(k, 0) >= val:
            return
        self.eng[e].wait_ge(sem, val)
        self.nwait = getattr(self, "nwait", 0) + 1
        self.seen[e][k] = val

    def _deps(self, e, reads, writes):
        deps = set()
        for b in reads:
            deps |= b.w
        for b in writes:
            deps |= b.w
            deps |= b.r
        best = {}
        for tok in deps:
            if e == "pe" and tok[0] is self.sem["pe"]:
                continue
            if SKIP_SAME_ENGINE and e in ("act", "dve") and tok[0] is self.sem[e]:
                continue
            kk_ = id(tok[0])
            if kk_ not in best or best[kk_][1] < tok[1]:
                best[kk_] = tok
        for tok in best.values():
            self._wait(e, tok)

    def _record(self, tok, reads, writes):
        self.all_tokens[id(tok[0])] = tok
        for b in reads:
            b.r.add(tok)
        for b in writes:
            b.w = {tok}
            b.r = set()

    def op(self, e, fn, reads=(), writes=(), inc=True):
        self._deps(e, reads, writes)
        inst = fn(self.eng[e])
        self.ninst += 1
        self.per_eng[e] = self.per_eng.get(e, 0) + 1
        if inc:
            if self.cnt[e] >= self.EPOCH and not self.pending[e]:
                self._new_sem(e)
            self.cnt[e] += 1
            inst.then_inc(self.sem[e], 1)
            tok = (self.sem[e], self.cnt[e])
            self.pending[e] = False
        else:
            assert e == "pe"
            tok = (self.sem[e], self.cnt[e] + 1)
            self.pending[e] = True
        self._record(tok, reads, writes)
        return inst

    def dma(self, q, out, in_, reads=(), writes=(), **kw):
        slot = self.dsem[q][self.dnext[q]]
        self.dnext[q] = (self.dnext[q] + 1) % self.NDMA
        sem, val = slot
        if val > 0:
            self._wait(q, (sem, val))
        self._deps(q, reads, writes)
        inst = self.eng[q].dma_start(out=out, in_=in_, **kw)
        self.ninst += 1
        self.per_eng['dma_' + q] = self.per_eng.get('dma_' + q, 0) + 1
        slot[1] = val + 16
        inst.then_inc(sem, 16)
        tok = (sem, slot[1])
        self._record(tok, reads, writes)
        return inst

    def barrier(self):
        toks = list(self.all_tokens.values())
        for e in self.eng:
            for tok in toks:
                self._wait(e, tok)


class Ctx:
    def __init__(self, nc, stack):
        self.nc = nc
        self.s = Sched(nc, stack)
        self.n = 0

    def sb(self, st, shape, dt, name="t"):
        self.n += 1
        return st.enter_context(self.nc.sbuf_tensor(f"{name}{self.n}", list(shape), dt))

    def ps(self, st, shape, dt=F32, name="p"):
        self.n += 1
        return st.enter_context(self.nc.psum_tensor(f"{name}{self.n}", list(shape), dt))

    def dram(self, name, shape, dt):
        return self.nc.dram_tensor(name, list(shape), dt, kind="Internal").ap()


def dma_rows(s, q, sb3, dram2, nchunks, reads=(), writes=(), to_dram=False, step=8):
    for c0 in range(0, nchunks, step):
        c1 = min(nchunks, c0 + step)
        d = dram2[c0 * 128:c1 * 128, :].rearrange("(c p) t -> p c t", p=128)
        if to_dram:
            s.dma(q, d, sb3[:, c0:c1, :], reads=reads, writes=writes)
        else:
            s.dma(q, sb3[:, c0:c1, :], d, reads=reads, writes=writes)


def bufs(n):
    return [Buf() for _ in range(n)]


NT = 512


def stage_norm(k, xT, g_dram, out, D, T, eps, out_dt):
    NT = min(512, T)
    s = k.s
    KC = D // 128
    with contextlib.ExitStack() as st:
        ones = k.sb(st, [128, 128], F32)
        gcol = k.sb(st, [128, KC], F32)
        xt = k.sb(st, [128, KC, NT], F32)
        ht = k.sb(st, [128, KC, NT], out_dt)
        sq = [k.sb(st, [128, NT], F32) for _ in range(2)]
        rs = k.sb(st, [128, NT], F32)
        rstd = k.sb(st, [128, NT], F32)
        pss = k.ps(st, [128, NT])
        b_ones, b_g, b_rs, b_rstd, b_ps = bufs(5)
        b_xt, b_ht, b_sq = bufs(KC), bufs(KC), bufs(2)
        s.op("pool", lambda e: e.memset(ones[:], 1.0), writes=[b_ones])
        s.dma("sp", gcol[:], g_dram, writes=[b_g])
        for t0 in range(0, T, NT):
            for c in range(KC):
                s.dma("sp", xt[:, c, :], xT[c * 128:(c + 1) * 128, t0:t0 + NT], writes=[b_xt[c]])
                s.op("act", lambda e: e.activation(out=sq[c % 2][:], in_=xt[:, c, :], func=AF.Square),
                     reads=[b_xt[c]], writes=[b_sq[c % 2]])
                s.op("pe", lambda e: e.matmul(pss[:], lhsT=ones[:], rhs=sq[c % 2][:],
                                             start=(c == 0), stop=(c == KC - 1)),
                     reads=[b_ones, b_sq[c % 2]], writes=[b_ps], inc=True)
            s.op("act", lambda e: e.activation(out=rs[:], in_=pss[:], func=AF.Sqrt, scale=1.0 / D, bias=eps_ap(k, eps)),
                 reads=[b_ps], writes=[b_rs])
            s.op("dve", lambda e: e.reciprocal(out=rstd[:], in_=rs[:]), reads=[b_rs], writes=[b_rstd])
            for c in range(KC):
                s.op("dve", lambda e: e.scalar_tensor_tensor(out=ht[:, c, :], in0=xt[:, c, :], scalar=gcol[:, c:c + 1],
                                                             in1=rstd[:], op0=ALU.mult, op1=ALU.mult),
                     reads=[b_xt[c], b_g, b_rstd], writes=[b_ht[c]])
            dma_rows(s, "sp", ht, out[:, t0:t0 + NT], KC, reads=b_ht, to_dram=True)
        s.barrier()


_EPS = {}


def eps_ap(k, val):
    return float(val)


def stage_proj(k, hT, W, PT, D, T, NB):
    NT = min(512, T)
    s = k.s
    KC = D // 128
    GRP = 6
    with contextlib.ExitStack() as st:
        wb = [[k.sb(st, [128, KC, 128], BF16) for _ in range(GRP)] for _ in range(2)]
        ht = [k.sb(st, [128, KC, NT], BF16) for _ in range(2)]
        ot = [k.sb(st, [128, NT], F32) for _ in range(4)]
        ps = [k.ps(st, [128, NT]) for _ in range(8)]
        b_wb, b_ht, b_ot, b_ps = [bufs(GRP), bufs(GRP)], bufs(2), bufs(4), bufs(8)
        groups = list(range(0, NB, GRP))

        def load_group(gi):
            g0 = groups[gi]
            for b in range(min(GRP, NB - g0)):
                s.dma("pool", wb[gi % 2][b][:], W[g0 + b], writes=[b_wb[gi % 2][b]], max_dma_last_dim=4096)

        nht = 0
        no = 0
        npp = 0
        load_group(0)
        for gi, g0 in enumerate(groups):
            nb = min(GRP, NB - g0)
            if gi + 1 < len(groups):
                load_group(gi + 1)
            wset, bset = wb[gi % 2], b_wb[gi % 2]
            for t0 in range(0, T, NT):
                h = nht % 2
                nht += 1
                dma_rows(s, "act", ht[h], hT[:, t0:t0 + NT], KC, writes=[b_ht[h]])
                for b in range(nb):
                    pi = npp % 8
                    npp += 1
                    p = ps[pi]
                    for c in range(KC):
                        s.op("pe", lambda e: e.matmul(p[:], lhsT=wset[b][:, c, :], rhs=ht[h][:, c, :],
                                                     start=(c == 0), stop=(c == KC - 1)),
                             reads=[bset[b], b_ht[h]], writes=[b_ps[pi]], inc=(c == KC - 1))
                    o = no % 4
                    no += 1
                    if no % 2 == 0:
                        s.op("dve", lambda e: e.tensor_copy(out=ot[o][:], in_=p[:]), reads=[b_ps[pi]], writes=[b_ot[o]])
                    else:
                        s.op("act", lambda e: e.copy(out=ot[o][:], in_=p[:]), reads=[b_ps[pi]], writes=[b_ot[o]])
                    pt_t, pt_r = PT(g0 + b)
                    s.dma("sp", pt_t[pt_r:pt_r + 128, t0:t0 + NT], ot[o][:], reads=[b_ot[o]])
        s.barrier()


def cast_dram(k, dst, src, n_elems, q="pool", width=1024):
    s = k.s
    ROW = width
    assert n_elems % ROW == 0
    rows = n_elems // ROW
    CH = 2048
    for r0 in range(0, rows, CH):
        r1 = min(rows, r0 + CH)
        s.dma(q, dst[r0:r1, :], src[r0:r1, :])


def stage_merge(k, hT, yT, br_kc, Wg, Wbr, Wo, xT, xnT, D, T):
    s = k.s
    KC = D // 128
    OB = D // 128
    YC = sum(br_kc)
    yoff = [sum(br_kc[:i]) for i in range(len(br_kc))]
    NBR = len(br_kc)
    with contextlib.ExitStack() as st:
        ht = k.sb(st, [128, KC, NT], BF16)
        yt = k.sb(st, [128, YC, NT], BF16)
        mg = k.sb(st, [128, KC, NT], BF16)
        wg = [k.sb(st, [128, KC, 128], BF16) for _ in range(3)]
        wbr = [k.sb(st, [128, max(br_kc), 128], BF16) for _ in range(3)]
        gs = [k.sb(st, [128, NT], F32) for _ in range(2)]
        acc = [k.sb(st, [128, NT], F32) for _ in range(2)]
        tmp = [k.sb(st, [128, NT], F32) for _ in range(2)]
        xt = [k.sb(st, [128, NT], F32) for _ in range(2)]
        xn = [k.sb(st, [128, NT], F32) for _ in range(2)]
        psg = [k.ps(st, [128, NT]) for _ in range(2)]
        psp = [k.ps(st, [128, NT]) for _ in range(2)]
        pso = [k.ps(st, [128, NT]) for _ in range(2)]
        b_ht, b_yt = Buf(), Buf()
        b_mg = bufs(KC)
        b_wg, b_wbr, b_gs, b_acc, b_tmp, b_xt, b_xn = bufs(3), bufs(3), bufs(2), bufs(2), bufs(2), bufs(2), bufs(2)
        b_psg, b_psp, b_pso = bufs(2), bufs(2), bufs(2)
        nw = 0
        ng = 0
        na = 0
        for t0 in range(0, T, NT):
            dma_rows(s, "act", ht, hT[:, t0:t0 + NT], KC, writes=[b_ht])
            dma_rows(s, "act", yt, yT[:, t0:t0 + NT], YC, writes=[b_yt])
            for ob in range(OB):
                a = na % 2
                na += 1
                for br in range(NBR):
                    w = nw % 3
                    nw += 1
                    g = ng % 2
                    ng += 1
                    kcb = br_kc[br]
                    s.dma("sp", wg[w][:], Wg[br, ob], writes=[b_wg[w]])
                    s.dma("sp", wbr[w][:, 0:kcb, :], Wbr[br][ob], writes=[b_wbr[w]])
                    for c in range(KC):
                        s.op("pe", lambda e: e.matmul(psg[g][:], lhsT=wg[w][:, c, :], rhs=ht[:, c, :],
                                                     start=(c == 0), stop=(c == KC - 1)),
                             reads=[b_wg[w], b_ht], writes=[b_psg[g]], inc=(c == KC - 1))
                    for c in range(kcb):
                        s.op("pe", lambda e: e.matmul(psp[g][:], lhsT=wbr[w][:, c, :], rhs=yt[:, yoff[br] + c, :],
                                                     start=(c == 0), stop=(c == kcb - 1)),
                             reads=[b_wbr[w], b_yt], writes=[b_psp[g]], inc=(c == kcb - 1))
                    s.op("act", lambda e: e.activation(out=gs[g][:], in_=psg[g][:], func=AF.Sigmoid),
                         reads=[b_psg[g]], writes=[b_gs[g]])
                    last = (br == NBR - 1)
                    if br == 0:
                        dst, bdst = (mg[:, ob, :], b_mg[ob]) if last else (acc[a][:], b_acc[a])
                        s.op("dve", lambda e: e.tensor_tensor(out=dst, in0=psp[g][:], in1=gs[g][:], op=ALU.mult),
                             reads=[b_psp[g], b_gs[g]], writes=[bdst])
                    else:
                        s.op("dve", lambda e: e.tensor_tensor(out=tmp[g][:], in0=psp[g][:], in1=gs[g][:], op=ALU.mult),
                             reads=[b_psp[g], b_gs[g]], writes=[b_tmp[g]])
                        if last:
                            s.op("pool", lambda e: e.tensor_tensor(out=mg[:, ob, :], in0=acc[a][:], in1=tmp[g][:], op=ALU.add),
                                 reads=[b_acc[a], b_tmp[g]], writes=[b_mg[ob]])
                        else:
                            s.op("pool", lambda e: e.tensor_tensor(out=acc[a][:], in0=acc[a][:], in1=tmp[g][:], op=ALU.add),
                                 reads=[b_acc[a], b_tmp[g]], writes=[b_acc[a]])
            for ob in range(OB):
                w = nw % 3
                nw += 1
                g = ng % 2
                ng += 1
                s.dma("sp", wg[w][:], Wo[ob], writes=[b_wg[w]])
                s.dma("act", xt[g][:], xT[ob * 128:(ob + 1) * 128, t0:t0 + NT], writes=[b_xt[g]])
                for c in range(KC):
                    s.op("pe", lambda e: e.matmul(pso[g][:], lhsT=wg[w][:, c, :], rhs=mg[:, c, :],
                                                 start=(c == 0), stop=(c == KC - 1)),
                         reads=[b_wg[w], b_mg[c]], writes=[b_pso[g]], inc=(c == KC - 1))
                s.op("dve", lambda e: e.tensor_tensor(out=xn[g][:], in0=pso[g][:], in1=xt[g][:], op=ALU.add),
                     reads=[b_pso[g], b_xt[g]], writes=[b_xn[g]])
                s.dma("sp", xnT[ob * 128:(ob + 1) * 128, t0:t0 + NT], xn[g][:], reads=[b_xn[g]])
        s.barrier()


def stage_lru(k, PT, row_ax, row_ag, yT, row_y, prm, T, NBLK):
    s = k.s
    NCH = NBLK * 2
    with contextlib.ExitStack() as st:
        cw = k.sb(st, [128, NCH, 4], F32)
        cb, ba, bx, lam, c1, c2, tt = [k.sb(st, [128, NCH], F32) for _ in range(7)]
        b_prm = Buf()
        for dst, nm in ((cw, "conv_w"), (cb, "conv_b"), (ba, "ba"), (bx, "bx"), (lam, "lam")):
            s.dma("sp", dst[:], prm[nm], writes=[b_prm])
        s.op("act", lambda e: e.activation(out=tt[:], in_=lam[:], func=AF.Exp, scale=-1.0), reads=[b_prm], writes=[b_prm])
        s.op("act", lambda e: e.activation(out=tt[:], in_=tt[:], func=AF.Ln, bias=1.0), reads=[b_prm], writes=[b_prm])
        s.op("dve", lambda e: e.tensor_scalar(out=c1[:], in0=tt[:], scalar1=-8.0, scalar2=None, op0=ALU.mult), reads=[b_prm], writes=[b_prm])
        s.op("dve", lambda e: e.tensor_scalar(out=c2[:], in0=tt[:], scalar1=-16.0, scalar2=None, op0=ALU.mult), reads=[b_prm], writes=[b_prm])
        waf = k.sb(st, [128, 2, 2, 128], F32)
        wab = [k.sb(st, [128, 2, 2, 128], BF16) for _ in range(2)]
        xin = [k.sb(st, [128, 3 + NT], F32) for _ in range(2)]
        u = [k.sb(st, [128, NT], F32) for _ in range(2)]
        ub = [k.sb(st, [128, NT], BF16) for _ in range(2)]
        rr, ii, aa, a2, mm, bt, gt, sg = [k.sb(st, [128, NT], F32) for _ in range(8)]
        hs = [[k.sb(st, [128, NT], F32) for _ in range(2)] for _ in range(2)]
        yo = [k.sb(st, [128, NT], BF16) for _ in range(2)]
        psr, psi = k.ps(st, [128, NT]), k.ps(st, [128, NT])
        b_waf, b_psr, b_psi, b_rr, b_ii, b_aa, b_a2, b_mm, b_bt, b_gt, b_sg = bufs(11)
        b_wab, b_xin, b_u, b_ub, b_yo = bufs(2), bufs(2), bufs(2), bufs(2), bufs(2)
        b_hs = [bufs(2), bufs(2)]
        for nb in range(NBLK):
            for wi, nm in enumerate(("wa", "wx")):
                s.dma("sp", waf[:], prm[nm][nb].rearrange("o p c m -> p o c m"), writes=[b_waf])
                s.op("act", lambda e: e.copy(out=wab[wi][:], in_=waf[:]), reads=[b_waf], writes=[b_wab[wi]])
            for ti, t0 in enumerate(range(0, T, NT)):
                par = ti % 2
                for kc in range(2):
                    ch = nb * 2 + kc
                    if ti == 0:
                        s.op("pool", lambda e: e.memset(xin[kc][:, 0:3], 0.0), writes=[b_xin[kc]])
                    else:
                        s.op("pool", lambda e: e.tensor_copy(out=xin[kc][:, 0:3], in_=xin[kc][:, NT:NT + 3]),
                             reads=[b_xin[kc]], writes=[b_xin[kc]])
                    s.dma("sp", xin[kc][:, 3:3 + NT], PT[row_ax + ch * 128: row_ax + (ch + 1) * 128, t0:t0 + NT],
                          reads=[b_xin[kc]], writes=[b_xin[kc]])
                    s.op("dve", lambda e: e.tensor_scalar(out=u[kc][:], in0=xin[kc][:, 3:3 + NT], scalar1=cw[:, ch, 3:4],
                                                          scalar2=cb[:, ch:ch + 1], op0=ALU.mult, op1=ALU.add),
                         reads=[b_xin[kc], b_prm], writes=[b_u[kc]])
                    for j in (2, 1, 0):
                        s.op("dve", lambda e: e.scalar_tensor_tensor(out=u[kc][:], in0=xin[kc][:, j:j + NT], scalar=cw[:, ch, j:j + 1],
                                                                     in1=u[kc][:], op0=ALU.mult, op1=ALU.add),
                             reads=[b_xin[kc], b_prm, b_u[kc]], writes=[b_u[kc]])
                    s.op("act", lambda e: e.copy(out=ub[kc][:], in_=u[kc][:]), reads=[b_u[kc]], writes=[b_ub[kc]])
                for oc in range(2):
                    ch = nb * 2 + oc
                    for kc in range(2):
                        s.op("pe", lambda e: e.matmul(psr[:], lhsT=wab[0][:, oc, kc, :], rhs=ub[kc][:], start=(kc == 0), stop=(kc == 1)),
                             reads=[b_wab[0], b_ub[kc]], writes=[b_psr], inc=(kc == 1))
                    for kc in range(2):
                        s.op("pe", lambda e: e.matmul(psi[:], lhsT=wab[1][:, oc, kc, :], rhs=ub[kc][:], start=(kc == 0), stop=(kc == 1)),
                             reads=[b_wab[1], b_ub[kc]], writes=[b_psi], inc=(kc == 1))
                    s.op("act", lambda e: e.activation(out=rr[:], in_=psr[:], func=AF.Sigmoid, bias=ba[:, ch:ch + 1]),
                         reads=[b_psr, b_prm], writes=[b_rr])
                    s.op("act", lambda e: e.activation(out=ii[:], in_=psi[:], func=AF.Sigmoid, bias=bx[:, ch:ch + 1]),
                         reads=[b_psi, b_prm], writes=[b_ii])
                    s.op("act", lambda e: e.activation(out=aa[:], in_=rr[:], func=AF.Exp, scale=c1[:, ch:ch + 1]),
                         reads=[b_rr, b_prm], writes=[b_aa])
                    s.op("act", lambda e: e.activation(out=a2[:], in_=rr[:], func=AF.Exp, scale=c2[:, ch:ch + 1]),
                         reads=[b_rr, b_prm], writes=[b_a2])
                    s.op("act", lambda e: e.activation(out=mm[:], in_=a2[:], func=AF.Sqrt, scale=-1.0, bias=1.0),
                         reads=[b_a2], writes=[b_mm])
                    s.op("dve", lambda e: e.tensor_tensor(out=bt[:], in0=mm[:], in1=ii[:], op=ALU.mult),
                         reads=[b_mm, b_ii], writes=[b_bt])
                    s.op("dve", lambda e: e.tensor_tensor(out=bt[:], in0=bt[:], in1=u[oc][:], op=ALU.mult),
                         reads=[b_bt, b_u[oc]], writes=[b_bt])
                    init = 0.0 if ti == 0 else hs[oc][1 - par][:, NT - 1:NT]
                    s.op("dve", lambda e: e.tensor_tensor_scan(out=hs[oc][par][:], data0=aa[:], data1=bt[:], initial=init,
                                                               op0=ALU.mult, op1=ALU.add),
                         reads=[b_aa, b_bt] + ([] if ti == 0 else [b_hs[oc][1 - par]]), writes=[b_hs[oc][par]])
                    s.dma("act", gt[:], PT[row_ag + ch * 128: row_ag + (ch + 1) * 128, t0:t0 + NT], writes=[b_gt])
                    s.op("act", lambda e: e.activation(out=sg[:], in_=gt[:], func=AF.Silu), reads=[b_gt], writes=[b_sg])
                    s.op("dve", lambda e: e.tensor_tensor(out=yo[oc][:], in0=hs[oc][par][:], in1=sg[:], op=ALU.mult),
                         reads=[b_hs[oc][par], b_sg], writes=[b_yo[oc]])
                    s.dma("sp", yT[row_y + ch * 128: row_y + (ch + 1) * 128, t0:t0 + NT], yo[oc][:], reads=[b_yo[oc]])
        s.barrier()


def stage_xattn(k, PT, row_q, row_g, yT, row_y, memnT, Wk, Wv, ident_d, T, D, M, H):
    s = k.s
    KC = D // 128
    MC = M // 128
    sc = 128 ** -0.5
    with contextlib.ExitStack() as st:
        ident = k.sb(st, [128, 128], F32)
        memn = k.sb(st, [128, KC, M], BF16)
        kT = k.sb(st, [128, H, M], BF16)
        vt = k.sb(st, [128, MC, H * 128], BF16)
        b_id, b_memn, b_kT, b_vt = bufs(4)
        s.dma("sp", ident[:], ident_d, writes=[b_id])
        dma_rows(s, "sp", memn, memnT, KC, writes=[b_memn])
        with contextlib.ExitStack() as st1:
            wf = k.sb(st1, [128, KC, 128], F32)
            wb = k.sb(st1, [128, KC, 128], BF16)
            vf = [k.sb(st1, [128, H * 128], F32) for _ in range(2)]
            vb = [k.sb(st1, [128, H * 128], BF16) for _ in range(2)]
            psk = k.ps(st1, [128, M])
            psv = [k.ps(st1, [128, H * 128]) for _ in range(MC)]
            b_wf, b_wb, b_psk = bufs(3)
            b_vf, b_vb, b_psv = bufs(2), bufs(2), bufs(MC)
            for hd in range(H):
                s.dma("sp", wf[:], Wk[hd], writes=[b_wf])
                s.op("act", lambda e: e.copy(out=wb[:], in_=wf[:]), reads=[b_wf], writes=[b_wb])
                for c in range(KC):
                    s.op("pe", lambda e: e.matmul(psk[:], lhsT=wb[:, c, :], rhs=memn[:, c, :], start=(c == 0), stop=(c == KC - 1)),
                         reads=[b_wb, b_memn], writes=[b_psk], inc=(c == KC - 1))
                s.op("dve", lambda e: e.tensor_copy(out=kT[:, hd, :], in_=psk[:]), reads=[b_psk], writes=[b_kT])
            for c in range(KC):
                i = c % 2
                s.dma("sp", vf[i][:], Wv[c], writes=[b_vf[i]])
                s.op("act", lambda e: e.copy(out=vb[i][:], in_=vf[i][:]), reads=[b_vf[i]], writes=[b_vb[i]])
                for mc in range(MC):
                    s.op("pe", lambda e: e.matmul(psv[mc][:], lhsT=memn[:, c, mc * 128:(mc + 1) * 128], rhs=vb[i][:],
                                                 start=(c == 0), stop=(c == KC - 1)),
                         reads=[b_vb[i], b_memn], writes=[b_psv[mc]], inc=True)
            for mc in range(MC):
                s.op("dve", lambda e: e.tensor_copy(out=vt[:, mc, :], in_=psv[mc][:]), reads=[b_psv[mc]], writes=[b_vt])
            s.barrier()
        qf = k.sb(st, [128, H, NT], F32)
        qb = k.sb(st, [128, H, NT], BF16)
        gf = k.sb(st, [128, H, NT], F32)
        sg = k.sb(st, [128, H, NT], F32)
        pf = [k.sb(st, [128, M], F32) for _ in range(2)]
        pn = [k.sb(st, [128, M], F32) for _ in range(2)]
        pT = [k.sb(st, [128, MC, 128], BF16) for _ in range(2)]
        mx, nmx, rsum, rinv = [[k.sb(st, [128, 1], F32) for _ in range(2)] for _ in range(4)]
        yo = [k.sb(st, [128, NT], BF16) for _ in range(2)]
        pss = [k.ps(st, [128, M]) for _ in range(2)]
        pst = [k.ps(st, [128, MC, 128]) for _ in range(2)]
        pso = [k.ps(st, [128, 128]) for _ in range(2)]
        b_qf, b_qb, b_gf, b_sg = bufs(4)
        b_pf, b_pn, b_pT, b_mx, b_nmx, b_rsum, b_rinv, b_yo, b_pss, b_pst, b_pso = [bufs(2) for _ in range(11)]
        it = 0
        for t0 in range(0, T, NT):
            s.dma("sp", qf[:], PT[row_q:row_q + H * 128, t0:t0 + NT].rearrange("(h p) t -> p h t", p=128), writes=[b_qf])
            s.dma("act", gf[:], PT[row_g:row_g + H * 128, t0:t0 + NT].rearrange("(h p) t -> p h t", p=128), writes=[b_gf])
            s.op("act", lambda e: e.copy(out=qb[:], in_=qf[:]), reads=[b_qf], writes=[b_qb])
            s.op("act", lambda e: e.activation(out=sg[:], in_=gf[:], func=AF.Silu), reads=[b_gf], writes=[b_sg])
            for hd in range(H):
                yi = hd % 2
                for tb in range(NT // 128):
                    i = it % 2
                    it += 1
                    tsl = slice(tb * 128, (tb + 1) * 128)
                    s.op("pe", lambda e: e.matmul(pss[i][:], lhsT=qb[:, hd, tsl], rhs=kT[:, hd, :], start=True, stop=True),
                         reads=[b_qb, b_kT], writes=[b_pss[i]])
                    s.op("dve", lambda e: e.tensor_reduce(out=mx[i][:], in_=pss[i][:], axis=AX.X, op=ALU.max),
                         reads=[b_pss[i]], writes=[b_mx[i]])
                    s.op("dve", lambda e: e.tensor_scalar(out=nmx[i][:], in0=mx[i][:], scalar1=-sc, scalar2=None, op0=ALU.mult),
                         reads=[b_mx[i]], writes=[b_nmx[i]])
                    s.op("act", lambda e: e.activation(out=pf[i][:], in_=pss[i][:], func=AF.Exp, scale=sc, bias=nmx[i][:],
                                                       accum_out=rsum[i][:]),
                         reads=[b_pss[i], b_nmx[i]], writes=[b_pf[i], b_rsum[i]])
                    s.op("dve", lambda e: e.reciprocal(out=rinv[i][:], in_=rsum[i][:]), reads=[b_rsum[i]], writes=[b_rinv[i]])
                    s.op("dve", lambda e: e.tensor_scalar(out=pn[i][:], in0=pf[i][:], scalar1=rinv[i][:], scalar2=None, op0=ALU.mult),
                         reads=[b_pf[i], b_rinv[i]], writes=[b_pn[i]])
                    for mc in range(MC):
                        s.op("pe", lambda e: e.transpose(pst[i][:, mc, :], pn[i][:, mc * 128:(mc + 1) * 128], ident[:]),
                             reads=[b_pn[i], b_id], writes=[b_pst[i]], inc=True)
                    s.op("act", lambda e: e.copy(out=pT[i][:], in_=pst[i][:]), reads=[b_pst[i]], writes=[b_pT[i]])
                    for mc in range(MC):
                        s.op("pe", lambda e: e.matmul(pso[i][:], lhsT=vt[:, mc, hd * 128:(hd + 1) * 128], rhs=pT[i][:, mc, :],
                                                     start=(mc == 0), stop=(mc == MC - 1)),
                             reads=[b_vt, b_pT[i]], writes=[b_pso[i]], inc=(mc == MC - 1))
                    s.op("dve", lambda e: e.tensor_tensor(out=yo[yi][:, tsl], in0=pso[i][:], in1=sg[:, hd, tsl], op=ALU.mult),
                         reads=[b_pso[i], b_sg], writes=[b_yo[yi]])
                s.dma("sp", yT[row_y + hd * 128: row_y + (hd + 1) * 128, t0:t0 + NT], yo[yi][:], reads=[b_yo[yi]])
        s.barrier()


LCH = 64
MLSTM_L = 128
NEG = -1.0e30
RWKV_G = 2
RWKV_L = 128


def stage_mlstm(k, PT, rows, yT, row_y, prm, cst, T, HC):
    s = k.s
    ML = MLSTM_L
    NCK = T // ML
    QC = 2 * HC
    with contextlib.ExitStack() as st:
        ident = k.sb(st, [128, 128], F32)
        sel = k.sb(st, [HC, HC, 128], F32)
        i4 = k.sb(st, [HC, HC], F32)
        negmask = k.sb(st, [ML, ML], F32)
        ones = k.sb(st, [128, 128], F32)
        cw = k.sb(st, [128, 2 * QC, 4], F32)
        cb = k.sb(st, [128, 2 * QC], F32)
        gnw = k.sb(st, [128, HC * 4], F32)
        bi = k.sb(st, [HC, 1], F32)
        bfn = k.sb(st, [HC, 1], F32)
        b_c = Buf()
        for dst, src in ((ident, cst["ident"]), (sel, cst["sel"]), (i4, cst["i4"]), (negmask, cst["negmask"]),
                         (cw, prm["conv_w"]), (cb, prm["conv_b"]), (gnw, prm["gn_w"]), (bi, prm["b_i"]), (bfn, prm["b_f"])):
            s.dma("sp", dst[:], src, writes=[b_c])
        s.op("pool", lambda e: e.memset(ones[:], 1.0), writes=[b_c])
        s.op("dve", lambda e: e.tensor_scalar(out=bfn[:], in0=bfn[:], scalar1=-1.0, scalar2=None, op0=ALU.mult), reads=[b_c], writes=[b_c])
        gi = k.sb(st, [HC, T], F32)
        gf = k.sb(st, [HC, T], F32)
        gm = k.sb(st, [HC, T], F32)
        ge = k.sb(st, [HC, T], F32)
        b_g = Buf()
        s.dma("sp", gi[:], PT[rows["ifg"]:rows["ifg"] + HC, :], writes=[b_g])
        s.dma("sp", gf[:], PT[rows["ifg"] + HC:rows["ifg"] + 2 * HC, :], writes=[b_g])
        s.op("act", lambda e: e.activation(out=gf[:], in_=gf[:], func=AF.Exp, scale=-1.0, bias=bfn[:]), reads=[b_g, b_c], writes=[b_g])
        s.op("act", lambda e: e.activation(out=gf[:], in_=gf[:], func=AF.Ln, bias=1.0), reads=[b_g], writes=[b_g])
        s.op("dve", lambda e: e.tensor_scalar(out=gf[:], in0=gf[:], scalar1=-1.0, scalar2=None, op0=ALU.mult), reads=[b_g], writes=[b_g])
        s.op("pool", lambda e: e.memset(gm[:], 1.0), writes=[b_g])
        s.op("dve", lambda e: e.tensor_tensor_scan(out=gf[:], data0=gm[:], data1=gf[:], initial=0.0, op0=ALU.mult, op1=ALU.add),
             reads=[b_g], writes=[b_g])
        s.op("dve", lambda e: e.scalar_tensor_tensor(out=gi[:], in0=gi[:], scalar=bi[:], in1=gf[:], op0=ALU.add, op1=ALU.subtract),
             reads=[b_g, b_c], writes=[b_g])
        s.op("dve", lambda e: e.tensor_tensor_scan(out=gm[:], data0=gi[:], data1=gi[:], initial=NEG, op0=ALU.max, op1=ALU.max),
             reads=[b_g], writes=[b_g])
        s.op("dve", lambda e: e.tensor_tensor(out=ge[:], in0=gf[:], in1=gm[:], op=ALU.add), reads=[b_g], writes=[b_g])
        cols = k.sb(st, [ML, NCK, 2 * HC], F32)
        b_cols = Buf()
        with contextlib.ExitStack() as st1:
            psc = [k.ps(st1, [ML, 8, 2 * HC]) for _ in range(2)]
            b_psc = bufs(2)
            for c8 in range(0, NCK, 8):
                i = (c8 // 8) % 2
                for cc in range(8):
                    c = c8 + cc
                    s.op("pe", lambda e: e.matmul(psc[i][:, cc, 0:HC], lhsT=gi[:, c * ML:(c + 1) * ML], rhs=i4[:, 0:HC], start=True, stop=True),
                         reads=[b_g, b_c], writes=[b_psc[i]])
                    s.op("pe", lambda e: e.matmul(psc[i][:, cc, HC:2 * HC], lhsT=ge[:, c * ML:(c + 1) * ML], rhs=i4[:, 0:HC], start=True, stop=True),
                         reads=[b_g, b_c], writes=[b_psc[i]])
                s.op("dve", lambda e: e.tensor_copy(out=cols[:, c8:c8 + 8, :], in_=psc[i][:]), reads=[b_psc[i]], writes=[b_cols])
            s.op("act", lambda e: e.activation(out=cols[:, :, HC:2 * HC], in_=cols[:, :, HC:2 * HC], func=AF.Exp, scale=-1.0),
                 reads=[b_cols], writes=[b_cols])
            s.barrier()
        xin = k.sb(st, [128, 4, 3 + NT], F32)
        qk = k.sb(st, [128, 4, NT], F32)
        vT = k.sb(st, [128, 4, NT], F32)
        oT = k.sb(st, [128, 4, NT], F32)
        gT = k.sb(st, [128, 4, NT], F32)
        negMb = k.sb(st, [128, NT], F32)
        ms = k.sb(st, [128, NT // ML + 1], F32)
        keep2 = [k.sb(st, [128, 1], F32) for _ in range(2)]
        b_keep2 = bufs(2)
        ho = k.sb(st, [128, 4, NT], F32)
        Cst = k.sb(st, [128, 2, 512], F32)
        nst = k.sb(st, [128, 2], F32)
        ktok2 = [k.sb(st, [ML, 256], F32) for _ in range(2)]
        b_ktok2 = bufs(2)
        khat2 = [k.sb(st, [ML, 256], F32) for _ in range(2)]
        b_khat2 = bufs(2)
        vtok2 = [k.sb(st, [ML, 512], F32) for _ in range(2)]
        b_vtok2 = bufs(2)
        wk2 = [k.sb(st, [ML, 1], F32) for _ in range(2)]
        b_wk2 = bufs(2)
        argE = k.sb(st, [ML, ML], F32)
        ET = k.sb(st, [ML, ML], F32)
        scT2 = [k.sb(st, [ML, ML], F32) for _ in range(2)]
        b_scT2 = bufs(2)
        wint = k.sb(st, [128, ML], F32)
        qt2 = [k.sb(st, [128, 2, ML], F32) for _ in range(2)]
        b_qt2 = bufs(2)
        den = k.sb(st, [ML, 1], F32)
        rden = k.sb(st, [ML, 1], F32)
        htok = k.sb(st, [ML, 512], F32)
        mean = k.sb(st, [128, NT], F32)
        yc = k.sb(st, [128, 4, NT], F32)
        ysq = k.sb(st, [128, NT], F32)
        rstd = k.sb(st, [128, NT], F32)
        yo = [k.sb(st, [128, NT], BF16) for _ in range(2)]
        ps_b = k.ps(st, [128, NT])
        ps_t = k.ps(st, [ML, 512])
        ps_s = k.ps(st, [ML, ML])
        ps_n = k.ps(st, [ML, 512])
        ps_d = k.ps(st, [ML, 8])
        ps_c = k.ps(st, [128, 512])
        ps_h = k.ps(st, [128, 4, ML])
        ps_m = k.ps(st, [128, 8])
        (b_xin, b_qk, b_vT, b_oT, b_gT, b_negMb, b_ms, b_keep, b_ho, b_C, b_n, b_ktok, b_khat, b_vtok, b_wk, b_argE, b_ET, b_scT,
         b_wint, b_qt, b_den, b_rden, b_htok, b_mean, b_yc, b_ysq, b_rstd, b_psb, b_pst, b_pss, b_psn, b_psd, b_psc2, b_psh, b_psm) = bufs(35)
        b_yo = bufs(2)
        for h in range(HC):
            qrows = [rows["q"] + h * 256, rows["q"] + h * 256 + 128, rows["k"] + h * 256, rows["k"] + h * 256 + 128]
            cidx = [h * 2, h * 2 + 1, QC + h * 2, QC + h * 2 + 1]
            s.op("pool", lambda e: e.memset(Cst[:], 0.0), reads=[b_C], writes=[b_C])
            s.op("pool", lambda e: e.memset(nst[:], 0.0), reads=[b_n], writes=[b_n])
            s.op("pool", lambda e: e.memset(ms[:, 0:1], NEG), reads=[b_ms], writes=[b_ms])
            for ti, t0 in enumerate(range(0, T, NT)):
                for j in range(4):
                    if ti == 0:
                        s.op("pool", lambda e: e.memset(xin[:, j, 0:3], 0.0), reads=[b_xin], writes=[b_xin])
                    else:
                        s.op("pool", lambda e: e.tensor_copy(out=xin[:, j, 0:3], in_=xin[:, j, NT:NT + 3]), reads=[b_xin], writes=[b_xin])
                for j in range(4):
                    s.dma("sp", xin[:, j, 3:3 + NT], PT[qrows[j]:qrows[j] + 128, t0:t0 + NT], reads=[b_xin], writes=[b_xin])
                s.dma("act", vT[:], PT[rows["v"] + h * 512: rows["v"] + (h + 1) * 512, t0:t0 + NT].rearrange("(c p) t -> p c t", p=128), writes=[b_vT])
                s.dma("act", oT[:], PT[rows["o"] + h * 512: rows["o"] + (h + 1) * 512, t0:t0 + NT].rearrange("(c p) t -> p c t", p=128), writes=[b_oT])
                s.dma("act", gT[:], PT[rows["g"] + h * 512: rows["g"] + (h + 1) * 512, t0:t0 + NT].rearrange("(c p) t -> p c t", p=128), writes=[b_gT])
                for j in range(4):
                    ci = cidx[j]
                    s.op("dve", lambda e: e.tensor_scalar(out=qk[:, j, :], in0=xin[:, j, 3:3 + NT], scalar1=cw[:, ci, 3:4],
                                                          scalar2=cb[:, ci:ci + 1], op0=ALU.mult, op1=ALU.add),
                         reads=[b_xin, b_c, b_qk], writes=[b_qk])
                    for jj in (2, 1, 0):
                        s.op("dve", lambda e: e.scalar_tensor_tensor(out=qk[:, j, :], in0=xin[:, j, jj:jj + NT], scalar=cw[:, ci, jj:jj + 1],
                                                                     in1=qk[:, j, :], op0=ALU.mult, op1=ALU.add),
                             reads=[b_xin, b_c, b_qk], writes=[b_qk])
                s.op("act", lambda e: e.activation(out=qk[:], in_=qk[:], func=AF.Silu), reads=[b_qk], writes=[b_qk])
                s.op("dve", lambda e: e.tensor_scalar(out=qk[:, 2:4, :], in0=qk[:, 2:4, :], scalar1=256 ** -0.5, scalar2=None, op0=ALU.mult),
                     reads=[b_qk], writes=[b_qk])
                s.op("act", lambda e: e.activation(out=oT[:], in_=oT[:], func=AF.Sigmoid), reads=[b_oT], writes=[b_oT])
                s.op("act", lambda e: e.activation(out=gT[:], in_=gT[:], func=AF.Silu), reads=[b_gT], writes=[b_gT])
                s.op("pe", lambda e: e.matmul(ps_b[:], lhsT=sel[:, h, :], rhs=gm[:, t0:t0 + NT], start=True, stop=True),
                     reads=[b_c, b_g, b_psb], writes=[b_psb])
                if ti > 0:
                    s.op("dve", lambda e: e.tensor_scalar(out=ms[:, 0:1], in0=negMb[:, NT - 1:NT], scalar1=-1.0, scalar2=None, op0=ALU.mult),
                         reads=[b_negMb, b_ms], writes=[b_ms])
                s.op("act", lambda e: e.activation(out=negMb[:], in_=ps_b[:], func=AF.Copy, scale=-1.0), reads=[b_psb, b_negMb], writes=[b_negMb])
                nck = NT // ML
                s.op("dve", lambda e: e.tensor_scalar(out=ms[:, 1:nck + 1], in0=negMb[:, ML - 1:NT:ML], scalar1=-1.0, scalar2=None, op0=ALU.mult),
                     reads=[b_negMb, b_ms], writes=[b_ms])
                def m_indep(cc):
                    c = ti * nck + cc
                    cp = c % 2
                    csl = slice(cc * ML, (cc + 1) * ML)
                    ktok, khat, vtok, scT, qt, keep, wk = ktok2[cp], khat2[cp], vtok2[cp], scT2[cp], qt2[cp], keep2[cp], wk2[cp]
                    b_ktok, b_khat, b_vtok, b_scT, b_qt, b_keep, b_wk = (b_ktok2[cp], b_khat2[cp], b_vtok2[cp], b_scT2[cp], b_qt2[cp],
                                                                          b_keep2[cp], b_wk2[cp])
                    def t1():
                        for j in range(2):
                            s.op("pe", lambda e: e.transpose(ps_t[:, j * 128:(j + 1) * 128], qk[:, 2 + j, csl], ident[:]),
                                 reads=[b_qk, b_c, b_pst], writes=[b_pst])
                        s.op("act", lambda e: e.copy(out=ktok[:], in_=ps_t[:, 0:256]), reads=[b_pst, b_ktok], writes=[b_ktok])
                        for j in range(4):
                            s.op("pe", lambda e: e.transpose(ps_t[:, j * 128:(j + 1) * 128], vT[:, j, csl], ident[:]),
                                 reads=[b_vT, b_c, b_pst], writes=[b_pst])
                        s.op("act", lambda e: e.copy(out=vtok[:], in_=ps_t[:]), reads=[b_pst, b_vtok], writes=[b_vtok])
                    def t2():
                        for j in range(2):
                            s.op("pe", lambda e: e.matmul(ps_s[:], lhsT=qk[:, 2 + j, csl], rhs=qk[:, j, csl], start=(j == 0), stop=(j == 1)),
                                 reads=[b_qk, b_pss], writes=[b_pss])
                        s.op("dve", lambda e: e.tensor_tensor(out=argE[:], in0=negMb[0:ML, csl], in1=negmask[:], op=ALU.add),
                             reads=[b_negMb, b_c, b_argE], writes=[b_argE])
                        s.op("act", lambda e: e.activation(out=ET[:], in_=argE[:], func=AF.Exp, bias=cols[:, c, h:h + 1]),
                             reads=[b_argE, b_cols, b_ET], writes=[b_ET])
                        s.op("dve", lambda e: e.tensor_tensor(out=scT[:], in0=ps_s[:], in1=ET[:], op=ALU.mult),
                             reads=[b_pss, b_ET, b_scT], writes=[b_scT])
                    def t3():
                        s.op("act", lambda e: e.activation(out=wint[:], in_=negMb[:, csl], func=AF.Exp, bias=ms[:, cc:cc + 1]),
                             reads=[b_negMb, b_ms, b_wint], writes=[b_wint])
                        for j in range(2):
                            s.op("dve", lambda e: e.tensor_tensor(out=qt[:, j, :], in0=qk[:, j, csl], in1=wint[:], op=ALU.mult),
                                 reads=[b_qk, b_wint, b_qt], writes=[b_qt])
                    def t4():
                        s.op("act", lambda e: e.activation(out=keep[:], in_=negMb[:, (cc + 1) * ML - 1:(cc + 1) * ML], func=AF.Exp, bias=ms[:, cc:cc + 1]),
                             reads=[b_negMb, b_ms, b_keep], writes=[b_keep])
                        s.op("act", lambda e: e.activation(out=wk[:], in_=negMb[0:ML, (cc + 1) * ML - 1:(cc + 1) * ML], func=AF.Exp, bias=cols[:, c, h:h + 1]),
                             reads=[b_negMb, b_cols, b_wk], writes=[b_wk])
                        s.op("dve", lambda e: e.tensor_scalar(out=khat[:], in0=ktok[:], scalar1=wk[:], scalar2=None, op0=ALU.mult),
                             reads=[b_ktok, b_wk, b_khat], writes=[b_khat])
                    return [t1, t2, t3, t4]

                def m_dep(cc):
                    c = ti * nck + cc
                    cp = c % 2
                    csl = slice(cc * ML, (cc + 1) * ML)
                    ktok, khat, vtok, scT, qt, keep, wk = ktok2[cp], khat2[cp], vtok2[cp], scT2[cp], qt2[cp], keep2[cp], wk2[cp]
                    b_ktok, b_khat, b_vtok, b_scT, b_qt, b_keep, b_wk = (b_ktok2[cp], b_khat2[cp], b_vtok2[cp], b_scT2[cp], b_qt2[cp],
                                                                          b_keep2[cp], b_wk2[cp])
                    def d1():
                        for j in range(2):
                            s.op("pe", lambda e: e.matmul(ps_n[:], lhsT=qt[:, j, :], rhs=Cst[:, j, :], start=(j == 0), stop=False),
                                 reads=[b_qt, b_C, b_psn], writes=[b_psn], inc=False)
                        s.op("pe", lambda e: e.matmul(ps_n[:], lhsT=scT[:], rhs=vtok[:], start=False, stop=True),
                             reads=[b_scT, b_vtok, b_psn], writes=[b_psn])
                        for j in range(2):
                            s.op("pe", lambda e: e.matmul(ps_d[:, 0:1], lhsT=qt[:, j, :], rhs=nst[:, j:j + 1], start=(j == 0), stop=False),
                                 reads=[b_qt, b_n, b_psd], writes=[b_psd], inc=False)
                        s.op("pe", lambda e: e.matmul(ps_d[:, 0:1], lhsT=scT[:], rhs=ones[0:ML, 0:1], start=False, stop=True),
                             reads=[b_scT, b_c, b_psd], writes=[b_psd])
                    def d2():
                        s.op("act", lambda e: e.activation(out=den[:], in_=ps_d[:, 0:1], func=AF.Abs),
                             reads=[b_psd, b_den], writes=[b_den])
                        s.op("dve", lambda e: e.tensor_tensor(out=den[:], in0=den[:], in1=cols[:, c, HC + h:HC + h + 1], op=ALU.max),
                             reads=[b_den, b_cols], writes=[b_den])
                        s.op("dve", lambda e: e.reciprocal(out=rden[:], in_=den[:]), reads=[b_den, b_rden], writes=[b_rden])
                        s.op("act", lambda e: e.activation(out=htok[:], in_=ps_n[:], func=AF.Copy, scale=rden[:]),
                             reads=[b_psn, b_rden, b_htok], writes=[b_htok])
                    def d3():
                        for j in range(4):
                            s.op("pe", lambda e: e.transpose(ps_h[:, j, :], htok[:, j * 128:(j + 1) * 128], ident[0:ML, 0:ML]),
                                 reads=[b_htok, b_c, b_psh], writes=[b_psh])
                        s.op("dve", lambda e: e.tensor_tensor(out=ho[:, :, csl], in0=ps_h[:], in1=oT[:, :, csl], op=ALU.mult),
                             reads=[b_psh, b_oT, b_ho], writes=[b_ho])
                    def d4():
                        for j in range(2):
                            s.op("pe", lambda e: e.matmul(ps_c[:], lhsT=khat[:, j * 128:(j + 1) * 128], rhs=vtok[:], start=True, stop=True),
                                 reads=[b_khat, b_vtok, b_psc2], writes=[b_psc2])
                            s.op("dve", lambda e: e.scalar_tensor_tensor(out=Cst[:, j, :], in0=Cst[:, j, :], scalar=keep[:], in1=ps_c[:],
                                                                         op0=ALU.mult, op1=ALU.add),
                                 reads=[b_C, b_keep, b_psc2], writes=[b_C])
                            s.op("pe", lambda e: e.matmul(ps_m[:, 0:1], lhsT=khat[:, j * 128:(j + 1) * 128], rhs=ones[0:ML, 0:1], start=True, stop=True),
                                 reads=[b_khat, b_c, b_psm], writes=[b_psm])
                            s.op("dve", lambda e: e.scalar_tensor_tensor(out=nst[:, j:j + 1], in0=nst[:, j:j + 1], scalar=keep[:], in1=ps_m[:, 0:1],
                                                                         op0=ALU.mult, op1=ALU.add),
                                 reads=[b_n, b_keep, b_psm], writes=[b_n])
                    return [d1, d2, d3, d4]

                for f_ in m_indep(0):
                    f_()
                for cc in range(nck):
                    dst_ = m_dep(cc)
                    ist_ = m_indep(cc + 1) if cc + 1 < nck else []
                    for j_ in range(max(len(dst_), len(ist_))):
                        if j_ < len(dst_):
                            dst_[j_]()
                        if j_ < len(ist_):
                            ist_[j_]()
                for j in range(4):
                    s.op("pe", lambda e: e.matmul(ps_b[:], lhsT=ones[:], rhs=ho[:, j, :], start=(j == 0), stop=(j == 3)),
                         reads=[b_c, b_ho, b_psb], writes=[b_psb], inc=(j == 3))
                s.op("act", lambda e: e.activation(out=mean[:], in_=ps_b[:], func=AF.Copy, scale=1.0 / 512), reads=[b_psb, b_mean], writes=[b_mean])
                for j in range(4):
                    s.op("dve", lambda e: e.tensor_tensor(out=yc[:, j, :], in0=ho[:, j, :], in1=mean[:], op=ALU.subtract),
                         reads=[b_ho, b_mean, b_yc], writes=[b_yc])
                for j in range(4):
                    s.op("act", lambda e: e.activation(out=ysq[:], in_=yc[:, j, :], func=AF.Square), reads=[b_yc, b_ysq], writes=[b_ysq])
                    s.op("pe", lambda e: e.matmul(ps_b[:], lhsT=ones[:], rhs=ysq[:], start=(j == 0), stop=(j == 3)),
                         reads=[b_c, b_ysq, b_psb], writes=[b_psb])
                s.op("act", lambda e: e.activation(out=rstd[:], in_=ps_b[:], func=AF.Sqrt, scale=1.0 / 512, bias=1e-6),
                     reads=[b_psb, b_rstd], writes=[b_rstd])
                s.op("dve", lambda e: e.reciprocal(out=rstd[:], in_=rstd[:]), reads=[b_rstd], writes=[b_rstd])
                for j in range(4):
                    yi = j % 2
                    s.op("dve", lambda e: e.scalar_tensor_tensor(out=yc[:, j, :], in0=yc[:, j, :], scalar=gnw[:, h * 4 + j:h * 4 + j + 1],
                                                                 in1=rstd[:], op0=ALU.mult, op1=ALU.mult),
                         reads=[b_yc, b_c, b_rstd], writes=[b_yc])
                    s.op("dve", lambda e: e.tensor_tensor(out=yo[yi][:], in0=yc[:, j, :], in1=gT[:, j, :], op=ALU.mult),
                         reads=[b_yc, b_gT, b_yo[yi]], writes=[b_yo[yi]])
                    r0 = row_y + h * 512 + j * 128
                    s.dma("sp", yT[r0:r0 + 128, t0:t0 + NT], yo[yi][:], reads=[b_yo[yi]])
        s.barrier()


def stage_rwkv(k, PT, rows, yT, row_y, prm, cst, T, NH, GN_EPS):
    s = k.s
    G = RWKV_G
    assert G == 2
    NLVL = {64: 6, 128: 7}[RWKV_L]
    RL = RWKV_L
    NCK = NT // RL
    assert NH % G == 0 and T % NT == 0
    with contextlib.ExitStack() as st:
        ident = k.sb(st, [128, 128], F32)
        ones64 = k.sb(st, [64, 64], F32)
        rmask = k.sb(st, [64, NT], F32)
        maskA = k.sb(st, [RL, G, 2 * RL], F32)
        maskN = k.sb(st, [RL, G, RL], F32)
        P = {}
        for nm in ("mu_r", "mu_k", "mu_v", "w0", "a0", "k_k", "k_a", "r_k", "gn_w", "gn_b"):
            P[nm] = k.sb(st, [64, NH], F32)
        om = {nm: k.sb(st, [64, NH], F32) for nm in ("mu_r", "mu_k", "mu_v", "k_a")}
        nw0 = k.sb(st, [64, NH], F32)
        mud = {nm: k.sb(st, [96, 1], F32) for nm in ("mu_wd", "mu_ad")}
        omd = {nm: k.sb(st, [96, 1], F32) for nm in ("mu_wd", "mu_ad")}
        b_c = Buf()
        for dst, src in ((ident, cst["ident"]), (rmask, cst["resetmask"]), (maskA, cst["maskA"]), (maskN, cst["maskN"])):
            s.dma("sp", dst[:], src, writes=[b_c])
        for nm in P:
            s.dma("sp", P[nm][:], prm[nm], writes=[b_c])
        for nm in mud:
            s.dma("sp", mud[nm][:], prm[nm], writes=[b_c])
        s.op("pool", lambda e: e.memset(ones64[:], 1.0), writes=[b_c])
        for nm in om:
            s.op("dve", lambda e: e.tensor_scalar(out=om[nm][:], in0=P[nm][:], scalar1=-1.0, scalar2=1.0, op0=ALU.mult, op1=ALU.add),
                 reads=[b_c], writes=[b_c])
        for nm in omd:
            s.op("dve", lambda e: e.tensor_scalar(out=omd[nm][:], in0=mud[nm][:], scalar1=-1.0, scalar2=1.0, op0=ALU.mult, op1=ALU.add),
                 reads=[b_c], writes=[b_c])
        s.op("dve", lambda e: e.tensor_scalar(out=nw0[:], in0=P["w0"][:], scalar1=-1.0, scalar2=None, op0=ALU.mult), reads=[b_c], writes=[b_c])

        wup = k.sb(st, [96, G * 64], F32)
        aup = k.sb(st, [96, G * 64], F32)
        rawd = k.sb(st, [96, 1 + NT], F32)
        rawa = k.sb(st, [96, 1 + NT], F32)
        twd = k.sb(st, [96, NT], F32)
        adq = k.sb(st, [96, NT], F32)
        raw = {nm: k.sb(st, [64, G, 1 + NT], F32) for nm in ("r", "k", "v")}
        gate = None
        (r_, k_, e2, c_, eP, eN, a_, kk, kkn, kmod, kka, t1, t2) = [[k.sb(st, [64, NT], F32) for _ in range(G)] for _ in range(13)]
        yfm = k.sb(st, [64, G, NT], F32)
        dbl_tiles = [[k.sb(st, shp, F32) for _ in range(2)] for shp in
                     ([64, G, NT], [64, G, NCK, 2, RL], [64, G, NT], [64, G, NT], [64, G, NT], [64, G, NT], [64, G, NT], [64, G, NT], [64, G, NCK])]
        dbl_bufs = [bufs(2) for _ in range(9)]
        P1_PER_STEP = 2
        tokb = [k.sb(st, [RL, 3, G, 64], F32) for _ in range(2)]
        GAb = [k.sb(st, [RL, G, 2 * RL], F32) for _ in range(2)]
        GKb = [k.sb(st, [RL, G, 2 * RL], F32) for _ in range(2)]
        Pnb = [k.sb(st, [RL, G, RL], F32) for _ in range(2)]
        PPb = [[k.sb(st, [RL, G, 2, RL], F32) for _ in range(NLVL - 1)] for _ in range(2)]
        b_tokb, b_GAb, b_GKb, b_Pnb = bufs(2), bufs(2), bufs(2), bufs(2)
        b_PPb = [bufs(NLVL - 1), bufs(NLVL - 1)]
        U = k.sb(st, [RL, G, 64], F32)
        Ysb = k.sb(st, [RL, G, 64], F32)
        ST = k.sb(st, [64, G, 64], F32)
        yo = [k.sb(st, [64, NT], BF16) for _ in range(2)]
        bank = [k.ps(st, [128, 512]) for _ in range(8)]
        b_bank = bufs(8)
        ps_A, ps_B = bank[0], bank[1]
        ps_ga = bank[0][0:RL, 0:G * 2 * RL].rearrange("p (g x) -> p g x", g=G)
        ps_gk = bank[1][0:RL, 0:G * 2 * RL].rearrange("p (g x) -> p g x", g=G)
        ps_x1 = bank[2][0:RL, 0:2 * G * 64].rearrange("p (a g x) -> p a g x", a=2, g=G)
        ps_x2k = bank[3][0:RL, 0:G * 64].rearrange("p (g x) -> p g x", g=G)
        ps_gn = bank[3][0:RL, G * 64:G * 64 + G * RL].rearrange("p (g x) -> p g x", g=G)
        ps_z = bank[4][0:RL, 0:2 * G * 64].rearrange("p (a g x) -> p a g x", a=2, g=G)
        ps_pp = bank[5][0:RL, 0:G * 2 * RL].rearrange("p (g a x) -> p g a x", g=G, a=2)
        ps_y = bank[6][0:RL, 0:G * 64].rearrange("p (g x) -> p g x", g=G)
        ps_yT = bank[6][0:64, G * 64:G * 64 + G * RL].rearrange("p (g x) -> p g x", g=G)
        ps_s = bank[7][0:64, 0:G * 64].rearrange("p (g x) -> p g x", g=G)
        ps_ln = bank[7]
        (b_wup, b_rawd, b_rawa, b_twd, b_adq, b_gate,
         b_Vt, b_AR, b_Bt, b_Kt, b_Bh, b_Kh, b_bonus, b_yfm, b_gL, b_tok, b_GAs, b_GKs, b_Pn, b_U, b_Ysb, b_ST) = bufs(22)
        (b_r, b_k, b_e2, b_cc, b_eP, b_eN, b_a, b_kk, b_kkn, b_kmod, b_kka, b_t1, b_t2) = [bufs(G) for _ in range(13)]
        b_raw = {nm: Buf() for nm in raw}
        b_PP = bufs(2)
        b_yo = bufs(2)
        i64 = ident[0:64, 0:64]

        def shift(dst, src3, g, mu_t, om_t, H, reads, bdst):
            s.op("dve", lambda e: e.tensor_scalar(out=dst, in0=src3[:, g, 1:1 + NT], scalar1=om_t[:, H:H + 1], scalar2=None, op0=ALU.mult),
                 reads=reads + [b_c, bdst], writes=[bdst])
            s.op("dve", lambda e: e.scalar_tensor_tensor(out=dst, in0=src3[:, g, 0:NT], scalar=mu_t[:, H:H + 1], in1=dst,
                                                         op0=ALU.mult, op1=ALU.add),
                 reads=reads + [b_c, bdst], writes=[bdst])

        for h0 in range(0, NH, G):
            s.dma("sp", wup[:], prm["w_up"][:, h0 * 64:(h0 + G) * 64], reads=[b_wup], writes=[b_wup])
            s.dma("sp", aup[:], prm["a_up"][:, h0 * 64:(h0 + G) * 64], reads=[b_wup], writes=[b_wup])
            s.op("pool", lambda e: e.memset(ST[:], 0.0), reads=[b_ST], writes=[b_ST])
            def make_tile(ti, t0):
                tp = ti % 2
                Vt, AR, Bt, Kt, Bh, Kh, bonus, gate, gL = (x[tp] for x in dbl_tiles)
                b_Vt, b_AR, b_Bt, b_Kt, b_Bh, b_Kh, b_bonus, b_gate, b_gL = (x[tp] for x in dbl_bufs)
                def head_gen(g):
                        H = h0 + g
                        hc = slice(g * 64, (g + 1) * 64)
                        shift(r_[g][:], raw["r"], g, P["mu_r"], om["mu_r"], H, [b_raw["r"]], b_r[g])
                        yield
                        shift(k_[g][:], raw["k"], g, P["mu_k"], om["mu_k"], H, [b_raw["k"]], b_k[g])
                        yield
                        shift(Vt[:, g, :], raw["v"], g, P["mu_v"], om["mu_v"], H, [b_raw["v"]], b_Vt)
                        yield
                        s.op("pe", lambda e: e.matmul(bank[g][0:64, :], lhsT=wup[:, hc], rhs=twd[:], start=True, stop=True),
                             reads=[b_wup, b_twd, b_bank[g]], writes=[b_bank[g]])
                        yield
                        s.op("act", lambda e: e.activation(out=t1[g][:], in_=bank[g][0:64, :], func=AF.Exp, scale=-1.0, bias=nw0[:, H:H + 1]),
                             reads=[b_bank[g], b_c, b_t1[g]], writes=[b_t1[g]])
                        yield
                        s.op("act", lambda e: e.activation(out=t1[g][:], in_=t1[g][:], func=AF.Ln, bias=1.0), reads=[b_t1[g]], writes=[b_t1[g]])
                        yield
                        s.op("act", lambda e: e.activation(out=e2[g][:], in_=t1[g][:], func=AF.Exp, scale=-1.0, bias=-0.5), reads=[b_t1[g], b_e2[g]], writes=[b_e2[g]])
                        yield
                        s.op("dve", lambda e: e.tensor_tensor_scan(out=c_[g][:], data0=rmask[:], data1=e2[g][:], initial=0.0, op0=ALU.mult, op1=ALU.add),
                             reads=[b_c, b_e2[g], b_cc[g]], writes=[b_cc[g]])
                        yield
                        s.op("act", lambda e: e.activation(out=eP[g][:], in_=c_[g][:], func=AF.Exp, scale=-1.0), reads=[b_cc[g], b_eP[g]], writes=[b_eP[g]])
                        yield
                        s.op("act", lambda e: e.activation(out=eN[g][:], in_=c_[g][:], func=AF.Exp), reads=[b_cc[g], b_eN[g]], writes=[b_eN[g]])
                        yield
                        s.op("pool", lambda e: e.tensor_copy(out=gL[:, g, :], in_=eP[g][:, RL - 1:NT:RL]), reads=[b_eP[g], b_gL], writes=[b_gL])
                        yield
                        yield
                        s.op("pe", lambda e: e.matmul(bank[g][0:64, :], lhsT=aup[:, hc], rhs=adq[:], start=True, stop=True),
                             reads=[b_wup, b_adq, b_bank[g]], writes=[b_bank[g]])
                        yield
                        s.op("act", lambda e: e.activation(out=a_[g][:], in_=bank[g][0:64, :], func=AF.Sigmoid, bias=P["a0"][:, H:H + 1]),
                             reads=[b_bank[g], b_c, b_a[g]], writes=[b_a[g]])
                        yield
                        s.op("dve", lambda e: e.tensor_scalar(out=kk[g][:], in0=k_[g][:], scalar1=P["k_k"][:, H:H + 1], scalar2=None, op0=ALU.mult),
                             reads=[b_k[g], b_c, b_kk[g]], writes=[b_kk[g]])
                        yield
                        s.op("act", lambda e: e.activation(out=t2[g][:], in_=kk[g][:], func=AF.Square), reads=[b_kk[g], b_t2[g]], writes=[b_t2[g]])
                        yield
                        s.op("pe", lambda e: e.matmul(bank[g][0:64, :], lhsT=ones64[:], rhs=t2[g][:], start=True, stop=True),
                             reads=[b_c, b_t2[g], b_bank[g]], writes=[b_bank[g]])
                        yield
                        s.op("act", lambda e: e.activation(out=t2[g][:], in_=bank[g][0:64, :], func=AF.Sqrt), reads=[b_bank[g], b_t2[g]], writes=[b_t2[g]])
                        yield
                        s.op("dve", lambda e: e.tensor_scalar(out=t2[g][:], in0=t2[g][:], scalar1=1e-12, scalar2=None, op0=ALU.max), reads=[b_t2[g]], writes=[b_t2[g]])
                        yield
                        s.op("dve", lambda e: e.reciprocal(out=t2[g][:], in_=t2[g][:]), reads=[b_t2[g]], writes=[b_t2[g]])
                        yield
                        s.op("dve", lambda e: e.tensor_tensor(out=kkn[g][:], in0=kk[g][:], in1=t2[g][:], op=ALU.mult), reads=[b_kk[g], b_t2[g], b_kkn[g]], writes=[b_kkn[g]])
                        yield
                        s.op("dve", lambda e: e.tensor_scalar(out=t1[g][:], in0=a_[g][:], scalar1=P["k_a"][:, H:H + 1], scalar2=om["k_a"][:, H:H + 1],
                                                              op0=ALU.mult, op1=ALU.add), reads=[b_a[g], b_c, b_t1[g]], writes=[b_t1[g]])
                        yield
                        s.op("dve", lambda e: e.tensor_tensor(out=kmod[g][:], in0=k_[g][:], in1=t1[g][:], op=ALU.mult), reads=[b_k[g], b_t1[g], b_kmod[g]], writes=[b_kmod[g]])
                        yield
                        s.op("dve", lambda e: e.tensor_tensor(out=kka[g][:], in0=kkn[g][:], in1=a_[g][:], op=ALU.mult), reads=[b_kkn[g], b_a[g], b_kka[g]], writes=[b_kka[g]])
                        yield
                        s.op("dve", lambda e: e.scalar_tensor_tensor(out=t2[g][:], in0=r_[g][:], scalar=P["r_k"][:, H:H + 1], in1=kmod[g][:],
                                                                     op0=ALU.mult, op1=ALU.mult), reads=[b_r[g], b_c, b_kmod[g], b_t2[g]], writes=[b_t2[g]])
                        yield
                        s.op("pe", lambda e: e.matmul(bank[g][0:64, :], lhsT=ones64[:], rhs=t2[g][:], start=True, stop=True),
                             reads=[b_c, b_t2[g], b_bank[g]], writes=[b_bank[g]])
                        yield
                        s.op("dve", lambda e: e.tensor_tensor(out=bonus[:, g, :], in0=bank[g][0:64, :], in1=Vt[:, g, :], op=ALU.mult),
                             reads=[b_bank[g], b_Vt, b_bonus], writes=[b_bonus])
                        yield
                        s.op("dve", lambda e: e.tensor_tensor(out=t1[g][:], in0=e2[g][:], in1=c_[g][:], op=ALU.subtract), reads=[b_e2[g], b_cc[g], b_t1[g]], writes=[b_t1[g]])
                        yield
                        s.op("act", lambda e: e.activation(out=t1[g][:], in_=t1[g][:], func=AF.Exp), reads=[b_t1[g]], writes=[b_t1[g]])
                        yield
                        v3 = lambda ap: ap.rearrange("p (n l) -> p n l", l=RL)
                        s.op("dve", lambda e: e.scalar_tensor_tensor(out=AR[:, g, :, 0, :], in0=v3(kkn[g][:]), scalar=-1.0, in1=v3(t1[g][:]),
                                                                     op0=ALU.mult, op1=ALU.mult), reads=[b_kkn[g], b_t1[g], b_AR], writes=[b_AR])
                        yield
                        s.op("dve", lambda e: e.tensor_tensor(out=AR[:, g, :, 1, :], in0=v3(r_[g][:]), in1=v3(eP[g][:]), op=ALU.mult),
                             reads=[b_r[g], b_eP[g], b_AR], writes=[b_AR])
                        yield
                        s.op("dve", lambda e: e.tensor_tensor(out=Bt[:, g, :], in0=kka[g][:], in1=eN[g][:], op=ALU.mult), reads=[b_kka[g], b_eN[g], b_Bt], writes=[b_Bt])
                        yield
                        s.op("dve", lambda e: e.tensor_tensor(out=Kt[:, g, :], in0=kmod[g][:], in1=eN[g][:], op=ALU.mult), reads=[b_kmod[g], b_eN[g], b_Kt], writes=[b_Kt])
                        yield
                        gbc = gL[:, g, :].unsqueeze(2).broadcast_to([64, NCK, RL])
                        s.op("dve", lambda e: e.tensor_tensor(out=v3(Bh[:, g, :]), in0=v3(Bt[:, g, :]), in1=gbc, op=ALU.mult),
                             reads=[b_Bt, b_gL, b_Bh], writes=[b_Bh])
                        yield
                        s.op("dve", lambda e: e.tensor_tensor(out=v3(Kh[:, g, :]), in0=v3(Kt[:, g, :]), in1=gbc, op=ALU.mult),
                             reads=[b_Kt, b_gL, b_Kh], writes=[b_Kh])
                        yield


                def phase1():
                    for rw, brw, nm, rowk in ((rawd, b_rawd, "mu_wd", "wd"), (rawa, b_rawa, "mu_ad", "ad")):
                        if ti == 0:
                            s.op("pool", lambda e: e.memset(rw[:, 0:1], 0.0), reads=[brw], writes=[brw])
                        else:
                            s.op("pool", lambda e: e.tensor_copy(out=rw[:, 0:1], in_=rw[:, NT:NT + 1]), reads=[brw], writes=[brw])
                        s.dma("sp", rw[:, 1:1 + NT], PT[rows[rowk]:rows[rowk] + 96, t0:t0 + NT], reads=[brw], writes=[brw])
                    dsts = ((twd, b_twd, rawd, b_rawd, "mu_wd"), (adq, b_adq, rawa, b_rawa, "mu_ad"))
                    for dst, bdst, rw, brw, nm in dsts:
                        s.op("dve", lambda e: e.tensor_scalar(out=dst[:], in0=rw[:, 1:1 + NT], scalar1=omd[nm][:], scalar2=None, op0=ALU.mult),
                             reads=[brw, b_c, bdst], writes=[bdst])
                        s.op("dve", lambda e: e.scalar_tensor_tensor(out=dst[:], in0=rw[:, 0:NT], scalar=mud[nm][:], in1=dst[:],
                                                                     op0=ALU.mult, op1=ALU.add),
                             reads=[brw, b_c, bdst], writes=[bdst])
                    s.op("act", lambda e: e.activation(out=twd[:], in_=twd[:], func=AF.Tanh), reads=[b_twd], writes=[b_twd])
                    for nm in ("r", "k", "v"):
                        if ti == 0:
                            s.op("pool", lambda e: e.memset(raw[nm][:, :, 0:1], 0.0), reads=[b_raw[nm]], writes=[b_raw[nm]])
                        else:
                            s.op("pool", lambda e: e.tensor_copy(out=raw[nm][:, :, 0:1], in_=raw[nm][:, :, NT:NT + 1]),
                                 reads=[b_raw[nm]], writes=[b_raw[nm]])
                        r0 = rows[nm] + h0 * 64
                        s.dma("sp", raw[nm][:, :, 1:1 + NT], PT[r0:r0 + G * 64, t0:t0 + NT].rearrange("(h j) t -> j h t", j=64),
                              reads=[b_raw[nm]], writes=[b_raw[nm]])
                    r0 = rows["g"] + h0 * 64
                    s.dma("act", gate[:], PT[r0:r0 + G * 64, t0:t0 + NT].rearrange("(h j) t -> j h t", j=64), reads=[b_gate], writes=[b_gate])
                    s.op("act", lambda e: e.activation(out=gate[:], in_=gate[:], func=AF.Silu), reads=[b_gate], writes=[b_gate])
                    alive = [head_gen(g) for g in range(G)]
                    while alive:
                        for gg in list(alive):
                            try:
                                next(gg)
                            except StopIteration:
                                alive.remove(gg)
                        yield
                def indep_steps(cc):
                    par = cc % 2
                    cs = slice(cc * RL, (cc + 1) * RL)
                    tk, ga_s, gk_s, pn_s, ppl = tokb[par], GAb[par], GKb[par], Pnb[par], PPb[par]
                    b_tk, b_ga, b_gk, b_pn, b_ppl = b_tokb[par], b_GAb[par], b_GKb[par], b_Pnb[par], b_PPb[par]

                    def tr():
                        for g in range(G):
                            s.op("pe", lambda e: e.transpose(ps_x1[:, 0, g, :], Vt[:, g, cs], i64), reads=[b_Vt, b_c, b_bank[2]], writes=[b_bank[2]])
                            s.op("pe", lambda e: e.transpose(ps_x1[:, 1, g, :], Bh[:, g, cs], i64), reads=[b_Bh, b_c, b_bank[2]], writes=[b_bank[2]])
                            s.op("pe", lambda e: e.transpose(ps_x2k[:, g, :], Kh[:, g, cs], i64), reads=[b_Kh, b_c, b_bank[3]], writes=[b_bank[3]])
                        s.op("act", lambda e: e.copy(out=tk[:, 0:2], in_=ps_x1), reads=[b_bank[2], b_tk], writes=[b_tk])
                        s.op("act", lambda e: e.copy(out=tk[:, 2], in_=ps_x2k), reads=[b_bank[3], b_tk], writes=[b_tk])

                    def gm():
                        for g in range(G):
                            arc = AR[:, g, cc].rearrange("p a l -> p (a l)")
                            s.op("pe", lambda e: e.matmul(ps_ga[:, g, :], lhsT=Bt[:, g, cs], rhs=arc, start=True, stop=True),
                                 reads=[b_Bt, b_AR, b_bank[0]], writes=[b_bank[0]])
                            s.op("pe", lambda e: e.matmul(ps_gk[:, g, :], lhsT=Kt[:, g, cs], rhs=arc, start=True, stop=True),
                                 reads=[b_Kt, b_AR, b_bank[1]], writes=[b_bank[1]])
                            s.op("pe", lambda e: e.matmul(ps_gn[:, g, :], lhsT=AR[:, g, cc, 0, :], rhs=Bt[:, g, cs], start=True, stop=True),
                                 reads=[b_Bt, b_AR, b_bank[3]], writes=[b_bank[3]])
                        s.op("dve", lambda e: e.tensor_tensor(out=ga_s[:], in0=ps_ga, in1=maskA[:], op=ALU.mult),
                             reads=[b_bank[0], b_c, b_ga], writes=[b_ga])
                        s.op("dve", lambda e: e.tensor_tensor(out=gk_s[:], in0=ps_gk, in1=maskA[:], op=ALU.mult),
                             reads=[b_bank[1], b_c, b_gk], writes=[b_gk])
                        s.op("dve", lambda e: e.tensor_tensor(out=pn_s[:], in0=ps_gn, in1=maskN[:], op=ALU.mult),
                             reads=[b_bank[3], b_c, b_pn], writes=[b_pn])

                    def sq(lvl):
                        def f():
                            if lvl == 0:
                                Pl = lambda g: pn_s[:, g, :]
                                PTl = lambda g: ga_s[:, g, 0:RL]
                                rd = [b_pn, b_ga]
                            else:
                                Pl = lambda g: ppl[lvl - 1][:, g, 0, :]
                                PTl = lambda g: ppl[lvl - 1][:, g, 1, :]
                                rd = [b_ppl[lvl - 1]]
                            for g in range(G):
                                s.op("pe", lambda e: e.matmul(ps_pp[:, g, 0, :], lhsT=PTl(g), rhs=Pl(g), start=True, stop=True),
                                     reads=rd + [b_bank[5]], writes=[b_bank[5]])
                                s.op("pe", lambda e: e.matmul(ps_pp[:, g, 1, :], lhsT=Pl(g), rhs=PTl(g), start=True, stop=True),
                                     reads=rd + [b_bank[5]], writes=[b_bank[5]])
                            s.op("act", lambda e: e.copy(out=ppl[lvl][:], in_=ps_pp), reads=[b_bank[5], b_ppl[lvl]], writes=[b_ppl[lvl]])
                        return f
                    return [tr, gm] + [sq(l) for l in range(NLVL - 1)]

                def dep_steps(cc):
                    par = cc % 2
                    cs = slice(cc * RL, (cc + 1) * RL)
                    tk, ga_s, gk_s, pn_s, ppl = tokb[par], GAb[par], GKb[par], Pnb[par], PPb[par]
                    b_tk, b_ga, b_gk, b_pn, b_ppl = b_tokb[par], b_GAb[par], b_GKb[par], b_Pnb[par], b_PPb[par]

                    def zz():
                        for g in range(G):
                            s.op("pe", lambda e: e.matmul(ps_z[:, 0, g, :], lhsT=AR[:, g, cc, 0, :], rhs=ST[:, g, :], start=True, stop=False),
                                 reads=[b_AR, b_ST, b_bank[4]], writes=[b_bank[4]], inc=False)
                            s.op("pe", lambda e: e.matmul(ps_z[:, 0, g, :], lhsT=gk_s[:, g, 0:RL], rhs=tk[:, 0, g, :], start=False, stop=True),
                                 reads=[b_gk, b_tk, b_bank[4]], writes=[b_bank[4]])
                        s.op("act", lambda e: e.copy(out=U[:], in_=ps_z[:, 0]), reads=[b_bank[4], b_U], writes=[b_U])

                    def app(lvl):
                        def f():
                            if lvl == 0:
                                PTl = lambda g: ga_s[:, g, 0:RL]
                                rd = [b_ga]
                            else:
                                PTl = lambda g: ppl[lvl - 1][:, g, 1, :]
                                rd = [b_ppl[lvl - 1]]
                            for g in range(G):
                                s.op("pe", lambda e: e.matmul(ps_z[:, 1, g, :], lhsT=PTl(g), rhs=U[:, g, :], start=True, stop=True),
                                     reads=rd + [b_U, b_bank[4]], writes=[b_bank[4]])
                            s.op("dve", lambda e: e.tensor_tensor(out=U[:], in0=ps_z[:, 1], in1=U[:], op=ALU.add),
                                 reads=[b_bank[4], b_U], writes=[b_U])
                        return f

                    def yy():
                        for g in range(G):
                            s.op("pe", lambda e: e.matmul(ps_y[:, g, :], lhsT=AR[:, g, cc, 1, :], rhs=ST[:, g, :], start=True, stop=False),
                                 reads=[b_AR, b_ST, b_bank[6]], writes=[b_bank[6]], inc=False)
                            s.op("pe", lambda e: e.matmul(ps_y[:, g, :], lhsT=ga_s[:, g, RL:2 * RL], rhs=U[:, g, :], start=False, stop=False),
                                 reads=[b_ga, b_U, b_bank[6]], writes=[b_bank[6]], inc=False)
                            s.op("pe", lambda e: e.matmul(ps_y[:, g, :], lhsT=gk_s[:, g, RL:2 * RL], rhs=tk[:, 0, g, :], start=False, stop=True),
                                 reads=[b_gk, b_tk, b_bank[6]], writes=[b_bank[6]])
                        s.op("act", lambda e: e.copy(out=Ysb[:], in_=ps_y), reads=[b_bank[6], b_Ysb], writes=[b_Ysb])

                    def ss():
                        for g in range(G):
                            s.op("pe", lambda e: e.matmul(ps_s[:, g, :], lhsT=tk[:, 1, g, :], rhs=U[:, g, :], start=True, stop=False),
                                 reads=[b_tk, b_U, b_bank[7]], writes=[b_bank[7]], inc=False)
                            s.op("pe", lambda e: e.matmul(ps_s[:, g, :], lhsT=tk[:, 2, g, :], rhs=tk[:, 0, g, :], start=False, stop=True),
                                 reads=[b_tk, b_bank[7]], writes=[b_bank[7]])
                        for g in range(G):
                            s.op("dve", lambda e: e.scalar_tensor_tensor(out=ST[:, g, :], in0=ST[:, g, :], scalar=gL[:, g, cc:cc + 1], in1=ps_s[:, g, :],
                                                                         op0=ALU.mult, op1=ALU.add),
                                 reads=[b_ST, b_gL, b_bank[7]], writes=[b_ST])

                    def yt():
                        for g in range(G):
                            s.op("pe", lambda e: e.transpose(ps_yT[:, g, :], Ysb[:, g, :], ident[0:RL, 0:RL]), reads=[b_Ysb, b_c, b_bank[6]], writes=[b_bank[6]])
                        s.op("dve", lambda e: e.tensor_copy(out=yfm[:, :, cs], in_=ps_yT), reads=[b_bank[6], b_yfm], writes=[b_yfm])
                    return [zz] + [app(l) for l in range(NLVL)] + [yy, ss, yt]


                def chunks():
                    for f in indep_steps(0):
                        f()
                        yield
                    for cc in range(NCK):
                        dsteps = dep_steps(cc)
                        isteps = indep_steps(cc + 1) if cc + 1 < NCK else []
                        for j in range(max(len(dsteps), len(isteps))):
                            if j < len(dsteps):
                                dsteps[j]()
                            if j < len(isteps):
                                isteps[j]()
                            yield
                def ph4(g):
                    H = h0 + g
                    yi = g % 2
                    s.op("pe", lambda e: e.matmul(bank[(7, 4)[g]][0:64, :], lhsT=ones64[:], rhs=yfm[:, g, :], start=True, stop=True),
                         reads=[b_c, b_yfm, b_bank[(7, 4)[g]]], writes=[b_bank[(7, 4)[g]]])
                    yield
                    s.op("act", lambda e: e.activation(out=t1[g][:], in_=bank[(7, 4)[g]][0:64, :], func=AF.Copy, scale=1.0 / 64), reads=[b_bank[(7, 4)[g]], b_t1[g]], writes=[b_t1[g]])
                    yield
                    s.op("dve", lambda e: e.tensor_tensor(out=t1[g][:], in0=yfm[:, g, :], in1=t1[g][:], op=ALU.subtract), reads=[b_yfm, b_t1[g]], writes=[b_t1[g]])
                    yield
                    s.op("act", lambda e: e.activation(out=t2[g][:], in_=t1[g][:], func=AF.Square), reads=[b_t1[g], b_t2[g]], writes=[b_t2[g]])
                    yield
                    s.op("pe", lambda e: e.matmul(bank[(7, 4)[g]][0:64, :], lhsT=ones64[:], rhs=t2[g][:], start=True, stop=True),
                         reads=[b_c, b_t2[g], b_bank[(7, 4)[g]]], writes=[b_bank[(7, 4)[g]]])
                    yield
                    s.op("act", lambda e: e.activation(out=t2[g][:], in_=bank[(7, 4)[g]][0:64, :], func=AF.Sqrt, scale=1.0 / 64, bias=GN_EPS),
                         reads=[b_bank[(7, 4)[g]], b_t2[g]], writes=[b_t2[g]])
                    yield
                    s.op("dve", lambda e: e.reciprocal(out=t2[g][:], in_=t2[g][:]), reads=[b_t2[g]], writes=[b_t2[g]])
                    yield
                    s.op("dve", lambda e: e.tensor_tensor(out=t1[g][:], in0=t1[g][:], in1=t2[g][:], op=ALU.mult), reads=[b_t1[g], b_t2[g]], writes=[b_t1[g]])
                    yield
                    s.op("dve", lambda e: e.tensor_scalar(out=t1[g][:], in0=t1[g][:], scalar1=P["gn_w"][:, H:H + 1], scalar2=P["gn_b"][:, H:H + 1],
                                                          op0=ALU.mult, op1=ALU.add), reads=[b_t1[g], b_c], writes=[b_t1[g]])
                    yield
                    s.op("dve", lambda e: e.tensor_tensor(out=t1[g][:], in0=t1[g][:], in1=bonus[:, g, :], op=ALU.add), reads=[b_t1[g], b_bonus], writes=[b_t1[g]])
                    yield
                    s.op("dve", lambda e: e.tensor_tensor(out=yo[yi][:], in0=t1[g][:], in1=gate[:, g, :], op=ALU.mult),
                         reads=[b_t1[g], b_gate, b_yo[yi]], writes=[b_yo[yi]])
                    yield
                    r0 = row_y + H * 64
                    s.dma("sp", yT[r0:r0 + 64, t0:t0 + NT], yo[yi][:], reads=[b_yo[yi]])
                    yield

                def phase4():
                    alive = [ph4(g) for g in range(G)]
                    while alive:
                        for gg in list(alive):
                            try:
                                next(gg)
                            except StopIteration:
                                alive.remove(gg)
                        yield
                return phase1, chunks, phase4

            tiles = [make_tile(ti, t0) for ti, t0 in enumerate(range(0, T, NT))]
            for _ in tiles[0][0]():
                pass
            for ti in range(len(tiles)):
                cg = tiles[ti][1]()
                pg = tiles[ti + 1][0]() if ti + 1 < len(tiles) else iter(())
                c_alive, p_alive = True, True
                while c_alive or p_alive:
                    if c_alive:
                        try:
                            next(cg)
                        except StopIteration:
                            c_alive = False
                    for _ in range(P1_PER_STEP):
                        if p_alive:
                            try:
                                next(pg)
                            except StopIteration:
                                p_alive = False
                for _ in tiles[ti][2]():
                    pass
        s.barrier()


class Cfg:
    def __init__(self, D=4096, T=4096, NBLK_A=8, NH_B=32, HC=4, HX=4, M=256, DEPTH=2):
        self.D, self.T, self.NBLK_A, self.NH_B, self.HC, self.HX, self.M, self.DEPTH = D, T, NBLK_A, NH_B, HC, HX, M, DEPTH
        self.KC = D // 128
        self.WA = NBLK_A * 256
        self.WB = NH_B * 64
        self.QKW = HC * 256
        self.WC = HC * 512
        self.WX = HX * 128
        self.in_sizes = (self.WA, self.WA, 3 * self.WB + 192, self.WB, 2 * self.QKW, self.WC, self.WC, self.WC, 2 * HC,
                         self.WX, self.WX, 4 * D)
        self.c_in = sum(self.in_sizes)
        off = np.concatenate([[0], np.cumsum(self.in_sizes)])
        self.col = dict(a_x=off[0], a_g=off[1], b_s=off[2], b_g=off[3], c_qk=off[4], c_v=off[5], c_o=off[6], c_g=off[7],
                        c_if=off[8], x_q=off[9], x_g=off[10], gates=off[11])
        segs = [("a_x", self.col["a_x"], self.WA), ("a_g", self.col["a_g"], self.WA),
                ("r", self.col["b_s"], self.WB), ("k", self.col["b_s"] + self.WB, self.WB), ("v", self.col["b_s"] + 2 * self.WB, self.WB),
                ("wd", self.col["b_s"] + 3 * self.WB, 96), ("ad", self.col["b_s"] + 3 * self.WB + 96, 96),
                ("b_g", self.col["b_g"], self.WB),
                ("c_q", self.col["c_qk"], self.QKW), ("c_k", self.col["c_qk"] + self.QKW, self.QKW),
                ("c_v", self.col["c_v"], self.WC), ("c_o", self.col["c_o"], self.WC), ("c_g", self.col["c_g"], self.WC),
                ("c_if", self.col["c_if"], 2 * HC), ("x_q", self.col["x_q"], self.WX), ("x_g", self.col["x_g"], self.WX)]
        self.segs = segs
        self.row = {}
        r = 0
        for nm, c0, w in segs:
            self.row[nm] = r
            r += ((w + 127) // 128) * 128
        self.NB1 = r // 128
        self.grp_first = {"A": "a_x", "B": "r", "C": "c_q", "X": "x_q"}
        order = ["A", "B", "C", "X"]
        starts = [self.row[self.grp_first[g]] for g in order] + [r]
        self.grp_rows = {g: (starts[i], starts[i + 1]) for i, g in enumerate(order)}
        self.lrow = {}
        for nm, c0, w in segs:
            for g in order:
                lo, hi = self.grp_rows[g]
                if lo <= self.row[nm] < hi:
                    self.lrow[nm] = (g, self.row[nm] - lo)
        self.br_kc = [self.WA // 128, self.WB // 128, self.WC // 128, self.WX // 128]
        self.FY = sum(self.br_kc) * 128


RMS_EPS = 1e-6
RWKV_GN_EPS = 64e-5


def tile_layout(W):
    Kd, M = W.shape
    return np.ascontiguousarray(W.reshape(Kd // 128, 128, M // 128, 128).transpose(2, 1, 0, 3))


def chunk_cols(v):
    return np.ascontiguousarray(v.reshape(-1, 128).T)


def head_cols(v):
    return np.ascontiguousarray(v.reshape(-1, 64).T)


def const_inputs(cfg):
    HC = cfg.HC
    sel = np.zeros((HC, HC, 128), np.float32)
    for h in range(HC):
        sel[h, h, :] = 1
    a_, b_ = np.meshgrid(np.arange(RWKV_L), np.arange(RWKV_L), indexing="ij")
    mA = np.concatenate([(a_ < b_), (a_ <= b_)], 1).astype(np.float32)
    mN = (b_ < a_).astype(np.float32)
    rm = np.ones((64, NT), np.float32)
    rm[:, ::RWKV_L] = 0
    a_, b_ = np.meshgrid(np.arange(MLSTM_L), np.arange(MLSTM_L), indexing="ij")
    return {"c_ident": np.eye(128, dtype=np.float32), "c_sel": sel, "c_i4": np.eye(HC, dtype=np.float32),
            "c_negmask": np.where(a_ <= b_, 0.0, NEG).astype(np.float32), "c_resetmask": rm,
            "c_maskA": np.ascontiguousarray(np.broadcast_to(mA[:, None, :], (RWKV_L, RWKV_G, 2 * RWKV_L))),
            "c_maskN": np.ascontiguousarray(np.broadcast_to(mN[:, None, :], (RWKV_L, RWKV_G, RWKV_L)))}


def layer_inputs(cfg, inp, l):
    D, KC, HC = cfg.D, cfg.KC, cfg.HC
    w_in = inp["w_in"][l]
    W1 = np.zeros((D, cfg.NB1 * 128), np.float32)
    for nm, c0, w in cfg.segs:
        W1[:, cfg.row[nm]:cfg.row[nm] + w] = w_in[:, c0:c0 + w]
    o = {}
    o["W1"] = tile_layout(W1)
    g0 = cfg.col["gates"]
    o["Wg"] = np.stack([tile_layout(w_in[:, g0 + i * D: g0 + (i + 1) * D]) for i in range(4)])
    o["Wbr0"] = tile_layout(inp["w_branch_a"][l])
    o["Wbr1"] = tile_layout(inp["w_branch_b"][l])
    o["Wbr2"] = tile_layout(inp["w_branch_c"][l])
    o["Wbr3"] = tile_layout(inp["w_branch_x"][l])
    o["Wo"] = tile_layout(inp["w_out"][l])
    o["norm_g"] = chunk_cols(inp["norm_g"][l])
    o["mem_norm_g"] = chunk_cols(inp["mem_norm_g"][l])
    NCH = cfg.NBLK_A * 2
    o["lru_conv_w"] = np.ascontiguousarray(inp["lru_conv_w"][l].reshape(4, NCH, 128).transpose(2, 1, 0))
    for nm in ("lru_conv_b", "lru_ba", "lru_bx", "lru_lambda"):
        o[nm] = chunk_cols(inp[nm][l])
    for nm in ("lru_wa", "lru_wx"):
        o[nm] = np.ascontiguousarray(inp[nm][l].reshape(cfg.NBLK_A, 2, 128, 2, 128).transpose(0, 3, 2, 1, 4))
    WB = cfg.WB
    mu = inp["rwkv_mu"][l]
    o["rw_mu_r"], o["rw_mu_k"], o["rw_mu_v"] = head_cols(mu[:WB]), head_cols(mu[WB:2 * WB]), head_cols(mu[2 * WB:3 * WB])
    o["rw_mu_wd"] = np.ascontiguousarray(mu[3 * WB:3 * WB + 96].reshape(96, 1))
    o["rw_mu_ad"] = np.ascontiguousarray(mu[3 * WB + 96:].reshape(96, 1))
    for nm, src in (("w0", "rwkv_w0"), ("a0", "rwkv_a0"), ("k_k", "rwkv_k_k"), ("k_a", "rwkv_k_a"), ("gn_w", "rwkv_gn_w"), ("gn_b", "rwkv_gn_b")):
        o["rw_" + nm] = head_cols(inp[src][l])
    o["rw_r_k"] = head_cols(inp["rwkv_r_k"][l].reshape(-1))
    o["rw_w_up"] = np.ascontiguousarray(inp["rwkv_w_up"][l])
    o["rw_a_up"] = np.ascontiguousarray(inp["rwkv_a_up"][l])
    o["ml_conv_w"] = np.ascontiguousarray(inp["mlstm_conv_w"][l].reshape(4, 4 * HC, 128).transpose(2, 1, 0))
    o["ml_conv_b"] = chunk_cols(inp["mlstm_conv_b"][l])
    o["ml_gn_w"] = chunk_cols(inp["mlstm_gn_w"][l])
    o["ml_b_i"] = np.ascontiguousarray(inp["mlstm_b_i"][l].reshape(HC, 1))
    o["ml_b_f"] = np.ascontiguousarray(inp["mlstm_b_f"][l].reshape(HC, 1))
    wkv = inp["xattn_w_kv"][l]
    o["xa_Wk"] = tile_layout(wkv[:, :cfg.WX])
    o["xa_Wv"] = np.ascontiguousarray(wkv[:, cfg.WX:].reshape(KC, 128, cfg.WX))
    return {f"L{l}_{k_}": np.ascontiguousarray(v, dtype=np.float32) for k_, v in o.items()}


def flat2d(ap, ndim, width):
    names = "abcdefgh"[:ndim]
    f = ap.rearrange(f"{' '.join(names)} -> ({' '.join(names)})")
    return f.rearrange("(r c) -> r c", c=width)


def build_program(cfg, shapes):
    nc = bass.Bass("TRN2", target_bir_lowering=False)
    D, T, KC = cfg.D, cfg.T, cfg.KC
    ins = {nm: nc.dram_tensor(nm, list(sh), F32, kind="ExternalInput").ap() for nm, sh in shapes.items()}
    outT = nc.dram_tensor("outT", [D, T], F32, kind="ExternalOutput").ap()
    with contextlib.ExitStack() as st:
        k = Ctx(nc, st)
        hT = k.dram("hT", [D, T], BF16)
        PTs = {g: k.dram(f"PT{g}", [hi - lo, T], F32) for g, (lo, hi) in cfg.grp_rows.items()}

        def pt_block(b):
            r = b * 128
            for g, (lo, hi) in cfg.grp_rows.items():
                if lo <= r < hi:
                    return PTs[g], r - lo
            raise AssertionError
        lr = lambda nm: cfg.lrow[nm][1]
        yT = k.dram("yT", [cfg.FY, T], BF16)
        memnT = k.dram("memnT", [D, cfg.M], BF16)
        xs = [ins["xT"]] + [k.dram(f"x{l + 1}T", [D, T], F32) for l in range(cfg.DEPTH)]
        cst = {nm[2:]: ins[nm] for nm in ins if nm.startswith("c_")}
        OB = D // 128
        for l in range(cfg.DEPTH):
            L = lambda nm: ins[f"L{l}_{nm}"]
            Wg = k.dram(f"Wg{l}", [4, OB, 128, KC, 128], BF16)
            Wo = k.dram(f"Wo{l}", [OB, 128, KC, 128], BF16)
            Wbr = [k.dram(f"Wbr{l}_{i}", [OB, 128, cfg.br_kc[i], 128], BF16) for i in range(4)]
            for dst, src, nd in [(Wg, L("Wg"), 5), (Wo, L("Wo"), 4)] + [(Wbr[i], L(f"Wbr{i}"), 4) for i in range(4)]:
                n = int(np.prod(dst.shape))
                wdt = 1024 if n % 1024 == 0 else 512
                cast_dram(k, flat2d(dst, nd, wdt), flat2d(src, nd, wdt), n, width=wdt)
            stage_norm(k, xs[l], L("norm_g"), hT, D, T, RMS_EPS, BF16)
            stage_norm(k, ins["memT"], L("mem_norm_g"), memnT, D, cfg.M, RMS_EPS, BF16)
            stage_proj(k, hT, L("W1"), pt_block, D, T, cfg.NB1)
            lru_prm = {"conv_w": L("lru_conv_w"), "conv_b": L("lru_conv_b"), "ba": L("lru_ba"), "bx": L("lru_bx"), "lam": L("lru_lambda"),
                       "wa": L("lru_wa"), "wx": L("lru_wx")}
            stage_lru(k, PTs["A"], lr("a_x"), lr("a_g"), yT, 0, lru_prm, T, cfg.NBLK_A)
            rw_prm = {nm: L("rw_" + nm) for nm in ("mu_r", "mu_k", "mu_v", "w0", "a0", "k_k", "k_a", "r_k", "gn_w", "gn_b", "mu_wd", "mu_ad", "w_up", "a_up")}
            rw_rows = {"r": lr("r"), "k": lr("k"), "v": lr("v"), "wd": lr("wd"), "ad": lr("ad"), "g": lr("b_g")}
            stage_rwkv(k, PTs["B"], rw_rows, yT, cfg.WA, rw_prm, cst, T, cfg.NH_B, RWKV_GN_EPS)
            ml_prm = {"conv_w": L("ml_conv_w"), "conv_b": L("ml_conv_b"), "gn_w": L("ml_gn_w"), "b_i": L("ml_b_i"), "b_f": L("ml_b_f")}
            ml_rows = {"q": lr("c_q"), "k": lr("c_k"), "v": lr("c_v"), "o": lr("c_o"), "g": lr("c_g"), "ifg": lr("c_if")}
            stage_mlstm(k, PTs["C"], ml_rows, yT, cfg.WA + cfg.WB, ml_prm, cst, T, cfg.HC)
            stage_xattn(k, PTs["X"], lr("x_q"), lr("x_g"), yT, cfg.WA + cfg.WB + cfg.WC, memnT, L("xa_Wk"), L("xa_Wv"), cst["ident"],
                        T, D, cfg.M, cfg.HX)
            stage_merge(k, hT, yT, cfg.br_kc, Wg, Wbr, Wo, xs[l], xs[l + 1], D, T)
        stage_norm(k, xs[cfg.DEPTH], ins["final_g"], outT, D, T, RMS_EPS, F32)
        k.s.barrier()
        build_program.ninst = k.s.ninst
        build_program.per_eng = dict(k.s.per_eng)
        build_program.nsem = k.s.nsem
        build_program.nwait = k.s.nwait
    return nc


def run_module(cfg, inputs):
    B = inputs["x"].shape[0]
    shared = const_inputs(cfg)
    for l in range(cfg.DEPTH):
        shared.update(layer_inputs(cfg, inputs, l))
    shared["final_g"] = chunk_cols(np.asarray(inputs["final_norm_g"], dtype=np.float32))
    in_maps = []
    for b in range(B):
        m = dict(shared)
        m["xT"] = np.ascontiguousarray(np.asarray(inputs["x"][b], dtype=np.float32).T)
        m["memT"] = np.ascontiguousarray(np.asarray(inputs["mem"][b], dtype=np.float32).T)
        in_maps.append(m)
    shapes = {nm: v.shape for nm, v in in_maps[0].items()}
    nc = build_program(cfg, shapes)
    res = run_bass_kernel_spmd(nc, in_maps, core_ids=list(range(B)))
    out = np.stack([np.ascontiguousarray(res.results[b]["outT"].T) for b in range(B)])
    return out.astype(np.float32)


def kernel(**inputs):
    inputs = {k_: np.asarray(v) for k_, v in inputs.items()}
    return run_module(Cfg(), inputs)
```

```python
import contextlib
import numpy as np
import concourse.bass as bass
import concourse.mybir as mybir
from concourse.bass_utils import run_bass_kernel_spmd

F32 = mybir.dt.float32
BF16 = mybir.dt.bfloat16
AF = mybir.ActivationFunctionType
ALU = mybir.AluOpType
AX = mybir.AxisListType


class Buf:
    __slots__ = ("name", "w", "r")

    def __init__(self, name=""):
        self.name = name
        self.w = set()
        self.r = set()


SKIP_SAME_ENGINE = False


class Sched:
    EPOCH = 4000
    NDMA = 12

    def __init__(self, nc, stack):
        self.nc = nc
        self.stack = stack
        self.eng = {"pe": nc.tensor, "act": nc.scalar, "dve": nc.vector,
                    "pool": nc.gpsimd, "sp": nc.sync}
        self.sem = {}
        self.cnt = {}
        self.pending = {e: False for e in self.eng}
        self.nsem = 0
        for e in self.eng:
            self._new_sem(e)
        self.dsem = {}
        for q in ("sp", "act", "pool"):
            self.dsem[q] = [[self._alloc(f"d{q}{i}"), 0] for i in range(self.NDMA)]
        self.dnext = {q: 0 for q in self.dsem}
        self.seen = {e: {} for e in self.eng}
        self.all_tokens = {}
        self.ninst = 0
        self.per_eng = {}

    def _alloc(self, name):
        self.nsem += 1
        return self.stack.enter_context(self.nc.semaphore(f"{name}_{self.nsem}"))

    def _new_sem(self, e):
        self.sem[e] = self._alloc(f"s{e}")
        self.cnt[e] = 0

    def _wait(self, e, tok):
        sem, val = tok
        k = id(sem)
        if self.seen[e].get(k, 0) >= val:
            return
        self.eng[e].wait_ge(sem, val)
        self.nwait = getattr(self, "nwait", 0) + 1
        self.seen[e][k] = val

    def _deps(self, e, reads, writes):
        deps = set()
        for b in reads:
            deps |= b.w
        for b in writes:
            deps |= b.w
            deps |= b.r
        best = {}
        for tok in deps:
            if e == "pe" and tok[0] is self.sem["pe"]:
                continue
            if SKIP_SAME_ENGINE and e in ("act", "dve") and tok[0] is self.sem[e]:
                continue
            kk_ = id(tok[0])
            if kk_ not in best or best[kk_][1] < tok[1]:
                best[kk_] = tok
        for tok in best.values():
            self._wait(e, tok)

    def _record(self, tok, reads, writes):
        self.all_tokens[id(tok[0])] = tok
        for b in reads:
            b.r.add(tok)
        for b in writes:
            b.w = {tok}
            b.r = set()

    def op(self, e, fn, reads=(), writes=(), inc=True):
        self._deps(e, reads, writes)
        inst = fn(self.eng[e])
        self.ninst += 1
        self.per_eng[e] = self.per_eng.get(e, 0) + 1
        if inc:
            if self.cnt[e] >= self.EPOCH and not self.pending[e]:
                self._new_sem(e)
            self.cnt[e] += 1
            inst.then_inc(self.sem[e], 1)
            tok = (self.sem[e], self.cnt[e])
            self.pending[e] = False
        else:
            assert e == "pe"
            tok = (self.sem[e], self.cnt[e] + 1)
            self.pending[e] = True
        self._record(tok, reads, writes)
        return inst

    def dma(self, q, out, in_, reads=(), writes=(), **kw):
        slot = self.dsem[q][self.dnext[q]]
        self.dnext[q] = (self.dnext[q] + 1) % self.NDMA
        sem, val = slot
        if val > 0:
            self._wait(q, (sem, val))
        self._deps(q, reads, writes)
        inst = self.eng[q].dma_start(out=out, in_=in_, **kw)
        self.ninst += 1
        self.per_eng['dma_' + q] = self.per_eng.get('dma_' + q, 0) + 1
        slot[1] = val + 16
        inst.then_inc(sem, 16)
        tok = (sem, slot[1])
        self._record(tok, reads, writes)
        return inst

    def barrier(self):
        toks = list(self.all_tokens.values())
        for e in self.eng:
            for tok in toks:
                self._wait(e, tok)


class Ctx:
    def __init__(self, nc, stack):
        self.nc = nc
        self.s = Sched(nc, stack)
        self.n = 0

    def sb(self, st, shape, dt, name="t"):
        self.n += 1
        return st.enter_context(self.nc.sbuf_tensor(f"{name}{self.n}", list(shape), dt))

    def ps(self, st, shape, dt=F32, name="p"):
        self.n += 1
        return st.enter_context(self.nc.psum_tensor(f"{name}{self.n}", list(shape), dt))

    def dram(self, name, shape, dt):
        return self.nc.dram_tensor(name, list(shape), dt, kind="Internal").ap()


def dma_rows(s, q, sb3, dram2, nchunks, reads=(), writes=(), to_dram=False, step=8):
    for c0 in range(0, nchunks, step):
        c1 = min(nchunks, c0 + step)
        d = dram2[c0 * 128:c1 * 128, :].rearrange("(c p) t -> p c t", p=128)
        if to_dram:
            s.dma(q, d, sb3[:, c0:c1, :], reads=reads, writes=writes)
        else:
            s.dma(q, sb3[:, c0:c1, :], d, reads=reads, writes=writes)


def bufs(n):
    return [Buf() for _ in range(n)]


NT = 512


def stage_norm(k, xT, g_dram, out, D, T, eps, out_dt):
    NT = min(512, T)
    s = k.s
    KC = D // 128
    with contextlib.ExitStack() as st:
        ones = k.sb(st, [128, 128], F32)
        gcol = k.sb(st, [128, KC], F32)
        xt = k.sb(st, [128, KC, NT], F32)
        ht = k.sb(st, [128, KC, NT], out_dt)
        sq = [k.sb(st, [128, NT], F32) for _ in range(2)]
        rs = k.sb(st, [128, NT], F32)
        rstd = k.sb(st, [128, NT], F32)
        pss = k.ps(st, [128, NT])
        b_ones, b_g, b_rs, b_rstd, b_ps = bufs(5)
        b_xt, b_ht, b_sq = bufs(KC), bufs(KC), bufs(2)
        s.op("pool", lambda e: e.memset(ones[:], 1.0), writes=[b_ones])
        s.dma("sp", gcol[:], g_dram, writes=[b_g])
        for t0 in range(0, T, NT):
            for c in range(KC):
                s.dma("sp", xt[:, c, :], xT[c * 128:(c + 1) * 128, t0:t0 + NT], writes=[b_xt[c]])
                s.op("act", lambda e: e.activation(out=sq[c % 2][:], in_=xt[:, c, :], func=AF.Square),
                     reads=[b_xt[c]], writes=[b_sq[c % 2]])
                s.op("pe", lambda e: e.matmul(pss[:], lhsT=ones[:], rhs=sq[c % 2][:],
                                             start=(c == 0), stop=(c == KC - 1)),
                     reads=[b_ones, b_sq[c % 2]], writes=[b_ps], inc=True)
            s.op("act", lambda e: e.activation(out=rs[:], in_=pss[:], func=AF.Sqrt, scale=1.0 / D, bias=eps_ap(k, eps)),
                 reads=[b_ps], writes=[b_rs])
            s.op("dve", lambda e: e.reciprocal(out=rstd[:], in_=rs[:]), reads=[b_rs], writes=[b_rstd])
            for c in range(KC):
                s.op("dve", lambda e: e.scalar_tensor_tensor(out=ht[:, c, :], in0=xt[:, c, :], scalar=gcol[:, c:c + 1],
                                                             in1=rstd[:], op0=ALU.mult, op1=ALU.mult),
                     reads=[b_xt[c], b_g, b_rstd], writes=[b_ht[c]])
            dma_rows(s, "sp", ht, out[:, t0:t0 + NT], KC, reads=b_ht, to_dram=True)
        s.barrier()


_EPS = {}


def eps_ap(k, val):
    return float(val)


def stage_proj(k, hT, W, PT, D, T, NB):
    NT = min(512, T)
    s = k.s
    KC = D // 128
    GRP = 6
    with contextlib.ExitStack() as st:
        wb = [[k.sb(st, [128, KC, 128], BF16) for _ in range(GRP)] for _ in range(2)]
        ht = [k.sb(st, [128, KC, NT], BF16) for _ in range(2)]
        ot = [k.sb(st, [128, NT], F32) for _ in range(4)]
        ps = [k.ps(st, [128, NT]) for _ in range(8)]
        b_wb, b_ht, b_ot, b_ps = [bufs(GRP), bufs(GRP)], bufs(2), bufs(4), bufs(8)
        groups = list(range(0, NB, GRP))

        def load_group(gi):
            g0 = groups[gi]
            for b in range(min(GRP, NB - g0)):
                s.dma("pool", wb[gi % 2][b][:], W[g0 + b], writes=[b_wb[gi % 2][b]], max_dma_last_dim=4096)

        nht = 0
        no = 0
        npp = 0
        load_group(0)
        for gi, g0 in enumerate(groups):
            nb = min(GRP, NB - g0)
            if gi + 1 < len(groups):
                load_group(gi + 1)
            wset, bset = wb[gi % 2], b_wb[gi % 2]
            for t0 in range(0, T, NT):
                h = nht % 2
                nht += 1
                dma_rows(s, "act", ht[h], hT[:, t0:t0 + NT], KC, writes=[b_ht[h]])
                for b in range(nb):
                    pi = npp % 8
                    npp += 1
                    p = ps[pi]
                    for c in range(KC):
                        s.op("pe", lambda e: e.matmul(p[:], lhsT=wset[b][:, c, :], rhs=ht[h][:, c, :],
                                                     start=(c == 0), stop=(c == KC - 1)),
                             reads=[bset[b], b_ht[h]], writes=[b_ps[pi]], inc=(c == KC - 1))
                    o = no % 4
                    no += 1
                    if no % 2 == 0:
                        s.op("dve", lambda e: e.tensor_copy(out=ot[o][:], in_=p[:]), reads=[b_ps[pi]], writes=[b_ot[o]])
                    else:
                        s.op("act", lambda e: e.copy(out=ot[o][:], in_=p[:]), reads=[b_ps[pi]], writes=[b_ot[o]])
                    pt_t, pt_r = PT(g0 + b)
                    s.dma("sp", pt_t[pt_r:pt_r + 128, t0:t0 + NT], ot[o][:], reads=[b_ot[o]])
        s.barrier()


def cast_dram(k, dst, src, n_elems, q="pool", width=1024):
    s = k.s
    ROW = width
    assert n_elems % ROW == 0
    rows = n_elems // ROW
    CH = 2048
    for r0 in range(0, rows, CH):
        r1 = min(rows, r0 + CH)
        s.dma(q, dst[r0:r1, :], src[r0:r1, :])


def stage_merge(k, hT, yT, br_kc, Wg, Wbr, Wo, xT, xnT, D, T):
    s = k.s
    KC = D // 128
    OB = D // 128
    YC = sum(br_kc)
    yoff = [sum(br_kc[:i]) for i in range(len(br_kc))]
    NBR = len(br_kc)
    with contextlib.ExitStack() as st:
        ht = k.sb(st, [128, KC, NT], BF16)
        yt = k.sb(st, [128, YC, NT], BF16)
        mg = k.sb(st, [128, KC, NT], BF16)
        wg = [k.sb(st, [128, KC, 128], BF16) for _ in range(3)]
        wbr = [k.sb(st, [128, max(br_kc), 128], BF16) for _ in range(3)]
        gs = [k.sb(st, [128, NT], F32) for _ in range(2)]
        acc = [k.sb(st, [128, NT], F32) for _ in range(2)]
        tmp = [k.sb(st, [128, NT], F32) for _ in range(2)]
        xt = [k.sb(st, [128, NT], F32) for _ in range(2)]
        xn = [k.sb(st, [128, NT], F32) for _ in range(2)]
        psg = [k.ps(st, [128, NT]) for _ in range(2)]
        psp = [k.ps(st, [128, NT]) for _ in range(2)]
        pso = [k.ps(st, [128, NT]) for _ in range(2)]
        b_ht, b_yt = Buf(), Buf()
        b_mg = bufs(KC)
        b_wg, b_wbr, b_gs, b_acc, b_tmp, b_xt, b_xn = bufs(3), bufs(3), bufs(2), bufs(2), bufs(2), bufs(2), bufs(2)
        b_psg, b_psp, b_pso = bufs(2), bufs(2), bufs(2)
        nw = 0
        ng = 0
        na = 0
        for t0 in range(0, T, NT):
            dma_rows(s, "act", ht, hT[:, t0:t0 + NT], KC, writes=[b_ht])
            dma_rows(s, "act", yt, yT[:, t0:t0 + NT], YC, writes=[b_yt])
            for ob in range(OB):
                a = na % 2
                na += 1
                for br in range(NBR):
                    w = nw % 3
                    nw += 1
                    g = ng % 2
                    ng += 1
                    kcb = br_kc[br]
                    s.dma("sp", wg[w][:], Wg[br, ob], writes=[b_wg[w]])
                    s.dma("sp", wbr[w][:, 0:kcb, :], Wbr[br][ob], writes=[b_wbr[w]])
                    for c in range(KC):
                        s.op("pe", lambda e: e.matmul(psg[g][:], lhsT=wg[w][:, c, :], rhs=ht[:, c, :],
                                                     start=(c == 0), stop=(c == KC - 1)),
                             reads=[b_wg[w], b_ht], writes=[b_psg[g]], inc=(c == KC - 1))
                    for c in range(kcb):
                        s.op("pe", lambda e: e.matmul(psp[g][:], lhsT=wbr[w][:, c, :], rhs=yt[:, yoff[br] + c, :],
                                                     start=(c == 0), stop=(c == kcb - 1)),
                             reads=[b_wbr[w], b_yt], writes=[b_psp[g]], inc=(c == kcb - 1))
                    s.op("act", lambda e: e.activation(out=gs[g][:], in_=psg[g][:], func=AF.Sigmoid),
                         reads=[b_psg[g]], writes=[b_gs[g]])
                    last = (br == NBR - 1)
                    if br == 0:
                        dst, bdst = (mg[:, ob, :], b_mg[ob]) if last else (acc[a][:], b_acc[a])
                        s.op("dve", lambda e: e.tensor_tensor(out=dst, in0=psp[g][:], in1=gs[g][:], op=ALU.mult),
                             reads=[b_psp[g], b_gs[g]], writes=[bdst])
                    else:
                        s.op("dve", lambda e: e.tensor_tensor(out=tmp[g][:], in0=psp[g][:], in1=gs[g][:], op=ALU.mult),
                             reads=[b_psp[g], b_gs[g]], writes=[b_tmp[g]])
                        if last:
                            s.op("pool", lambda e: e.tensor_tensor(out=mg[:, ob, :], in0=acc[a][:], in1=tmp[g][:], op=ALU.add),
                                 reads=[b_acc[a], b_tmp[g]], writes=[b_mg[ob]])
                        else:
                            s.op("pool", lambda e: e.tensor_tensor(out=acc[a][:], in0=acc[a][:], in1=tmp[g][:], op=ALU.add),
                                 reads=[b_acc[a], b_tmp[g]], writes=[b_acc[a]])
            for ob in range(OB):
                w = nw % 3
                nw += 1
                g = ng % 2
                ng += 1
                s.dma("sp", wg[w][:], Wo[ob], writes=[b_wg[w]])
                s.dma("act", xt[g][:], xT[ob * 128:(ob + 1) * 128, t0:t0 + NT], writes=[b_xt[g]])
                for c in range(KC):
                    s.op("pe", lambda e: e.matmul(pso[g][:], lhsT=wg[w][:, c, :], rhs=mg[:, c, :],
                                                 start=(c == 0), stop=(c == KC - 1)),
                         reads=[b_wg[w], b_mg[c]], writes=[b_pso[g]], inc=(c == KC - 1))
                s.op("dve", lambda e: e.tensor_tensor(out=xn[g][:], in0=pso[g][:], in1=xt[g][:], op=ALU.add),
                     reads=[b_pso[g], b_xt[g]], writes=[b_xn[g]])
                s.dma("sp", xnT[ob * 128:(ob + 1) * 128, t0:t0 + NT], xn[g][:], reads=[b_xn[g]])
        s.barrier()


def stage_lru(k, PT, row_ax, row_ag, yT, row_y, prm, T, NBLK):
    s = k.s
    NCH = NBLK * 2
    with contextlib.ExitStack() as st:
        cw = k.sb(st, [128, NCH, 4], F32)
        cb, ba, bx, lam, c1, c2, tt = [k.sb(st, [128, NCH], F32) for _ in range(7)]
        b_prm = Buf()
        for dst, nm in ((cw, "conv_w"), (cb, "conv_b"), (ba, "ba"), (bx, "bx"), (lam, "lam")):
            s.dma("sp", dst[:], prm[nm], writes=[b_prm])
        s.op("act", lambda e: e.activation(out=tt[:], in_=lam[:], func=AF.Exp, scale=-1.0), reads=[b_prm], writes=[b_prm])
        s.op("act", lambda e: e.activation(out=tt[:], in_=tt[:], func=AF.Ln, bias=1.0), reads=[b_prm], writes=[b_prm])
        s.op("dve", lambda e: e.tensor_scalar(out=c1[:], in0=tt[:], scalar1=-8.0, scalar2=None, op0=ALU.mult), reads=[b_prm], writes=[b_prm])
        s.op("dve", lambda e: e.tensor_scalar(out=c2[:], in0=tt[:], scalar1=-16.0, scalar2=None, op0=ALU.mult), reads=[b_prm], writes=[b_prm])
        waf = k.sb(st, [128, 2, 2, 128], F32)
        wab = [k.sb(st, [128, 2, 2, 128], BF16) for _ in range(2)]
        xin = [k.sb(st, [128, 3 + NT], F32) for _ in range(2)]
        u = [k.sb(st, [128, NT], F32) for _ in range(2)]
        ub = [k.sb(st, [128, NT], BF16) for _ in range(2)]
        rr, ii, aa, a2, mm, bt, gt, sg = [k.sb(st, [128, NT], F32) for _ in range(8)]
        hs = [[k.sb(st, [128, NT], F32) for _ in range(2)] for _ in range(2)]
        yo = [k.sb(st, [128, NT], BF16) for _ in range(2)]
        psr, psi = k.ps(st, [128, NT]), k.ps(st, [128, NT])
        b_waf, b_psr, b_psi, b_rr, b_ii, b_aa, b_a2, b_mm, b_bt, b_gt, b_sg = bufs(11)
        b_wab, b_xin, b_u, b_ub, b_yo = bufs(2), bufs(2), bufs(2), bufs(2), bufs(2)
        b_hs = [bufs(2), bufs(2)]
        for nb in range(NBLK):
            for wi, nm in enumerate(("wa", "wx")):
                s.dma("sp", waf[:], prm[nm][nb].rearrange("o p c m -> p o c m"), writes=[b_waf])
                s.op("act", lambda e: e.copy(out=wab[wi][:], in_=waf[:]), reads=[b_waf], writes=[b_wab[wi]])
            for ti, t0 in enumerate(range(0, T, NT)):
                par = ti % 2
                for kc in range(2):
                    ch = nb * 2 + kc
                    if ti == 0:
                        s.op("pool", lambda e: e.memset(xin[kc][:, 0:3], 0.0), writes=[b_xin[kc]])
                    else:
                        s.op("pool", lambda e: e.tensor_copy(out=xin[kc][:, 0:3], in_=xin[kc][:, NT:NT + 3]),
                             reads=[b_xin[kc]], writes=[b_xin[kc]])
                    s.dma("sp", xin[kc][:, 3:3 + NT], PT[row_ax + ch * 128: row_ax + (ch + 1) * 128, t0:t0 + NT],
                          reads=[b_xin[kc]], writes=[b_xin[kc]])
                    s.op("dve", lambda e: e.tensor_scalar(out=u[kc][:], in0=xin[kc][:, 3:3 + NT], scalar1=cw[:, ch, 3:4],
                                                          scalar2=cb[:, ch:ch + 1], op0=ALU.mult, op1=ALU.add),
                         reads=[b_xin[kc], b_prm], writes=[b_u[kc]])
                    for j in (2, 1, 0):
                        s.op("dve", lambda e: e.scalar_tensor_tensor(out=u[kc][:], in0=xin[kc][:, j:j + NT], scalar=cw[:, ch, j:j + 1],
                                                                     in1=u[kc][:], op0=ALU.mult, op1=ALU.add),
                             reads=[b_xin[kc], b_prm, b_u[kc]], writes=[b_u[kc]])
                    s.op("act", lambda e: e.copy(out=ub[kc][:], in_=u[kc][:]), reads=[b_u[kc]], writes=[b_ub[kc]])
                for oc in range(2):
                    ch = nb * 2 + oc
                    for kc in range(2):
                        s.op("pe", lambda e: e.matmul(psr[:], lhsT=wab[0][:, oc, kc, :], rhs=ub[kc][:], start=(kc == 0), stop=(kc == 1)),
                             reads=[b_wab[0], b_ub[kc]], writes=[b_psr], inc=(kc == 1))
                    for kc in range(2):
                        s.op("pe", lambda e: e.matmul(psi[:], lhsT=wab[1][:, oc, kc, :], rhs=ub[kc][:], start=(kc == 0), stop=(kc == 1)),
                             reads=[b_wab[1], b_ub[kc]], writes=[b_psi], inc=(kc == 1))
                    s.op("act", lambda e: e.activation(out=rr[:], in_=psr[:], func=AF.Sigmoid, bias=ba[:, ch:ch + 1]),
                         reads=[b_psr, b_prm], writes=[b_rr])
                    s.op("act", lambda e: e.activation(out=ii[:], in_=psi[:], func=AF.Sigmoid, bias=bx[:, ch:ch + 1]),
                         reads=[b_psi, b_prm], writes=[b_ii])
                    s.op("act", lambda e: e.activation(out=aa[:], in_=rr[:], func=AF.Exp, scale=c1[:, ch:ch + 1]),
                         reads=[b_rr, b_prm], writes=[b_aa])
                    s.op("act", lambda e: e.activation(out=a2[:], in_=rr[:], func=AF.Exp, scale=c2[:, ch:ch + 1]),
                         reads=[b_rr, b_prm], writes=[b_a2])
                    s.op("act", lambda e: e.activation(out=mm[:], in_=a2[:], func=AF.Sqrt, scale=-1.0, bias=1.0),
                         reads=[b_a2], writes=[b_mm])
                    s.op("dve", lambda e: e.tensor_tensor(out=bt[:], in0=mm[:], in1=ii[:], op=ALU.mult),
                         reads=[b_mm, b_ii], writes=[b_bt])
                    s.op("dve", lambda e: e.tensor_tensor(out=bt[:], in0=bt[:], in1=u[oc][:], op=ALU.mult),
                         reads=[b_bt, b_u[oc]], writes=[b_bt])
                    init = 0.0 if ti == 0 else hs[oc][1 - par][:, NT - 1:NT]
                    s.op("dve", lambda e: e.tensor_tensor_scan(out=hs[oc][par][:], data0=aa[:], data1=bt[:], initial=init,
                                                               op0=ALU.mult, op1=ALU.add),
                         reads=[b_aa, b_bt] + ([] if ti == 0 else [b_hs[oc][1 - par]]), writes=[b_hs[oc][par]])
                    s.dma("act", gt[:], PT[row_ag + ch * 128: row_ag + (ch + 1) * 128, t0:t0 + NT], writes=[b_gt])
                    s.op("act", lambda e: e.activation(out=sg[:], in_=gt[:], func=AF.Silu), reads=[b_gt], writes=[b_sg])
                    s.op("dve", lambda e: e.tensor_tensor(out=yo[oc][:], in0=hs[oc][par][:], in1=sg[:], op=ALU.mult),
                         reads=[b_hs[oc][par], b_sg], writes=[b_yo[oc]])
                    s.dma("sp", yT[row_y + ch * 128: row_y + (ch + 1) * 128, t0:t0 + NT], yo[oc][:], reads=[b_yo[oc]])
        s.barrier()


def stage_xattn(k, PT, row_q, row_g, yT, row_y, memnT, Wk, Wv, ident_d, T, D, M, H):
    s = k.s
    KC = D // 128
    MC = M // 128
    sc = 128 ** -0.5
    with contextlib.ExitStack() as st:
        ident = k.sb(st, [128, 128], F32)
        memn = k.sb(st, [128, KC, M], BF16)
        kT = k.sb(st, [128, H, M], BF16)
        vt = k.sb(st, [128, MC, H * 128], BF16)
        b_id, b_memn, b_kT, b_vt = bufs(4)
        s.dma("sp", ident[:], ident_d, writes=[b_id])
        dma_rows(s, "sp", memn, memnT, KC, writes=[b_memn])
        with contextlib.ExitStack() as st1:
            wf = k.sb(st1, [128, KC, 128], F32)
            wb = k.sb(st1, [128, KC, 128], BF16)
            vf = [k.sb(st1, [128, H * 128], F32) for _ in range(2)]
            vb = [k.sb(st1, [128, H * 128], BF16) for _ in range(2)]
            psk = k.ps(st1, [128, M])
            psv = [k.ps(st1, [128, H * 128]) for _ in range(MC)]
            b_wf, b_wb, b_psk = bufs(3)
            b_vf, b_vb, b_psv = bufs(2), bufs(2), bufs(MC)
            for hd in range(H):
                s.dma("sp", wf[:], Wk[hd], writes=[b_wf])
                s.op("act", lambda e: e.copy(out=wb[:], in_=wf[:]), reads=[b_wf], writes=[b_wb])
                for c in range(KC):
                    s.op("pe", lambda e: e.matmul(psk[:], lhsT=wb[:, c, :], rhs=memn[:, c, :], start=(c == 0), stop=(c == KC - 1)),
                         reads=[b_wb, b_memn], writes=[b_psk], inc=(c == KC - 1))
                s.op("dve", lambda e: e.tensor_copy(out=kT[:, hd, :], in_=psk[:]), reads=[b_psk], writes=[b_kT])
            for c in range(KC):
                i = c % 2
                s.dma("sp", vf[i][:], Wv[c], writes=[b_vf[i]])
                s.op("act", lambda e: e.copy(out=vb[i][:], in_=vf[i][:]), reads=[b_vf[i]], writes=[b_vb[i]])
                for mc in range(MC):
                    s.op("pe", lambda e: e.matmul(psv[mc][:], lhsT=memn[:, c, mc * 128:(mc + 1) * 128], rhs=vb[i][:],
                                                 start=(c == 0), stop=(c == KC - 1)),
                         reads=[b_vb[i], b_memn], writes=[b_psv[mc]], inc=True)
            for mc in range(MC):
                s.op("dve", lambda e: e.tensor_copy(out=vt[:, mc, :], in_=psv[mc][:]), reads=[b_psv[mc]], writes=[b_vt])
            s.barrier()
        qf = k.sb(st, [128, H, NT], F32)
        qb = k.sb(st, [128, H, NT], BF16)
        gf = k.sb(st, [128, H, NT], F32)
        sg = k.sb(st, [128, H, NT], F32)
        pf = [k.sb(st, [128, M], F32) for _ in range(2)]
        pn = [k.sb(st, [128, M], F32) for _ in range(2)]
        pT = [k.sb(st, [128, MC, 128], BF16) for _ in range(2)]
        mx, nmx, rsum, rinv = [[k.sb(st, [128, 1], F32) for _ in range(2)] for _ in range(4)]
        yo = [k.sb(st, [128, NT], BF16) for _ in range(2)]
        pss = [k.ps(st, [128, M]) for _ in range(2)]
        pst = [k.ps(st, [128, MC, 128]) for _ in range(2)]
        pso = [k.ps(st, [128, 128]) for _ in range(2)]
        b_qf, b_qb, b_gf, b_sg = bufs(4)
        b_pf, b_pn, b_pT, b_mx, b_nmx, b_rsum, b_rinv, b_yo, b_pss, b_pst, b_pso = [bufs(2) for _ in range(11)]
        it = 0
        for t0 in range(0, T, NT):
            s.dma("sp", qf[:], PT[row_q:row_q + H * 128, t0:t0 + NT].rearrange("(h p) t -> p h t", p=128), writes=[b_qf])
            s.dma("act", gf[:], PT[row_g:row_g + H * 128, t0:t0 + NT].rearrange("(h p) t -> p h t", p=128), writes=[b_gf])
            s.op("act", lambda e: e.copy(out=qb[:], in_=qf[:]), reads=[b_qf], writes=[b_qb])
            s.op("act", lambda e: e.activation(out=sg[:], in_=gf[:], func=AF.Silu), reads=[b_gf], writes=[b_sg])
            for hd in range(H):
                yi = hd % 2
                for tb in range(NT // 128):
                    i = it % 2
                    it += 1
                    tsl = slice(tb * 128, (tb + 1) * 128)
                    s.op("pe", lambda e: e.matmul(pss[i][:], lhsT=qb[:, hd, tsl], rhs=kT[:, hd, :], start=True, stop=True),
                         reads=[b_qb, b_kT], writes=[b_pss[i]])
                    s.op("dve", lambda e: e.tensor_reduce(out=mx[i][:], in_=pss[i][:], axis=AX.X, op=ALU.max),
                         reads=[b_pss[i]], writes=[b_mx[i]])
                    s.op("dve", lambda e: e.tensor_scalar(out=nmx[i][:], in0=mx[i][:], scalar1=-sc, scalar2=None, op0=ALU.mult),
                         reads=[b_mx[i]], writes=[b_nmx[i]])
                    s.op("act", lambda e: e.activation(out=pf[i][:], in_=pss[i][:], func=AF.Exp, scale=sc, bias=nmx[i][:],
                                                       accum_out=rsum[i][:]),
                         reads=[b_pss[i], b_nmx[i]], writes=[b_pf[i], b_rsum[i]])
                    s.op("dve", lambda e: e.reciprocal(out=rinv[i][:], in_=rsum[i][:]), reads=[b_rsum[i]], writes=[b_rinv[i]])
                    s.op("dve", lambda e: e.tensor_scalar(out=pn[i][:], in0=pf[i][:], scalar1=rinv[i][:], scalar2=None, op0=ALU.mult),
                         reads=[b_pf[i], b_rinv[i]], writes=[b_pn[i]])
                    for mc in range(MC):
                        s.op("pe", lambda e: e.transpose(pst[i][:, mc, :], pn[i][:, mc * 128:(mc + 1) * 128], ident[:]),
                             reads=[b_pn[i], b_id], writes=[b_pst[i]], inc=True)
                    s.op("act", lambda e: e.copy(out=pT[i][:], in_=pst[i][:]), reads=[b_pst[i]], writes=[b_pT[i]])
                    for mc in range(MC):
                        s.op("pe", lambda e: e.matmul(pso[i][:], lhsT=vt[:, mc, hd * 128:(hd + 1) * 128], rhs=pT[i][:, mc, :],
                                                     start=(mc == 0), stop=(mc == MC - 1)),
                             reads=[b_vt, b_pT[i]], writes=[b_pso[i]], inc=(mc == MC - 1))
                    s.op("dve", lambda e: e.tensor_tensor(out=yo[yi][:, tsl], in0=pso[i][:], in1=sg[:, hd, tsl], op=ALU.mult),
                         reads=[b_pso[i], b_sg], writes=[b_yo[yi]])
                s.dma("sp", yT[row_y + hd * 128: row_y + (hd + 1) * 128, t0:t0 + NT], yo[yi][:], reads=[b_yo[yi]])
        s.barrier()


LCH = 64
MLSTM_L = 128
MLSTM_FP32R = True
NEG = -1.0e30
RWKV_G = 2
RWKV_L = 128
RWKV_FP32R = True


def stage_mlstm(k, PT, rows, yT, row_y, prm, cst, T, HC):
    s = k.s
    ML = MLSTM_L
    NCK = T // ML
    QC = 2 * HC
    with contextlib.ExitStack() as st:
        ident = k.sb(st, [128, 128], F32)
        sel = k.sb(st, [HC, HC, 128], F32)
        i4 = k.sb(st, [HC, HC], F32)
        negmask = k.sb(st, [ML, ML], F32)
        ones = k.sb(st, [128, 128], F32)
        cw = k.sb(st, [128, 2 * QC, 4], F32)
        cb = k.sb(st, [128, 2 * QC], F32)
        gnw = k.sb(st, [128, HC * 4], F32)
        bi = k.sb(st, [HC, 1], F32)
        bfn = k.sb(st, [HC, 1], F32)
        b_c = Buf()
        for dst, src in ((ident, cst["ident"]), (sel, cst["sel"]), (i4, cst["i4"]), (negmask, cst["negmask"]),
                         (cw, prm["conv_w"]), (cb, prm["conv_b"]), (gnw, prm["gn_w"]), (bi, prm["b_i"]), (bfn, prm["b_f"])):
            s.dma("sp", dst[:], src, writes=[b_c])
        s.op("pool", lambda e: e.memset(ones[:], 1.0), writes=[b_c])
        s.op("dve", lambda e: e.tensor_scalar(out=bfn[:], in0=bfn[:], scalar1=-1.0, scalar2=None, op0=ALU.mult), reads=[b_c], writes=[b_c])
        gi = k.sb(st, [HC, T], F32)
        gf = k.sb(st, [HC, T], F32)
        gm = k.sb(st, [HC, T], F32)
        ge = k.sb(st, [HC, T], F32)
        b_g = Buf()
        s.dma("sp", gi[:], PT[rows["ifg"]:rows["ifg"] + HC, :], writes=[b_g])
        s.dma("sp", gf[:], PT[rows["ifg"] + HC:rows["ifg"] + 2 * HC, :], writes=[b_g])
        s.op("act", lambda e: e.activation(out=gf[:], in_=gf[:], func=AF.Exp, scale=-1.0, bias=bfn[:]), reads=[b_g, b_c], writes=[b_g])
        s.op("act", lambda e: e.activation(out=gf[:], in_=gf[:], func=AF.Ln, bias=1.0), reads=[b_g], writes=[b_g])
        s.op("dve", lambda e: e.tensor_scalar(out=gf[:], in0=gf[:], scalar1=-1.0, scalar2=None, op0=ALU.mult), reads=[b_g], writes=[b_g])
        s.op("pool", lambda e: e.memset(gm[:], 1.0), writes=[b_g])
        s.op("dve", lambda e: e.tensor_tensor_scan(out=gf[:], data0=gm[:], data1=gf[:], initial=0.0, op0=ALU.mult, op1=ALU.add),
             reads=[b_g], writes=[b_g])
        s.op("dve", lambda e: e.scalar_tensor_tensor(out=gi[:], in0=gi[:], scalar=bi[:], in1=gf[:], op0=ALU.add, op1=ALU.subtract),
             reads=[b_g, b_c], writes=[b_g])
        s.op("dve", lambda e: e.tensor_tensor_scan(out=gm[:], data0=gi[:], data1=gi[:], initial=NEG, op0=ALU.max, op1=ALU.max),
             reads=[b_g], writes=[b_g])
        s.op("dve", lambda e: e.tensor_tensor(out=ge[:], in0=gf[:], in1=gm[:], op=ALU.add), reads=[b_g], writes=[b_g])
        cols = k.sb(st, [ML, NCK, 2 * HC], F32)
        b_cols = Buf()
        with contextlib.ExitStack() as st1:
            psc = [k.ps(st1, [ML, 8, 2 * HC]) for _ in range(2)]
            b_psc = bufs(2)
            for c8 in range(0, NCK, 8):
                i = (c8 // 8) % 2
                for cc in range(8):
                    c = c8 + cc
                    s.op("pe", lambda e: e.matmul(psc[i][:, cc, 0:HC], lhsT=gi[:, c * ML:(c + 1) * ML], rhs=i4[:, 0:HC], start=True, stop=True),
                         reads=[b_g, b_c], writes=[b_psc[i]])
                    s.op("pe", lambda e: e.matmul(psc[i][:, cc, HC:2 * HC], lhsT=ge[:, c * ML:(c + 1) * ML], rhs=i4[:, 0:HC], start=True, stop=True),
                         reads=[b_g, b_c], writes=[b_psc[i]])
                s.op("dve", lambda e: e.tensor_copy(out=cols[:, c8:c8 + 8, :], in_=psc[i][:]), reads=[b_psc[i]], writes=[b_cols])
            s.op("act", lambda e: e.activation(out=cols[:, :, HC:2 * HC], in_=cols[:, :, HC:2 * HC], func=AF.Exp, scale=-1.0),
                 reads=[b_cols], writes=[b_cols])
            s.barrier()
        xin = k.sb(st, [128, 4, 3 + NT], F32)
        qk = k.sb(st, [128, 4, NT], F32)
        vT = k.sb(st, [128, 4, NT], F32)
        oT = k.sb(st, [128, 4, NT], F32)
        gT = k.sb(st, [128, 4, NT], F32)
        negMb = k.sb(st, [128, NT], F32)
        ms = k.sb(st, [128, NT // ML + 1], F32)
        keep2 = [k.sb(st, [128, 1], F32) for _ in range(2)]
        b_keep2 = bufs(2)
        ho = k.sb(st, [128, 4, NT], F32)
        Cst = k.sb(st, [128, 2, 512], F32)
        Cr = k.sb(st, [128, 2, 512], F32)
        b_Cr = Buf()
        R = (lambda ap: ap.bitcast(mybir.dt.float32r)) if MLSTM_FP32R else (lambda ap: ap)
        nst = k.sb(st, [128, 2], F32)
        ktok2 = [k.sb(st, [ML, 256], F32) for _ in range(2)]
        b_ktok2 = bufs(2)
        khat2 = [k.sb(st, [ML, 256], F32) for _ in range(2)]
        b_khat2 = bufs(2)
        vtok2 = [k.sb(st, [ML, 512], F32) for _ in range(2)]
        b_vtok2 = bufs(2)
        wk2 = [k.sb(st, [ML, 1], F32) for _ in range(2)]
        b_wk2 = bufs(2)
        argE = k.sb(st, [ML, ML], F32)
        ET = k.sb(st, [ML, ML], F32)
        scT2 = [k.sb(st, [ML, ML], F32) for _ in range(2)]
        b_scT2 = bufs(2)
        wint = k.sb(st, [128, ML], F32)
        qt2 = [k.sb(st, [128, 2, ML], F32) for _ in range(2)]
        b_qt2 = bufs(2)
        den = k.sb(st, [ML, 1], F32)
        rden = k.sb(st, [ML, 1], F32)
        htok = k.sb(st, [ML, 512], F32)
        mean = k.sb(st, [128, NT], F32)
        yc = k.sb(st, [128, 4, NT], F32)
        ysq = k.sb(st, [128, NT], F32)
        rstd = k.sb(st, [128, NT], F32)
        yo = [k.sb(st, [128, NT], BF16) for _ in range(2)]
        ps_b = k.ps(st, [128, NT])
        ps_t = k.ps(st, [ML, 512])
        ps_s = k.ps(st, [ML, ML])
        ps_n = k.ps(st, [ML, 512])
        ps_d = k.ps(st, [ML, 8])
        ps_c = k.ps(st, [128, 512])
        ps_h = k.ps(st, [128, 4, ML])
        ps_m = k.ps(st, [128, 8])
        (b_xin, b_qk, b_vT, b_oT, b_gT, b_negMb, b_ms, b_keep, b_ho, b_C, b_n, b_ktok, b_khat, b_vtok, b_wk, b_argE, b_ET, b_scT,
         b_wint, b_qt, b_den, b_rden, b_htok, b_mean, b_yc, b_ysq, b_rstd, b_psb, b_pst, b_pss, b_psn, b_psd, b_psc2, b_psh, b_psm) = bufs(35)
        b_yo = bufs(2)
        for h in range(HC):
            qrows = [rows["q"] + h * 256, rows["q"] + h * 256 + 128, rows["k"] + h * 256, rows["k"] + h * 256 + 128]
            cidx = [h * 2, h * 2 + 1, QC + h * 2, QC + h * 2 + 1]
            s.op("pool", lambda e: e.memset(Cst[:], 0.0), reads=[b_C], writes=[b_C])
            s.op("act", lambda e: e.copy(out=R(Cr[:]), in_=Cst[:]), reads=[b_C, b_Cr], writes=[b_Cr])
            s.op("pool", lambda e: e.memset(nst[:], 0.0), reads=[b_n], writes=[b_n])
            s.op("pool", lambda e: e.memset(ms[:, 0:1], NEG), reads=[b_ms], writes=[b_ms])
            for ti, t0 in enumerate(range(0, T, NT)):
                for j in range(4):
                    if ti == 0:
                        s.op("pool", lambda e: e.memset(xin[:, j, 0:3], 0.0), reads=[b_xin], writes=[b_xin])
                    else:
                        s.op("pool", lambda e: e.tensor_copy(out=xin[:, j, 0:3], in_=xin[:, j, NT:NT + 3]), reads=[b_xin], writes=[b_xin])
                for j in range(4):
                    s.dma("sp", xin[:, j, 3:3 + NT], PT[qrows[j]:qrows[j] + 128, t0:t0 + NT], reads=[b_xin], writes=[b_xin])
                s.dma("act", vT[:], PT[rows["v"] + h * 512: rows["v"] + (h + 1) * 512, t0:t0 + NT].rearrange("(c p) t -> p c t", p=128), writes=[b_vT])
                s.dma("act", oT[:], PT[rows["o"] + h * 512: rows["o"] + (h + 1) * 512, t0:t0 + NT].rearrange("(c p) t -> p c t", p=128), writes=[b_oT])
                s.dma("act", gT[:], PT[rows["g"] + h * 512: rows["g"] + (h + 1) * 512, t0:t0 + NT].rearrange("(c p) t -> p c t", p=128), writes=[b_gT])
                for j in range(4):
                    ci = cidx[j]
                    s.op("dve", lambda e: e.tensor_scalar(out=qk[:, j, :], in0=xin[:, j, 3:3 + NT], scalar1=cw[:, ci, 3:4],
                                                          scalar2=cb[:, ci:ci + 1], op0=ALU.mult, op1=ALU.add),
                         reads=[b_xin, b_c, b_qk], writes=[b_qk])
                    for jj in (2, 1, 0):
                        s.op("dve", lambda e: e.scalar_tensor_tensor(out=qk[:, j, :], in0=xin[:, j, jj:jj + NT], scalar=cw[:, ci, jj:jj + 1],
                                                                     in1=qk[:, j, :], op0=ALU.mult, op1=ALU.add),
                             reads=[b_xin, b_c, b_qk], writes=[b_qk])
                s.op("act", lambda e: e.activation(out=qk[:], in_=qk[:], func=AF.Silu), reads=[b_qk], writes=[b_qk])
                s.op("dve", lambda e: e.tensor_scalar(out=qk[:, 2:4, :], in0=qk[:, 2:4, :], scalar1=256 ** -0.5, scalar2=None, op0=ALU.mult),
                     reads=[b_qk], writes=[b_qk])
                s.op("act", lambda e: e.activation(out=oT[:], in_=oT[:], func=AF.Sigmoid), reads=[b_oT], writes=[b_oT])
                s.op("act", lambda e: e.activation(out=gT[:], in_=gT[:], func=AF.Silu), reads=[b_gT], writes=[b_gT])
                s.op("pe", lambda e: e.matmul(ps_b[:], lhsT=sel[:, h, :], rhs=gm[:, t0:t0 + NT], start=True, stop=True),
                     reads=[b_c, b_g, b_psb], writes=[b_psb])
                if ti > 0:
                    s.op("dve", lambda e: e.tensor_scalar(out=ms[:, 0:1], in0=negMb[:, NT - 1:NT], scalar1=-1.0, scalar2=None, op0=ALU.mult),
                         reads=[b_negMb, b_ms], writes=[b_ms])
                s.op("act", lambda e: e.activation(out=negMb[:], in_=ps_b[:], func=AF.Copy, scale=-1.0), reads=[b_psb, b_negMb], writes=[b_negMb])
                nck = NT // ML
                s.op("dve", lambda e: e.tensor_scalar(out=ms[:, 1:nck + 1], in0=negMb[:, ML - 1:NT:ML], scalar1=-1.0, scalar2=None, op0=ALU.mult),
                     reads=[b_negMb, b_ms], writes=[b_ms])
                def m_indep(cc):
                    c = ti * nck + cc
                    cp = c % 2
                    csl = slice(cc * ML, (cc + 1) * ML)
                    ktok, khat, vtok, scT, qt, keep, wk = ktok2[cp], khat2[cp], vtok2[cp], scT2[cp], qt2[cp], keep2[cp], wk2[cp]
                    b_ktok, b_khat, b_vtok, b_scT, b_qt, b_keep, b_wk = (b_ktok2[cp], b_khat2[cp], b_vtok2[cp], b_scT2[cp], b_qt2[cp],
                                                                          b_keep2[cp], b_wk2[cp])
                    def t1():
                        for j in range(2):
                            s.op("pe", lambda e: e.transpose(ps_t[:, j * 128:(j + 1) * 128], qk[:, 2 + j, csl], ident[:]),
                                 reads=[b_qk, b_c, b_pst], writes=[b_pst])
                        s.op("act", lambda e: e.copy(out=ktok[:], in_=ps_t[:, 0:256]), reads=[b_pst, b_ktok], writes=[b_ktok])
                        for j in range(4):
                            s.op("pe", lambda e: e.transpose(ps_t[:, j * 128:(j + 1) * 128], vT[:, j, csl], ident[:]),
                                 reads=[b_vT, b_c, b_pst], writes=[b_pst])
                        s.op("act", lambda e: e.copy(out=R(vtok[:]), in_=ps_t[:]), reads=[b_pst, b_vtok], writes=[b_vtok])
                    def t2():
                        for j in range(2):
                            s.op("pe", lambda e: e.matmul(ps_s[:], lhsT=qk[:, 2 + j, csl], rhs=qk[:, j, csl], start=(j == 0), stop=(j == 1)),
                                 reads=[b_qk, b_pss], writes=[b_pss])
                        s.op("dve", lambda e: e.tensor_tensor(out=argE[:], in0=negMb[0:ML, csl], in1=negmask[:], op=ALU.add),
                             reads=[b_negMb, b_c, b_argE], writes=[b_argE])
                        s.op("act", lambda e: e.activation(out=ET[:], in_=argE[:], func=AF.Exp, bias=cols[:, c, h:h + 1]),
                             reads=[b_argE, b_cols, b_ET], writes=[b_ET])
                        s.op("dve", lambda e: e.tensor_tensor(out=R(scT[:]), in0=ps_s[:], in1=ET[:], op=ALU.mult),
                             reads=[b_pss, b_ET, b_scT], writes=[b_scT])
                    def t3():
                        s.op("act", lambda e: e.activation(out=wint[:], in_=negMb[:, csl], func=AF.Exp, bias=ms[:, cc:cc + 1]),
                             reads=[b_negMb, b_ms, b_wint], writes=[b_wint])
                        for j in range(2):
                            s.op("dve", lambda e: e.tensor_tensor(out=R(qt[:, j, :]), in0=qk[:, j, csl], in1=wint[:], op=ALU.mult),
                                 reads=[b_qk, b_wint, b_qt], writes=[b_qt])
                    def t4():
                        s.op("act", lambda e: e.activation(out=keep[:], in_=negMb[:, (cc + 1) * ML - 1:(cc + 1) * ML], func=AF.Exp, bias=ms[:, cc:cc + 1]),
                             reads=[b_negMb, b_ms, b_keep], writes=[b_keep])
                        s.op("act", lambda e: e.activation(out=wk[:], in_=negMb[0:ML, (cc + 1) * ML - 1:(cc + 1) * ML], func=AF.Exp, bias=cols[:, c, h:h + 1]),
                             reads=[b_negMb, b_cols, b_wk], writes=[b_wk])
                        s.op("dve", lambda e: e.tensor_scalar(out=R(khat[:]), in0=ktok[:], scalar1=wk[:], scalar2=None, op0=ALU.mult),
                             reads=[b_ktok, b_wk, b_khat], writes=[b_khat])
                    return [t1, t2, t3, t4]

                def m_dep(cc):
                    c = ti * nck + cc
                    cp = c % 2
                    csl = slice(cc * ML, (cc + 1) * ML)
                    ktok, khat, vtok, scT, qt, keep, wk = ktok2[cp], khat2[cp], vtok2[cp], scT2[cp], qt2[cp], keep2[cp], wk2[cp]
                    b_ktok, b_khat, b_vtok, b_scT, b_qt, b_keep, b_wk = (b_ktok2[cp], b_khat2[cp], b_vtok2[cp], b_scT2[cp], b_qt2[cp],
                                                                          b_keep2[cp], b_wk2[cp])
                    def d1():
                        for j in range(2):
                            s.op("pe", lambda e: e.matmul(ps_n[:], lhsT=R(qt[:, j, :]), rhs=R(Cr[:, j, :]), start=(j == 0), stop=False),
                                 reads=[b_qt, b_Cr, b_psn], writes=[b_psn], inc=False)
                        s.op("pe", lambda e: e.matmul(ps_n[:], lhsT=R(scT[:]), rhs=R(vtok[:]), start=False, stop=True),
                             reads=[b_scT, b_vtok, b_psn], writes=[b_psn])
                        for j in range(2):
                            s.op("pe", lambda e: e.matmul(ps_d[:, 0:1], lhsT=qt[:, j, :], rhs=nst[:, j:j + 1], start=(j == 0), stop=False),
                                 reads=[b_qt, b_n, b_psd], writes=[b_psd], inc=False)
                        s.op("pe", lambda e: e.matmul(ps_d[:, 0:1], lhsT=scT[:], rhs=ones[0:ML, 0:1], start=False, stop=True),
                             reads=[b_scT, b_c, b_psd], writes=[b_psd])
                    def d2():
                        s.op("act", lambda e: e.activation(out=den[:], in_=ps_d[:, 0:1], func=AF.Abs),
                             reads=[b_psd, b_den], writes=[b_den])
                        s.op("dve", lambda e: e.tensor_tensor(out=den[:], in0=den[:], in1=cols[:, c, HC + h:HC + h + 1], op=ALU.max),
                             reads=[b_den, b_cols], writes=[b_den])
                        s.op("dve", lambda e: e.reciprocal(out=rden[:], in_=den[:]), reads=[b_den, b_rden], writes=[b_rden])
                        s.op("act", lambda e: e.activation(out=htok[:], in_=ps_n[:], func=AF.Copy, scale=rden[:]),
                             reads=[b_psn, b_rden, b_htok], writes=[b_htok])
                    def d3():
                        for j in range(4):
                            s.op("pe", lambda e: e.transpose(ps_h[:, j, :], htok[:, j * 128:(j + 1) * 128], ident[0:ML, 0:ML]),
                                 reads=[b_htok, b_c, b_psh], writes=[b_psh])
                        s.op("dve", lambda e: e.tensor_tensor(out=ho[:, :, csl], in0=ps_h[:], in1=oT[:, :, csl], op=ALU.mult),
                             reads=[b_psh, b_oT, b_ho], writes=[b_ho])
                    def d4():
                        for j in range(2):
                            s.op("pe", lambda e: e.matmul(ps_c[:], lhsT=R(khat[:, j * 128:(j + 1) * 128]), rhs=R(vtok[:]), start=True, stop=True),
                                 reads=[b_khat, b_vtok, b_psc2], writes=[b_psc2])
                            s.op("dve", lambda e: e.scalar_tensor_tensor(out=Cst[:, j, :], in0=Cst[:, j, :], scalar=keep[:], in1=ps_c[:],
                                                                         op0=ALU.mult, op1=ALU.add),
                                 reads=[b_C, b_keep, b_psc2], writes=[b_C])
                            s.op("act", lambda e: e.copy(out=R(Cr[:, j, :]), in_=Cst[:, j, :]), reads=[b_C, b_Cr], writes=[b_Cr])
                            s.op("pe", lambda e: e.matmul(ps_m[:, 0:1], lhsT=khat[:, j * 128:(j + 1) * 128], rhs=ones[0:ML, 0:1], start=True, stop=True),
                                 reads=[b_khat, b_c, b_psm], writes=[b_psm])
                            s.op("dve", lambda e: e.scalar_tensor_tensor(out=nst[:, j:j + 1], in0=nst[:, j:j + 1], scalar=keep[:], in1=ps_m[:, 0:1],
                                                                         op0=ALU.mult, op1=ALU.add),
                                 reads=[b_n, b_keep, b_psm], writes=[b_n])
                    return [d1, d2, d3, d4]

                for f_ in m_indep(0):
                    f_()
                for cc in range(nck):
                    dst_ = m_dep(cc)
                    ist_ = m_indep(cc + 1) if cc + 1 < nck else []
                    for j_ in range(max(len(dst_), len(ist_))):
                        if j_ < len(dst_):
                            dst_[j_]()
                        if j_ < len(ist_):
                            ist_[j_]()
                for j in range(4):
                    s.op("pe", lambda e: e.matmul(ps_b[:], lhsT=ones[:], rhs=ho[:, j, :], start=(j == 0), stop=(j == 3)),
                         reads=[b_c, b_ho, b_psb], writes=[b_psb], inc=(j == 3))
                s.op("act", lambda e: e.activation(out=mean[:], in_=ps_b[:], func=AF.Copy, scale=1.0 / 512), reads=[b_psb, b_mean], writes=[b_mean])
                for j in range(4):
                    s.op("dve", lambda e: e.tensor_tensor(out=yc[:, j, :], in0=ho[:, j, :], in1=mean[:], op=ALU.subtract),
                         reads=[b_ho, b_mean, b_yc], writes=[b_yc])
                for j in range(4):
                    s.op("act", lambda e: e.activation(out=ysq[:], in_=yc[:, j, :], func=AF.Square), reads=[b_yc, b_ysq], writes=[b_ysq])
                    s.op("pe", lambda e: e.matmul(ps_b[:], lhsT=ones[:], rhs=ysq[:], start=(j == 0), stop=(j == 3)),
                         reads=[b_c, b_ysq, b_psb], writes=[b_psb])
                s.op("act", lambda e: e.activation(out=rstd[:], in_=ps_b[:], func=AF.Sqrt, scale=1.0 / 512, bias=1e-6),
                     reads=[b_psb, b_rstd], writes=[b_rstd])
                s.op("dve", lambda e: e.reciprocal(out=rstd[:], in_=rstd[:]), reads=[b_rstd], writes=[b_rstd])
                for j in range(4):
                    yi = j % 2
                    s.op("dve", lambda e: e.scalar_tensor_tensor(out=yc[:, j, :], in0=yc[:, j, :], scalar=gnw[:, h * 4 + j:h * 4 + j + 1],
                                                                 in1=rstd[:], op0=ALU.mult, op1=ALU.mult),
                         reads=[b_yc, b_c, b_rstd], writes=[b_yc])
                    s.op("dve", lambda e: e.tensor_tensor(out=yo[yi][:], in0=yc[:, j, :], in1=gT[:, j, :], op=ALU.mult),
                         reads=[b_yc, b_gT, b_yo[yi]], writes=[b_yo[yi]])
                    r0 = row_y + h * 512 + j * 128
                    s.dma("sp", yT[r0:r0 + 128, t0:t0 + NT], yo[yi][:], reads=[b_yo[yi]])
        s.barrier()


def stage_rwkv(k, PT, rows, yT, row_y, prm, cst, T, NH, GN_EPS):
    s = k.s
    G = RWKV_G
    assert G == 2
    NLVL = {64: 6, 128: 7}[RWKV_L]
    RL = RWKV_L
    NCK = NT // RL
    assert NH % G == 0 and T % NT == 0
    with contextlib.ExitStack() as st:
        ident = k.sb(st, [128, 128], F32)
        ones64 = k.sb(st, [64, 64], F32)
        rmask = k.sb(st, [64, NT], F32)
        maskA = k.sb(st, [RL, G, 2 * RL], F32)
        maskN = k.sb(st, [RL, G, RL], F32)
        P = {}
        for nm in ("mu_r", "mu_k", "mu_v", "w0", "a0", "k_k", "k_a", "r_k", "gn_w", "gn_b"):
            P[nm] = k.sb(st, [64, NH], F32)
        om = {nm: k.sb(st, [64, NH], F32) for nm in ("mu_r", "mu_k", "mu_v", "k_a")}
        nw0 = k.sb(st, [64, NH], F32)
        mud = {nm: k.sb(st, [96, 1], F32) for nm in ("mu_wd", "mu_ad")}
        omd = {nm: k.sb(st, [96, 1], F32) for nm in ("mu_wd", "mu_ad")}
        b_c = Buf()
        for dst, src in ((ident, cst["ident"]), (rmask, cst["resetmask"]), (maskA, cst["maskA"]), (maskN, cst["maskN"])):
            s.dma("sp", dst[:], src, writes=[b_c])
        for nm in P:
            s.dma("sp", P[nm][:], prm[nm], writes=[b_c])
        for nm in mud:
            s.dma("sp", mud[nm][:], prm[nm], writes=[b_c])
        s.op("pool", lambda e: e.memset(ones64[:], 1.0), writes=[b_c])
        zer = k.sb(st, [64, G, 64], F32)
        s.op("pool", lambda e: e.memset(zer[:], 0.0), writes=[b_c])
        for nm in om:
            s.op("dve", lambda e: e.tensor_scalar(out=om[nm][:], in0=P[nm][:], scalar1=-1.0, scalar2=1.0, op0=ALU.mult, op1=ALU.add),
                 reads=[b_c], writes=[b_c])
        for nm in omd:
            s.op("dve", lambda e: e.tensor_scalar(out=omd[nm][:], in0=mud[nm][:], scalar1=-1.0, scalar2=1.0, op0=ALU.mult, op1=ALU.add),
                 reads=[b_c], writes=[b_c])
        s.op("dve", lambda e: e.tensor_scalar(out=nw0[:], in0=P["w0"][:], scalar1=-1.0, scalar2=None, op0=ALU.mult), reads=[b_c], writes=[b_c])

        wup = k.sb(st, [96, G * 64], F32)
        aup = k.sb(st, [96, G * 64], F32)
        rawd = k.sb(st, [96, 1 + NT], F32)
        rawa = k.sb(st, [96, 1 + NT], F32)
        twd = k.sb(st, [96, NT], F32)
        adq = k.sb(st, [96, NT], F32)
        raw = {nm: k.sb(st, [64, G, 1 + NT], F32) for nm in ("r", "k", "v")}
        gate = None
        (r_, k_, e2, c_, eP, eN, a_, kk, kkn, kmod, kka, t1, t2) = [[k.sb(st, [64, NT], F32) for _ in range(G)] for _ in range(13)]
        yfm = k.sb(st, [64, G, NT], F32)
        dbl_tiles = [[k.sb(st, shp, F32) for _ in range(2)] for shp in
                     ([64, G, NT], [64, G, NCK, 2, RL], [64, G, NT], [64, G, NT], [64, G, NT], [64, G, NT], [64, G, NT], [64, G, NT], [64, G, NCK])]
        dbl_bufs = [bufs(2) for _ in range(9)]
        P1_PER_STEP = 2
        tokb = [k.sb(st, [RL, 3, G, 64], F32) for _ in range(2)]
        GAb = [k.sb(st, [RL, G, 2 * RL], F32) for _ in range(2)]
        GKb = [k.sb(st, [RL, G, 2 * RL], F32) for _ in range(2)]
        Pnb = [k.sb(st, [RL, G, RL], F32) for _ in range(2)]
        PPb = [[k.sb(st, [RL, G, 2, RL], F32) for _ in range(NLVL - 1)] for _ in range(2)]
        b_tokb, b_GAb, b_GKb, b_Pnb = bufs(2), bufs(2), bufs(2), bufs(2)
        b_PPb = [bufs(NLVL - 1), bufs(NLVL - 1)]
        U = k.sb(st, [RL, G, 64], F32)
        Ysb = k.sb(st, [RL, G, 64], F32)
        ST = k.sb(st, [64, G, 64], F32)
        yo = [k.sb(st, [64, NT], BF16) for _ in range(2)]
        bank = [k.ps(st, [128, 512]) for _ in range(8)]
        b_bank = bufs(8)
        ps_A, ps_B = bank[0], bank[1]
        ps_ga = bank[0][0:RL, 0:G * 2 * RL].rearrange("p (g x) -> p g x", g=G)
        ps_gk = bank[1][0:RL, 0:G * 2 * RL].rearrange("p (g x) -> p g x", g=G)
        ps_x1 = bank[2][0:RL, 0:2 * G * 64].rearrange("p (a g x) -> p a g x", a=2, g=G)
        ps_x2k = bank[3][0:RL, 0:G * 64].rearrange("p (g x) -> p g x", g=G)
        ps_gn = bank[3][0:RL, G * 64:G * 64 + G * RL].rearrange("p (g x) -> p g x", g=G)
        ps_z = bank[4][0:RL, 0:2 * G * 64].rearrange("p (a g x) -> p a g x", a=2, g=G)
        ps_pp = bank[5][0:RL, 0:G * 2 * RL].rearrange("p (g a x) -> p g a x", g=G, a=2)
        ps_y = bank[6][0:RL, 0:G * 64].rearrange("p (g x) -> p g x", g=G)
        ps_yT = bank[6][0:64, G * 64:G * 64 + G * RL].rearrange("p (g x) -> p g x", g=G)
        ps_s = bank[7][0:64, 0:G * 64].rearrange("p (g x) -> p g x", g=G)
        ps_ln = bank[7]
        (b_wup, b_rawd, b_rawa, b_twd, b_adq, b_gate,
         b_Vt, b_AR, b_Bt, b_Kt, b_Bh, b_Kh, b_bonus, b_yfm, b_gL, b_tok, b_GAs, b_GKs, b_Pn, b_U, b_Ysb, b_ST) = bufs(22)
        (b_r, b_k, b_e2, b_cc, b_eP, b_eN, b_a, b_kk, b_kkn, b_kmod, b_kka, b_t1, b_t2) = [bufs(G) for _ in range(13)]
        b_raw = {nm: Buf() for nm in raw}
        b_PP = bufs(2)
        b_yo = bufs(2)
        i64 = ident[0:64, 0:64]
        R = (lambda ap: ap.bitcast(mybir.dt.float32r)) if RWKV_FP32R else (lambda ap: ap)

        def shift(dst, src3, g, mu_t, om_t, H, reads, bdst):
            s.op("dve", lambda e: e.tensor_scalar(out=dst, in0=src3[:, g, 1:1 + NT], scalar1=om_t[:, H:H + 1], scalar2=None, op0=ALU.mult),
                 reads=reads + [b_c, bdst], writes=[bdst])
            s.op("dve", lambda e: e.scalar_tensor_tensor(out=dst, in0=src3[:, g, 0:NT], scalar=mu_t[:, H:H + 1], in1=dst,
                                                         op0=ALU.mult, op1=ALU.add),
                 reads=reads + [b_c, bdst], writes=[bdst])

        for h0 in range(0, NH, G):
            s.dma("sp", wup[:], prm["w_up"][:, h0 * 64:(h0 + G) * 64], reads=[b_wup], writes=[b_wup])
            s.dma("sp", aup[:], prm["a_up"][:, h0 * 64:(h0 + G) * 64], reads=[b_wup], writes=[b_wup])
            s.op("dve", lambda e: e.tensor_copy(out=R(ST[:]), in_=zer[:]), reads=[b_ST, b_c], writes=[b_ST])
            def make_tile(ti, t0):
                tp = ti % 2
                Vt, AR, Bt, Kt, Bh, Kh, bonus, gate, gL = (x[tp] for x in dbl_tiles)
                b_Vt, b_AR, b_Bt, b_Kt, b_Bh, b_Kh, b_bonus, b_gate, b_gL = (x[tp] for x in dbl_bufs)
                def head_gen(g):
                        H = h0 + g
                        hc = slice(g * 64, (g + 1) * 64)
                        shift(r_[g][:], raw["r"], g, P["mu_r"], om["mu_r"], H, [b_raw["r"]], b_r[g])
                        yield
                        shift(k_[g][:], raw["k"], g, P["mu_k"], om["mu_k"], H, [b_raw["k"]], b_k[g])
                        yield
                        shift(Vt[:, g, :], raw["v"], g, P["mu_v"], om["mu_v"], H, [b_raw["v"]], b_Vt)
                        yield
                        s.op("pe", lambda e: e.matmul(bank[g][0:64, :], lhsT=wup[:, hc], rhs=twd[:], start=True, stop=True),
                             reads=[b_wup, b_twd, b_bank[g]], writes=[b_bank[g]])
                        yield
                        s.op("act", lambda e: e.activation(out=t1[g][:], in_=bank[g][0:64, :], func=AF.Exp, scale=-1.0, bias=nw0[:, H:H + 1]),
                             reads=[b_bank[g], b_c, b_t1[g]], writes=[b_t1[g]])
                        yield
                        s.op("act", lambda e: e.activation(out=t1[g][:], in_=t1[g][:], func=AF.Ln, bias=1.0), reads=[b_t1[g]], writes=[b_t1[g]])
                        yield
                        s.op("act", lambda e: e.activation(out=e2[g][:], in_=t1[g][:], func=AF.Exp, scale=-1.0, bias=-0.5), reads=[b_t1[g], b_e2[g]], writes=[b_e2[g]])
                        yield
                        s.op("dve", lambda e: e.tensor_tensor_scan(out=c_[g][:], data0=rmask[:], data1=e2[g][:], initial=0.0, op0=ALU.mult, op1=ALU.add),
                             reads=[b_c, b_e2[g], b_cc[g]], writes=[b_cc[g]])
                        yield
                        s.op("act", lambda e: e.activation(out=eP[g][:], in_=c_[g][:], func=AF.Exp, scale=-1.0), reads=[b_cc[g], b_eP[g]], writes=[b_eP[g]])
                        yield
                        s.op("act", lambda e: e.activation(out=eN[g][:], in_=c_[g][:], func=AF.Exp), reads=[b_cc[g], b_eN[g]], writes=[b_eN[g]])
                        yield
                        s.op("pool", lambda e: e.tensor_copy(out=gL[:, g, :], in_=eP[g][:, RL - 1:NT:RL]), reads=[b_eP[g], b_gL], writes=[b_gL])
                        yield
                        yield
                        s.op("pe", lambda e: e.matmul(bank[g][0:64, :], lhsT=aup[:, hc], rhs=adq[:], start=True, stop=True),
                             reads=[b_wup, b_adq, b_bank[g]], writes=[b_bank[g]])
                        yield
                        s.op("act", lambda e: e.activation(out=a_[g][:], in_=bank[g][0:64, :], func=AF.Sigmoid, bias=P["a0"][:, H:H + 1]),
                             reads=[b_bank[g], b_c, b_a[g]], writes=[b_a[g]])
                        yield
                        s.op("dve", lambda e: e.tensor_scalar(out=kk[g][:], in0=k_[g][:], scalar1=P["k_k"][:, H:H + 1], scalar2=None, op0=ALU.mult),
                             reads=[b_k[g], b_c, b_kk[g]], writes=[b_kk[g]])
                        yield
                        s.op("act", lambda e: e.activation(out=t2[g][:], in_=kk[g][:], func=AF.Square), reads=[b_kk[g], b_t2[g]], writes=[b_t2[g]])
                        yield
                        s.op("pe", lambda e: e.matmul(bank[g][0:64, :], lhsT=ones64[:], rhs=t2[g][:], start=True, stop=True),
                             reads=[b_c, b_t2[g], b_bank[g]], writes=[b_bank[g]])
                        yield
                        s.op("act", lambda e: e.activation(out=t2[g][:], in_=bank[g][0:64, :], func=AF.Sqrt), reads=[b_bank[g], b_t2[g]], writes=[b_t2[g]])
                        yield
                        s.op("dve", lambda e: e.tensor_scalar(out=t2[g][:], in0=t2[g][:], scalar1=1e-12, scalar2=None, op0=ALU.max), reads=[b_t2[g]], writes=[b_t2[g]])
                        yield
                        s.op("dve", lambda e: e.reciprocal(out=t2[g][:], in_=t2[g][:]), reads=[b_t2[g]], writes=[b_t2[g]])
                        yield
                        s.op("dve", lambda e: e.tensor_tensor(out=kkn[g][:], in0=kk[g][:], in1=t2[g][:], op=ALU.mult), reads=[b_kk[g], b_t2[g], b_kkn[g]], writes=[b_kkn[g]])
                        yield
                        s.op("dve", lambda e: e.tensor_scalar(out=t1[g][:], in0=a_[g][:], scalar1=P["k_a"][:, H:H + 1], scalar2=om["k_a"][:, H:H + 1],
                                                              op0=ALU.mult, op1=ALU.add), reads=[b_a[g], b_c, b_t1[g]], writes=[b_t1[g]])
                        yield
                        s.op("dve", lambda e: e.tensor_tensor(out=kmod[g][:], in0=k_[g][:], in1=t1[g][:], op=ALU.mult), reads=[b_k[g], b_t1[g], b_kmod[g]], writes=[b_kmod[g]])
                        yield
                        s.op("dve", lambda e: e.tensor_tensor(out=kka[g][:], in0=kkn[g][:], in1=a_[g][:], op=ALU.mult), reads=[b_kkn[g], b_a[g], b_kka[g]], writes=[b_kka[g]])
                        yield
                        s.op("dve", lambda e: e.scalar_tensor_tensor(out=t2[g][:], in0=r_[g][:], scalar=P["r_k"][:, H:H + 1], in1=kmod[g][:],
                                                                     op0=ALU.mult, op1=ALU.mult), reads=[b_r[g], b_c, b_kmod[g], b_t2[g]], writes=[b_t2[g]])
                        yield
                        s.op("pe", lambda e: e.matmul(bank[g][0:64, :], lhsT=ones64[:], rhs=t2[g][:], start=True, stop=True),
                             reads=[b_c, b_t2[g], b_bank[g]], writes=[b_bank[g]])
                        yield
                        s.op("dve", lambda e: e.tensor_tensor(out=bonus[:, g, :], in0=bank[g][0:64, :], in1=Vt[:, g, :], op=ALU.mult),
                             reads=[b_bank[g], b_Vt, b_bonus], writes=[b_bonus])
                        yield
                        s.op("dve", lambda e: e.tensor_tensor(out=t1[g][:], in0=e2[g][:], in1=c_[g][:], op=ALU.subtract), reads=[b_e2[g], b_cc[g], b_t1[g]], writes=[b_t1[g]])
                        yield
                        s.op("act", lambda e: e.activation(out=t1[g][:], in_=t1[g][:], func=AF.Exp), reads=[b_t1[g]], writes=[b_t1[g]])
                        yield
                        v3 = lambda ap: ap.rearrange("p (n l) -> p n l", l=RL)
                        s.op("dve", lambda e: e.scalar_tensor_tensor(out=R(AR[:, g, :, 0, :]), in0=v3(kkn[g][:]), scalar=-1.0, in1=v3(t1[g][:]),
                                                                     op0=ALU.mult, op1=ALU.mult), reads=[b_kkn[g], b_t1[g], b_AR], writes=[b_AR])
                        yield
                        s.op("dve", lambda e: e.tensor_tensor(out=R(AR[:, g, :, 1, :]), in0=v3(r_[g][:]), in1=v3(eP[g][:]), op=ALU.mult),
                             reads=[b_r[g], b_eP[g], b_AR], writes=[b_AR])
                        yield
                        s.op("dve", lambda e: e.tensor_tensor(out=R(Bt[:, g, :]), in0=kka[g][:], in1=eN[g][:], op=ALU.mult), reads=[b_kka[g], b_eN[g], b_Bt], writes=[b_Bt])
                        yield
                        s.op("dve", lambda e: e.tensor_tensor(out=R(Kt[:, g, :]), in0=kmod[g][:], in1=eN[g][:], op=ALU.mult), reads=[b_kmod[g], b_eN[g], b_Kt], writes=[b_Kt])
                        yield
                        gbc = gL[:, g, :].unsqueeze(2).broadcast_to([64, NCK, RL])
                        s.op("dve", lambda e: e.tensor_tensor(out=v3(Bh[:, g, :]), in0=v3(Bt[:, g, :]), in1=gbc, op=ALU.mult),
                             reads=[b_Bt, b_gL, b_Bh], writes=[b_Bh])
                        yield
                        s.op("dve", lambda e: e.tensor_tensor(out=v3(Kh[:, g, :]), in0=v3(Kt[:, g, :]), in1=gbc, op=ALU.mult),
                             reads=[b_Kt, b_gL, b_Kh], writes=[b_Kh])
                        yield


                def phase1():
                    for rw, brw, nm, rowk in ((rawd, b_rawd, "mu_wd", "wd"), (rawa, b_rawa, "mu_ad", "ad")):
                        if ti == 0:
                            s.op("pool", lambda e: e.memset(rw[:, 0:1], 0.0), reads=[brw], writes=[brw])
                        else:
                            s.op("pool", lambda e: e.tensor_copy(out=rw[:, 0:1], in_=rw[:, NT:NT + 1]), reads=[brw], writes=[brw])
                        s.dma("sp", rw[:, 1:1 + NT], PT[rows[rowk]:rows[rowk] + 96, t0:t0 + NT], reads=[brw], writes=[brw])
                    dsts = ((twd, b_twd, rawd, b_rawd, "mu_wd"), (adq, b_adq, rawa, b_rawa, "mu_ad"))
                    for dst, bdst, rw, brw, nm in dsts:
                        s.op("dve", lambda e: e.tensor_scalar(out=dst[:], in0=rw[:, 1:1 + NT], scalar1=omd[nm][:], scalar2=None, op0=ALU.mult),
                             reads=[brw, b_c, bdst], writes=[bdst])
                        s.op("dve", lambda e: e.scalar_tensor_tensor(out=dst[:], in0=rw[:, 0:NT], scalar=mud[nm][:], in1=dst[:],
                                                                     op0=ALU.mult, op1=ALU.add),
                             reads=[brw, b_c, bdst], writes=[bdst])
                    s.op("act", lambda e: e.activation(out=twd[:], in_=twd[:], func=AF.Tanh), reads=[b_twd], writes=[b_twd])
                    for nm in ("r", "k", "v"):
                        if ti == 0:
                            s.op("pool", lambda e: e.memset(raw[nm][:, :, 0:1], 0.0), reads=[b_raw[nm]], writes=[b_raw[nm]])
                        else:
                            s.op("pool", lambda e: e.tensor_copy(out=raw[nm][:, :, 0:1], in_=raw[nm][:, :, NT:NT + 1]),
                                 reads=[b_raw[nm]], writes=[b_raw[nm]])
                        r0 = rows[nm] + h0 * 64
                        s.dma("sp", raw[nm][:, :, 1:1 + NT], PT[r0:r0 + G * 64, t0:t0 + NT].rearrange("(h j) t -> j h t", j=64),
                              reads=[b_raw[nm]], writes=[b_raw[nm]])
                    r0 = rows["g"] + h0 * 64
                    s.dma("act", gate[:], PT[r0:r0 + G * 64, t0:t0 + NT].rearrange("(h j) t -> j h t", j=64), reads=[b_gate], writes=[b_gate])
                    s.op("act", lambda e: e.activation(out=gate[:], in_=gate[:], func=AF.Silu), reads=[b_gate], writes=[b_gate])
                    alive = [head_gen(g) for g in range(G)]
                    while alive:
                        for gg in list(alive):
                            try:
                                next(gg)
                            except StopIteration:
                                alive.remove(gg)
                        yield
                def indep_steps(cc):
                    par = cc % 2
                    cs = slice(cc * RL, (cc + 1) * RL)
                    tk, ga_s, gk_s, pn_s, ppl = tokb[par], GAb[par], GKb[par], Pnb[par], PPb[par]
                    b_tk, b_ga, b_gk, b_pn, b_ppl = b_tokb[par], b_GAb[par], b_GKb[par], b_Pnb[par], b_PPb[par]

                    def tr():
                        for g in range(G):
                            s.op("pe", lambda e: e.transpose(ps_x1[:, 0, g, :], Vt[:, g, cs], i64), reads=[b_Vt, b_c, b_bank[2]], writes=[b_bank[2]])
                            s.op("pe", lambda e: e.transpose(ps_x1[:, 1, g, :], Bh[:, g, cs], i64), reads=[b_Bh, b_c, b_bank[2]], writes=[b_bank[2]])
                            s.op("pe", lambda e: e.transpose(ps_x2k[:, g, :], Kh[:, g, cs], i64), reads=[b_Kh, b_c, b_bank[3]], writes=[b_bank[3]])
                        s.op("act", lambda e: e.copy(out=R(tk[:, 0:2]), in_=ps_x1), reads=[b_bank[2], b_tk], writes=[b_tk])
                        s.op("act", lambda e: e.copy(out=R(tk[:, 2]), in_=ps_x2k), reads=[b_bank[3], b_tk], writes=[b_tk])

                    def gm():
                        for g in range(G):
                            arc = AR[:, g, cc].rearrange("p a l -> p (a l)")
                            s.op("pe", lambda e: e.matmul(ps_ga[:, g, :], lhsT=R(Bt[:, g, cs]), rhs=R(arc), start=True, stop=True),
                                 reads=[b_Bt, b_AR, b_bank[0]], writes=[b_bank[0]])
                            s.op("pe", lambda e: e.matmul(ps_gk[:, g, :], lhsT=R(Kt[:, g, cs]), rhs=R(arc), start=True, stop=True),
                                 reads=[b_Kt, b_AR, b_bank[1]], writes=[b_bank[1]])
                            s.op("pe", lambda e: e.matmul(ps_gn[:, g, :], lhsT=R(AR[:, g, cc, 0, :]), rhs=R(Bt[:, g, cs]), start=True, stop=True),
                                 reads=[b_Bt, b_AR, b_bank[3]], writes=[b_bank[3]])
                        s.op("dve", lambda e: e.tensor_tensor(out=R(ga_s[:]), in0=ps_ga, in1=maskA[:], op=ALU.mult),
                             reads=[b_bank[0], b_c, b_ga], writes=[b_ga])
                        s.op("dve", lambda e: e.tensor_tensor(out=R(gk_s[:]), in0=ps_gk, in1=maskA[:], op=ALU.mult),
                             reads=[b_bank[1], b_c, b_gk], writes=[b_gk])
                        s.op("dve", lambda e: e.tensor_tensor(out=R(pn_s[:]), in0=ps_gn, in1=maskN[:], op=ALU.mult),
                             reads=[b_bank[3], b_c, b_pn], writes=[b_pn])

                    def sq(lvl):
                        def f():
                            if lvl == 0:
                                Pl = lambda g: pn_s[:, g, :]
                                PTl = lambda g: ga_s[:, g, 0:RL]
                                rd = [b_pn, b_ga]
                            else:
                                Pl = lambda g: ppl[lvl - 1][:, g, 0, :]
                                PTl = lambda g: ppl[lvl - 1][:, g, 1, :]
                                rd = [b_ppl[lvl - 1]]
                            for g in range(G):
                                s.op("pe", lambda e: e.matmul(ps_pp[:, g, 0, :], lhsT=R(PTl(g)), rhs=R(Pl(g)), start=True, stop=True),
                                     reads=rd + [b_bank[5]], writes=[b_bank[5]])
                                s.op("pe", lambda e: e.matmul(ps_pp[:, g, 1, :], lhsT=R(Pl(g)), rhs=R(PTl(g)), start=True, stop=True),
                                     reads=rd + [b_bank[5]], writes=[b_bank[5]])
                            s.op("act", lambda e: e.copy(out=R(ppl[lvl][:]), in_=ps_pp), reads=[b_bank[5], b_ppl[lvl]], writes=[b_ppl[lvl]])
                        return f
                    return [tr, gm] + [sq(l) for l in range(NLVL - 1)]

                def dep_steps(cc):
                    par = cc % 2
                    cs = slice(cc * RL, (cc + 1) * RL)
                    tk, ga_s, gk_s, pn_s, ppl = tokb[par], GAb[par], GKb[par], Pnb[par], PPb[par]
                    b_tk, b_ga, b_gk, b_pn, b_ppl = b_tokb[par], b_GAb[par], b_GKb[par], b_Pnb[par], b_PPb[par]

                    def zz():
                        for g in range(G):
                            s.op("pe", lambda e: e.matmul(ps_z[:, 0, g, :], lhsT=R(AR[:, g, cc, 0, :]), rhs=R(ST[:, g, :]), start=True, stop=False),
                                 reads=[b_AR, b_ST, b_bank[4]], writes=[b_bank[4]], inc=False)
                            s.op("pe", lambda e: e.matmul(ps_z[:, 0, g, :], lhsT=R(gk_s[:, g, 0:RL]), rhs=R(tk[:, 0, g, :]), start=False, stop=True),
                                 reads=[b_gk, b_tk, b_bank[4]], writes=[b_bank[4]])
                        s.op("act", lambda e: e.copy(out=R(U[:]), in_=ps_z[:, 0]), reads=[b_bank[4], b_U], writes=[b_U])

                    def app(lvl):
                        def f():
                            if lvl == 0:
                                PTl = lambda g: ga_s[:, g, 0:RL]
                                rd = [b_ga]
                            else:
                                PTl = lambda g: ppl[lvl - 1][:, g, 1, :]
                                rd = [b_ppl[lvl - 1]]
                            for g in range(G):
                                s.op("pe", lambda e: e.matmul(ps_z[:, 1, g, :], lhsT=R(PTl(g)), rhs=R(U[:, g, :]), start=True, stop=True),
                                     reads=rd + [b_U, b_bank[4]], writes=[b_bank[4]])
                            s.op("dve", lambda e: e.tensor_tensor(out=R(U[:]), in0=ps_z[:, 1], in1=U[:], op=ALU.add),
                                 reads=[b_bank[4], b_U], writes=[b_U])
                        return f

                    def yy():
                        for g in range(G):
                            s.op("pe", lambda e: e.matmul(ps_y[:, g, :], lhsT=R(AR[:, g, cc, 1, :]), rhs=R(ST[:, g, :]), start=True, stop=False),
                                 reads=[b_AR, b_ST, b_bank[6]], writes=[b_bank[6]], inc=False)
                            s.op("pe", lambda e: e.matmul(ps_y[:, g, :], lhsT=R(ga_s[:, g, RL:2 * RL]), rhs=R(U[:, g, :]), start=False, stop=False),
                                 reads=[b_ga, b_U, b_bank[6]], writes=[b_bank[6]], inc=False)
                            s.op("pe", lambda e: e.matmul(ps_y[:, g, :], lhsT=R(gk_s[:, g, RL:2 * RL]), rhs=R(tk[:, 0, g, :]), start=False, stop=True),
                                 reads=[b_gk, b_tk, b_bank[6]], writes=[b_bank[6]])
                        s.op("act", lambda e: e.copy(out=Ysb[:], in_=ps_y), reads=[b_bank[6], b_Ysb], writes=[b_Ysb])

                    def ss():
                        for g in range(G):
                            s.op("pe", lambda e: e.matmul(ps_s[:, g, :], lhsT=R(tk[:, 1, g, :]), rhs=R(U[:, g, :]), start=True, stop=False),
                                 reads=[b_tk, b_U, b_bank[7]], writes=[b_bank[7]], inc=False)
                            s.op("pe", lambda e: e.matmul(ps_s[:, g, :], lhsT=R(tk[:, 2, g, :]), rhs=R(tk[:, 0, g, :]), start=False, stop=True),
                                 reads=[b_tk, b_bank[7]], writes=[b_bank[7]])
                        for g in range(G):
                            s.op("dve", lambda e: e.scalar_tensor_tensor(out=R(ST[:, g, :]), in0=ST[:, g, :], scalar=gL[:, g, cc:cc + 1], in1=ps_s[:, g, :],
                                                                         op0=ALU.mult, op1=ALU.add),
                                 reads=[b_ST, b_gL, b_bank[7]], writes=[b_ST])

                    def yt():
                        for g in range(G):
                            s.op("pe", lambda e: e.transpose(ps_yT[:, g, :], Ysb[:, g, :], ident[0:RL, 0:RL]), reads=[b_Ysb, b_c, b_bank[6]], writes=[b_bank[6]])
                        s.op("dve", lambda e: e.tensor_copy(out=yfm[:, :, cs], in_=ps_yT), reads=[b_bank[6], b_yfm], writes=[b_yfm])
                    return [zz] + [app(l) for l in range(NLVL)] + [yy, ss, yt]


                def chunks():
                    for f in indep_steps(0):
                        f()
                        yield
                    for cc in range(NCK):
                        dsteps = dep_steps(cc)
                        isteps = indep_steps(cc + 1) if cc + 1 < NCK else []
                        for j in range(max(len(dsteps), len(isteps))):
                            if j < len(dsteps):
                                dsteps[j]()
                            if j < len(isteps):
                                isteps[j]()
                            yield
                def ph4(g):
                    H = h0 + g
                    yi = g % 2
                    s.op("pe", lambda e: e.matmul(bank[(7, 4)[g]][0:64, :], lhsT=ones64[:], rhs=yfm[:, g, :], start=True, stop=True),
                         reads=[b_c, b_yfm, b_bank[(7, 4)[g]]], writes=[b_bank[(7, 4)[g]]])
                    yield
                    s.op("act", lambda e: e.activation(out=t1[g][:], in_=bank[(7, 4)[g]][0:64, :], func=AF.Copy, scale=1.0 / 64), reads=[b_bank[(7, 4)[g]], b_t1[g]], writes=[b_t1[g]])
                    yield
                    s.op("dve", lambda e: e.tensor_tensor(out=t1[g][:], in0=yfm[:, g, :], in1=t1[g][:], op=ALU.subtract), reads=[b_yfm, b_t1[g]], writes=[b_t1[g]])
                    yield
                    s.op("act", lambda e: e.activation(out=t2[g][:], in_=t1[g][:], func=AF.Square), reads=[b_t1[g], b_t2[g]], writes=[b_t2[g]])
                    yield
                    s.op("pe", lambda e: e.matmul(bank[(7, 4)[g]][0:64, :], lhsT=ones64[:], rhs=t2[g][:], start=True, stop=True),
                         reads=[b_c, b_t2[g], b_bank[(7, 4)[g]]], writes=[b_bank[(7, 4)[g]]])
                    yield
                    s.op("act", lambda e: e.activation(out=t2[g][:], in_=bank[(7, 4)[g]][0:64, :], func=AF.Sqrt, scale=1.0 / 64, bias=GN_EPS),
                         reads=[b_bank[(7, 4)[g]], b_t2[g]], writes=[b_t2[g]])
                    yield
                    s.op("dve", lambda e: e.reciprocal(out=t2[g][:], in_=t2[g][:]), reads=[b_t2[g]], writes=[b_t2[g]])
                    yield
                    s.op("dve", lambda e: e.tensor_tensor(out=t1[g][:], in0=t1[g][:], in1=t2[g][:], op=ALU.mult), reads=[b_t1[g], b_t2[g]], writes=[b_t1[g]])
                    yield
                    s.op("dve", lambda e: e.tensor_scalar(out=t1[g][:], in0=t1[g][:], scalar1=P["gn_w"][:, H:H + 1], scalar2=P["gn_b"][:, H:H + 1],
                                                          op0=ALU.mult, op1=ALU.add), reads=[b_t1[g], b_c], writes=[b_t1[g]])
                    yield
                    s.op("dve", lambda e: e.tensor_tensor(out=t1[g][:], in0=t1[g][:], in1=bonus[:, g, :], op=ALU.add), reads=[b_t1[g], b_bonus], writes=[b_t1[g]])
                    yield
                    s.op("dve", lambda e: e.tensor_tensor(out=yo[yi][:], in0=t1[g][:], in1=gate[:, g, :], op=ALU.mult),
                         reads=[b_t1[g], b_gate, b_yo[yi]], writes=[b_yo[yi]])
                    yield
                    r0 = row_y + H * 64
                    s.dma("sp", yT[r0:r0 + 64, t0:t0 + NT], yo[yi][:], reads=[b_yo[yi]])
                    yield

                def phase4():
                    alive = [ph4(g) for g in range(G)]
                    while alive:
                        for gg in list(alive):
                            try:
                                next(gg)
                            except StopIteration:
                                alive.remove(gg)
                        yield
                return phase1, chunks, phase4

            tiles = [make_tile(ti, t0) for ti, t0 in enumerate(range(0, T, NT))]
            for _ in tiles[0][0]():
                pass
            for ti in range(len(tiles)):
                cg = tiles[ti][1]()
                pg = tiles[ti + 1][0]() if ti + 1 < len(tiles) else iter(())
                c_alive, p_alive = True, True
                while c_alive or p_alive:
                    if c_alive:
                        try:
                            next(cg)
                        except StopIteration:
                            c_alive = False
                    for _ in range(P1_PER_STEP):
                        if p_alive:
                            try:
                                next(pg)
                            except StopIteration:
                                p_alive = False
                for _ in tiles[ti][2]():
                    pass
        s.barrier()


class Cfg:
    def __init__(self, D=4096, T=4096, NBLK_A=8, NH_B=32, HC=4, HX=4, M=256, DEPTH=2):
        self.D, self.T, self.NBLK_A, self.NH_B, self.HC, self.HX, self.M, self.DEPTH = D, T, NBLK_A, NH_B, HC, HX, M, DEPTH
        self.KC = D // 128
        self.WA = NBLK_A * 256
        self.WB = NH_B * 64
        self.QKW = HC * 256
        self.WC = HC * 512
        self.WX = HX * 128
        self.in_sizes = (self.WA, self.WA, 3 * self.WB + 192, self.WB, 2 * self.QKW, self.WC, self.WC, self.WC, 2 * HC,
                         self.WX, self.WX, 4 * D)
        self.c_in = sum(self.in_sizes)
        off = np.concatenate([[0], np.cumsum(self.in_sizes)])
        self.col = dict(a_x=off[0], a_g=off[1], b_s=off[2], b_g=off[3], c_qk=off[4], c_v=off[5], c_o=off[6], c_g=off[7],
                        c_if=off[8], x_q=off[9], x_g=off[10], gates=off[11])
        segs = [("a_x", self.col["a_x"], self.WA), ("a_g", self.col["a_g"], self.WA),
                ("r", self.col["b_s"], self.WB), ("k", self.col["b_s"] + self.WB, self.WB), ("v", self.col["b_s"] + 2 * self.WB, self.WB),
                ("wd", self.col["b_s"] + 3 * self.WB, 96), ("ad", self.col["b_s"] + 3 * self.WB + 96, 96),
                ("b_g", self.col["b_g"], self.WB),
                ("c_q", self.col["c_qk"], self.QKW), ("c_k", self.col["c_qk"] + self.QKW, self.QKW),
                ("c_v", self.col["c_v"], self.WC), ("c_o", self.col["c_o"], self.WC), ("c_g", self.col["c_g"], self.WC),
                ("c_if", self.col["c_if"], 2 * HC), ("x_q", self.col["x_q"], self.WX), ("x_g", self.col["x_g"], self.WX)]
        self.segs = segs
        self.row = {}
        r = 0
        for nm, c0, w in segs:
            self.row[nm] = r
            r += ((w + 127) // 128) * 128
        self.NB1 = r // 128
        self.grp_first = {"A": "a_x", "B": "r", "C": "c_q", "X": "x_q"}
        order = ["A", "B", "C", "X"]
        starts = [self.row[self.grp_first[g]] for g in order] + [r]
        self.grp_rows = {g: (starts[i], starts[i + 1]) for i, g in enumerate(order)}
        self.lrow = {}
        for nm, c0, w in segs:
            for g in order:
                lo, hi = self.grp_rows[g]
                if lo <= self.row[nm] < hi:
                    self.lrow[nm] = (g, self.row[nm] - lo)
        self.br_kc = [self.WA // 128, self.WB // 128, self.WC // 128, self.WX // 128]
        self.FY = sum(self.br_kc) * 128


RMS_EPS = 1e-6
RWKV_GN_EPS = 64e-5


def tile_layout(W):
    Kd, M = W.shape
    return np.ascontiguousarray(W.reshape(Kd // 128, 128, M // 128, 128).transpose(2, 1, 0, 3))


def chunk_cols(v):
    return np.ascontiguousarray(v.reshape(-1, 128).T)


def head_cols(v):
    return np.ascontiguousarray(v.reshape(-1, 64).T)


def const_inputs(cfg):
    HC = cfg.HC
    sel = np.zeros((HC, HC, 128), np.float32)
    for h in range(HC):
        sel[h, h, :] = 1
    a_, b_ = np.meshgrid(np.arange(RWKV_L), np.arange(RWKV_L), indexing="ij")
    mA = np.concatenate([(a_ < b_), (a_ <= b_)], 1).astype(np.float32)
    mN = (b_ < a_).astype(np.float32)
    rm = np.ones((64, NT), np.float32)
    rm[:, ::RWKV_L] = 0
    a_, b_ = np.meshgrid(np.arange(MLSTM_L), np.arange(MLSTM_L), indexing="ij")
    return {"c_ident": np.eye(128, dtype=np.float32), "c_sel": sel, "c_i4": np.eye(HC, dtype=np.float32),
            "c_negmask": np.where(a_ <= b_, 0.0, NEG).astype(np.float32), "c_resetmask": rm,
            "c_maskA": np.ascontiguousarray(np.broadcast_to(mA[:, None, :], (RWKV_L, RWKV_G, 2 * RWKV_L))),
            "c_maskN": np.ascontiguousarray(np.broadcast_to(mN[:, None, :], (RWKV_L, RWKV_G, RWKV_L)))}


def layer_inputs(cfg, inp, l):
    D, KC, HC = cfg.D, cfg.KC, cfg.HC
    w_in = inp["w_in"][l]
    W1 = np.zeros((D, cfg.NB1 * 128), np.float32)
    for nm, c0, w in cfg.segs:
        W1[:, cfg.row[nm]:cfg.row[nm] + w] = w_in[:, c0:c0 + w]
    o = {}
    o["W1"] = tile_layout(W1)
    g0 = cfg.col["gates"]
    o["Wg"] = np.stack([tile_layout(w_in[:, g0 + i * D: g0 + (i + 1) * D]) for i in range(4)])
    o["Wbr0"] = tile_layout(inp["w_branch_a"][l])
    o["Wbr1"] = tile_layout(inp["w_branch_b"][l])
    o["Wbr2"] = tile_layout(inp["w_branch_c"][l])
    o["Wbr3"] = tile_layout(inp["w_branch_x"][l])
    o["Wo"] = tile_layout(inp["w_out"][l])
    o["norm_g"] = chunk_cols(inp["norm_g"][l])
    o["mem_norm_g"] = chunk_cols(inp["mem_norm_g"][l])
    NCH = cfg.NBLK_A * 2
    o["lru_conv_w"] = np.ascontiguousarray(inp["lru_conv_w"][l].reshape(4, NCH, 128).transpose(2, 1, 0))
    for nm in ("lru_conv_b", "lru_ba", "lru_bx", "lru_lambda"):
        o[nm] = chunk_cols(inp[nm][l])
    for nm in ("lru_wa", "lru_wx"):
        o[nm] = np.ascontiguousarray(inp[nm][l].reshape(cfg.NBLK_A, 2, 128, 2, 128).transpose(0, 3, 2, 1, 4))
    WB = cfg.WB
    mu = inp["rwkv_mu"][l]
    o["rw_mu_r"], o["rw_mu_k"], o["rw_mu_v"] = head_cols(mu[:WB]), head_cols(mu[WB:2 * WB]), head_cols(mu[2 * WB:3 * WB])
    o["rw_mu_wd"] = np.ascontiguousarray(mu[3 * WB:3 * WB + 96].reshape(96, 1))
    o["rw_mu_ad"] = np.ascontiguousarray(mu[3 * WB + 96:].reshape(96, 1))
    for nm, src in (("w0", "rwkv_w0"), ("a0", "rwkv_a0"), ("k_k", "rwkv_k_k"), ("k_a", "rwkv_k_a"), ("gn_w", "rwkv_gn_w"), ("gn_b", "rwkv_gn_b")):
        o["rw_" + nm] = head_cols(inp[src][l])
    o["rw_r_k"] = head_cols(inp["rwkv_r_k"][l].reshape(-1))
    o["rw_w_up"] = np.ascontiguousarray(inp["rwkv_w_up"][l])
    o["rw_a_up"] = np.ascontiguousarray(inp["rwkv_a_up"][l])
    o["ml_conv_w"] = np.ascontiguousarray(inp["mlstm_conv_w"][l].reshape(4, 4 * HC, 128).transpose(2, 1, 0))
    o["ml_conv_b"] = chunk_cols(inp["mlstm_conv_b"][l])
    o["ml_gn_w"] = chunk_cols(inp["mlstm_gn_w"][l])
    o["ml_b_i"] = np.ascontiguousarray(inp["mlstm_b_i"][l].reshape(HC, 1))
    o["ml_b_f"] = np.ascontiguousarray(inp["mlstm_b_f"][l].reshape(HC, 1))
    wkv = inp["xattn_w_kv"][l]
    o["xa_Wk"] = tile_layout(wkv[:, :cfg.WX])
    o["xa_Wv"] = np.ascontiguousarray(wkv[:, cfg.WX:].reshape(KC, 128, cfg.WX))
    return {f"L{l}_{k_}": np.ascontiguousarray(v, dtype=np.float32) for k_, v in o.items()}


def flat2d(ap, ndim, width):
    names = "abcdefgh"[:ndim]
    f = ap.rearrange(f"{' '.join(names)} -> ({' '.join(names)})")
    return f.rearrange("(r c) -> r c", c=width)


def build_program(cfg, shapes):
    nc = bass.Bass("TRN2", target_bir_lowering=False)
    D, T, KC = cfg.D, cfg.T, cfg.KC
    ins = {nm: nc.dram_tensor(nm, list(sh), F32, kind="ExternalInput").ap() for nm, sh in shapes.items()}
    outT = nc.dram_tensor("outT", [D, T], F32, kind="ExternalOutput").ap()
    with contextlib.ExitStack() as st:
        k = Ctx(nc, st)
        hT = k.dram("hT", [D, T], BF16)
        PTs = {g: k.dram(f"PT{g}", [hi - lo, T], F32) for g, (lo, hi) in cfg.grp_rows.items()}

        def pt_block(b):
            r = b * 128
            for g, (lo, hi) in cfg.grp_rows.items():
                if lo <= r < hi:
                    return PTs[g], r - lo
            raise AssertionError
        lr = lambda nm: cfg.lrow[nm][1]
        yT = k.dram("yT", [cfg.FY, T], BF16)
        memnT = k.dram("memnT", [D, cfg.M], BF16)
        xs = [ins["xT"]] + [k.dram(f"x{l + 1}T", [D, T], F32) for l in range(cfg.DEPTH)]
        cst = {nm[2:]: ins[nm] for nm in ins if nm.startswith("c_")}
        OB = D // 128
        for l in range(cfg.DEPTH):
            L = lambda nm: ins[f"L{l}_{nm}"]
            Wg = k.dram(f"Wg{l}", [4, OB, 128, KC, 128], BF16)
            Wo = k.dram(f"Wo{l}", [OB, 128, KC, 128], BF16)
            Wbr = [k.dram(f"Wbr{l}_{i}", [OB, 128, cfg.br_kc[i], 128], BF16) for i in range(4)]
            for dst, src, nd in [(Wg, L("Wg"), 5), (Wo, L("Wo"), 4)] + [(Wbr[i], L(f"Wbr{i}"), 4) for i in range(4)]:
                n = int(np.prod(dst.shape))
                wdt = 1024 if n % 1024 == 0 else 512
                cast_dram(k, flat2d(dst, nd, wdt), flat2d(src, nd, wdt), n, width=wdt)
            stage_norm(k, xs[l], L("norm_g"), hT, D, T, RMS_EPS, BF16)
            stage_norm(k, ins["memT"], L("mem_norm_g"), memnT, D, cfg.M, RMS_EPS, BF16)
            stage_proj(k, hT, L("W1"), pt_block, D, T, cfg.NB1)
            lru_prm = {"conv_w": L("lru_conv_w"), "conv_b": L("lru_conv_b"), "ba": L("lru_ba"), "bx": L("lru_bx"), "lam": L("lru_lambda"),
                       "wa": L("lru_wa"), "wx": L("lru_wx")}
            stage_lru(k, PTs["A"], lr("a_x"), lr("a_g"), yT, 0, lru_prm, T, cfg.NBLK_A)
            rw_prm = {nm: L("rw_" + nm) for nm in ("mu_r", "mu_k", "mu_v", "w0", "a0", "k_k", "k_a", "r_k", "gn_w", "gn_b", "mu_wd", "mu_ad", "w_up", "a_up")}
            rw_rows = {"r": lr("r"), "k": lr("k"), "v": lr("v"), "wd": lr("wd"), "ad": lr("ad"), "g": lr("b_g")}
            stage_rwkv(k, PTs["B"], rw_rows, yT, cfg.WA, rw_prm, cst, T, cfg.NH_B, RWKV_GN_EPS)
            ml_prm = {"conv_w": L("ml_conv_w"), "conv_b": L("ml_conv_b"), "gn_w": L("ml_gn_w"), "b_i": L("ml_b_i"), "b_f": L("ml_b_f")}
            ml_rows = {"q": lr("c_q"), "k": lr("c_k"), "v": lr("c_v"), "o": lr("c_o"), "g": lr("c_g"), "ifg": lr("c_if")}
            stage_mlstm(k, PTs["C"], ml_rows, yT, cfg.WA + cfg.WB, ml_prm, cst, T, cfg.HC)
            stage_xattn(k, PTs["X"], lr("x_q"), lr("x_g"), yT, cfg.WA + cfg.WB + cfg.WC, memnT, L("xa_Wk"), L("xa_Wv"), cst["ident"],
                        T, D, cfg.M, cfg.HX)
            stage_merge(k, hT, yT, cfg.br_kc, Wg, Wbr, Wo, xs[l], xs[l + 1], D, T)
        stage_norm(k, xs[cfg.DEPTH], ins["final_g"], outT, D, T, RMS_EPS, F32)
        k.s.barrier()
        build_program.ninst = k.s.ninst
        build_program.per_eng = dict(k.s.per_eng)
        build_program.nsem = k.s.nsem
        build_program.nwait = k.s.nwait
    return nc


def run_module(cfg, inputs):
    B = inputs["x"].shape[0]
    shared = const_inputs(cfg)
    for l in range(cfg.DEPTH):
        shared.update(layer_inputs(cfg, inputs, l))
    shared["final_g"] = chunk_cols(np.asarray(inputs["final_norm_g"], dtype=np.float32))
    in_maps = []
    for b in range(B):
        m = dict(shared)
        m["xT"] = np.ascontiguousarray(np.asarray(inputs["x"][b], dtype=np.float32).T)
        m["memT"] = np.ascontiguousarray(np.asarray(inputs["mem"][b], dtype=np.float32).T)
        in_maps.append(m)
    shapes = {nm: v.shape for nm, v in in_maps[0].items()}
    nc = build_program(cfg, shapes)
    res = run_bass_kernel_spmd(nc, in_maps, core_ids=list(range(B)))
    out = np.stack([np.ascontiguousarray(res.results[b]["outT"].T) for b in range(B)])
    return out.astype(np.float32)


def kernel(**inputs):
    inputs = {k_: np.asarray(v) for k_, v in inputs.items()}
    return run_module(Cfg(), inputs)
```

```python
import contextlib
import numpy as np
import concourse.bass as bass
import concourse.mybir as mybir
from concourse.bass_utils import run_bass_kernel_spmd

F32 = mybir.dt.float32
BF16 = mybir.dt.bfloat16
AF = mybir.ActivationFunctionType
ALU = mybir.AluOpType
AX = mybir.AxisListType


class Buf:
    __slots__ = ("name", "w", "r")

    def __init__(self, name=""):
        self.name = name
        self.w = set()
        self.r = set()


SKIP_SAME_ENGINE = False


class Sched:
    EPOCH = 4000
    NDMA = 12

    def __init__(self, nc, stack):
        self.nc = nc
        self.stack = stack
        self.eng = {"pe": nc.tensor, "act": nc.scalar, "dve": nc.vector,
                    "pool": nc.gpsimd, "sp": nc.sync}
        self.sem = {}
        self.cnt = {}
        self.pending = {e: False for e in self.eng}
        self.nsem = 0
        for e in self.eng:
            self._new_sem(e)
        self.dsem = {}
        for q in ("sp", "act", "pool"):
            self.dsem[q] = [[self._alloc(f"d{q}{i}"), 0] for i in range(self.NDMA)]
        self.dnext = {q: 0 for q in self.dsem}
        self.seen = {e: {} for e in self.eng}
        self.all_tokens = {}
        self.ninst = 0
        self.per_eng = {}

    def _alloc(self, name):
        self.nsem += 1
        return self.stack.enter_context(self.nc.semaphore(f"{name}_{self.nsem}"))

    def _new_sem(self, e):
        self.sem[e] = self._alloc(f"s{e}")
        self.cnt[e] = 0

    def _wait(self, e, tok):
        sem, val = tok
        k = id(sem)
        if self.seen[e].get(k, 0) >= val:
            return
        self.eng[e].wait_ge(sem, val)
        self.nwait = getattr(self, "nwait", 0) + 1
        self.seen[e][k] = val

    def _deps(self, e, reads, writes):
        deps = set()
        for b in reads:
            deps |= b.w
        for b in writes:
            deps |= b.w
            deps |= b.r
        best = {}
        for tok in deps:
            if e == "pe" and tok[0] is self.sem["pe"]:
                continue
            if SKIP_SAME_ENGINE and e in ("act", "dve") and tok[0] is self.sem[e]:
                continue
            kk_ = id(tok[0])
            if kk_ not in best or best[kk_][1] < tok[1]:
                best[kk_] = tok
        for tok in best.values():
            self._wait(e, tok)

    def _record(self, tok, reads, writes):
        self.all_tokens[id(tok[0])] = tok
        for b in reads:
            b.r.add(tok)
        for b in writes:
            b.w = {tok}
            b.r = set()

    def op(self, e, fn, reads=(), writes=(), inc=True):
        self._deps(e, reads, writes)
        inst = fn(self.eng[e])
        self.ninst += 1
        self.per_eng[e] = self.per_eng.get(e, 0) + 1
        if inc:
            if self.cnt[e] >= self.EPOCH and not self.pending[e]:
                self._new_sem(e)
            self.cnt[e] += 1
            inst.then_inc(self.sem[e], 1)
            tok = (self.sem[e], self.cnt[e])
            self.pending[e] = False
        else:
            assert e == "pe"
            tok = (self.sem[e], self.cnt[e] + 1)
            self.pending[e] = True
        self._record(tok, reads, writes)
        return inst

    def dma(self, q, out, in_, reads=(), writes=(), **kw):
        slot = self.dsem[q][self.dnext[q]]
        self.dnext[q] = (self.dnext[q] + 1) % self.NDMA
        sem, val = slot
        if val > 0:
            self._wait(q, (sem, val))
        self._deps(q, reads, writes)
        inst = self.eng[q].dma_start(out=out, in_=in_, **kw)
        self.ninst += 1
        self.per_eng['dma_' + q] = self.per_eng.get('dma_' + q, 0) + 1
        slot[1] = val + 16
        inst.then_inc(sem, 16)
        tok = (sem, slot[1])
        self._record(tok, reads, writes)
        return inst

    def barrier(self):
        toks = list(self.all_tokens.values())
        for e in self.eng:
            for tok in toks:
                self._wait(e, tok)


class Ctx:
    def __init__(self, nc, stack):
        self.nc = nc
        self.s = Sched(nc, stack)
        self.n = 0

    def sb(self, st, shape, dt, name="t"):
        self.n += 1
        return st.enter_context(self.nc.sbuf_tensor(f"{name}{self.n}", list(shape), dt))

    def ps(self, st, shape, dt=F32, name="p"):
        self.n += 1
        return st.enter_context(self.nc.psum_tensor(f"{name}{self.n}", list(shape), dt))

    def dram(self, name, shape, dt):
        return self.nc.dram_tensor(name, list(shape), dt, kind="Internal").ap()


def dma_rows(s, q, sb3, dram2, nchunks, reads=(), writes=(), to_dram=False, step=8):
    for c0 in range(0, nchunks, step):
        c1 = min(nchunks, c0 + step)
        d = dram2[c0 * 128:c1 * 128, :].rearrange("(c p) t -> p c t", p=128)
        if to_dram:
            s.dma(q, d, sb3[:, c0:c1, :], reads=reads, writes=writes)
        else:
            s.dma(q, sb3[:, c0:c1, :], d, reads=reads, writes=writes)


def bufs(n):
    return [Buf() for _ in range(n)]


NT = 512


def stage_norm(k, xT, g_dram, out, D, T, eps, out_dt):
    NT = min(512, T)
    s = k.s
    KC = D // 128
    with contextlib.ExitStack() as st:
        ones = k.sb(st, [128, 128], F32)
        gcol = k.sb(st, [128, KC], F32)
        xt = k.sb(st, [128, KC, NT], F32)
        ht = k.sb(st, [128, KC, NT], out_dt)
        sq = [k.sb(st, [128, NT], F32) for _ in range(2)]
        rs = k.sb(st, [128, NT], F32)
        rstd = k.sb(st, [128, NT], F32)
        pss = k.ps(st, [128, NT])
        b_ones, b_g, b_rs, b_rstd, b_ps = bufs(5)
        b_xt, b_ht, b_sq = bufs(KC), bufs(KC), bufs(2)
        ones_f = k.sb(st, [128, 128], F32)
        s.op("pool", lambda e: e.memset(ones_f[:], 1.0), writes=[b_ones])
        s.op("act", lambda e: e.copy(out=ones[:].bitcast(mybir.dt.float32r), in_=ones_f[:]), reads=[b_ones], writes=[b_ones])
        s.dma("sp", gcol[:], g_dram, writes=[b_g])
        for t0 in range(0, T, NT):
            for c in range(KC):
                s.dma("sp", xt[:, c, :], xT[c * 128:(c + 1) * 128, t0:t0 + NT], writes=[b_xt[c]])
                s.op("act", lambda e: e.activation(out=sq[c % 2][:].bitcast(mybir.dt.float32r), in_=xt[:, c, :], func=AF.Square),
                     reads=[b_xt[c]], writes=[b_sq[c % 2]])
                s.op("pe", lambda e: e.matmul(pss[:], lhsT=ones[:].bitcast(mybir.dt.float32r), rhs=sq[c % 2][:].bitcast(mybir.dt.float32r),
                                             start=(c == 0), stop=(c == KC - 1)),
                     reads=[b_ones, b_sq[c % 2]], writes=[b_ps], inc=True)
            s.op("act", lambda e: e.activation(out=rs[:], in_=pss[:], func=AF.Sqrt, scale=1.0 / D, bias=eps_ap(k, eps)),
                 reads=[b_ps], writes=[b_rs])
            s.op("dve", lambda e: e.reciprocal(out=rstd[:], in_=rs[:]), reads=[b_rs], writes=[b_rstd])
            for c in range(KC):
                s.op("dve", lambda e: e.scalar_tensor_tensor(out=ht[:, c, :], in0=xt[:, c, :], scalar=gcol[:, c:c + 1],
                                                             in1=rstd[:], op0=ALU.mult, op1=ALU.mult),
                     reads=[b_xt[c], b_g, b_rstd], writes=[b_ht[c]])
            dma_rows(s, "sp", ht, out[:, t0:t0 + NT], KC, reads=b_ht, to_dram=True)
        s.barrier()


_EPS = {}


def eps_ap(k, val):
    return float(val)


def stage_proj(k, hT, W, PT, D, T, NB):
    NT = min(512, T)
    s = k.s
    KC = D // 128
    GRP = 6
    with contextlib.ExitStack() as st:
        wb = [[k.sb(st, [128, KC, 128], BF16) for _ in range(GRP)] for _ in range(2)]
        ht = [k.sb(st, [128, KC, NT], BF16) for _ in range(2)]
        ot = [k.sb(st, [128, NT], F32) for _ in range(4)]
        ps = [k.ps(st, [128, NT]) for _ in range(8)]
        b_wb, b_ht, b_ot, b_ps = [bufs(GRP), bufs(GRP)], bufs(2), bufs(4), bufs(8)
        groups = list(range(0, NB, GRP))

        def load_group(gi):
            g0 = groups[gi]
            for b in range(min(GRP, NB - g0)):
                s.dma("pool", wb[gi % 2][b][:], W[g0 + b], writes=[b_wb[gi % 2][b]], max_dma_last_dim=4096)

        nht = 0
        no = 0
        npp = 0
        load_group(0)
        for gi, g0 in enumerate(groups):
            nb = min(GRP, NB - g0)
            if gi + 1 < len(groups):
                load_group(gi + 1)
            wset, bset = wb[gi % 2], b_wb[gi % 2]
            for t0 in range(0, T, NT):
                h = nht % 2
                nht += 1
                dma_rows(s, "act", ht[h], hT[:, t0:t0 + NT], KC, writes=[b_ht[h]])
                for b in range(nb):
                    pi = npp % 8
                    npp += 1
                    p = ps[pi]
                    for c in range(KC):
                        s.op("pe", lambda e: e.matmul(p[:], lhsT=wset[b][:, c, :], rhs=ht[h][:, c, :],
                                                     start=(c == 0), stop=(c == KC - 1)),
                             reads=[bset[b], b_ht[h]], writes=[b_ps[pi]], inc=(c == KC - 1))
                    o = no % 4
                    no += 1
                    if no % 2 == 0:
                        s.op("dve", lambda e: e.tensor_copy(out=ot[o][:], in_=p[:]), reads=[b_ps[pi]], writes=[b_ot[o]])
                    else:
                        s.op("act", lambda e: e.copy(out=ot[o][:], in_=p[:]), reads=[b_ps[pi]], writes=[b_ot[o]])
                    pt_t, pt_r = PT(g0 + b)
                    s.dma("sp", pt_t[pt_r:pt_r + 128, t0:t0 + NT], ot[o][:], reads=[b_ot[o]])
        s.barrier()


def cast_dram(k, dst, src, n_elems, q="pool", width=1024):
    s = k.s
    ROW = width
    assert n_elems % ROW == 0
    rows = n_elems // ROW
    CH = 2048
    for r0 in range(0, rows, CH):
        r1 = min(rows, r0 + CH)
        s.dma(q, dst[r0:r1, :], src[r0:r1, :])


def stage_merge(k, hT, yT, br_kc, Wg, Wbr, Wo, xT, xnT, D, T):
    s = k.s
    KC = D // 128
    OB = D // 128
    YC = sum(br_kc)
    yoff = [sum(br_kc[:i]) for i in range(len(br_kc))]
    NBR = len(br_kc)
    with contextlib.ExitStack() as st:
        ht = k.sb(st, [128, KC, NT], BF16)
        yt = k.sb(st, [128, YC, NT], BF16)
        mg = k.sb(st, [128, KC, NT], BF16)
        wg = [k.sb(st, [128, KC, 128], BF16) for _ in range(3)]
        wbr = [k.sb(st, [128, max(br_kc), 128], BF16) for _ in range(3)]
        gs = [k.sb(st, [128, NT], F32) for _ in range(2)]
        acc = [k.sb(st, [128, NT], F32) for _ in range(2)]
        tmp = [k.sb(st, [128, NT], F32) for _ in range(2)]
        xt = [k.sb(st, [128, NT], F32) for _ in range(2)]
        xn = [k.sb(st, [128, NT], F32) for _ in range(2)]
        psg = [k.ps(st, [128, NT]) for _ in range(2)]
        psp = [k.ps(st, [128, NT]) for _ in range(2)]
        pso = [k.ps(st, [128, NT]) for _ in range(2)]
        b_ht, b_yt = Buf(), Buf()
        b_mg = bufs(KC)
        b_wg, b_wbr, b_gs, b_acc, b_tmp, b_xt, b_xn = bufs(3), bufs(3), bufs(2), bufs(2), bufs(2), bufs(2), bufs(2)
        b_psg, b_psp, b_pso = bufs(2), bufs(2), bufs(2)
        nw = 0
        ng = 0
        na = 0
        for t0 in range(0, T, NT):
            dma_rows(s, "act", ht, hT[:, t0:t0 + NT], KC, writes=[b_ht])
            dma_rows(s, "act", yt, yT[:, t0:t0 + NT], YC, writes=[b_yt])
            for ob in range(OB):
                a = na % 2
                na += 1
                for br in range(NBR):
                    w = nw % 3
                    nw += 1
                    g = ng % 2
                    ng += 1
                    kcb = br_kc[br]
                    s.dma("sp", wg[w][:], Wg[br, ob], writes=[b_wg[w]])
                    s.dma("sp", wbr[w][:, 0:kcb, :], Wbr[br][ob], writes=[b_wbr[w]])
                    for c in range(KC):
                        s.op("pe", lambda e: e.matmul(psg[g][:], lhsT=wg[w][:, c, :], rhs=ht[:, c, :],
                                                     start=(c == 0), stop=(c == KC - 1)),
                             reads=[b_wg[w], b_ht], writes=[b_psg[g]], inc=(c == KC - 1))
                    for c in range(kcb):
                        s.op("pe", lambda e: e.matmul(psp[g][:], lhsT=wbr[w][:, c, :], rhs=yt[:, yoff[br] + c, :],
                                                     start=(c == 0), stop=(c == kcb - 1)),
                             reads=[b_wbr[w], b_yt], writes=[b_psp[g]], inc=(c == kcb - 1))
                    s.op("act", lambda e: e.activation(out=gs[g][:], in_=psg[g][:], func=AF.Sigmoid),
                         reads=[b_psg[g]], writes=[b_gs[g]])
                    last = (br == NBR - 1)
                    if br == 0:
                        dst, bdst = (mg[:, ob, :], b_mg[ob]) if last else (acc[a][:], b_acc[a])
                        s.op("dve", lambda e: e.tensor_tensor(out=dst, in0=psp[g][:], in1=gs[g][:], op=ALU.mult),
                             reads=[b_psp[g], b_gs[g]], writes=[bdst])
                    else:
                        s.op("dve", lambda e: e.tensor_tensor(out=tmp[g][:], in0=psp[g][:], in1=gs[g][:], op=ALU.mult),
                             reads=[b_psp[g], b_gs[g]], writes=[b_tmp[g]])
                        if last:
                            s.op("pool", lambda e: e.tensor_tensor(out=mg[:, ob, :], in0=acc[a][:], in1=tmp[g][:], op=ALU.add),
                                 reads=[b_acc[a], b_tmp[g]], writes=[b_mg[ob]])
                        else:
                            s.op("pool", lambda e: e.tensor_tensor(out=acc[a][:], in0=acc[a][:], in1=tmp[g][:], op=ALU.add),
                                 reads=[b_acc[a], b_tmp[g]], writes=[b_acc[a]])
            for ob in range(OB):
                w = nw % 3
                nw += 1
                g = ng % 2
                ng += 1
                s.dma("sp", wg[w][:], Wo[ob], writes=[b_wg[w]])
                s.dma("act", xt[g][:], xT[ob * 128:(ob + 1) * 128, t0:t0 + NT], writes=[b_xt[g]])
                for c in range(KC):
                    s.op("pe", lambda e: e.matmul(pso[g][:], lhsT=wg[w][:, c, :], rhs=mg[:, c, :],
                                                 start=(c == 0), stop=(c == KC - 1)),
                         reads=[b_wg[w], b_mg[c]], writes=[b_pso[g]], inc=(c == KC - 1))
                s.op("dve", lambda e: e.tensor_tensor(out=xn[g][:], in0=pso[g][:], in1=xt[g][:], op=ALU.add),
                     reads=[b_pso[g], b_xt[g]], writes=[b_xn[g]])
                s.dma("sp", xnT[ob * 128:(ob + 1) * 128, t0:t0 + NT], xn[g][:], reads=[b_xn[g]])
        s.barrier()


def stage_lru(k, PT, row_ax, row_ag, yT, row_y, prm, T, NBLK):
    s = k.s
    NCH = NBLK * 2
    with contextlib.ExitStack() as st:
        cw = k.sb(st, [128, NCH, 4], F32)
        cb, ba, bx, lam, c1, c2, tt = [k.sb(st, [128, NCH], F32) for _ in range(7)]
        b_prm = Buf()
        for dst, nm in ((cw, "conv_w"), (cb, "conv_b"), (ba, "ba"), (bx, "bx"), (lam, "lam")):
            s.dma("sp", dst[:], prm[nm], writes=[b_prm])
        s.op("act", lambda e: e.activation(out=tt[:], in_=lam[:], func=AF.Exp, scale=-1.0), reads=[b_prm], writes=[b_prm])
        s.op("act", lambda e: e.activation(out=tt[:], in_=tt[:], func=AF.Ln, bias=1.0), reads=[b_prm], writes=[b_prm])
        s.op("dve", lambda e: e.tensor_scalar(out=c1[:], in0=tt[:], scalar1=-8.0, scalar2=None, op0=ALU.mult), reads=[b_prm], writes=[b_prm])
        s.op("dve", lambda e: e.tensor_scalar(out=c2[:], in0=tt[:], scalar1=-16.0, scalar2=None, op0=ALU.mult), reads=[b_prm], writes=[b_prm])
        waf = k.sb(st, [128, 2, 2, 128], F32)
        wab = [k.sb(st, [128, 2, 2, 128], BF16) for _ in range(2)]
        xin = [k.sb(st, [128, 3 + NT], F32) for _ in range(2)]
        u = [k.sb(st, [128, NT], F32) for _ in range(2)]
        ub = [k.sb(st, [128, NT], BF16) for _ in range(2)]
        rr, ii, aa, a2, mm, bt, gt, sg = [k.sb(st, [128, NT], F32) for _ in range(8)]
        hs = [[k.sb(st, [128, NT], F32) for _ in range(2)] for _ in range(2)]
        yo = [k.sb(st, [128, NT], BF16) for _ in range(2)]
        psr, psi = k.ps(st, [128, NT]), k.ps(st, [128, NT])
        b_waf, b_psr, b_psi, b_rr, b_ii, b_aa, b_a2, b_mm, b_bt, b_gt, b_sg = bufs(11)
        b_wab, b_xin, b_u, b_ub, b_yo = bufs(2), bufs(2), bufs(2), bufs(2), bufs(2)
        b_hs = [bufs(2), bufs(2)]
        for nb in range(NBLK):
            for wi, nm in enumerate(("wa", "wx")):
                s.dma("sp", waf[:], prm[nm][nb].rearrange("o p c m -> p o c m"), writes=[b_waf])
                s.op("act", lambda e: e.copy(out=wab[wi][:], in_=waf[:]), reads=[b_waf], writes=[b_wab[wi]])
            for ti, t0 in enumerate(range(0, T, NT)):
                par = ti % 2
                for kc in range(2):
                    ch = nb * 2 + kc
                    if ti == 0:
                        s.op("pool", lambda e: e.memset(xin[kc][:, 0:3], 0.0), writes=[b_xin[kc]])
                    else:
                        s.op("pool", lambda e: e.tensor_copy(out=xin[kc][:, 0:3], in_=xin[kc][:, NT:NT + 3]),
                             reads=[b_xin[kc]], writes=[b_xin[kc]])
                    s.dma("sp", xin[kc][:, 3:3 + NT], PT[row_ax + ch * 128: row_ax + (ch + 1) * 128, t0:t0 + NT],
                          reads=[b_xin[kc]], writes=[b_xin[kc]])
                    s.op("dve", lambda e: e.tensor_scalar(out=u[kc][:], in0=xin[kc][:, 3:3 + NT], scalar1=cw[:, ch, 3:4],
                                                          scalar2=cb[:, ch:ch + 1], op0=ALU.mult, op1=ALU.add),
                         reads=[b_xin[kc], b_prm], writes=[b_u[kc]])
                    for j in (2, 1, 0):
                        s.op("dve", lambda e: e.scalar_tensor_tensor(out=u[kc][:], in0=xin[kc][:, j:j + NT], scalar=cw[:, ch, j:j + 1],
                                                                     in1=u[kc][:], op0=ALU.mult, op1=ALU.add),
                             reads=[b_xin[kc], b_prm, b_u[kc]], writes=[b_u[kc]])
                    s.op("act", lambda e: e.copy(out=ub[kc][:], in_=u[kc][:]), reads=[b_u[kc]], writes=[b_ub[kc]])
                for oc in range(2):
                    ch = nb * 2 + oc
                    for kc in range(2):
                        s.op("pe", lambda e: e.matmul(psr[:], lhsT=wab[0][:, oc, kc, :], rhs=ub[kc][:], start=(kc == 0), stop=(kc == 1)),
                             reads=[b_wab[0], b_ub[kc]], writes=[b_psr], inc=(kc == 1))
                    for kc in range(2):
                        s.op("pe", lambda e: e.matmul(psi[:], lhsT=wab[1][:, oc, kc, :], rhs=ub[kc][:], start=(kc == 0), stop=(kc == 1)),
                             reads=[b_wab[1], b_ub[kc]], writes=[b_psi], inc=(kc == 1))
                    s.op("act", lambda e: e.activation(out=rr[:], in_=psr[:], func=AF.Sigmoid, bias=ba[:, ch:ch + 1]),
                         reads=[b_psr, b_prm], writes=[b_rr])
                    s.op("act", lambda e: e.activation(out=ii[:], in_=psi[:], func=AF.Sigmoid, bias=bx[:, ch:ch + 1]),
                         reads=[b_psi, b_prm], writes=[b_ii])
                    s.op("act", lambda e: e.activation(out=aa[:], in_=rr[:], func=AF.Exp, scale=c1[:, ch:ch + 1]),
                         reads=[b_rr, b_prm], writes=[b_aa])
                    s.op("act", lambda e: e.activation(out=a2[:], in_=rr[:], func=AF.Exp, scale=c2[:, ch:ch + 1]),
                         reads=[b_rr, b_prm], writes=[b_a2])
                    s.op("act", lambda e: e.activation(out=mm[:], in_=a2[:], func=AF.Sqrt, scale=-1.0, bias=1.0),
                         reads=[b_a2], writes=[b_mm])
                    s.op("dve", lambda e: e.tensor_tensor(out=bt[:], in0=mm[:], in1=ii[:], op=ALU.mult),
                         reads=[b_mm, b_ii], writes=[b_bt])
                    s.op("dve", lambda e: e.tensor_tensor(out=bt[:], in0=bt[:], in1=u[oc][:], op=ALU.mult),
                         reads=[b_bt, b_u[oc]], writes=[b_bt])
                    init = 0.0 if ti == 0 else hs[oc][1 - par][:, NT - 1:NT]
                    s.op("dve", lambda e: e.tensor_tensor_scan(out=hs[oc][par][:], data0=aa[:], data1=bt[:], initial=init,
                                                               op0=ALU.mult, op1=ALU.add),
                         reads=[b_aa, b_bt] + ([] if ti == 0 else [b_hs[oc][1 - par]]), writes=[b_hs[oc][par]])
                    s.dma("act", gt[:], PT[row_ag + ch * 128: row_ag + (ch + 1) * 128, t0:t0 + NT], writes=[b_gt])
                    s.op("act", lambda e: e.activation(out=sg[:], in_=gt[:], func=AF.Silu), reads=[b_gt], writes=[b_sg])
                    s.op("dve", lambda e: e.tensor_tensor(out=yo[oc][:], in0=hs[oc][par][:], in1=sg[:], op=ALU.mult),
                         reads=[b_hs[oc][par], b_sg], writes=[b_yo[oc]])
                    s.dma("sp", yT[row_y + ch * 128: row_y + (ch + 1) * 128, t0:t0 + NT], yo[oc][:], reads=[b_yo[oc]])
        s.barrier()


def stage_xattn(k, PT, row_q, row_g, yT, row_y, memnT, Wk, Wv, ident_d, T, D, M, H):
    s = k.s
    KC = D // 128
    MC = M // 128
    sc = 128 ** -0.5
    with contextlib.ExitStack() as st:
        ident = k.sb(st, [128, 128], F32)
        memn = k.sb(st, [128, KC, M], BF16)
        kT = k.sb(st, [128, H, M], BF16)
        vt = k.sb(st, [128, MC, H * 128], BF16)
        b_id, b_memn, b_kT, b_vt = bufs(4)
        s.dma("sp", ident[:], ident_d, writes=[b_id])
        dma_rows(s, "sp", memn, memnT, KC, writes=[b_memn])
        with contextlib.ExitStack() as st1:
            wf = k.sb(st1, [128, KC, 128], F32)
            wb = k.sb(st1, [128, KC, 128], BF16)
            vf = [k.sb(st1, [128, H * 128], F32) for _ in range(2)]
            vb = [k.sb(st1, [128, H * 128], BF16) for _ in range(2)]
            psk = k.ps(st1, [128, M])
            psv = [k.ps(st1, [128, H * 128]) for _ in range(MC)]
            b_wf, b_wb, b_psk = bufs(3)
            b_vf, b_vb, b_psv = bufs(2), bufs(2), bufs(MC)
            for hd in range(H):
                s.dma("sp", wf[:], Wk[hd], writes=[b_wf])
                s.op("act", lambda e: e.copy(out=wb[:], in_=wf[:]), reads=[b_wf], writes=[b_wb])
                for c in range(KC):
                    s.op("pe", lambda e: e.matmul(psk[:], lhsT=wb[:, c, :], rhs=memn[:, c, :], start=(c == 0), stop=(c == KC - 1)),
                         reads=[b_wb, b_memn], writes=[b_psk], inc=(c == KC - 1))
                s.op("dve", lambda e: e.tensor_copy(out=kT[:, hd, :], in_=psk[:]), reads=[b_psk], writes=[b_kT])
            for c in range(KC):
                i = c % 2
                s.dma("sp", vf[i][:], Wv[c], writes=[b_vf[i]])
                s.op("act", lambda e: e.copy(out=vb[i][:], in_=vf[i][:]), reads=[b_vf[i]], writes=[b_vb[i]])
                for mc in range(MC):
                    s.op("pe", lambda e: e.matmul(psv[mc][:], lhsT=memn[:, c, mc * 128:(mc + 1) * 128], rhs=vb[i][:],
                                                 start=(c == 0), stop=(c == KC - 1)),
                         reads=[b_vb[i], b_memn], writes=[b_psv[mc]], inc=True)
            for mc in range(MC):
                s.op("dve", lambda e: e.tensor_copy(out=vt[:, mc, :], in_=psv[mc][:]), reads=[b_psv[mc]], writes=[b_vt])
            s.barrier()
        qf = k.sb(st, [128, H, NT], F32)
        qb = k.sb(st, [128, H, NT], BF16)
        gf = k.sb(st, [128, H, NT], F32)
        sg = k.sb(st, [128, H, NT], F32)
        pf = [k.sb(st, [128, M], F32) for _ in range(2)]
        pn = [k.sb(st, [128, M], F32) for _ in range(2)]
        pT = [k.sb(st, [128, MC, 128], BF16) for _ in range(2)]
        mx, nmx, rsum, rinv = [[k.sb(st, [128, 1], F32) for _ in range(2)] for _ in range(4)]
        yo = [k.sb(st, [128, NT], BF16) for _ in range(2)]
        pss = [k.ps(st, [128, M]) for _ in range(2)]
        pst = [k.ps(st, [128, MC, 128]) for _ in range(2)]
        pso = [k.ps(st, [128, 128]) for _ in range(2)]
        b_qf, b_qb, b_gf, b_sg = bufs(4)
        b_pf, b_pn, b_pT, b_mx, b_nmx, b_rsum, b_rinv, b_yo, b_pss, b_pst, b_pso = [bufs(2) for _ in range(11)]
        it = 0
        for t0 in range(0, T, NT):
            s.dma("sp", qf[:], PT[row_q:row_q + H * 128, t0:t0 + NT].rearrange("(h p) t -> p h t", p=128), writes=[b_qf])
            s.dma("act", gf[:], PT[row_g:row_g + H * 128, t0:t0 + NT].rearrange("(h p) t -> p h t", p=128), writes=[b_gf])
            s.op("act", lambda e: e.copy(out=qb[:], in_=qf[:]), reads=[b_qf], writes=[b_qb])
            s.op("act", lambda e: e.activation(out=sg[:], in_=gf[:], func=AF.Silu), reads=[b_gf], writes=[b_sg])
            for hd in range(H):
                yi = hd % 2
                for tb in range(NT // 128):
                    i = it % 2
                    it += 1
                    tsl = slice(tb * 128, (tb + 1) * 128)
                    s.op("pe", lambda e: e.matmul(pss[i][:], lhsT=qb[:, hd, tsl], rhs=kT[:, hd, :], start=True, stop=True),
                         reads=[b_qb, b_kT], writes=[b_pss[i]])
                    s.op("dve", lambda e: e.tensor_reduce(out=mx[i][:], in_=pss[i][:], axis=AX.X, op=ALU.max),
                         reads=[b_pss[i]], writes=[b_mx[i]])
                    s.op("dve", lambda e: e.tensor_scalar(out=nmx[i][:], in0=mx[i][:], scalar1=-sc, scalar2=None, op0=ALU.mult),
                         reads=[b_mx[i]], writes=[b_nmx[i]])
                    s.op("act", lambda e: e.activation(out=pf[i][:], in_=pss[i][:], func=AF.Exp, scale=sc, bias=nmx[i][:],
                                                       accum_out=rsum[i][:]),
                         reads=[b_pss[i], b_nmx[i]], writes=[b_pf[i], b_rsum[i]])
                    s.op("dve", lambda e: e.reciprocal(out=rinv[i][:], in_=rsum[i][:]), reads=[b_rsum[i]], writes=[b_rinv[i]])
                    s.op("dve", lambda e: e.tensor_scalar(out=pn[i][:], in0=pf[i][:], scalar1=rinv[i][:], scalar2=None, op0=ALU.mult),
                         reads=[b_pf[i], b_rinv[i]], writes=[b_pn[i]])
                    for mc in range(MC):
                        s.op("pe", lambda e: e.transpose(pst[i][:, mc, :], pn[i][:, mc * 128:(mc + 1) * 128], ident[:]),
                             reads=[b_pn[i], b_id], writes=[b_pst[i]], inc=True)
                    s.op("act", lambda e: e.copy(out=pT[i][:], in_=pst[i][:]), reads=[b_pst[i]], writes=[b_pT[i]])
                    for mc in range(MC):
                        s.op("pe", lambda e: e.matmul(pso[i][:], lhsT=vt[:, mc, hd * 128:(hd + 1) * 128], rhs=pT[i][:, mc, :],
                                                     start=(mc == 0), stop=(mc == MC - 1)),
                             reads=[b_vt, b_pT[i]], writes=[b_pso[i]], inc=(mc == MC - 1))
                    s.op("dve", lambda e: e.tensor_tensor(out=yo[yi][:, tsl], in0=pso[i][:], in1=sg[:, hd, tsl], op=ALU.mult),
                         reads=[b_pso[i], b_sg], writes=[b_yo[yi]])
                s.dma("sp", yT[row_y + hd * 128: row_y + (hd + 1) * 128, t0:t0 + NT], yo[yi][:], reads=[b_yo[yi]])
        s.barrier()


LCH = 64
MLSTM_L = 128
MLSTM_FP32R = True
NEG = -1.0e30
RWKV_G = 2
RWKV_L = 128
RWKV_FP32R = True


def stage_mlstm(k, PT, rows, yT, row_y, prm, cst, T, HC):
    s = k.s
    ML = MLSTM_L
    NCK = T // ML
    QC = 2 * HC
    with contextlib.ExitStack() as st:
        ident = k.sb(st, [128, 128], F32)
        sel = k.sb(st, [HC, HC, 128], F32)
        i4 = k.sb(st, [HC, HC], F32)
        negmask = k.sb(st, [ML, ML], F32)
        ones = k.sb(st, [128, 128], F32)
        cw = k.sb(st, [128, 2 * QC, 4], F32)
        cb = k.sb(st, [128, 2 * QC], F32)
        gnw = k.sb(st, [128, HC * 4], F32)
        bi = k.sb(st, [HC, 1], F32)
        bfn = k.sb(st, [HC, 1], F32)
        b_c = Buf()
        for dst, src in ((ident, cst["ident"]), (sel, cst["sel"]), (i4, cst["i4"]), (negmask, cst["negmask"]),
                         (cw, prm["conv_w"]), (cb, prm["conv_b"]), (gnw, prm["gn_w"]), (bi, prm["b_i"]), (bfn, prm["b_f"])):
            s.dma("sp", dst[:], src, writes=[b_c])
        s.op("pool", lambda e: e.memset(ones[:], 1.0), writes=[b_c])
        ones_r = k.sb(st, [128, 128], F32)
        s.op("act", lambda e: e.copy(out=ones_r[:].bitcast(mybir.dt.float32r), in_=ones[:]), reads=[b_c], writes=[b_c])
        s.op("dve", lambda e: e.tensor_scalar(out=bfn[:], in0=bfn[:], scalar1=-1.0, scalar2=None, op0=ALU.mult), reads=[b_c], writes=[b_c])
        gi = k.sb(st, [HC, T], F32)
        gf = k.sb(st, [HC, T], F32)
        gm = k.sb(st, [HC, T], F32)
        ge = k.sb(st, [HC, T], F32)
        b_g = Buf()
        s.dma("sp", gi[:], PT[rows["ifg"]:rows["ifg"] + HC, :], writes=[b_g])
        s.dma("sp", gf[:], PT[rows["ifg"] + HC:rows["ifg"] + 2 * HC, :], writes=[b_g])
        s.op("act", lambda e: e.activation(out=gf[:], in_=gf[:], func=AF.Exp, scale=-1.0, bias=bfn[:]), reads=[b_g, b_c], writes=[b_g])
        s.op("act", lambda e: e.activation(out=gf[:], in_=gf[:], func=AF.Ln, bias=1.0), reads=[b_g], writes=[b_g])
        s.op("dve", lambda e: e.tensor_scalar(out=gf[:], in0=gf[:], scalar1=-1.0, scalar2=None, op0=ALU.mult), reads=[b_g], writes=[b_g])
        s.op("pool", lambda e: e.memset(gm[:], 1.0), writes=[b_g])
        s.op("dve", lambda e: e.tensor_tensor_scan(out=gf[:], data0=gm[:], data1=gf[:], initial=0.0, op0=ALU.mult, op1=ALU.add),
             reads=[b_g], writes=[b_g])
        s.op("dve", lambda e: e.scalar_tensor_tensor(out=gi[:], in0=gi[:], scalar=bi[:], in1=gf[:], op0=ALU.add, op1=ALU.subtract),
             reads=[b_g, b_c], writes=[b_g])
        s.op("dve", lambda e: e.tensor_tensor_scan(out=gm[:], data0=gi[:], data1=gi[:], initial=NEG, op0=ALU.max, op1=ALU.max),
             reads=[b_g], writes=[b_g])
        s.op("dve", lambda e: e.tensor_tensor(out=ge[:], in0=gf[:], in1=gm[:], op=ALU.add), reads=[b_g], writes=[b_g])
        cols = k.sb(st, [ML, NCK, 2 * HC], F32)
        b_cols = Buf()
        with contextlib.ExitStack() as st1:
            psc = [k.ps(st1, [ML, 8, 2 * HC]) for _ in range(2)]
            b_psc = bufs(2)
            for c8 in range(0, NCK, 8):
                i = (c8 // 8) % 2
                for cc in range(8):
                    c = c8 + cc
                    s.op("pe", lambda e: e.matmul(psc[i][:, cc, 0:HC], lhsT=gi[:, c * ML:(c + 1) * ML], rhs=i4[:, 0:HC], start=True, stop=True),
                         reads=[b_g, b_c], writes=[b_psc[i]])
                    s.op("pe", lambda e: e.matmul(psc[i][:, cc, HC:2 * HC], lhsT=ge[:, c * ML:(c + 1) * ML], rhs=i4[:, 0:HC], start=True, stop=True),
                         reads=[b_g, b_c], writes=[b_psc[i]])
                s.op("dve", lambda e: e.tensor_copy(out=cols[:, c8:c8 + 8, :], in_=psc[i][:]), reads=[b_psc[i]], writes=[b_cols])
            s.op("act", lambda e: e.activation(out=cols[:, :, HC:2 * HC], in_=cols[:, :, HC:2 * HC], func=AF.Exp, scale=-1.0),
                 reads=[b_cols], writes=[b_cols])
            s.barrier()
        xin = k.sb(st, [128, 4, 3 + NT], F32)
        qk = k.sb(st, [128, 4, NT], F32)
        vT = k.sb(st, [128, 4, NT], F32)
        oT = k.sb(st, [128, 4, NT], F32)
        gT = k.sb(st, [128, 4, NT], F32)
        negMb = k.sb(st, [128, NT], F32)
        ms = k.sb(st, [128, NT // ML + 1], F32)
        keep2 = [k.sb(st, [128, 1], F32) for _ in range(2)]
        b_keep2 = bufs(2)
        ho = k.sb(st, [128, 4, NT], F32)
        Cst = k.sb(st, [128, 2, 512], F32)
        Cr = k.sb(st, [128, 2, 512], F32)
        b_Cr = Buf()
        R = (lambda ap: ap.bitcast(mybir.dt.float32r)) if MLSTM_FP32R else (lambda ap: ap)
        nst = k.sb(st, [128, 2], F32)
        ktok2 = [k.sb(st, [ML, 256], F32) for _ in range(2)]
        b_ktok2 = bufs(2)
        khat2 = [k.sb(st, [ML, 256], F32) for _ in range(2)]
        b_khat2 = bufs(2)
        vtok2 = [k.sb(st, [ML, 512], F32) for _ in range(2)]
        b_vtok2 = bufs(2)
        wk2 = [k.sb(st, [ML, 1], F32) for _ in range(2)]
        b_wk2 = bufs(2)
        argE = k.sb(st, [ML, ML], F32)
        ET = k.sb(st, [ML, ML], F32)
        scT2 = [k.sb(st, [ML, ML], F32) for _ in range(2)]
        b_scT2 = bufs(2)
        wint = k.sb(st, [128, ML], F32)
        qt2 = [k.sb(st, [128, 2, ML], F32) for _ in range(2)]
        b_qt2 = bufs(2)
        den = k.sb(st, [ML, 1], F32)
        rden = k.sb(st, [ML, 1], F32)
        htok = k.sb(st, [ML, 512], F32)
        mean = k.sb(st, [128, NT], F32)
        yc = k.sb(st, [128, 4, NT], F32)
        ysq = k.sb(st, [128, NT], F32)
        rstd = k.sb(st, [128, NT], F32)
        yo = [k.sb(st, [128, NT], BF16) for _ in range(2)]
        ps_b = k.ps(st, [128, NT])
        ps_t = k.ps(st, [ML, 512])
        ps_s = k.ps(st, [ML, ML])
        ps_n = k.ps(st, [ML, 512])
        ps_d = k.ps(st, [ML, 8])
        ps_c = k.ps(st, [128, 512])
        ps_h = k.ps(st, [128, 4, ML])
        ps_m = k.ps(st, [128, 8])
        (b_xin, b_qk, b_vT, b_oT, b_gT, b_negMb, b_ms, b_keep, b_ho, b_C, b_n, b_ktok, b_khat, b_vtok, b_wk, b_argE, b_ET, b_scT,
         b_wint, b_qt, b_den, b_rden, b_htok, b_mean, b_yc, b_ysq, b_rstd, b_psb, b_pst, b_pss, b_psn, b_psd, b_psc2, b_psh, b_psm) = bufs(35)
        b_yo = bufs(2)
        for h in range(HC):
            qrows = [rows["q"] + h * 256, rows["q"] + h * 256 + 128, rows["k"] + h * 256, rows["k"] + h * 256 + 128]
            cidx = [h * 2, h * 2 + 1, QC + h * 2, QC + h * 2 + 1]
            s.op("pool", lambda e: e.memset(Cst[:], 0.0), reads=[b_C], writes=[b_C])
            s.op("act", lambda e: e.copy(out=R(Cr[:]), in_=Cst[:]), reads=[b_C, b_Cr], writes=[b_Cr])
            s.op("pool", lambda e: e.memset(nst[:], 0.0), reads=[b_n], writes=[b_n])
            s.op("pool", lambda e: e.memset(ms[:, 0:1], NEG), reads=[b_ms], writes=[b_ms])
            for ti, t0 in enumerate(range(0, T, NT)):
                for j in range(4):
                    if ti == 0:
                        s.op("pool", lambda e: e.memset(xin[:, j, 0:3], 0.0), reads=[b_xin], writes=[b_xin])
                    else:
                        s.op("pool", lambda e: e.tensor_copy(out=xin[:, j, 0:3], in_=xin[:, j, NT:NT + 3]), reads=[b_xin], writes=[b_xin])
                for j in range(4):
                    s.dma("sp", xin[:, j, 3:3 + NT], PT[qrows[j]:qrows[j] + 128, t0:t0 + NT], reads=[b_xin], writes=[b_xin])
                s.dma("act", vT[:], PT[rows["v"] + h * 512: rows["v"] + (h + 1) * 512, t0:t0 + NT].rearrange("(c p) t -> p c t", p=128), writes=[b_vT])
                s.dma("act", oT[:], PT[rows["o"] + h * 512: rows["o"] + (h + 1) * 512, t0:t0 + NT].rearrange("(c p) t -> p c t", p=128), writes=[b_oT])
                s.dma("act", gT[:], PT[rows["g"] + h * 512: rows["g"] + (h + 1) * 512, t0:t0 + NT].rearrange("(c p) t -> p c t", p=128), writes=[b_gT])
                for j in range(4):
                    ci = cidx[j]
                    s.op("dve", lambda e: e.tensor_scalar(out=qk[:, j, :], in0=xin[:, j, 3:3 + NT], scalar1=cw[:, ci, 3:4],
                                                          scalar2=cb[:, ci:ci + 1], op0=ALU.mult, op1=ALU.add),
                         reads=[b_xin, b_c, b_qk], writes=[b_qk])
                    for jj in (2, 1, 0):
                        s.op("dve", lambda e: e.scalar_tensor_tensor(out=qk[:, j, :], in0=xin[:, j, jj:jj + NT], scalar=cw[:, ci, jj:jj + 1],
                                                                     in1=qk[:, j, :], op0=ALU.mult, op1=ALU.add),
                             reads=[b_xin, b_c, b_qk], writes=[b_qk])
                s.op("act", lambda e: e.activation(out=qk[:], in_=qk[:], func=AF.Silu), reads=[b_qk], writes=[b_qk])
                s.op("dve", lambda e: e.tensor_scalar(out=qk[:, 2:4, :], in0=qk[:, 2:4, :], scalar1=256 ** -0.5, scalar2=None, op0=ALU.mult),
                     reads=[b_qk], writes=[b_qk])
                s.op("act", lambda e: e.activation(out=oT[:], in_=oT[:], func=AF.Sigmoid), reads=[b_oT], writes=[b_oT])
                s.op("act", lambda e: e.activation(out=gT[:], in_=gT[:], func=AF.Silu), reads=[b_gT], writes=[b_gT])
                s.op("pe", lambda e: e.matmul(ps_b[:], lhsT=sel[:, h, :], rhs=gm[:, t0:t0 + NT], start=True, stop=True),
                     reads=[b_c, b_g, b_psb], writes=[b_psb])
                if ti > 0:
                    s.op("dve", lambda e: e.tensor_scalar(out=ms[:, 0:1], in0=negMb[:, NT - 1:NT], scalar1=-1.0, scalar2=None, op0=ALU.mult),
                         reads=[b_negMb, b_ms], writes=[b_ms])
                s.op("act", lambda e: e.activation(out=negMb[:], in_=ps_b[:], func=AF.Copy, scale=-1.0), reads=[b_psb, b_negMb], writes=[b_negMb])
                nck = NT // ML
                s.op("dve", lambda e: e.tensor_scalar(out=ms[:, 1:nck + 1], in0=negMb[:, ML - 1:NT:ML], scalar1=-1.0, scalar2=None, op0=ALU.mult),
                     reads=[b_negMb, b_ms], writes=[b_ms])
                def m_indep(cc):
                    c = ti * nck + cc
                    cp = c % 2
                    csl = slice(cc * ML, (cc + 1) * ML)
                    ktok, khat, vtok, scT, qt, keep, wk = ktok2[cp], khat2[cp], vtok2[cp], scT2[cp], qt2[cp], keep2[cp], wk2[cp]
                    b_ktok, b_khat, b_vtok, b_scT, b_qt, b_keep, b_wk = (b_ktok2[cp], b_khat2[cp], b_vtok2[cp], b_scT2[cp], b_qt2[cp],
                                                                          b_keep2[cp], b_wk2[cp])
                    def t1():
                        for j in range(2):
                            s.op("pe", lambda e: e.transpose(ps_t[:, j * 128:(j + 1) * 128], qk[:, 2 + j, csl], ident[:]),
                                 reads=[b_qk, b_c, b_pst], writes=[b_pst])
                        s.op("act", lambda e: e.copy(out=ktok[:], in_=ps_t[:, 0:256]), reads=[b_pst, b_ktok], writes=[b_ktok])
                        for j in range(4):
                            s.op("pe", lambda e: e.transpose(ps_t[:, j * 128:(j + 1) * 128], vT[:, j, csl], ident[:]),
                                 reads=[b_vT, b_c, b_pst], writes=[b_pst])
                        s.op("act", lambda e: e.copy(out=R(vtok[:]), in_=ps_t[:]), reads=[b_pst, b_vtok], writes=[b_vtok])
                    def t2():
                        for j in range(2):
                            s.op("pe", lambda e: e.matmul(ps_s[:], lhsT=qk[:, 2 + j, csl], rhs=qk[:, j, csl], start=(j == 0), stop=(j == 1)),
                                 reads=[b_qk, b_pss], writes=[b_pss])
                        s.op("dve", lambda e: e.tensor_tensor(out=argE[:], in0=negMb[0:ML, csl], in1=negmask[:], op=ALU.add),
                             reads=[b_negMb, b_c, b_argE], writes=[b_argE])
                        s.op("act", lambda e: e.activation(out=ET[:], in_=argE[:], func=AF.Exp, bias=cols[:, c, h:h + 1]),
                             reads=[b_argE, b_cols, b_ET], writes=[b_ET])
                        s.op("dve", lambda e: e.tensor_tensor(out=R(scT[:]), in0=ps_s[:], in1=ET[:], op=ALU.mult),
                             reads=[b_pss, b_ET, b_scT], writes=[b_scT])
                    def t3():
                        s.op("act", lambda e: e.activation(out=wint[:], in_=negMb[:, csl], func=AF.Exp, bias=ms[:, cc:cc + 1]),
                             reads=[b_negMb, b_ms, b_wint], writes=[b_wint])
                        for j in range(2):
                            s.op("dve", lambda e: e.tensor_tensor(out=R(qt[:, j, :]), in0=qk[:, j, csl], in1=wint[:], op=ALU.mult),
                                 reads=[b_qk, b_wint, b_qt], writes=[b_qt])
                    def t4():
                        s.op("act", lambda e: e.activation(out=keep[:], in_=negMb[:, (cc + 1) * ML - 1:(cc + 1) * ML], func=AF.Exp, bias=ms[:, cc:cc + 1]),
                             reads=[b_negMb, b_ms, b_keep], writes=[b_keep])
                        s.op("act", lambda e: e.activation(out=wk[:], in_=negMb[0:ML, (cc + 1) * ML - 1:(cc + 1) * ML], func=AF.Exp, bias=cols[:, c, h:h + 1]),
                             reads=[b_negMb, b_cols, b_wk], writes=[b_wk])
                        s.op("dve", lambda e: e.tensor_scalar(out=R(khat[:]), in0=ktok[:], scalar1=wk[:], scalar2=None, op0=ALU.mult),
                             reads=[b_ktok, b_wk, b_khat], writes=[b_khat])
                    return [t1, t2, t3, t4]

                def m_dep(cc):
                    c = ti * nck + cc
                    cp = c % 2
                    csl = slice(cc * ML, (cc + 1) * ML)
                    ktok, khat, vtok, scT, qt, keep, wk = ktok2[cp], khat2[cp], vtok2[cp], scT2[cp], qt2[cp], keep2[cp], wk2[cp]
                    b_ktok, b_khat, b_vtok, b_scT, b_qt, b_keep, b_wk = (b_ktok2[cp], b_khat2[cp], b_vtok2[cp], b_scT2[cp], b_qt2[cp],
                                                                          b_keep2[cp], b_wk2[cp])
                    def d1():
                        for j in range(2):
                            s.op("pe", lambda e: e.matmul(ps_n[:], lhsT=R(qt[:, j, :]), rhs=R(Cr[:, j, :]), start=(j == 0), stop=False),
                                 reads=[b_qt, b_Cr, b_psn], writes=[b_psn], inc=False)
                        s.op("pe", lambda e: e.matmul(ps_n[:], lhsT=R(scT[:]), rhs=R(vtok[:]), start=False, stop=True),
                             reads=[b_scT, b_vtok, b_psn], writes=[b_psn])
                        for j in range(2):
                            s.op("pe", lambda e: e.matmul(ps_d[:, 0:1], lhsT=qt[:, j, :], rhs=nst[:, j:j + 1], start=(j == 0), stop=False),
                                 reads=[b_qt, b_n, b_psd], writes=[b_psd], inc=False)
                        s.op("pe", lambda e: e.matmul(ps_d[:, 0:1], lhsT=scT[:], rhs=ones[0:ML, 0:1], start=False, stop=True),
                             reads=[b_scT, b_c, b_psd], writes=[b_psd])
                    def d2():
                        s.op("act", lambda e: e.activation(out=den[:], in_=ps_d[:, 0:1], func=AF.Abs),
                             reads=[b_psd, b_den], writes=[b_den])
                        s.op("dve", lambda e: e.tensor_tensor(out=den[:], in0=den[:], in1=cols[:, c, HC + h:HC + h + 1], op=ALU.max),
                             reads=[b_den, b_cols], writes=[b_den])
                        s.op("dve", lambda e: e.reciprocal(out=rden[:], in_=den[:]), reads=[b_den, b_rden], writes=[b_rden])
                        s.op("act", lambda e: e.activation(out=htok[:], in_=ps_n[:], func=AF.Copy, scale=rden[:]),
                             reads=[b_psn, b_rden, b_htok], writes=[b_htok])
                    def d3():
                        for j in range(4):
                            s.op("pe", lambda e: e.transpose(ps_h[:, j, :], htok[:, j * 128:(j + 1) * 128], ident[0:ML, 0:ML]),
                                 reads=[b_htok, b_c, b_psh], writes=[b_psh])
                        s.op("dve", lambda e: e.tensor_tensor(out=ho[:, :, csl].bitcast(mybir.dt.float32r), in0=ps_h[:], in1=oT[:, :, csl], op=ALU.mult),
                             reads=[b_psh, b_oT, b_ho], writes=[b_ho])
                    def d4():
                        for j in range(2):
                            s.op("pe", lambda e: e.matmul(ps_c[:], lhsT=R(khat[:, j * 128:(j + 1) * 128]), rhs=R(vtok[:]), start=True, stop=True),
                                 reads=[b_khat, b_vtok, b_psc2], writes=[b_psc2])
                            s.op("dve", lambda e: e.scalar_tensor_tensor(out=Cst[:, j, :], in0=Cst[:, j, :], scalar=keep[:], in1=ps_c[:],
                                                                         op0=ALU.mult, op1=ALU.add),
                                 reads=[b_C, b_keep, b_psc2], writes=[b_C])
                            s.op("act", lambda e: e.copy(out=R(Cr[:, j, :]), in_=Cst[:, j, :]), reads=[b_C, b_Cr], writes=[b_Cr])
                            s.op("pe", lambda e: e.matmul(ps_m[:, 0:1], lhsT=khat[:, j * 128:(j + 1) * 128], rhs=ones[0:ML, 0:1], start=True, stop=True),
                                 reads=[b_khat, b_c, b_psm], writes=[b_psm])
                            s.op("dve", lambda e: e.scalar_tensor_tensor(out=nst[:, j:j + 1], in0=nst[:, j:j + 1], scalar=keep[:], in1=ps_m[:, 0:1],
                                                                         op0=ALU.mult, op1=ALU.add),
                                 reads=[b_n, b_keep, b_psm], writes=[b_n])
                    return [d1, d2, d3, d4]

                for f_ in m_indep(0):
                    f_()
                for cc in range(nck):
                    dst_ = m_dep(cc)
                    ist_ = m_indep(cc + 1) if cc + 1 < nck else []
                    for j_ in range(max(len(dst_), len(ist_))):
                        if j_ < len(dst_):
                            dst_[j_]()
                        if j_ < len(ist_):
                            ist_[j_]()
                for j in range(4):
                    s.op("pe", lambda e: e.matmul(ps_b[:], lhsT=ones_r[:].bitcast(mybir.dt.float32r), rhs=ho[:, j, :].bitcast(mybir.dt.float32r), start=(j == 0), stop=(j == 3)),
                         reads=[b_c, b_ho, b_psb], writes=[b_psb], inc=(j == 3))
                s.op("act", lambda e: e.activation(out=mean[:], in_=ps_b[:], func=AF.Copy, scale=1.0 / 512), reads=[b_psb, b_mean], writes=[b_mean])
                for j in range(4):
                    s.op("dve", lambda e: e.tensor_tensor(out=yc[:, j, :], in0=ho[:, j, :], in1=mean[:], op=ALU.subtract),
                         reads=[b_ho, b_mean, b_yc], writes=[b_yc])
                for j in range(4):
                    s.op("act", lambda e: e.activation(out=ysq[:].bitcast(mybir.dt.float32r), in_=yc[:, j, :], func=AF.Square), reads=[b_yc, b_ysq], writes=[b_ysq])
                    s.op("pe", lambda e: e.matmul(ps_b[:], lhsT=ones_r[:].bitcast(mybir.dt.float32r), rhs=ysq[:].bitcast(mybir.dt.float32r), start=(j == 0), stop=(j == 3)),
                         reads=[b_c, b_ysq, b_psb], writes=[b_psb])
                s.op("act", lambda e: e.activation(out=rstd[:], in_=ps_b[:], func=AF.Sqrt, scale=1.0 / 512, bias=1e-6),
                     reads=[b_psb, b_rstd], writes=[b_rstd])
                s.op("dve", lambda e: e.reciprocal(out=rstd[:], in_=rstd[:]), reads=[b_rstd], writes=[b_rstd])
                for j in range(4):
                    yi = j % 2
                    s.op("dve", lambda e: e.scalar_tensor_tensor(out=yc[:, j, :], in0=yc[:, j, :], scalar=gnw[:, h * 4 + j:h * 4 + j + 1],
                                                                 in1=rstd[:], op0=ALU.mult, op1=ALU.mult),
                         reads=[b_yc, b_c, b_rstd], writes=[b_yc])
                    s.op("dve", lambda e: e.tensor_tensor(out=yo[yi][:], in0=yc[:, j, :], in1=gT[:, j, :], op=ALU.mult),
                         reads=[b_yc, b_gT, b_yo[yi]], writes=[b_yo[yi]])
                    r0 = row_y + h * 512 + j * 128
                    s.dma("sp", yT[r0:r0 + 128, t0:t0 + NT], yo[yi][:], reads=[b_yo[yi]])
        s.barrier()


def stage_rwkv(k, PT, rows, yT, row_y, prm, cst, T, NH, GN_EPS):
    s = k.s
    G = RWKV_G
    assert G == 2
    NLVL = {64: 6, 128: 7}[RWKV_L]
    RL = RWKV_L
    NCK = NT // RL
    assert NH % G == 0 and T % NT == 0
    with contextlib.ExitStack() as st:
        ident = k.sb(st, [128, 128], F32)
        ones64 = k.sb(st, [64, 64], F32)
        rmask = k.sb(st, [64, NT], F32)
        maskA = k.sb(st, [RL, G, 2 * RL], F32)
        maskN = k.sb(st, [RL, G, RL], F32)
        P = {}
        for nm in ("mu_r", "mu_k", "mu_v", "w0", "a0", "k_k", "k_a", "r_k", "gn_w", "gn_b"):
            P[nm] = k.sb(st, [64, NH], F32)
        om = {nm: k.sb(st, [64, NH], F32) for nm in ("mu_r", "mu_k", "mu_v", "k_a")}
        nw0 = k.sb(st, [64, NH], F32)
        mud = {nm: k.sb(st, [96, 1], F32) for nm in ("mu_wd", "mu_ad")}
        omd = {nm: k.sb(st, [96, 1], F32) for nm in ("mu_wd", "mu_ad")}
        b_c = Buf()
        for dst, src in ((ident, cst["ident"]), (rmask, cst["resetmask"]), (maskA, cst["maskA"]), (maskN, cst["maskN"])):
            s.dma("sp", dst[:], src, writes=[b_c])
        for nm in P:
            s.dma("sp", P[nm][:], prm[nm], writes=[b_c])
        for nm in mud:
            s.dma("sp", mud[nm][:], prm[nm], writes=[b_c])
        s.op("pool", lambda e: e.memset(ones64[:], 1.0), writes=[b_c])
        zer = k.sb(st, [64, G, 64], F32)
        s.op("pool", lambda e: e.memset(zer[:], 0.0), writes=[b_c])
        for nm in om:
            s.op("dve", lambda e: e.tensor_scalar(out=om[nm][:], in0=P[nm][:], scalar1=-1.0, scalar2=1.0, op0=ALU.mult, op1=ALU.add),
                 reads=[b_c], writes=[b_c])
        for nm in omd:
            s.op("dve", lambda e: e.tensor_scalar(out=omd[nm][:], in0=mud[nm][:], scalar1=-1.0, scalar2=1.0, op0=ALU.mult, op1=ALU.add),
                 reads=[b_c], writes=[b_c])
        s.op("dve", lambda e: e.tensor_scalar(out=nw0[:], in0=P["w0"][:], scalar1=-1.0, scalar2=None, op0=ALU.mult), reads=[b_c], writes=[b_c])

        wup = k.sb(st, [96, G * 64], F32)
        aup = k.sb(st, [96, G * 64], F32)
        rawd = k.sb(st, [96, 1 + NT], F32)
        rawa = k.sb(st, [96, 1 + NT], F32)
        twd = k.sb(st, [96, NT], F32)
        adq = k.sb(st, [96, NT], F32)
        raw = {nm: k.sb(st, [64, G, 1 + NT], F32) for nm in ("r", "k", "v")}
        gate = None
        (r_, k_, e2, c_, eP, eN, a_, kk, kkn, kmod, kka, t1, t2) = [[k.sb(st, [64, NT], F32) for _ in range(G)] for _ in range(13)]
        yfm = k.sb(st, [64, G, NT], F32)
        dbl_tiles = [[k.sb(st, shp, F32) for _ in range(2)] for shp in
                     ([64, G, NT], [64, G, NCK, 2, RL], [64, G, NT], [64, G, NT], [64, G, NT], [64, G, NT], [64, G, NT], [64, G, NT], [64, G, NCK])]
        dbl_bufs = [bufs(2) for _ in range(9)]
        P1_PER_STEP = 2
        tokb = [k.sb(st, [RL, 3, G, 64], F32) for _ in range(2)]
        GAb = [k.sb(st, [RL, G, 2 * RL], F32) for _ in range(2)]
        GKb = [k.sb(st, [RL, G, 2 * RL], F32) for _ in range(2)]
        Pnb = [k.sb(st, [RL, G, RL], F32) for _ in range(2)]
        PPb = [[k.sb(st, [RL, G, 2, RL], F32) for _ in range(NLVL - 1)] for _ in range(2)]
        b_tokb, b_GAb, b_GKb, b_Pnb = bufs(2), bufs(2), bufs(2), bufs(2)
        b_PPb = [bufs(NLVL - 1), bufs(NLVL - 1)]
        U = k.sb(st, [RL, G, 64], F32)
        Ysb = k.sb(st, [RL, G, 64], F32)
        ST = k.sb(st, [64, G, 64], F32)
        yo = [k.sb(st, [64, NT], BF16) for _ in range(2)]
        bank = [k.ps(st, [128, 512]) for _ in range(8)]
        b_bank = bufs(8)
        ps_A, ps_B = bank[0], bank[1]
        ps_ga = bank[0][0:RL, 0:G * 2 * RL].rearrange("p (g x) -> p g x", g=G)
        ps_gk = bank[1][0:RL, 0:G * 2 * RL].rearrange("p (g x) -> p g x", g=G)
        ps_x1 = bank[2][0:RL, 0:2 * G * 64].rearrange("p (a g x) -> p a g x", a=2, g=G)
        ps_x2k = bank[3][0:RL, 0:G * 64].rearrange("p (g x) -> p g x", g=G)
        ps_gn = bank[3][0:RL, G * 64:G * 64 + G * RL].rearrange("p (g x) -> p g x", g=G)
        ps_z = bank[4][0:RL, 0:2 * G * 64].rearrange("p (a g x) -> p a g x", a=2, g=G)
        ps_pp = bank[5][0:RL, 0:G * 2 * RL].rearrange("p (g a x) -> p g a x", g=G, a=2)
        ps_y = bank[6][0:RL, 0:G * 64].rearrange("p (g x) -> p g x", g=G)
        ps_yT = bank[6][0:64, G * 64:G * 64 + G * RL].rearrange("p (g x) -> p g x", g=G)
        ps_s = bank[7][0:64, 0:G * 64].rearrange("p (g x) -> p g x", g=G)
        ps_ln = bank[7]
        (b_wup, b_rawd, b_rawa, b_twd, b_adq, b_gate,
         b_Vt, b_AR, b_Bt, b_Kt, b_Bh, b_Kh, b_bonus, b_yfm, b_gL, b_tok, b_GAs, b_GKs, b_Pn, b_U, b_Ysb, b_ST) = bufs(22)
        (b_r, b_k, b_e2, b_cc, b_eP, b_eN, b_a, b_kk, b_kkn, b_kmod, b_kka, b_t1, b_t2) = [bufs(G) for _ in range(13)]
        b_raw = {nm: Buf() for nm in raw}
        b_PP = bufs(2)
        b_yo = bufs(2)
        i64 = ident[0:64, 0:64]
        R = (lambda ap: ap.bitcast(mybir.dt.float32r)) if RWKV_FP32R else (lambda ap: ap)

        def shift(dst, src3, g, mu_t, om_t, H, reads, bdst):
            s.op("dve", lambda e: e.tensor_scalar(out=dst, in0=src3[:, g, 1:1 + NT], scalar1=om_t[:, H:H + 1], scalar2=None, op0=ALU.mult),
                 reads=reads + [b_c, bdst], writes=[bdst])
            s.op("dve", lambda e: e.scalar_tensor_tensor(out=dst, in0=src3[:, g, 0:NT], scalar=mu_t[:, H:H + 1], in1=dst,
                                                         op0=ALU.mult, op1=ALU.add),
                 reads=reads + [b_c, bdst], writes=[bdst])

        for h0 in range(0, NH, G):
            s.dma("sp", wup[:], prm["w_up"][:, h0 * 64:(h0 + G) * 64], reads=[b_wup], writes=[b_wup])
            s.dma("sp", aup[:], prm["a_up"][:, h0 * 64:(h0 + G) * 64], reads=[b_wup], writes=[b_wup])
            s.op("dve", lambda e: e.tensor_copy(out=R(ST[:]), in_=zer[:]), reads=[b_ST, b_c], writes=[b_ST])
            def make_tile(ti, t0):
                tp = ti % 2
                Vt, AR, Bt, Kt, Bh, Kh, bonus, gate, gL = (x[tp] for x in dbl_tiles)
                b_Vt, b_AR, b_Bt, b_Kt, b_Bh, b_Kh, b_bonus, b_gate, b_gL = (x[tp] for x in dbl_bufs)
                def head_gen(g):
                        H = h0 + g
                        hc = slice(g * 64, (g + 1) * 64)
                        shift(r_[g][:], raw["r"], g, P["mu_r"], om["mu_r"], H, [b_raw["r"]], b_r[g])
                        yield
                        shift(k_[g][:], raw["k"], g, P["mu_k"], om["mu_k"], H, [b_raw["k"]], b_k[g])
                        yield
                        shift(Vt[:, g, :], raw["v"], g, P["mu_v"], om["mu_v"], H, [b_raw["v"]], b_Vt)
                        yield
                        s.op("pe", lambda e: e.matmul(bank[g][0:64, :], lhsT=wup[:, hc], rhs=twd[:], start=True, stop=True),
                             reads=[b_wup, b_twd, b_bank[g]], writes=[b_bank[g]])
                        yield
                        s.op("act", lambda e: e.activation(out=t1[g][:], in_=bank[g][0:64, :], func=AF.Exp, scale=-1.0, bias=nw0[:, H:H + 1]),
                             reads=[b_bank[g], b_c, b_t1[g]], writes=[b_t1[g]])
                        yield
                        s.op("act", lambda e: e.activation(out=t1[g][:], in_=t1[g][:], func=AF.Ln, bias=1.0), reads=[b_t1[g]], writes=[b_t1[g]])
                        yield
                        s.op("act", lambda e: e.activation(out=e2[g][:], in_=t1[g][:], func=AF.Exp, scale=-1.0, bias=-0.5), reads=[b_t1[g], b_e2[g]], writes=[b_e2[g]])
                        yield
                        s.op("dve", lambda e: e.tensor_tensor_scan(out=c_[g][:], data0=rmask[:], data1=e2[g][:], initial=0.0, op0=ALU.mult, op1=ALU.add),
                             reads=[b_c, b_e2[g], b_cc[g]], writes=[b_cc[g]])
                        yield
                        s.op("act", lambda e: e.activation(out=eP[g][:], in_=c_[g][:], func=AF.Exp, scale=-1.0), reads=[b_cc[g], b_eP[g]], writes=[b_eP[g]])
                        yield
                        s.op("act", lambda e: e.activation(out=eN[g][:], in_=c_[g][:], func=AF.Exp), reads=[b_cc[g], b_eN[g]], writes=[b_eN[g]])
                        yield
                        s.op("pool", lambda e: e.tensor_copy(out=gL[:, g, :], in_=eP[g][:, RL - 1:NT:RL]), reads=[b_eP[g], b_gL], writes=[b_gL])
                        yield
                        yield
                        s.op("pe", lambda e: e.matmul(bank[g][0:64, :], lhsT=aup[:, hc], rhs=adq[:], start=True, stop=True),
                             reads=[b_wup, b_adq, b_bank[g]], writes=[b_bank[g]])
                        yield
                        s.op("act", lambda e: e.activation(out=a_[g][:], in_=bank[g][0:64, :], func=AF.Sigmoid, bias=P["a0"][:, H:H + 1]),
                             reads=[b_bank[g], b_c, b_a[g]], writes=[b_a[g]])
                        yield
                        s.op("dve", lambda e: e.tensor_scalar(out=kk[g][:], in0=k_[g][:], scalar1=P["k_k"][:, H:H + 1], scalar2=None, op0=ALU.mult),
                             reads=[b_k[g], b_c, b_kk[g]], writes=[b_kk[g]])
                        yield
                        s.op("act", lambda e: e.activation(out=t2[g][:], in_=kk[g][:], func=AF.Square), reads=[b_kk[g], b_t2[g]], writes=[b_t2[g]])
                        yield
                        s.op("pe", lambda e: e.matmul(bank[g][0:64, :], lhsT=ones64[:], rhs=t2[g][:], start=True, stop=True),
                             reads=[b_c, b_t2[g], b_bank[g]], writes=[b_bank[g]])
                        yield
                        s.op("act", lambda e: e.activation(out=t2[g][:], in_=bank[g][0:64, :], func=AF.Sqrt), reads=[b_bank[g], b_t2[g]], writes=[b_t2[g]])
                        yield
                        s.op("dve", lambda e: e.tensor_scalar(out=t2[g][:], in0=t2[g][:], scalar1=1e-12, scalar2=None, op0=ALU.max), reads=[b_t2[g]], writes=[b_t2[g]])
                        yield
                        s.op("dve", lambda e: e.reciprocal(out=t2[g][:], in_=t2[g][:]), reads=[b_t2[g]], writes=[b_t2[g]])
                        yield
                        s.op("dve", lambda e: e.tensor_tensor(out=kkn[g][:], in0=kk[g][:], in1=t2[g][:], op=ALU.mult), reads=[b_kk[g], b_t2[g], b_kkn[g]], writes=[b_kkn[g]])
                        yield
                        s.op("dve", lambda e: e.tensor_scalar(out=t1[g][:], in0=a_[g][:], scalar1=P["k_a"][:, H:H + 1], scalar2=om["k_a"][:, H:H + 1],
                                                              op0=ALU.mult, op1=ALU.add), reads=[b_a[g], b_c, b_t1[g]], writes=[b_t1[g]])
                        yield
                        s.op("dve", lambda e: e.tensor_tensor(out=kmod[g][:], in0=k_[g][:], in1=t1[g][:], op=ALU.mult), reads=[b_k[g], b_t1[g], b_kmod[g]], writes=[b_kmod[g]])
                        yield
                        s.op("dve", lambda e: e.tensor_tensor(out=kka[g][:], in0=kkn[g][:], in1=a_[g][:], op=ALU.mult), reads=[b_kkn[g], b_a[g], b_kka[g]], writes=[b_kka[g]])
                        yield
                        s.op("dve", lambda e: e.scalar_tensor_tensor(out=t2[g][:], in0=r_[g][:], scalar=P["r_k"][:, H:H + 1], in1=kmod[g][:],
                                                                     op0=ALU.mult, op1=ALU.mult), reads=[b_r[g], b_c, b_kmod[g], b_t2[g]], writes=[b_t2[g]])
                        yield
                        s.op("pe", lambda e: e.matmul(bank[g][0:64, :], lhsT=ones64[:], rhs=t2[g][:], start=True, stop=True),
                             reads=[b_c, b_t2[g], b_bank[g]], writes=[b_bank[g]])
                        yield
                        s.op("dve", lambda e: e.tensor_tensor(out=bonus[:, g, :], in0=bank[g][0:64, :], in1=Vt[:, g, :], op=ALU.mult),
                             reads=[b_bank[g], b_Vt, b_bonus], writes=[b_bonus])
                        yield
                        s.op("dve", lambda e: e.tensor_tensor(out=t1[g][:], in0=e2[g][:], in1=c_[g][:], op=ALU.subtract), reads=[b_e2[g], b_cc[g], b_t1[g]], writes=[b_t1[g]])
                        yield
                        s.op("act", lambda e: e.activation(out=t1[g][:], in_=t1[g][:], func=AF.Exp), reads=[b_t1[g]], writes=[b_t1[g]])
                        yield
                        v3 = lambda ap: ap.rearrange("p (n l) -> p n l", l=RL)
                        s.op("dve", lambda e: e.scalar_tensor_tensor(out=R(AR[:, g, :, 0, :]), in0=v3(kkn[g][:]), scalar=-1.0, in1=v3(t1[g][:]),
                                                                     op0=ALU.mult, op1=ALU.mult), reads=[b_kkn[g], b_t1[g], b_AR], writes=[b_AR])
                        yield
                        s.op("dve", lambda e: e.tensor_tensor(out=R(AR[:, g, :, 1, :]), in0=v3(r_[g][:]), in1=v3(eP[g][:]), op=ALU.mult),
                             reads=[b_r[g], b_eP[g], b_AR], writes=[b_AR])
                        yield
                        s.op("dve", lambda e: e.tensor_tensor(out=R(Bt[:, g, :]), in0=kka[g][:], in1=eN[g][:], op=ALU.mult), reads=[b_kka[g], b_eN[g], b_Bt], writes=[b_Bt])
                        yield
                        s.op("dve", lambda e: e.tensor_tensor(out=R(Kt[:, g, :]), in0=kmod[g][:], in1=eN[g][:], op=ALU.mult), reads=[b_kmod[g], b_eN[g], b_Kt], writes=[b_Kt])
                        yield
                        gbc = gL[:, g, :].unsqueeze(2).broadcast_to([64, NCK, RL])
                        s.op("dve", lambda e: e.tensor_tensor(out=v3(Bh[:, g, :]), in0=v3(Bt[:, g, :]), in1=gbc, op=ALU.mult),
                             reads=[b_Bt, b_gL, b_Bh], writes=[b_Bh])
                        yield
                        s.op("dve", lambda e: e.tensor_tensor(out=v3(Kh[:, g, :]), in0=v3(Kt[:, g, :]), in1=gbc, op=ALU.mult),
                             reads=[b_Kt, b_gL, b_Kh], writes=[b_Kh])
                        yield


                def phase1():
                    for rw, brw, nm, rowk in ((rawd, b_rawd, "mu_wd", "wd"), (rawa, b_rawa, "mu_ad", "ad")):
                        if ti == 0:
                            s.op("pool", lambda e: e.memset(rw[:, 0:1], 0.0), reads=[brw], writes=[brw])
                        else:
                            s.op("pool", lambda e: e.tensor_copy(out=rw[:, 0:1], in_=rw[:, NT:NT + 1]), reads=[brw], writes=[brw])
                        s.dma("sp", rw[:, 1:1 + NT], PT[rows[rowk]:rows[rowk] + 96, t0:t0 + NT], reads=[brw], writes=[brw])
                    dsts = ((twd, b_twd, rawd, b_rawd, "mu_wd"), (adq, b_adq, rawa, b_rawa, "mu_ad"))
                    for dst, bdst, rw, brw, nm in dsts:
                        s.op("dve", lambda e: e.tensor_scalar(out=dst[:], in0=rw[:, 1:1 + NT], scalar1=omd[nm][:], scalar2=None, op0=ALU.mult),
                             reads=[brw, b_c, bdst], writes=[bdst])
                        s.op("dve", lambda e: e.scalar_tensor_tensor(out=dst[:], in0=rw[:, 0:NT], scalar=mud[nm][:], in1=dst[:],
                                                                     op0=ALU.mult, op1=ALU.add),
                             reads=[brw, b_c, bdst], writes=[bdst])
                    s.op("act", lambda e: e.activation(out=twd[:], in_=twd[:], func=AF.Tanh), reads=[b_twd], writes=[b_twd])
                    for nm in ("r", "k", "v"):
                        if ti == 0:
                            s.op("pool", lambda e: e.memset(raw[nm][:, :, 0:1], 0.0), reads=[b_raw[nm]], writes=[b_raw[nm]])
                        else:
                            s.op("pool", lambda e: e.tensor_copy(out=raw[nm][:, :, 0:1], in_=raw[nm][:, :, NT:NT + 1]),
                                 reads=[b_raw[nm]], writes=[b_raw[nm]])
                        r0 = rows[nm] + h0 * 64
                        s.dma("sp", raw[nm][:, :, 1:1 + NT], PT[r0:r0 + G * 64, t0:t0 + NT].rearrange("(h j) t -> j h t", j=64),
                              reads=[b_raw[nm]], writes=[b_raw[nm]])
                    r0 = rows["g"] + h0 * 64
                    s.dma("act", gate[:], PT[r0:r0 + G * 64, t0:t0 + NT].rearrange("(h j) t -> j h t", j=64), reads=[b_gate], writes=[b_gate])
                    s.op("act", lambda e: e.activation(out=gate[:], in_=gate[:], func=AF.Silu), reads=[b_gate], writes=[b_gate])
                    alive = [head_gen(g) for g in range(G)]
                    while alive:
                        for gg in list(alive):
                            try:
                                next(gg)
                            except StopIteration:
                                alive.remove(gg)
                        yield
                def indep_steps(cc):
                    par = cc % 2
                    cs = slice(cc * RL, (cc + 1) * RL)
                    tk, ga_s, gk_s, pn_s, ppl = tokb[par], GAb[par], GKb[par], Pnb[par], PPb[par]
                    b_tk, b_ga, b_gk, b_pn, b_ppl = b_tokb[par], b_GAb[par], b_GKb[par], b_Pnb[par], b_PPb[par]

                    def tr():
                        for g in range(G):
                            s.op("pe", lambda e: e.transpose(ps_x1[:, 0, g, :], Vt[:, g, cs], i64), reads=[b_Vt, b_c, b_bank[2]], writes=[b_bank[2]])
                            s.op("pe", lambda e: e.transpose(ps_x1[:, 1, g, :], Bh[:, g, cs], i64), reads=[b_Bh, b_c, b_bank[2]], writes=[b_bank[2]])
                            s.op("pe", lambda e: e.transpose(ps_x2k[:, g, :], Kh[:, g, cs], i64), reads=[b_Kh, b_c, b_bank[3]], writes=[b_bank[3]])
                        s.op("act", lambda e: e.copy(out=R(tk[:, 0:2]), in_=ps_x1), reads=[b_bank[2], b_tk], writes=[b_tk])
                        s.op("act", lambda e: e.copy(out=R(tk[:, 2]), in_=ps_x2k), reads=[b_bank[3], b_tk], writes=[b_tk])

                    def gm():
                        for g in range(G):
                            arc = AR[:, g, cc].rearrange("p a l -> p (a l)")
                            s.op("pe", lambda e: e.matmul(ps_ga[:, g, :], lhsT=R(Bt[:, g, cs]), rhs=R(arc), start=True, stop=True),
                                 reads=[b_Bt, b_AR, b_bank[0]], writes=[b_bank[0]])
                            s.op("pe", lambda e: e.matmul(ps_gk[:, g, :], lhsT=R(Kt[:, g, cs]), rhs=R(arc), start=True, stop=True),
                                 reads=[b_Kt, b_AR, b_bank[1]], writes=[b_bank[1]])
                            s.op("pe", lambda e: e.matmul(ps_gn[:, g, :], lhsT=R(AR[:, g, cc, 0, :]), rhs=R(Bt[:, g, cs]), start=True, stop=True),
                                 reads=[b_Bt, b_AR, b_bank[3]], writes=[b_bank[3]])
                        s.op("dve", lambda e: e.tensor_tensor(out=R(ga_s[:]), in0=ps_ga, in1=maskA[:], op=ALU.mult),
                             reads=[b_bank[0], b_c, b_ga], writes=[b_ga])
                        s.op("dve", lambda e: e.tensor_tensor(out=R(gk_s[:]), in0=ps_gk, in1=maskA[:], op=ALU.mult),
                             reads=[b_bank[1], b_c, b_gk], writes=[b_gk])
                        s.op("dve", lambda e: e.tensor_tensor(out=R(pn_s[:]), in0=ps_gn, in1=maskN[:], op=ALU.mult),
                             reads=[b_bank[3], b_c, b_pn], writes=[b_pn])

                    def sq(lvl):
                        def f():
                            if lvl == 0:
                                Pl = lambda g: pn_s[:, g, :]
                                PTl = lambda g: ga_s[:, g, 0:RL]
                                rd = [b_pn, b_ga]
                            else:
                                Pl = lambda g: ppl[lvl - 1][:, g, 0, :]
                                PTl = lambda g: ppl[lvl - 1][:, g, 1, :]
                                rd = [b_ppl[lvl - 1]]
                            for g in range(G):
                                s.op("pe", lambda e: e.matmul(ps_pp[:, g, 0, :], lhsT=R(PTl(g)), rhs=R(Pl(g)), start=True, stop=True),
                                     reads=rd + [b_bank[5]], writes=[b_bank[5]])
                                s.op("pe", lambda e: e.matmul(ps_pp[:, g, 1, :], lhsT=R(Pl(g)), rhs=R(PTl(g)), start=True, stop=True),
                                     reads=rd + [b_bank[5]], writes=[b_bank[5]])
                            s.op("act", lambda e: e.copy(out=R(ppl[lvl][:]), in_=ps_pp), reads=[b_bank[5], b_ppl[lvl]], writes=[b_ppl[lvl]])
                        return f
                    return [tr, gm] + [sq(l) for l in range(NLVL - 1)]

                def dep_steps(cc):
                    par = cc % 2
                    cs = slice(cc * RL, (cc + 1) * RL)
                    tk, ga_s, gk_s, pn_s, ppl = tokb[par], GAb[par], GKb[par], Pnb[par], PPb[par]
                    b_tk, b_ga, b_gk, b_pn, b_ppl = b_tokb[par], b_GAb[par], b_GKb[par], b_Pnb[par], b_PPb[par]

                    def zz():
                        for g in range(G):
                            s.op("pe", lambda e: e.matmul(ps_z[:, 0, g, :], lhsT=R(AR[:, g, cc, 0, :]), rhs=R(ST[:, g, :]), start=True, stop=False),
                                 reads=[b_AR, b_ST, b_bank[4]], writes=[b_bank[4]], inc=False)
                            s.op("pe", lambda e: e.matmul(ps_z[:, 0, g, :], lhsT=R(gk_s[:, g, 0:RL]), rhs=R(tk[:, 0, g, :]), start=False, stop=True),
                                 reads=[b_gk, b_tk, b_bank[4]], writes=[b_bank[4]])
                        s.op("act", lambda e: e.copy(out=R(U[:]), in_=ps_z[:, 0]), reads=[b_bank[4], b_U], writes=[b_U])

                    def app(lvl):
                        def f():
                            if lvl == 0:
                                PTl = lambda g: ga_s[:, g, 0:RL]
                                rd = [b_ga]
                            else:
                                PTl = lambda g: ppl[lvl - 1][:, g, 1, :]
                                rd = [b_ppl[lvl - 1]]
                            for g in range(G):
                                s.op("pe", lambda e: e.matmul(ps_z[:, 1, g, :], lhsT=R(PTl(g)), rhs=R(U[:, g, :]), start=True, stop=True),
                                     reads=rd + [b_U, b_bank[4]], writes=[b_bank[4]])
                            s.op("dve", lambda e: e.tensor_tensor(out=R(U[:]), in0=ps_z[:, 1], in1=U[:], op=ALU.add),
                                 reads=[b_bank[4], b_U], writes=[b_U])
                        return f

                    def yy():
                        for g in range(G):
                            s.op("pe", lambda e: e.matmul(ps_y[:, g, :], lhsT=R(AR[:, g, cc, 1, :]), rhs=R(ST[:, g, :]), start=True, stop=False),
                                 reads=[b_AR, b_ST, b_bank[6]], writes=[b_bank[6]], inc=False)
                            s.op("pe", lambda e: e.matmul(ps_y[:, g, :], lhsT=R(ga_s[:, g, RL:2 * RL]), rhs=R(U[:, g, :]), start=False, stop=False),
                                 reads=[b_ga, b_U, b_bank[6]], writes=[b_bank[6]], inc=False)
                            s.op("pe", lambda e: e.matmul(ps_y[:, g, :], lhsT=R(gk_s[:, g, RL:2 * RL]), rhs=R(tk[:, 0, g, :]), start=False, stop=True),
                                 reads=[b_gk, b_tk, b_bank[6]], writes=[b_bank[6]])
                        s.op("act", lambda e: e.copy(out=Ysb[:], in_=ps_y), reads=[b_bank[6], b_Ysb], writes=[b_Ysb])

                    def ss():
                        for g in range(G):
                            s.op("pe", lambda e: e.matmul(ps_s[:, g, :], lhsT=R(tk[:, 1, g, :]), rhs=R(U[:, g, :]), start=True, stop=False),
                                 reads=[b_tk, b_U, b_bank[7]], writes=[b_bank[7]], inc=False)
                            s.op("pe", lambda e: e.matmul(ps_s[:, g, :], lhsT=R(tk[:, 2, g, :]), rhs=R(tk[:, 0, g, :]), start=False, stop=True),
                                 reads=[b_tk, b_bank[7]], writes=[b_bank[7]])
                        for g in range(G):
                            s.op("dve", lambda e: e.scalar_tensor_tensor(out=R(ST[:, g, :]), in0=ST[:, g, :], scalar=gL[:, g, cc:cc + 1], in1=ps_s[:, g, :],
                                                                         op0=ALU.mult, op1=ALU.add),
                                 reads=[b_ST, b_gL, b_bank[7]], writes=[b_ST])

                    def yt():
                        for g in range(G):
                            s.op("pe", lambda e: e.transpose(ps_yT[:, g, :], Ysb[:, g, :], ident[0:RL, 0:RL]), reads=[b_Ysb, b_c, b_bank[6]], writes=[b_bank[6]])
                        s.op("dve", lambda e: e.tensor_copy(out=yfm[:, :, cs], in_=ps_yT), reads=[b_bank[6], b_yfm], writes=[b_yfm])
                    return [zz] + [app(l) for l in range(NLVL)] + [yy, ss, yt]


                def chunks():
                    for f in indep_steps(0):
                        f()
                        yield
                    for cc in range(NCK):
                        dsteps = dep_steps(cc)
                        isteps = indep_steps(cc + 1) if cc + 1 < NCK else []
                        for j in range(max(len(dsteps), len(isteps))):
                            if j < len(dsteps):
                                dsteps[j]()
                            if j < len(isteps):
                                isteps[j]()
                            yield
                def ph4(g):
                    H = h0 + g
                    yi = g % 2
                    s.op("pe", lambda e: e.matmul(bank[(7, 4)[g]][0:64, :], lhsT=ones64[:], rhs=yfm[:, g, :], start=True, stop=True),
                         reads=[b_c, b_yfm, b_bank[(7, 4)[g]]], writes=[b_bank[(7, 4)[g]]])
                    yield
                    s.op("act", lambda e: e.activation(out=t1[g][:], in_=bank[(7, 4)[g]][0:64, :], func=AF.Copy, scale=1.0 / 64), reads=[b_bank[(7, 4)[g]], b_t1[g]], writes=[b_t1[g]])
                    yield
                    s.op("dve", lambda e: e.tensor_tensor(out=t1[g][:], in0=yfm[:, g, :], in1=t1[g][:], op=ALU.subtract), reads=[b_yfm, b_t1[g]], writes=[b_t1[g]])
                    yield
                    s.op("act", lambda e: e.activation(out=t2[g][:], in_=t1[g][:], func=AF.Square), reads=[b_t1[g], b_t2[g]], writes=[b_t2[g]])
                    yield
                    s.op("pe", lambda e: e.matmul(bank[(7, 4)[g]][0:64, :], lhsT=ones64[:], rhs=t2[g][:], start=True, stop=True),
                         reads=[b_c, b_t2[g], b_bank[(7, 4)[g]]], writes=[b_bank[(7, 4)[g]]])
                    yield
                    s.op("act", lambda e: e.activation(out=t2[g][:], in_=bank[(7, 4)[g]][0:64, :], func=AF.Sqrt, scale=1.0 / 64, bias=GN_EPS),
                         reads=[b_bank[(7, 4)[g]], b_t2[g]], writes=[b_t2[g]])
                    yield
                    s.op("dve", lambda e: e.reciprocal(out=t2[g][:], in_=t2[g][:]), reads=[b_t2[g]], writes=[b_t2[g]])
                    yield
                    s.op("dve", lambda e: e.tensor_tensor(out=t1[g][:], in0=t1[g][:], in1=t2[g][:], op=ALU.mult), reads=[b_t1[g], b_t2[g]], writes=[b_t1[g]])
                    yield
                    s.op("dve", lambda e: e.tensor_scalar(out=t1[g][:], in0=t1[g][:], scalar1=P["gn_w"][:, H:H + 1], scalar2=P["gn_b"][:, H:H + 1],
                                                          op0=ALU.mult, op1=ALU.add), reads=[b_t1[g], b_c], writes=[b_t1[g]])
                    yield
                    s.op("dve", lambda e: e.tensor_tensor(out=t1[g][:], in0=t1[g][:], in1=bonus[:, g, :], op=ALU.add), reads=[b_t1[g], b_bonus], writes=[b_t1[g]])
                    yield
                    s.op("dve", lambda e: e.tensor_tensor(out=yo[yi][:], in0=t1[g][:], in1=gate[:, g, :], op=ALU.mult),
                         reads=[b_t1[g], b_gate, b_yo[yi]], writes=[b_yo[yi]])
                    yield
                    r0 = row_y + H * 64
                    s.dma("sp", yT[r0:r0 + 64, t0:t0 + NT], yo[yi][:], reads=[b_yo[yi]])
                    yield

                def phase4():
                    alive = [ph4(g) for g in range(G)]
                    while alive:
                        for gg in list(alive):
                            try:
                                next(gg)
                            except StopIteration:
                                alive.remove(gg)
                        yield
                return phase1, chunks, phase4

            tiles = [make_tile(ti, t0) for ti, t0 in enumerate(range(0, T, NT))]
            for _ in tiles[0][0]():
                pass
            for ti in range(len(tiles)):
                cg = tiles[ti][1]()
                pg = tiles[ti + 1][0]() if ti + 1 < len(tiles) else iter(())
                c_alive, p_alive = True, True
                while c_alive or p_alive:
                    if c_alive:
                        try:
                            next(cg)
                        except StopIteration:
                            c_alive = False
                    for _ in range(P1_PER_STEP):
                        if p_alive:
                            try:
                                next(pg)
                            except StopIteration:
                                p_alive = False
                for _ in tiles[ti][2]():
                    pass
        s.barrier()


class Cfg:
    def __init__(self, D=4096, T=4096, NBLK_A=8, NH_B=32, HC=4, HX=4, M=256, DEPTH=2):
        self.D, self.T, self.NBLK_A, self.NH_B, self.HC, self.HX, self.M, self.DEPTH = D, T, NBLK_A, NH_B, HC, HX, M, DEPTH
        self.KC = D // 128
        self.WA = NBLK_A * 256
        self.WB = NH_B * 64
        self.QKW = HC * 256
        self.WC = HC * 512
        self.WX = HX * 128
        self.in_sizes = (self.WA, self.WA, 3 * self.WB + 192, self.WB, 2 * self.QKW, self.WC, self.WC, self.WC, 2 * HC,
                         self.WX, self.WX, 4 * D)
        self.c_in = sum(self.in_sizes)
        off = np.concatenate([[0], np.cumsum(self.in_sizes)])
        self.col = dict(a_x=off[0], a_g=off[1], b_s=off[2], b_g=off[3], c_qk=off[4], c_v=off[5], c_o=off[6], c_g=off[7],
                        c_if=off[8], x_q=off[9], x_g=off[10], gates=off[11])
        segs = [("a_x", self.col["a_x"], self.WA), ("a_g", self.col["a_g"], self.WA),
                ("r", self.col["b_s"], self.WB), ("k", self.col["b_s"] + self.WB, self.WB), ("v", self.col["b_s"] + 2 * self.WB, self.WB),
                ("wd", self.col["b_s"] + 3 * self.WB, 96), ("ad", self.col["b_s"] + 3 * self.WB + 96, 96),
                ("b_g", self.col["b_g"], self.WB),
                ("c_q", self.col["c_qk"], self.QKW), ("c_k", self.col["c_qk"] + self.QKW, self.QKW),
                ("c_v", self.col["c_v"], self.WC), ("c_o", self.col["c_o"], self.WC), ("c_g", self.col["c_g"], self.WC),
                ("c_if", self.col["c_if"], 2 * HC), ("x_q", self.col["x_q"], self.WX), ("x_g", self.col["x_g"], self.WX)]
        self.segs = segs
        self.row = {}
        r = 0
        for nm, c0, w in segs:
            self.row[nm] = r
            r += ((w + 127) // 128) * 128
        self.NB1 = r // 128
        self.grp_first = {"A": "a_x", "B": "r", "C": "c_q", "X": "x_q"}
        order = ["A", "B", "C", "X"]
        starts = [self.row[self.grp_first[g]] for g in order] + [r]
        self.grp_rows = {g: (starts[i], starts[i + 1]) for i, g in enumerate(order)}
        self.lrow = {}
        for nm, c0, w in segs:
            for g in order:
                lo, hi = self.grp_rows[g]
                if lo <= self.row[nm] < hi:
                    self.lrow[nm] = (g, self.row[nm] - lo)
        self.br_kc = [self.WA // 128, self.WB // 128, self.WC // 128, self.WX // 128]
        self.FY = sum(self.br_kc) * 128


RMS_EPS = 1e-6
RWKV_GN_EPS = 64e-5


def tile_layout(W):
    Kd, M = W.shape
    return np.ascontiguousarray(W.reshape(Kd // 128, 128, M // 128, 128).transpose(2, 1, 0, 3))


def chunk_cols(v):
    return np.ascontiguousarray(v.reshape(-1, 128).T)


def head_cols(v):
    return np.ascontiguousarray(v.reshape(-1, 64).T)


def const_inputs(cfg):
    HC = cfg.HC
    sel = np.zeros((HC, HC, 128), np.float32)
    for h in range(HC):
        sel[h, h, :] = 1
    a_, b_ = np.meshgrid(np.arange(RWKV_L), np.arange(RWKV_L), indexing="ij")
    mA = np.concatenate([(a_ < b_), (a_ <= b_)], 1).astype(np.float32)
    mN = (b_ < a_).astype(np.float32)
    rm = np.ones((64, NT), np.float32)
    rm[:, ::RWKV_L] = 0
    a_, b_ = np.meshgrid(np.arange(MLSTM_L), np.arange(MLSTM_L), indexing="ij")
    return {"c_ident": np.eye(128, dtype=np.float32), "c_sel": sel, "c_i4": np.eye(HC, dtype=np.float32),
            "c_negmask": np.where(a_ <= b_, 0.0, NEG).astype(np.float32), "c_resetmask": rm,
            "c_maskA": np.ascontiguousarray(np.broadcast_to(mA[:, None, :], (RWKV_L, RWKV_G, 2 * RWKV_L))),
            "c_maskN": np.ascontiguousarray(np.broadcast_to(mN[:, None, :], (RWKV_L, RWKV_G, RWKV_L)))}


def layer_inputs(cfg, inp, l):
    D, KC, HC = cfg.D, cfg.KC, cfg.HC
    w_in = inp["w_in"][l]
    W1 = np.zeros((D, cfg.NB1 * 128), np.float32)
    for nm, c0, w in cfg.segs:
        W1[:, cfg.row[nm]:cfg.row[nm] + w] = w_in[:, c0:c0 + w]
    o = {}
    o["W1"] = tile_layout(W1)
    g0 = cfg.col["gates"]
    o["Wg"] = np.stack([tile_layout(w_in[:, g0 + i * D: g0 + (i + 1) * D]) for i in range(4)])
    o["Wbr0"] = tile_layout(inp["w_branch_a"][l])
    o["Wbr1"] = tile_layout(inp["w_branch_b"][l])
    o["Wbr2"] = tile_layout(inp["w_branch_c"][l])
    o["Wbr3"] = tile_layout(inp["w_branch_x"][l])
    o["Wo"] = tile_layout(inp["w_out"][l])
    o["norm_g"] = chunk_cols(inp["norm_g"][l])
    o["mem_norm_g"] = chunk_cols(inp["mem_norm_g"][l])
    NCH = cfg.NBLK_A * 2
    o["lru_conv_w"] = np.ascontiguousarray(inp["lru_conv_w"][l].reshape(4, NCH, 128).transpose(2, 1, 0))
    for nm in ("lru_conv_b", "lru_ba", "lru_bx", "lru_lambda"):
        o[nm] = chunk_cols(inp[nm][l])
    for nm in ("lru_wa", "lru_wx"):
        o[nm] = np.ascontiguousarray(inp[nm][l].reshape(cfg.NBLK_A, 2, 128, 2, 128).transpose(0, 3, 2, 1, 4))
    WB = cfg.WB
    mu = inp["rwkv_mu"][l]
    o["rw_mu_r"], o["rw_mu_k"], o["rw_mu_v"] = head_cols(mu[:WB]), head_cols(mu[WB:2 * WB]), head_cols(mu[2 * WB:3 * WB])
    o["rw_mu_wd"] = np.ascontiguousarray(mu[3 * WB:3 * WB + 96].reshape(96, 1))
    o["rw_mu_ad"] = np.ascontiguousarray(mu[3 * WB + 96:].reshape(96, 1))
    for nm, src in (("w0", "rwkv_w0"), ("a0", "rwkv_a0"), ("k_k", "rwkv_k_k"), ("k_a", "rwkv_k_a"), ("gn_w", "rwkv_gn_w"), ("gn_b", "rwkv_gn_b")):
        o["rw_" + nm] = head_cols(inp[src][l])
    o["rw_r_k"] = head_cols(inp["rwkv_r_k"][l].reshape(-1))
    o["rw_w_up"] = np.ascontiguousarray(inp["rwkv_w_up"][l])
    o["rw_a_up"] = np.ascontiguousarray(inp["rwkv_a_up"][l])
    o["ml_conv_w"] = np.ascontiguousarray(inp["mlstm_conv_w"][l].reshape(4, 4 * HC, 128).transpose(2, 1, 0))
    o["ml_conv_b"] = chunk_cols(inp["mlstm_conv_b"][l])
    o["ml_gn_w"] = chunk_cols(inp["mlstm_gn_w"][l])
    o["ml_b_i"] = np.ascontiguousarray(inp["mlstm_b_i"][l].reshape(HC, 1))
    o["ml_b_f"] = np.ascontiguousarray(inp["mlstm_b_f"][l].reshape(HC, 1))
    wkv = inp["xattn_w_kv"][l]
    o["xa_Wk"] = tile_layout(wkv[:, :cfg.WX])
    o["xa_Wv"] = np.ascontiguousarray(wkv[:, cfg.WX:].reshape(KC, 128, cfg.WX))
    return {f"L{l}_{k_}": np.ascontiguousarray(v, dtype=np.float32) for k_, v in o.items()}


def flat2d(ap, ndim, width):
    names = "abcdefgh"[:ndim]
    f = ap.rearrange(f"{' '.join(names)} -> ({' '.join(names)})")
    return f.rearrange("(r c) -> r c", c=width)


def build_program(cfg, shapes):
    nc = bass.Bass("TRN2", target_bir_lowering=False)
    D, T, KC = cfg.D, cfg.T, cfg.KC
    ins = {nm: nc.dram_tensor(nm, list(sh), F32, kind="ExternalInput").ap() for nm, sh in shapes.items()}
    outT = nc.dram_tensor("outT", [D, T], F32, kind="ExternalOutput").ap()
    with contextlib.ExitStack() as st:
        k = Ctx(nc, st)
        hT = k.dram("hT", [D, T], BF16)
        PTs = {g: k.dram(f"PT{g}", [hi - lo, T], F32) for g, (lo, hi) in cfg.grp_rows.items()}

        def pt_block(b):
            r = b * 128
            for g, (lo, hi) in cfg.grp_rows.items():
                if lo <= r < hi:
                    return PTs[g], r - lo
            raise AssertionError
        lr = lambda nm: cfg.lrow[nm][1]
        yT = k.dram("yT", [cfg.FY, T], BF16)
        memnT = k.dram("memnT", [D, cfg.M], BF16)
        xs = [ins["xT"]] + [k.dram(f"x{l + 1}T", [D, T], F32) for l in range(cfg.DEPTH)]
        cst = {nm[2:]: ins[nm] for nm in ins if nm.startswith("c_")}
        OB = D // 128
        for l in range(cfg.DEPTH):
            L = lambda nm: ins[f"L{l}_{nm}"]
            Wg = k.dram(f"Wg{l}", [4, OB, 128, KC, 128], BF16)
            Wo = k.dram(f"Wo{l}", [OB, 128, KC, 128], BF16)
            Wbr = [k.dram(f"Wbr{l}_{i}", [OB, 128, cfg.br_kc[i], 128], BF16) for i in range(4)]
            for dst, src, nd in [(Wg, L("Wg"), 5), (Wo, L("Wo"), 4)] + [(Wbr[i], L(f"Wbr{i}"), 4) for i in range(4)]:
                n = int(np.prod(dst.shape))
                wdt = 1024 if n % 1024 == 0 else 512
                cast_dram(k, flat2d(dst, nd, wdt), flat2d(src, nd, wdt), n, width=wdt)
            stage_norm(k, xs[l], L("norm_g"), hT, D, T, RMS_EPS, BF16)
            stage_norm(k, ins["memT"], L("mem_norm_g"), memnT, D, cfg.M, RMS_EPS, BF16)
            stage_proj(k, hT, L("W1"), pt_block, D, T, cfg.NB1)
            lru_prm = {"conv_w": L("lru_conv_w"), "conv_b": L("lru_conv_b"), "ba": L("lru_ba"), "bx": L("lru_bx"), "lam": L("lru_lambda"),
                       "wa": L("lru_wa"), "wx": L("lru_wx")}
            stage_lru(k, PTs["A"], lr("a_x"), lr("a_g"), yT, 0, lru_prm, T, cfg.NBLK_A)
            rw_prm = {nm: L("rw_" + nm) for nm in ("mu_r", "mu_k", "mu_v", "w0", "a0", "k_k", "k_a", "r_k", "gn_w", "gn_b", "mu_wd", "mu_ad", "w_up", "a_up")}
            rw_rows = {"r": lr("r"), "k": lr("k"), "v": lr("v"), "wd": lr("wd"), "ad": lr("ad"), "g": lr("b_g")}
            stage_rwkv(k, PTs["B"], rw_rows, yT, cfg.WA, rw_prm, cst, T, cfg.NH_B, RWKV_GN_EPS)
            ml_prm = {"conv_w": L("ml_conv_w"), "conv_b": L("ml_conv_b"), "gn_w": L("ml_gn_w"), "b_i": L("ml_b_i"), "b_f": L("ml_b_f")}
            ml_rows = {"q": lr("c_q"), "k": lr("c_k"), "v": lr("c_v"), "o": lr("c_o"), "g": lr("c_g"), "ifg": lr("c_if")}
            stage_mlstm(k, PTs["C"], ml_rows, yT, cfg.WA + cfg.WB, ml_prm, cst, T, cfg.HC)
            stage_xattn(k, PTs["X"], lr("x_q"), lr("x_g"), yT, cfg.WA + cfg.WB + cfg.WC, memnT, L("xa_Wk"), L("xa_Wv"), cst["ident"],
                        T, D, cfg.M, cfg.HX)
            stage_merge(k, hT, yT, cfg.br_kc, Wg, Wbr, Wo, xs[l], xs[l + 1], D, T)
        stage_norm(k, xs[cfg.DEPTH], ins["final_g"], outT, D, T, RMS_EPS, F32)
        k.s.barrier()
        build_program.ninst = k.s.ninst
        build_program.per_eng = dict(k.s.per_eng)
        build_program.nsem = k.s.nsem
        build_program.nwait = k.s.nwait
    return nc


def run_module(cfg, inputs):
    B = inputs["x"].shape[0]
    shared = const_inputs(cfg)
    for l in range(cfg.DEPTH):
        shared.update(layer_inputs(cfg, inputs, l))
    shared["final_g"] = chunk_cols(np.asarray(inputs["final_norm_g"], dtype=np.float32))
    in_maps = []
    for b in range(B):
        m = dict(shared)
        m["xT"] = np.ascontiguousarray(np.asarray(inputs["x"][b], dtype=np.float32).T)
        m["memT"] = np.ascontiguousarray(np.asarray(inputs["mem"][b], dtype=np.float32).T)
        in_maps.append(m)
    shapes = {nm: v.shape for nm, v in in_maps[0].items()}
    nc = build_program(cfg, shapes)
    res = run_bass_kernel_spmd(nc, in_maps, core_ids=list(range(B)))
    out = np.stack([np.ascontiguousarray(res.results[b]["outT"].T) for b in range(B)])
    return out.astype(np.float32)


def kernel(**inputs):
    inputs = {k_: np.asarray(v) for k_, v in inputs.items()}
    return run_module(Cfg(), inputs)
```

```python
import contextlib
import numpy as np
import concourse.bass as bass
import concourse.mybir as mybir
from concourse.bass_utils import run_bass_kernel_spmd

F32 = mybir.dt.float32
BF16 = mybir.dt.bfloat16
AF = mybir.ActivationFunctionType
ALU = mybir.AluOpType
AX = mybir.AxisListType


class Buf:
    __slots__ = ("name", "w", "r")

    def __init__(self, name=""):
        self.name = name
        self.w = set()
        self.r = set()


SKIP_SAME_ENGINE = False


class Sched:
    EPOCH = 4000
    NDMA = 12

    def __init__(self, nc, stack):
        self.nc = nc
        self.stack = stack
        self.eng = {"pe": nc.tensor, "act": nc.scalar, "dve": nc.vector,
                    "pool": nc.gpsimd, "sp": nc.sync}
        self.sem = {}
        self.cnt = {}
        self.pending = {e: False for e in self.eng}
        self.nsem = 0
        for e in self.eng:
            self._new_sem(e)
        self.dsem = {}
        for q in ("sp", "act", "pool"):
            self.dsem[q] = [[self._alloc(f"d{q}{i}"), 0] for i in range(self.NDMA)]
        self.dnext = {q: 0 for q in self.dsem}
        self.seen = {e: {} for e in self.eng}
        self.all_tokens = {}
        self.ninst = 0
        self.per_eng = {}

    def _alloc(self, name):
        self.nsem += 1
        return self.stack.enter_context(self.nc.semaphore(f"{name}_{self.nsem}"))

    def _new_sem(self, e):
        self.sem[e] = self._alloc(f"s{e}")
        self.cnt[e] = 0

    def _wait(self, e, tok):
        sem, val = tok
        k = id(sem)
        if self.seen[e].get(k, 0) >= val:
            return
        self.eng[e].wait_ge(sem, val)
        self.nwait = getattr(self, "nwait", 0) + 1
        self.seen[e][k] = val

    def _deps(self, e, reads, writes):
        deps = set()
        for b in reads:
            deps |= b.w
        for b in writes:
            deps |= b.w
            deps |= b.r
        best = {}
        for tok in deps:
            if e == "pe" and tok[0] is self.sem["pe"]:
                continue
            if SKIP_SAME_ENGINE and e in ("act", "dve") and tok[0] is self.sem[e]:
                continue
            kk_ = id(tok[0])
            if kk_ not in best or best[kk_][1] < tok[1]:
                best[kk_] = tok
        for tok in best.values():
            self._wait(e, tok)

    def _record(self, tok, reads, writes):
        self.all_tokens[id(tok[0])] = tok
        for b in reads:
            b.r.add(tok)
        for b in writes:
            b.w = {tok}
            b.r = set()

    def op(self, e, fn, reads=(), writes=(), inc=True):
        self._deps(e, reads, writes)
        inst = fn(self.eng[e])
        self.ninst += 1
        self.per_eng[e] = self.per_eng.get(e, 0) + 1
        if inc:
            if self.cnt[e] >= self.EPOCH and not self.pending[e]:
                self._new_sem(e)
            self.cnt[e] += 1
            inst.then_inc(self.sem[e], 1)
            tok = (self.sem[e], self.cnt[e])
            self.pending[e] = False
        else:
            assert e == "pe"
            tok = (self.sem[e], self.cnt[e] + 1)
            self.pending[e] = True
        self._record(tok, reads, writes)
        return inst

    def dma(self, q, out, in_, reads=(), writes=(), **kw):
        slot = self.dsem[q][self.dnext[q]]
        self.dnext[q] = (self.dnext[q] + 1) % self.NDMA
        sem, val = slot
        if val > 0:
            self._wait(q, (sem, val))
        self._deps(q, reads, writes)
        inst = self.eng[q].dma_start(out=out, in_=in_, **kw)
        self.ninst += 1
        self.per_eng['dma_' + q] = self.per_eng.get('dma_' + q, 0) + 1
        slot[1] = val + 16
        inst.then_inc(sem, 16)
        tok = (sem, slot[1])
        self._record(tok, reads, writes)
        return inst

    def barrier(self):
        toks = list(self.all_tokens.values())
        for e in self.eng:
            for tok in toks:
                self._wait(e, tok)


class Ctx:
    def __init__(self, nc, stack):
        self.nc = nc
        self.s = Sched(nc, stack)
        self.n = 0

    def sb(self, st, shape, dt, name="t"):
        self.n += 1
        return st.enter_context(self.nc.sbuf_tensor(f"{name}{self.n}", list(shape), dt))

    def ps(self, st, shape, dt=F32, name="p"):
        self.n += 1
        return st.enter_context(self.nc.psum_tensor(f"{name}{self.n}", list(shape), dt))

    def dram(self, name, shape, dt):
        return self.nc.dram_tensor(name, list(shape), dt, kind="Internal").ap()


def dma_rows(s, q, sb3, dram2, nchunks, reads=(), writes=(), to_dram=False, step=8):
    for c0 in range(0, nchunks, step):
        c1 = min(nchunks, c0 + step)
        d = dram2[c0 * 128:c1 * 128, :].rearrange("(c p) t -> p c t", p=128)
        if to_dram:
            s.dma(q, d, sb3[:, c0:c1, :], reads=reads, writes=writes)
        else:
            s.dma(q, sb3[:, c0:c1, :], d, reads=reads, writes=writes)


def bufs(n):
    return [Buf() for _ in range(n)]


NT = 512


def stage_norm(k, xT, g_dram, out, D, T, eps, out_dt):
    NT = min(512, T)
    s = k.s
    KC = D // 128
    with contextlib.ExitStack() as st:
        ones = k.sb(st, [128, 128], F32)
        gcol = k.sb(st, [128, KC], F32)
        xt = k.sb(st, [128, KC, NT], F32)
        ht = k.sb(st, [128, KC, NT], out_dt)
        sq = [k.sb(st, [128, NT], F32) for _ in range(2)]
        rs = k.sb(st, [128, NT], F32)
        rstd = k.sb(st, [128, NT], F32)
        pss = k.ps(st, [128, NT])
        b_ones, b_g, b_rs, b_rstd, b_ps = bufs(5)
        b_xt, b_ht, b_sq = bufs(KC), bufs(KC), bufs(2)
        ones_f = k.sb(st, [128, 128], F32)
        s.op("pool", lambda e: e.memset(ones_f[:], 1.0), writes=[b_ones])
        s.op("act", lambda e: e.copy(out=ones[:].bitcast(mybir.dt.float32r), in_=ones_f[:]), reads=[b_ones], writes=[b_ones])
        s.dma("sp", gcol[:], g_dram, writes=[b_g])
        for t0 in range(0, T, NT):
            for c in range(KC):
                s.dma("sp", xt[:, c, :], xT[c * 128:(c + 1) * 128, t0:t0 + NT], writes=[b_xt[c]])
                s.op("act", lambda e: e.activation(out=sq[c % 2][:].bitcast(mybir.dt.float32r), in_=xt[:, c, :], func=AF.Square),
                     reads=[b_xt[c]], writes=[b_sq[c % 2]])
                s.op("pe", lambda e: e.matmul(pss[:], lhsT=ones[:].bitcast(mybir.dt.float32r), rhs=sq[c % 2][:].bitcast(mybir.dt.float32r),
                                             start=(c == 0), stop=(c == KC - 1)),
                     reads=[b_ones, b_sq[c % 2]], writes=[b_ps], inc=True)
            s.op("act", lambda e: e.activation(out=rs[:], in_=pss[:], func=AF.Sqrt, scale=1.0 / D, bias=eps_ap(k, eps)),
                 reads=[b_ps], writes=[b_rs])
            s.op("dve", lambda e: e.reciprocal(out=rstd[:], in_=rs[:]), reads=[b_rs], writes=[b_rstd])
            for c in range(KC):
                s.op("dve", lambda e: e.scalar_tensor_tensor(out=ht[:, c, :], in0=xt[:, c, :], scalar=gcol[:, c:c + 1],
                                                             in1=rstd[:], op0=ALU.mult, op1=ALU.mult),
                     reads=[b_xt[c], b_g, b_rstd], writes=[b_ht[c]])
            dma_rows(s, "sp", ht, out[:, t0:t0 + NT], KC, reads=b_ht, to_dram=True)
        s.barrier()


_EPS = {}


def eps_ap(k, val):
    return float(val)


def stage_proj(k, hT, W, PT, D, T, NB):
    NT = min(512, T)
    s = k.s
    KC = D // 128
    GRP = 6
    with contextlib.ExitStack() as st:
        wb = [[k.sb(st, [128, KC, 128], BF16) for _ in range(GRP)] for _ in range(2)]
        ht = [k.sb(st, [128, KC, NT], BF16) for _ in range(2)]
        ot = [k.sb(st, [128, NT], F32) for _ in range(4)]
        ps = [k.ps(st, [128, NT]) for _ in range(8)]
        b_wb, b_ht, b_ot, b_ps = [bufs(GRP), bufs(GRP)], bufs(2), bufs(4), bufs(8)
        groups = list(range(0, NB, GRP))

        def load_group(gi):
            g0 = groups[gi]
            for b in range(min(GRP, NB - g0)):
                s.dma("pool", wb[gi % 2][b][:], W[g0 + b], writes=[b_wb[gi % 2][b]], max_dma_last_dim=4096)

        nht = 0
        no = 0
        npp = 0
        load_group(0)
        for gi, g0 in enumerate(groups):
            nb = min(GRP, NB - g0)
            if gi + 1 < len(groups):
                load_group(gi + 1)
            wset, bset = wb[gi % 2], b_wb[gi % 2]
            for t0 in range(0, T, NT):
                h = nht % 2
                nht += 1
                dma_rows(s, "act", ht[h], hT[:, t0:t0 + NT], KC, writes=[b_ht[h]])
                for b in range(nb):
                    pi = npp % 8
                    npp += 1
                    p = ps[pi]
                    for c in range(KC):
                        s.op("pe", lambda e: e.matmul(p[:], lhsT=wset[b][:, c, :], rhs=ht[h][:, c, :],
                                                     start=(c == 0), stop=(c == KC - 1)),
                             reads=[bset[b], b_ht[h]], writes=[b_ps[pi]], inc=(c == KC - 1))
                    o = no % 4
                    no += 1
                    if no % 2 == 0:
                        s.op("dve", lambda e: e.tensor_copy(out=ot[o][:], in_=p[:]), reads=[b_ps[pi]], writes=[b_ot[o]])
                    else:
                        s.op("act", lambda e: e.copy(out=ot[o][:], in_=p[:]), reads=[b_ps[pi]], writes=[b_ot[o]])
                    pt_t, pt_r = PT(g0 + b)
                    s.dma("sp", pt_t[pt_r:pt_r + 128, t0:t0 + NT], ot[o][:], reads=[b_ot[o]])
        s.barrier()


def cast_dram(k, dst, src, n_elems, q="pool", width=1024):
    s = k.s
    ROW = width
    assert n_elems % ROW == 0
    rows = n_elems // ROW
    CH = 2048
    for r0 in range(0, rows, CH):
        r1 = min(rows, r0 + CH)
        s.dma(q, dst[r0:r1, :], src[r0:r1, :])


def stage_merge(k, hT, yT, br_kc, Wg, Wbr, Wo, xT, xnT, D, T):
    s = k.s
    KC = D // 128
    OB = D // 128
    YC = sum(br_kc)
    yoff = [sum(br_kc[:i]) for i in range(len(br_kc))]
    NBR = len(br_kc)
    with contextlib.ExitStack() as st:
        ht = k.sb(st, [128, KC, NT], BF16)
        yt = k.sb(st, [128, YC, NT], BF16)
        mg = k.sb(st, [128, KC, NT], BF16)
        wg = [k.sb(st, [128, KC, 128], BF16) for _ in range(3)]
        wbr = [k.sb(st, [128, max(br_kc), 128], BF16) for _ in range(3)]
        gs = [k.sb(st, [128, NT], F32) for _ in range(2)]
        acc = [k.sb(st, [128, NT], F32) for _ in range(2)]
        tmp = [k.sb(st, [128, NT], F32) for _ in range(2)]
        xt = [k.sb(st, [128, NT], F32) for _ in range(2)]
        xn = [k.sb(st, [128, NT], F32) for _ in range(2)]
        psg = [k.ps(st, [128, NT]) for _ in range(2)]
        psp = [k.ps(st, [128, NT]) for _ in range(2)]
        pso = [k.ps(st, [128, NT]) for _ in range(2)]
        b_ht, b_yt = Buf(), Buf()
        b_mg = bufs(KC)
        b_wg, b_wbr, b_gs, b_acc, b_tmp, b_xt, b_xn = bufs(3), bufs(3), bufs(2), bufs(2), bufs(2), bufs(2), bufs(2)
        b_psg, b_psp, b_pso = bufs(2), bufs(2), bufs(2)
        nw = 0
        ng = 0
        na = 0
        for t0 in range(0, T, NT):
            dma_rows(s, "act", ht, hT[:, t0:t0 + NT], KC, writes=[b_ht])
            dma_rows(s, "act", yt, yT[:, t0:t0 + NT], YC, writes=[b_yt])
            for ob in range(OB):
                a = na % 2
                na += 1
                for br in range(NBR):
                    w = nw % 3
                    nw += 1
                    g = ng % 2
                    ng += 1
                    kcb = br_kc[br]
                    s.dma("sp", wg[w][:], Wg[br, ob], writes=[b_wg[w]])
                    s.dma("sp", wbr[w][:, 0:kcb, :], Wbr[br][ob], writes=[b_wbr[w]])
                    for c in range(KC):
                        s.op("pe", lambda e: e.matmul(psg[g][:], lhsT=wg[w][:, c, :], rhs=ht[:, c, :],
                                                     start=(c == 0), stop=(c == KC - 1)),
                             reads=[b_wg[w], b_ht], writes=[b_psg[g]], inc=(c == KC - 1))
                    for c in range(kcb):
                        s.op("pe", lambda e: e.matmul(psp[g][:], lhsT=wbr[w][:, c, :], rhs=yt[:, yoff[br] + c, :],
                                                     start=(c == 0), stop=(c == kcb - 1)),
                             reads=[b_wbr[w], b_yt], writes=[b_psp[g]], inc=(c == kcb - 1))
                    s.op("act", lambda e: e.activation(out=gs[g][:], in_=psg[g][:], func=AF.Sigmoid),
                         reads=[b_psg[g]], writes=[b_gs[g]])
                    last = (br == NBR - 1)
                    if br == 0:
                        dst, bdst = (mg[:, ob, :], b_mg[ob]) if last else (acc[a][:], b_acc[a])
                        s.op("dve", lambda e: e.tensor_tensor(out=dst, in0=psp[g][:], in1=gs[g][:], op=ALU.mult),
                             reads=[b_psp[g], b_gs[g]], writes=[bdst])
                    else:
                        s.op("dve", lambda e: e.tensor_tensor(out=tmp[g][:], in0=psp[g][:], in1=gs[g][:], op=ALU.mult),
                             reads=[b_psp[g], b_gs[g]], writes=[b_tmp[g]])
                        if last:
                            s.op("pool", lambda e: e.tensor_tensor(out=mg[:, ob, :], in0=acc[a][:], in1=tmp[g][:], op=ALU.add),
                                 reads=[b_acc[a], b_tmp[g]], writes=[b_mg[ob]])
                        else:
                            s.op("pool", lambda e: e.tensor_tensor(out=acc[a][:], in0=acc[a][:], in1=tmp[g][:], op=ALU.add),
                                 reads=[b_acc[a], b_tmp[g]], writes=[b_acc[a]])
            for ob in range(OB):
                w = nw % 3
                nw += 1
                g = ng % 2
                ng += 1
                s.dma("sp", wg[w][:], Wo[ob], writes=[b_wg[w]])
                s.dma("act", xt[g][:], xT[ob * 128:(ob + 1) * 128, t0:t0 + NT], writes=[b_xt[g]])
                for c in range(KC):
                    s.op("pe", lambda e: e.matmul(pso[g][:], lhsT=wg[w][:, c, :], rhs=mg[:, c, :],
                                                 start=(c == 0), stop=(c == KC - 1)),
                         reads=[b_wg[w], b_mg[c]], writes=[b_pso[g]], inc=(c == KC - 1))
                s.op("dve", lambda e: e.tensor_tensor(out=xn[g][:], in0=pso[g][:], in1=xt[g][:], op=ALU.add),
                     reads=[b_pso[g], b_xt[g]], writes=[b_xn[g]])
                s.dma("sp", xnT[ob * 128:(ob + 1) * 128, t0:t0 + NT], xn[g][:], reads=[b_xn[g]])
        s.barrier()


def stage_lru(k, PT, row_ax, row_ag, yT, row_y, prm, T, NBLK):
    s = k.s
    NCH = NBLK * 2
    with contextlib.ExitStack() as st:
        cw = k.sb(st, [128, NCH, 4], F32)
        cb, ba, bx, lam, c1, c2, tt = [k.sb(st, [128, NCH], F32) for _ in range(7)]
        b_prm = Buf()
        for dst, nm in ((cw, "conv_w"), (cb, "conv_b"), (ba, "ba"), (bx, "bx"), (lam, "lam")):
            s.dma("sp", dst[:], prm[nm], writes=[b_prm])
        s.op("act", lambda e: e.activation(out=tt[:], in_=lam[:], func=AF.Exp, scale=-1.0), reads=[b_prm], writes=[b_prm])
        s.op("act", lambda e: e.activation(out=tt[:], in_=tt[:], func=AF.Ln, bias=1.0), reads=[b_prm], writes=[b_prm])
        s.op("dve", lambda e: e.tensor_scalar(out=c1[:], in0=tt[:], scalar1=-8.0, scalar2=None, op0=ALU.mult), reads=[b_prm], writes=[b_prm])
        s.op("dve", lambda e: e.tensor_scalar(out=c2[:], in0=tt[:], scalar1=-16.0, scalar2=None, op0=ALU.mult), reads=[b_prm], writes=[b_prm])
        waf = k.sb(st, [128, 2, 2, 128], F32)
        wab = [k.sb(st, [128, 2, 2, 128], BF16) for _ in range(2)]
        xin = [k.sb(st, [128, 3 + NT], F32) for _ in range(2)]
        u = [k.sb(st, [128, NT], F32) for _ in range(2)]
        ub = [k.sb(st, [128, NT], BF16) for _ in range(2)]
        rr, ii, aa, a2, mm, bt, gt, sg = [k.sb(st, [128, NT], F32) for _ in range(8)]
        hs = [[k.sb(st, [128, NT], F32) for _ in range(2)] for _ in range(2)]
        yo = [k.sb(st, [128, NT], BF16) for _ in range(2)]
        psr, psi = k.ps(st, [128, NT]), k.ps(st, [128, NT])
        b_waf, b_psr, b_psi, b_rr, b_ii, b_aa, b_a2, b_mm, b_bt, b_gt, b_sg = bufs(11)
        b_wab, b_xin, b_u, b_ub, b_yo = bufs(2), bufs(2), bufs(2), bufs(2), bufs(2)
        b_hs = [bufs(2), bufs(2)]
        for nb in range(NBLK):
            for wi, nm in enumerate(("wa", "wx")):
                s.dma("sp", waf[:], prm[nm][nb].rearrange("o p c m -> p o c m"), writes=[b_waf])
                s.op("act", lambda e: e.copy(out=wab[wi][:], in_=waf[:]), reads=[b_waf], writes=[b_wab[wi]])
            for ti, t0 in enumerate(range(0, T, NT)):
                par = ti % 2
                for kc in range(2):
                    ch = nb * 2 + kc
                    if ti == 0:
                        s.op("pool", lambda e: e.memset(xin[kc][:, 0:3], 0.0), writes=[b_xin[kc]])
                    else:
                        s.op("pool", lambda e: e.tensor_copy(out=xin[kc][:, 0:3], in_=xin[kc][:, NT:NT + 3]),
                             reads=[b_xin[kc]], writes=[b_xin[kc]])
                    s.dma("sp", xin[kc][:, 3:3 + NT], PT[row_ax + ch * 128: row_ax + (ch + 1) * 128, t0:t0 + NT],
                          reads=[b_xin[kc]], writes=[b_xin[kc]])
                    s.op("dve", lambda e: e.tensor_scalar(out=u[kc][:], in0=xin[kc][:, 3:3 + NT], scalar1=cw[:, ch, 3:4],
                                                          scalar2=cb[:, ch:ch + 1], op0=ALU.mult, op1=ALU.add),
                         reads=[b_xin[kc], b_prm], writes=[b_u[kc]])
                    for j in (2, 1, 0):
                        s.op("dve", lambda e: e.scalar_tensor_tensor(out=u[kc][:], in0=xin[kc][:, j:j + NT], scalar=cw[:, ch, j:j + 1],
                                                                     in1=u[kc][:], op0=ALU.mult, op1=ALU.add),
                             reads=[b_xin[kc], b_prm, b_u[kc]], writes=[b_u[kc]])
                    s.op("act", lambda e: e.copy(out=ub[kc][:], in_=u[kc][:]), reads=[b_u[kc]], writes=[b_ub[kc]])
                for oc in range(2):
                    ch = nb * 2 + oc
                    for kc in range(2):
                        s.op("pe", lambda e: e.matmul(psr[:], lhsT=wab[0][:, oc, kc, :], rhs=ub[kc][:], start=(kc == 0), stop=(kc == 1)),
                             reads=[b_wab[0], b_ub[kc]], writes=[b_psr], inc=(kc == 1))
                    for kc in range(2):
                        s.op("pe", lambda e: e.matmul(psi[:], lhsT=wab[1][:, oc, kc, :], rhs=ub[kc][:], start=(kc == 0), stop=(kc == 1)),
                             reads=[b_wab[1], b_ub[kc]], writes=[b_psi], inc=(kc == 1))
                    s.op("act", lambda e: e.activation(out=rr[:], in_=psr[:], func=AF.Sigmoid, bias=ba[:, ch:ch + 1]),
                         reads=[b_psr, b_prm], writes=[b_rr])
                    s.op("act", lambda e: e.activation(out=ii[:], in_=psi[:], func=AF.Sigmoid, bias=bx[:, ch:ch + 1]),
                         reads=[b_psi, b_prm], writes=[b_ii])
                    s.op("act", lambda e: e.activation(out=aa[:], in_=rr[:], func=AF.Exp, scale=c1[:, ch:ch + 1]),
                         reads=[b_rr, b_prm], writes=[b_aa])
                    s.op("act", lambda e: e.activation(out=a2[:], in_=rr[:], func=AF.Exp, scale=c2[:, ch:ch + 1]),
                         reads=[b_rr, b_prm], writes=[b_a2])
                    s.op("act", lambda e: e.activation(out=mm[:], in_=a2[:], func=AF.Sqrt, scale=-1.0, bias=1.0),
                         reads=[b_a2], writes=[b_mm])
                    s.op("dve", lambda e: e.tensor_tensor(out=bt[:], in0=mm[:], in1=ii[:], op=ALU.mult),
                         reads=[b_mm, b_ii], writes=[b_bt])
                    s.op("dve", lambda e: e.tensor_tensor(out=bt[:], in0=bt[:], in1=u[oc][:], op=ALU.mult),
                         reads=[b_bt, b_u[oc]], writes=[b_bt])
                    init = 0.0 if ti == 0 else hs[oc][1 - par][:, NT - 1:NT]
                    s.op("dve", lambda e: e.tensor_tensor_scan(out=hs[oc][par][:], data0=aa[:], data1=bt[:], initial=init,
                                                               op0=ALU.mult, op1=ALU.add),
                         reads=[b_aa, b_bt] + ([] if ti == 0 else [b_hs[oc][1 - par]]), writes=[b_hs[oc][par]])
                    s.dma("act", gt[:], PT[row_ag + ch * 128: row_ag + (ch + 1) * 128, t0:t0 + NT], writes=[b_gt])
                    s.op("act", lambda e: e.activation(out=sg[:], in_=gt[:], func=AF.Silu), reads=[b_gt], writes=[b_sg])
                    s.op("dve", lambda e: e.tensor_tensor(out=yo[oc][:], in0=hs[oc][par][:], in1=sg[:], op=ALU.mult),
                         reads=[b_hs[oc][par], b_sg], writes=[b_yo[oc]])
                    s.dma("sp", yT[row_y + ch * 128: row_y + (ch + 1) * 128, t0:t0 + NT], yo[oc][:], reads=[b_yo[oc]])
        s.barrier()


def stage_xattn(k, PT, row_q, row_g, yT, row_y, memnT, Wk, Wv, ident_d, T, D, M, H):
    s = k.s
    KC = D // 128
    MC = M // 128
    sc = 128 ** -0.5
    with contextlib.ExitStack() as st:
        ident = k.sb(st, [128, 128], F32)
        memn = k.sb(st, [128, KC, M], BF16)
        kT = k.sb(st, [128, H, M], BF16)
        vt = k.sb(st, [128, MC, H * 128], BF16)
        b_id, b_memn, b_kT, b_vt = bufs(4)
        s.dma("sp", ident[:], ident_d, writes=[b_id])
        dma_rows(s, "sp", memn, memnT, KC, writes=[b_memn])
        with contextlib.ExitStack() as st1:
            wf = k.sb(st1, [128, KC, 128], F32)
            wb = k.sb(st1, [128, KC, 128], BF16)
            vf = [k.sb(st1, [128, H * 128], F32) for _ in range(2)]
            vb = [k.sb(st1, [128, H * 128], BF16) for _ in range(2)]
            psk = k.ps(st1, [128, M])
            psv = [k.ps(st1, [128, H * 128]) for _ in range(MC)]
            b_wf, b_wb, b_psk = bufs(3)
            b_vf, b_vb, b_psv = bufs(2), bufs(2), bufs(MC)
            for hd in range(H):
                s.dma("sp", wf[:], Wk[hd], writes=[b_wf])
                s.op("act", lambda e: e.copy(out=wb[:], in_=wf[:]), reads=[b_wf], writes=[b_wb])
                for c in range(KC):
                    s.op("pe", lambda e: e.matmul(psk[:], lhsT=wb[:, c, :], rhs=memn[:, c, :], start=(c == 0), stop=(c == KC - 1)),
                         reads=[b_wb, b_memn], writes=[b_psk], inc=(c == KC - 1))
                s.op("dve", lambda e: e.tensor_copy(out=kT[:, hd, :], in_=psk[:]), reads=[b_psk], writes=[b_kT])
            for c in range(KC):
                i = c % 2
                s.dma("sp", vf[i][:], Wv[c], writes=[b_vf[i]])
                s.op("act", lambda e: e.copy(out=vb[i][:], in_=vf[i][:]), reads=[b_vf[i]], writes=[b_vb[i]])
                for mc in range(MC):
                    s.op("pe", lambda e: e.matmul(psv[mc][:], lhsT=memn[:, c, mc * 128:(mc + 1) * 128], rhs=vb[i][:],
                                                 start=(c == 0), stop=(c == KC - 1)),
                         reads=[b_vb[i], b_memn], writes=[b_psv[mc]], inc=True)
            for mc in range(MC):
                s.op("dve", lambda e: e.tensor_copy(out=vt[:, mc, :], in_=psv[mc][:]), reads=[b_psv[mc]], writes=[b_vt])
            s.barrier()
        qf = k.sb(st, [128, H, NT], F32)
        qb = k.sb(st, [128, H, NT], BF16)
        gf = k.sb(st, [128, H, NT], F32)
        sg = k.sb(st, [128, H, NT], F32)
        pf = [k.sb(st, [128, M], F32) for _ in range(2)]
        pn = [k.sb(st, [128, M], F32) for _ in range(2)]
        pT = [k.sb(st, [128, MC, 128], BF16) for _ in range(2)]
        mx, nmx, rsum, rinv = [[k.sb(st, [128, 1], F32) for _ in range(2)] for _ in range(4)]
        yo = [k.sb(st, [128, NT], BF16) for _ in range(2)]
        pss = [k.ps(st, [128, M]) for _ in range(2)]
        pst = [k.ps(st, [128, MC, 128]) for _ in range(2)]
        pso = [k.ps(st, [128, 128]) for _ in range(2)]
        b_qf, b_qb, b_gf, b_sg = bufs(4)
        b_pf, b_pn, b_pT, b_mx, b_nmx, b_rsum, b_rinv, b_yo, b_pss, b_pst, b_pso = [bufs(2) for _ in range(11)]
        it = 0
        for t0 in range(0, T, NT):
            s.dma("sp", qf[:], PT[row_q:row_q + H * 128, t0:t0 + NT].rearrange("(h p) t -> p h t", p=128), writes=[b_qf])
            s.dma("act", gf[:], PT[row_g:row_g + H * 128, t0:t0 + NT].rearrange("(h p) t -> p h t", p=128), writes=[b_gf])
            s.op("act", lambda e: e.copy(out=qb[:], in_=qf[:]), reads=[b_qf], writes=[b_qb])
            s.op("act", lambda e: e.activation(out=sg[:], in_=gf[:], func=AF.Silu), reads=[b_gf], writes=[b_sg])
            for hd in range(H):
                yi = hd % 2
                for tb in range(NT // 128):
                    i = it % 2
                    it += 1
                    tsl = slice(tb * 128, (tb + 1) * 128)
                    s.op("pe", lambda e: e.matmul(pss[i][:], lhsT=qb[:, hd, tsl], rhs=kT[:, hd, :], start=True, stop=True),
                         reads=[b_qb, b_kT], writes=[b_pss[i]])
                    s.op("dve", lambda e: e.tensor_reduce(out=mx[i][:], in_=pss[i][:], axis=AX.X, op=ALU.max),
                         reads=[b_pss[i]], writes=[b_mx[i]])
                    s.op("dve", lambda e: e.tensor_scalar(out=nmx[i][:], in0=mx[i][:], scalar1=-sc, scalar2=None, op0=ALU.mult),
                         reads=[b_mx[i]], writes=[b_nmx[i]])
                    s.op("act", lambda e: e.activation(out=pf[i][:], in_=pss[i][:], func=AF.Exp, scale=sc, bias=nmx[i][:],
                                                       accum_out=rsum[i][:]),
                         reads=[b_pss[i], b_nmx[i]], writes=[b_pf[i], b_rsum[i]])
                    s.op("dve", lambda e: e.reciprocal(out=rinv[i][:], in_=rsum[i][:]), reads=[b_rsum[i]], writes=[b_rinv[i]])
                    s.op("dve", lambda e: e.tensor_scalar(out=pn[i][:], in0=pf[i][:], scalar1=rinv[i][:], scalar2=None, op0=ALU.mult),
                         reads=[b_pf[i], b_rinv[i]], writes=[b_pn[i]])
                    for mc in range(MC):
                        s.op("pe", lambda e: e.transpose(pst[i][:, mc, :], pn[i][:, mc * 128:(mc + 1) * 128], ident[:]),
                             reads=[b_pn[i], b_id], writes=[b_pst[i]], inc=True)
                    s.op("act", lambda e: e.copy(out=pT[i][:], in_=pst[i][:]), reads=[b_pst[i]], writes=[b_pT[i]])
                    for mc in range(MC):
                        s.op("pe", lambda e: e.matmul(pso[i][:], lhsT=vt[:, mc, hd * 128:(hd + 1) * 128], rhs=pT[i][:, mc, :],
                                                     start=(mc == 0), stop=(mc == MC - 1)),
                             reads=[b_vt, b_pT[i]], writes=[b_pso[i]], inc=(mc == MC - 1))
                    s.op("dve", lambda e: e.tensor_tensor(out=yo[yi][:, tsl], in0=pso[i][:], in1=sg[:, hd, tsl], op=ALU.mult),
                         reads=[b_pso[i], b_sg], writes=[b_yo[yi]])
                s.dma("sp", yT[row_y + hd * 128: row_y + (hd + 1) * 128, t0:t0 + NT], yo[yi][:], reads=[b_yo[yi]])
        s.barrier()


LCH = 64
MLSTM_L = 128
MLSTM_FP32R = True
NEG = -1.0e30
RWKV_G = 2
RWKV_L = 128
RWKV_FP32R = True


def stage_mlstm(k, PT, rows, yT, row_y, prm, cst, T, HC):
    s = k.s
    ML = MLSTM_L
    NCK = T // ML
    QC = 2 * HC
    with contextlib.ExitStack() as st:
        ident = k.sb(st, [128, 128], F32)
        sel = k.sb(st, [HC, HC, 128], F32)
        i4 = k.sb(st, [HC, HC], F32)
        negmask = k.sb(st, [ML, ML], F32)
        ones = k.sb(st, [128, 128], F32)
        cw = k.sb(st, [128, 2 * QC, 4], F32)
        cb = k.sb(st, [128, 2 * QC], F32)
        gnw = k.sb(st, [128, HC * 4], F32)
        bi = k.sb(st, [HC, 1], F32)
        bfn = k.sb(st, [HC, 1], F32)
        b_c = Buf()
        for dst, src in ((ident, cst["ident"]), (sel, cst["sel"]), (i4, cst["i4"]), (negmask, cst["negmask"]),
                         (cw, prm["conv_w"]), (cb, prm["conv_b"]), (gnw, prm["gn_w"]), (bi, prm["b_i"]), (bfn, prm["b_f"])):
            s.dma("sp", dst[:], src, writes=[b_c])
        s.op("pool", lambda e: e.memset(ones[:], 1.0), writes=[b_c])
        ones_r = k.sb(st, [128, 128], F32)
        s.op("act", lambda e: e.copy(out=ones_r[:].bitcast(mybir.dt.float32r), in_=ones[:]), reads=[b_c], writes=[b_c])
        s.op("dve", lambda e: e.tensor_scalar(out=bfn[:], in0=bfn[:], scalar1=-1.0, scalar2=None, op0=ALU.mult), reads=[b_c], writes=[b_c])
        gi = k.sb(st, [HC, T], F32)
        gf = k.sb(st, [HC, T], F32)
        gm = k.sb(st, [HC, T], F32)
        ge = k.sb(st, [HC, T], F32)
        b_g = Buf()
        s.dma("sp", gi[:], PT[rows["ifg"]:rows["ifg"] + HC, :], writes=[b_g])
        s.dma("sp", gf[:], PT[rows["ifg"] + HC:rows["ifg"] + 2 * HC, :], writes=[b_g])
        s.op("act", lambda e: e.activation(out=gf[:], in_=gf[:], func=AF.Exp, scale=-1.0, bias=bfn[:]), reads=[b_g, b_c], writes=[b_g])
        s.op("act", lambda e: e.activation(out=gf[:], in_=gf[:], func=AF.Ln, bias=1.0), reads=[b_g], writes=[b_g])
        s.op("dve", lambda e: e.tensor_scalar(out=gf[:], in0=gf[:], scalar1=-1.0, scalar2=None, op0=ALU.mult), reads=[b_g], writes=[b_g])
        s.op("pool", lambda e: e.memset(gm[:], 1.0), writes=[b_g])
        s.op("dve", lambda e: e.tensor_tensor_scan(out=gf[:], data0=gm[:], data1=gf[:], initial=0.0, op0=ALU.mult, op1=ALU.add),
             reads=[b_g], writes=[b_g])
        s.op("dve", lambda e: e.scalar_tensor_tensor(out=gi[:], in0=gi[:], scalar=bi[:], in1=gf[:], op0=ALU.add, op1=ALU.subtract),
             reads=[b_g, b_c], writes=[b_g])
        s.op("dve", lambda e: e.tensor_tensor_scan(out=gm[:], data0=gi[:], data1=gi[:], initial=NEG, op0=ALU.max, op1=ALU.max),
             reads=[b_g], writes=[b_g])
        s.op("dve", lambda e: e.tensor_tensor(out=ge[:], in0=gf[:], in1=gm[:], op=ALU.add), reads=[b_g], writes=[b_g])
        cols = k.sb(st, [ML, NCK, 2 * HC], F32)
        b_cols = Buf()
        with contextlib.ExitStack() as st1:
            psc = [k.ps(st1, [ML, 8, 2 * HC]) for _ in range(2)]
            b_psc = bufs(2)
            for c8 in range(0, NCK, 8):
                i = (c8 // 8) % 2
                for cc in range(8):
                    c = c8 + cc
                    s.op("pe", lambda e: e.matmul(psc[i][:, cc, 0:HC], lhsT=gi[:, c * ML:(c + 1) * ML], rhs=i4[:, 0:HC], start=True, stop=True),
                         reads=[b_g, b_c], writes=[b_psc[i]])
                    s.op("pe", lambda e: e.matmul(psc[i][:, cc, HC:2 * HC], lhsT=ge[:, c * ML:(c + 1) * ML], rhs=i4[:, 0:HC], start=True, stop=True),
                         reads=[b_g, b_c], writes=[b_psc[i]])
                s.op("dve", lambda e: e.tensor_copy(out=cols[:, c8:c8 + 8, :], in_=psc[i][:]), reads=[b_psc[i]], writes=[b_cols])
            s.op("act", lambda e: e.activation(out=cols[:, :, HC:2 * HC], in_=cols[:, :, HC:2 * HC], func=AF.Exp, scale=-1.0),
                 reads=[b_cols], writes=[b_cols])
            s.barrier()
        xin = k.sb(st, [128, 4, 3 + NT], F32)
        qk = k.sb(st, [128, 4, NT], F32)
        vT = k.sb(st, [128, 4, NT], F32)
        oT = k.sb(st, [128, 4, NT], F32)
        gT = k.sb(st, [128, 4, NT], F32)
        negMb = k.sb(st, [128, NT], F32)
        ms = k.sb(st, [128, NT // ML + 1], F32)
        keep2 = [k.sb(st, [128, 1], F32) for _ in range(2)]
        b_keep2 = bufs(2)
        ho = k.sb(st, [128, 4, NT], F32)
        Cst = k.sb(st, [128, 2, 512], F32)
        Cr = k.sb(st, [128, 2, 512], F32)
        b_Cr = Buf()
        R = (lambda ap: ap.bitcast(mybir.dt.float32r)) if MLSTM_FP32R else (lambda ap: ap)
        nst = k.sb(st, [128, 2], F32)
        ktok2 = [k.sb(st, [ML, 256], F32) for _ in range(2)]
        b_ktok2 = bufs(2)
        khat2 = [k.sb(st, [ML, 256], F32) for _ in range(2)]
        b_khat2 = bufs(2)
        vtok2 = [k.sb(st, [ML, 512], F32) for _ in range(2)]
        b_vtok2 = bufs(2)
        wk2 = [k.sb(st, [ML, 1], F32) for _ in range(2)]
        b_wk2 = bufs(2)
        argE = k.sb(st, [ML, ML], F32)
        ET = k.sb(st, [ML, ML], F32)
        scT2 = [k.sb(st, [ML, ML], F32) for _ in range(2)]
        b_scT2 = bufs(2)
        wint = k.sb(st, [128, ML], F32)
        qt2 = [k.sb(st, [128, 2, ML], F32) for _ in range(2)]
        b_qt2 = bufs(2)
        den = k.sb(st, [ML, 1], F32)
        rden = k.sb(st, [ML, 1], F32)
        htok = k.sb(st, [ML, 512], F32)
        mean = k.sb(st, [128, NT], F32)
        yc = k.sb(st, [128, 4, NT], F32)
        ysq = k.sb(st, [128, NT], F32)
        rstd = k.sb(st, [128, NT], F32)
        yo = [k.sb(st, [128, NT], BF16) for _ in range(2)]
        ps_b = k.ps(st, [128, NT])
        ps_t = k.ps(st, [ML, 512])
        ps_s = k.ps(st, [ML, ML])
        ps_n = k.ps(st, [ML, 512])
        ps_d = k.ps(st, [ML, 8])
        ps_c = k.ps(st, [128, 512])
        ps_h = k.ps(st, [128, 4, ML])
        ps_m = k.ps(st, [128, 8])
        (b_xin, b_qk, b_vT, b_oT, b_gT, b_negMb, b_ms, b_keep, b_ho, b_C, b_n, b_ktok, b_khat, b_vtok, b_wk, b_argE, b_ET, b_scT,
         b_wint, b_qt, b_den, b_rden, b_htok, b_mean, b_yc, b_ysq, b_rstd, b_psb, b_pst, b_pss, b_psn, b_psd, b_psc2, b_psh, b_psm) = bufs(35)
        b_yo = bufs(2)
        for h in range(HC):
            qrows = [rows["q"] + h * 256, rows["q"] + h * 256 + 128, rows["k"] + h * 256, rows["k"] + h * 256 + 128]
            cidx = [h * 2, h * 2 + 1, QC + h * 2, QC + h * 2 + 1]
            s.op("pool", lambda e: e.memset(Cst[:], 0.0), reads=[b_C], writes=[b_C])
            s.op("act", lambda e: e.copy(out=R(Cr[:]), in_=Cst[:]), reads=[b_C, b_Cr], writes=[b_Cr])
            s.op("pool", lambda e: e.memset(nst[:], 0.0), reads=[b_n], writes=[b_n])
            s.op("pool", lambda e: e.memset(ms[:, 0:1], NEG), reads=[b_ms], writes=[b_ms])
            for ti, t0 in enumerate(range(0, T, NT)):
                for j in range(4):
                    if ti == 0:
                        s.op("pool", lambda e: e.memset(xin[:, j, 0:3], 0.0), reads=[b_xin], writes=[b_xin])
                    else:
                        s.op("pool", lambda e: e.tensor_copy(out=xin[:, j, 0:3], in_=xin[:, j, NT:NT + 3]), reads=[b_xin], writes=[b_xin])
                for j in range(4):
                    s.dma("sp", xin[:, j, 3:3 + NT], PT[qrows[j]:qrows[j] + 128, t0:t0 + NT], reads=[b_xin], writes=[b_xin])
                s.dma("act", vT[:], PT[rows["v"] + h * 512: rows["v"] + (h + 1) * 512, t0:t0 + NT].rearrange("(c p) t -> p c t", p=128), writes=[b_vT])
                s.dma("act", oT[:], PT[rows["o"] + h * 512: rows["o"] + (h + 1) * 512, t0:t0 + NT].rearrange("(c p) t -> p c t", p=128), writes=[b_oT])
                s.dma("act", gT[:], PT[rows["g"] + h * 512: rows["g"] + (h + 1) * 512, t0:t0 + NT].rearrange("(c p) t -> p c t", p=128), writes=[b_gT])
                for j in range(4):
                    ci = cidx[j]
                    s.op("dve", lambda e: e.tensor_scalar(out=qk[:, j, :], in0=xin[:, j, 3:3 + NT], scalar1=cw[:, ci, 3:4],
                                                          scalar2=cb[:, ci:ci + 1], op0=ALU.mult, op1=ALU.add),
                         reads=[b_xin, b_c, b_qk], writes=[b_qk])
                    for jj in (2, 1, 0):
                        s.op("dve", lambda e: e.scalar_tensor_tensor(out=qk[:, j, :], in0=xin[:, j, jj:jj + NT], scalar=cw[:, ci, jj:jj + 1],
                                                                     in1=qk[:, j, :], op0=ALU.mult, op1=ALU.add),
                             reads=[b_xin, b_c, b_qk], writes=[b_qk])
                s.op("act", lambda e: e.activation(out=qk[:], in_=qk[:], func=AF.Silu), reads=[b_qk], writes=[b_qk])
                s.op("dve", lambda e: e.tensor_scalar(out=qk[:, 2:4, :], in0=qk[:, 2:4, :], scalar1=256 ** -0.5, scalar2=None, op0=ALU.mult),
                     reads=[b_qk], writes=[b_qk])
                s.op("act", lambda e: e.activation(out=oT[:], in_=oT[:], func=AF.Sigmoid), reads=[b_oT], writes=[b_oT])
                s.op("act", lambda e: e.activation(out=gT[:], in_=gT[:], func=AF.Silu), reads=[b_gT], writes=[b_gT])
                s.op("pe", lambda e: e.matmul(ps_b[:], lhsT=sel[:, h, :], rhs=gm[:, t0:t0 + NT], start=True, stop=True),
                     reads=[b_c, b_g, b_psb], writes=[b_psb])
                if ti > 0:
                    s.op("dve", lambda e: e.tensor_scalar(out=ms[:, 0:1], in0=negMb[:, NT - 1:NT], scalar1=-1.0, scalar2=None, op0=ALU.mult),
                         reads=[b_negMb, b_ms], writes=[b_ms])
                s.op("act", lambda e: e.activation(out=negMb[:], in_=ps_b[:], func=AF.Copy, scale=-1.0), reads=[b_psb, b_negMb], writes=[b_negMb])
                nck = NT // ML
                s.op("dve", lambda e: e.tensor_scalar(out=ms[:, 1:nck + 1], in0=negMb[:, ML - 1:NT:ML], scalar1=-1.0, scalar2=None, op0=ALU.mult),
                     reads=[b_negMb, b_ms], writes=[b_ms])
                def m_indep(cc):
                    c = ti * nck + cc
                    cp = c % 2
                    csl = slice(cc * ML, (cc + 1) * ML)
                    ktok, khat, vtok, scT, qt, keep, wk = ktok2[cp], khat2[cp], vtok2[cp], scT2[cp], qt2[cp], keep2[cp], wk2[cp]
                    b_ktok, b_khat, b_vtok, b_scT, b_qt, b_keep, b_wk = (b_ktok2[cp], b_khat2[cp], b_vtok2[cp], b_scT2[cp], b_qt2[cp],
                                                                          b_keep2[cp], b_wk2[cp])
                    def t1():
                        for j in range(2):
                            s.op("pe", lambda e: e.transpose(ps_t[:, j * 128:(j + 1) * 128], qk[:, 2 + j, csl], ident[:]),
                                 reads=[b_qk, b_c, b_pst], writes=[b_pst])
                        s.op("act", lambda e: e.copy(out=ktok[:], in_=ps_t[:, 0:256]), reads=[b_pst, b_ktok], writes=[b_ktok])
                        for j in range(4):
                            s.op("pe", lambda e: e.transpose(ps_t[:, j * 128:(j + 1) * 128], vT[:, j, csl], ident[:]),
                                 reads=[b_vT, b_c, b_pst], writes=[b_pst])
                        s.op("act", lambda e: e.copy(out=R(vtok[:]), in_=ps_t[:]), reads=[b_pst, b_vtok], writes=[b_vtok])
                    def t2():
                        for j in range(2):
                            s.op("pe", lambda e: e.matmul(ps_s[:], lhsT=qk[:, 2 + j, csl], rhs=qk[:, j, csl], start=(j == 0), stop=(j == 1)),
                                 reads=[b_qk, b_pss], writes=[b_pss])
                        s.op("dve", lambda e: e.tensor_tensor(out=argE[:], in0=negMb[0:ML, csl], in1=negmask[:], op=ALU.add),
                             reads=[b_negMb, b_c, b_argE], writes=[b_argE])
                        s.op("act", lambda e: e.activation(out=ET[:], in_=argE[:], func=AF.Exp, bias=cols[:, c, h:h + 1]),
                             reads=[b_argE, b_cols, b_ET], writes=[b_ET])
                        s.op("dve", lambda e: e.tensor_tensor(out=R(scT[:]), in0=ps_s[:], in1=ET[:], op=ALU.mult),
                             reads=[b_pss, b_ET, b_scT], writes=[b_scT])
                    def t3():
                        s.op("act", lambda e: e.activation(out=wint[:], in_=negMb[:, csl], func=AF.Exp, bias=ms[:, cc:cc + 1]),
                             reads=[b_negMb, b_ms, b_wint], writes=[b_wint])
                        for j in range(2):
                            s.op("dve", lambda e: e.tensor_tensor(out=R(qt[:, j, :]), in0=qk[:, j, csl], in1=wint[:], op=ALU.mult),
                                 reads=[b_qk, b_wint, b_qt], writes=[b_qt])
                    def t4():
                        s.op("act", lambda e: e.activation(out=keep[:], in_=negMb[:, (cc + 1) * ML - 1:(cc + 1) * ML], func=AF.Exp, bias=ms[:, cc:cc + 1]),
                             reads=[b_negMb, b_ms, b_keep], writes=[b_keep])
                        s.op("act", lambda e: e.activation(out=wk[:], in_=negMb[0:ML, (cc + 1) * ML - 1:(cc + 1) * ML], func=AF.Exp, bias=cols[:, c, h:h + 1]),
                             reads=[b_negMb, b_cols, b_wk], writes=[b_wk])
                        s.op("dve", lambda e: e.tensor_scalar(out=R(khat[:]), in0=ktok[:], scalar1=wk[:], scalar2=None, op0=ALU.mult),
                             reads=[b_ktok, b_wk, b_khat], writes=[b_khat])
                    return [t1, t2, t3, t4]

                def m_dep(cc):
                    c = ti * nck + cc
                    cp = c % 2
                    csl = slice(cc * ML, (cc + 1) * ML)
                    ktok, khat, vtok, scT, qt, keep, wk = ktok2[cp], khat2[cp], vtok2[cp], scT2[cp], qt2[cp], keep2[cp], wk2[cp]
                    b_ktok, b_khat, b_vtok, b_scT, b_qt, b_keep, b_wk = (b_ktok2[cp], b_khat2[cp], b_vtok2[cp], b_scT2[cp], b_qt2[cp],
                                                                          b_keep2[cp], b_wk2[cp])
                    def d1():
                        for j in range(2):
                            s.op("pe", lambda e: e.matmul(ps_n[:], lhsT=R(qt[:, j, :]), rhs=R(Cr[:, j, :]), start=(j == 0), stop=False),
                                 reads=[b_qt, b_Cr, b_psn], writes=[b_psn], inc=False)
                        s.op("pe", lambda e: e.matmul(ps_n[:], lhsT=R(scT[:]), rhs=R(vtok[:]), start=False, stop=True),
                             reads=[b_scT, b_vtok, b_psn], writes=[b_psn])
                        for j in range(2):
                            s.op("pe", lambda e: e.matmul(ps_d[:, 0:1], lhsT=qt[:, j, :], rhs=nst[:, j:j + 1], start=(j == 0), stop=False),
                                 reads=[b_qt, b_n, b_psd], writes=[b_psd], inc=False)
                        s.op("pe", lambda e: e.matmul(ps_d[:, 0:1], lhsT=scT[:], rhs=ones[0:ML, 0:1], start=False, stop=True),
                             reads=[b_scT, b_c, b_psd], writes=[b_psd])
                    def d2():
                        s.op("act", lambda e: e.activation(out=den[:], in_=ps_d[:, 0:1], func=AF.Abs),
                             reads=[b_psd, b_den], writes=[b_den])
                        s.op("dve", lambda e: e.tensor_tensor(out=den[:], in0=den[:], in1=cols[:, c, HC + h:HC + h + 1], op=ALU.max),
                             reads=[b_den, b_cols], writes=[b_den])
                        s.op("dve", lambda e: e.reciprocal(out=rden[:], in_=den[:]), reads=[b_den, b_rden], writes=[b_rden])
                        s.op("act", lambda e: e.activation(out=htok[:], in_=ps_n[:], func=AF.Copy, scale=rden[:]),
                             reads=[b_psn, b_rden, b_htok], writes=[b_htok])
                    def d3():
                        for j in range(4):
                            s.op("pe", lambda e: e.transpose(ps_h[:, j, :], htok[:, j * 128:(j + 1) * 128], ident[0:ML, 0:ML]),
                                 reads=[b_htok, b_c, b_psh], writes=[b_psh])
                        s.op("dve", lambda e: e.tensor_tensor(out=ho[:, :, csl].bitcast(mybir.dt.float32r), in0=ps_h[:], in1=oT[:, :, csl], op=ALU.mult),
                             reads=[b_psh, b_oT, b_ho], writes=[b_ho])
                    def d4():
                        for j in range(2):
                            s.op("pe", lambda e: e.matmul(ps_c[:], lhsT=R(khat[:, j * 128:(j + 1) * 128]), rhs=R(vtok[:]), start=True, stop=True),
                                 reads=[b_khat, b_vtok, b_psc2], writes=[b_psc2])
                            s.op("dve", lambda e: e.scalar_tensor_tensor(out=Cst[:, j, :], in0=Cst[:, j, :], scalar=keep[:], in1=ps_c[:],
                                                                         op0=ALU.mult, op1=ALU.add),
                                 reads=[b_C, b_keep, b_psc2], writes=[b_C])
                            s.op("act", lambda e: e.copy(out=R(Cr[:, j, :]), in_=Cst[:, j, :]), reads=[b_C, b_Cr], writes=[b_Cr])
                            s.op("pe", lambda e: e.matmul(ps_m[:, 0:1], lhsT=khat[:, j * 128:(j + 1) * 128], rhs=ones[0:ML, 0:1], start=True, stop=True),
                                 reads=[b_khat, b_c, b_psm], writes=[b_psm])
                            s.op("dve", lambda e: e.scalar_tensor_tensor(out=nst[:, j:j + 1], in0=nst[:, j:j + 1], scalar=keep[:], in1=ps_m[:, 0:1],
                                                                         op0=ALU.mult, op1=ALU.add),
                                 reads=[b_n, b_keep, b_psm], writes=[b_n])
                    return [d1, d2, d3, d4]

                for f_ in m_indep(0):
                    f_()
                for cc in range(nck):
                    dst_ = m_dep(cc)
                    ist_ = m_indep(cc + 1) if cc + 1 < nck else []
                    for j_ in range(max(len(dst_), len(ist_))):
                        if j_ < len(dst_):
                            dst_[j_]()
                        if j_ < len(ist_):
                            ist_[j_]()
                for j in range(4):
                    s.op("pe", lambda e: e.matmul(ps_b[:], lhsT=ones_r[:].bitcast(mybir.dt.float32r), rhs=ho[:, j, :].bitcast(mybir.dt.float32r), start=(j == 0), stop=(j == 3)),
                         reads=[b_c, b_ho, b_psb], writes=[b_psb], inc=(j == 3))
                s.op("act", lambda e: e.activation(out=mean[:], in_=ps_b[:], func=AF.Copy, scale=1.0 / 512), reads=[b_psb, b_mean], writes=[b_mean])
                for j in range(4):
                    s.op("dve", lambda e: e.tensor_tensor(out=yc[:, j, :], in0=ho[:, j, :], in1=mean[:], op=ALU.subtract),
                         reads=[b_ho, b_mean, b_yc], writes=[b_yc])
                for j in range(4):
                    s.op("act", lambda e: e.activation(out=ysq[:].bitcast(mybir.dt.float32r), in_=yc[:, j, :], func=AF.Square), reads=[b_yc, b_ysq], writes=[b_ysq])
                    s.op("pe", lambda e: e.matmul(ps_b[:], lhsT=ones_r[:].bitcast(mybir.dt.float32r), rhs=ysq[:].bitcast(mybir.dt.float32r), start=(j == 0), stop=(j == 3)),
                         reads=[b_c, b_ysq, b_psb], writes=[b_psb])
                s.op("act", lambda e: e.activation(out=rstd[:], in_=ps_b[:], func=AF.Sqrt, scale=1.0 / 512, bias=1e-6),
                     reads=[b_psb, b_rstd], writes=[b_rstd])
                s.op("dve", lambda e: e.reciprocal(out=rstd[:], in_=rstd[:]), reads=[b_rstd], writes=[b_rstd])
                for j in range(4):
                    yi = j % 2
                    s.op("dve", lambda e: e.scalar_tensor_tensor(out=yc[:, j, :], in0=yc[:, j, :], scalar=gnw[:, h * 4 + j:h * 4 + j + 1],
                                                                 in1=rstd[:], op0=ALU.mult, op1=ALU.mult),
                         reads=[b_yc, b_c, b_rstd], writes=[b_yc])
                    s.op("dve", lambda e: e.tensor_tensor(out=yo[yi][:], in0=yc[:, j, :], in1=gT[:, j, :], op=ALU.mult),
                         reads=[b_yc, b_gT, b_yo[yi]], writes=[b_yo[yi]])
                    r0 = row_y + h * 512 + j * 128
                    s.dma("sp", yT[r0:r0 + 128, t0:t0 + NT], yo[yi][:], reads=[b_yo[yi]])
        s.barrier()


def stage_rwkv(k, PT, rows, yT, row_y, prm, cst, T, NH, GN_EPS):
    s = k.s
    G = RWKV_G
    assert G == 2
    NLVL = {64: 6, 128: 7}[RWKV_L]
    RL = RWKV_L
    NCK = NT // RL
    assert NH % G == 0 and T % NT == 0
    with contextlib.ExitStack() as st:
        ident = k.sb(st, [128, 128], F32)
        ones64 = k.sb(st, [64, 64], F32)
        rmask = k.sb(st, [64, NT], F32)
        maskA = k.sb(st, [RL, G, 2 * RL], F32)
        maskN = k.sb(st, [RL, G, RL], F32)
        P = {}
        for nm in ("mu_r", "mu_k", "mu_v", "w0", "a0", "k_k", "k_a", "r_k", "gn_w", "gn_b"):
            P[nm] = k.sb(st, [64, NH], F32)
        om = {nm: k.sb(st, [64, NH], F32) for nm in ("mu_r", "mu_k", "mu_v", "k_a")}
        nw0 = k.sb(st, [64, NH], F32)
        mud = {nm: k.sb(st, [96, 1], F32) for nm in ("mu_wd", "mu_ad")}
        omd = {nm: k.sb(st, [96, 1], F32) for nm in ("mu_wd", "mu_ad")}
        b_c = Buf()
        for dst, src in ((ident, cst["ident"]), (rmask, cst["resetmask"]), (maskA, cst["maskA"]), (maskN, cst["maskN"])):
            s.dma("sp", dst[:], src, writes=[b_c])
        for nm in P:
            s.dma("sp", P[nm][:], prm[nm], writes=[b_c])
        for nm in mud:
            s.dma("sp", mud[nm][:], prm[nm], writes=[b_c])
        s.op("pool", lambda e: e.memset(ones64[:], 1.0), writes=[b_c])
        ones64r = k.sb(st, [64, 64], F32)
        s.op("act", lambda e: e.copy(out=ones64r[:].bitcast(mybir.dt.float32r), in_=ones64[:]), reads=[b_c], writes=[b_c])
        t3 = [k.sb(st, [64, NT], F32) for _ in range(G)]
        b_t3 = bufs(G)
        zer = k.sb(st, [64, G, 64], F32)
        s.op("pool", lambda e: e.memset(zer[:], 0.0), writes=[b_c])
        for nm in om:
            s.op("dve", lambda e: e.tensor_scalar(out=om[nm][:], in0=P[nm][:], scalar1=-1.0, scalar2=1.0, op0=ALU.mult, op1=ALU.add),
                 reads=[b_c], writes=[b_c])
        for nm in omd:
            s.op("dve", lambda e: e.tensor_scalar(out=omd[nm][:], in0=mud[nm][:], scalar1=-1.0, scalar2=1.0, op0=ALU.mult, op1=ALU.add),
                 reads=[b_c], writes=[b_c])
        s.op("dve", lambda e: e.tensor_scalar(out=nw0[:], in0=P["w0"][:], scalar1=-1.0, scalar2=None, op0=ALU.mult), reads=[b_c], writes=[b_c])

        wup = k.sb(st, [96, G * 64], F32)
        aup = k.sb(st, [96, G * 64], F32)
        rawd = k.sb(st, [96, 1 + NT], F32)
        rawa = k.sb(st, [96, 1 + NT], F32)
        twd = k.sb(st, [96, NT], F32)
        adq = k.sb(st, [96, NT], F32)
        raw = {nm: k.sb(st, [64, G, 1 + NT], F32) for nm in ("r", "k", "v")}
        gate = None
        (r_, k_, e2, c_, eP, eN, a_, kk, kkn, kmod, kka, t1, t2) = [[k.sb(st, [64, NT], F32) for _ in range(G)] for _ in range(13)]
        yfm = k.sb(st, [64, G, NT], F32)
        dbl_tiles = [[k.sb(st, shp, F32) for _ in range(2)] for shp in
                     ([64, G, NT], [64, G, NCK, 2, RL], [64, G, NT], [64, G, NT], [64, G, NT], [64, G, NT], [64, G, NT], [64, G, NT], [64, G, NCK])]
        dbl_bufs = [bufs(2) for _ in range(9)]
        P1_PER_STEP = 2
        tokb = [k.sb(st, [RL, 3, G, 64], F32) for _ in range(2)]
        GAb = [k.sb(st, [RL, G, 2 * RL], F32) for _ in range(2)]
        GKb = [k.sb(st, [RL, G, 2 * RL], F32) for _ in range(2)]
        Pnb = [k.sb(st, [RL, G, RL], F32) for _ in range(2)]
        PPb = [[k.sb(st, [RL, G, 2, RL], F32) for _ in range(NLVL - 1)] for _ in range(2)]
        b_tokb, b_GAb, b_GKb, b_Pnb = bufs(2), bufs(2), bufs(2), bufs(2)
        b_PPb = [bufs(NLVL - 1), bufs(NLVL - 1)]
        U = k.sb(st, [RL, G, 64], F32)
        Ysb = k.sb(st, [RL, G, 64], F32)
        ST = k.sb(st, [64, G, 64], F32)
        yo = [k.sb(st, [64, NT], BF16) for _ in range(2)]
        bank = [k.ps(st, [128, 512]) for _ in range(8)]
        b_bank = bufs(8)
        ps_A, ps_B = bank[0], bank[1]
        ps_ga = bank[0][0:RL, 0:G * 2 * RL].rearrange("p (g x) -> p g x", g=G)
        ps_gk = bank[1][0:RL, 0:G * 2 * RL].rearrange("p (g x) -> p g x", g=G)
        ps_x1 = bank[2][0:RL, 0:2 * G * 64].rearrange("p (a g x) -> p a g x", a=2, g=G)
        ps_x2k = bank[3][0:RL, 0:G * 64].rearrange("p (g x) -> p g x", g=G)
        ps_gn = bank[3][0:RL, G * 64:G * 64 + G * RL].rearrange("p (g x) -> p g x", g=G)
        ps_z = bank[4][0:RL, 0:2 * G * 64].rearrange("p (a g x) -> p a g x", a=2, g=G)
        ps_pp = bank[5][0:RL, 0:G * 2 * RL].rearrange("p (g a x) -> p g a x", g=G, a=2)
        ps_y = bank[6][0:RL, 0:G * 64].rearrange("p (g x) -> p g x", g=G)
        ps_yT = bank[6][0:64, G * 64:G * 64 + G * RL].rearrange("p (g x) -> p g x", g=G)
        ps_s = bank[7][0:64, 0:G * 64].rearrange("p (g x) -> p g x", g=G)
        ps_ln = bank[7]
        (b_wup, b_rawd, b_rawa, b_twd, b_adq, b_gate,
         b_Vt, b_AR, b_Bt, b_Kt, b_Bh, b_Kh, b_bonus, b_yfm, b_gL, b_tok, b_GAs, b_GKs, b_Pn, b_U, b_Ysb, b_ST) = bufs(22)
        (b_r, b_k, b_e2, b_cc, b_eP, b_eN, b_a, b_kk, b_kkn, b_kmod, b_kka, b_t1, b_t2) = [bufs(G) for _ in range(13)]
        b_raw = {nm: Buf() for nm in raw}
        b_PP = bufs(2)
        b_yo = bufs(2)
        i64 = ident[0:64, 0:64]
        R = (lambda ap: ap.bitcast(mybir.dt.float32r)) if RWKV_FP32R else (lambda ap: ap)

        def shift(dst, src3, g, mu_t, om_t, H, reads, bdst):
            s.op("dve", lambda e: e.tensor_scalar(out=dst, in0=src3[:, g, 1:1 + NT], scalar1=om_t[:, H:H + 1], scalar2=None, op0=ALU.mult),
                 reads=reads + [b_c, bdst], writes=[bdst])
            s.op("dve", lambda e: e.scalar_tensor_tensor(out=dst, in0=src3[:, g, 0:NT], scalar=mu_t[:, H:H + 1], in1=dst,
                                                         op0=ALU.mult, op1=ALU.add),
                 reads=reads + [b_c, bdst], writes=[bdst])

        for h0 in range(0, NH, G):
            s.dma("sp", wup[:], prm["w_up"][:, h0 * 64:(h0 + G) * 64], reads=[b_wup], writes=[b_wup])
            s.dma("sp", aup[:], prm["a_up"][:, h0 * 64:(h0 + G) * 64], reads=[b_wup], writes=[b_wup])
            s.op("dve", lambda e: e.tensor_copy(out=R(ST[:]), in_=zer[:]), reads=[b_ST, b_c], writes=[b_ST])
            def make_tile(ti, t0):
                tp = ti % 2
                Vt, AR, Bt, Kt, Bh, Kh, bonus, gate, gL = (x[tp] for x in dbl_tiles)
                b_Vt, b_AR, b_Bt, b_Kt, b_Bh, b_Kh, b_bonus, b_gate, b_gL = (x[tp] for x in dbl_bufs)
                def head_gen(g):
                        H = h0 + g
                        hc = slice(g * 64, (g + 1) * 64)
                        shift(r_[g][:], raw["r"], g, P["mu_r"], om["mu_r"], H, [b_raw["r"]], b_r[g])
                        yield
                        shift(k_[g][:], raw["k"], g, P["mu_k"], om["mu_k"], H, [b_raw["k"]], b_k[g])
                        yield
                        shift(Vt[:, g, :], raw["v"], g, P["mu_v"], om["mu_v"], H, [b_raw["v"]], b_Vt)
                        yield
                        s.op("pe", lambda e: e.matmul(bank[g][0:64, :], lhsT=wup[:, hc], rhs=twd[:], start=True, stop=True),
                             reads=[b_wup, b_twd, b_bank[g]], writes=[b_bank[g]])
                        yield
                        s.op("act", lambda e: e.activation(out=t1[g][:], in_=bank[g][0:64, :], func=AF.Exp, scale=-1.0, bias=nw0[:, H:H + 1]),
                             reads=[b_bank[g], b_c, b_t1[g]], writes=[b_t1[g]])
                        yield
                        s.op("act", lambda e: e.activation(out=t1[g][:], in_=t1[g][:], func=AF.Ln, bias=1.0), reads=[b_t1[g]], writes=[b_t1[g]])
                        yield
                        s.op("act", lambda e: e.activation(out=e2[g][:], in_=t1[g][:], func=AF.Exp, scale=-1.0, bias=-0.5), reads=[b_t1[g], b_e2[g]], writes=[b_e2[g]])
                        yield
                        s.op("dve", lambda e: e.tensor_tensor_scan(out=c_[g][:], data0=rmask[:], data1=e2[g][:], initial=0.0, op0=ALU.mult, op1=ALU.add),
                             reads=[b_c, b_e2[g], b_cc[g]], writes=[b_cc[g]])
                        yield
                        s.op("act", lambda e: e.activation(out=eP[g][:], in_=c_[g][:], func=AF.Exp, scale=-1.0), reads=[b_cc[g], b_eP[g]], writes=[b_eP[g]])
                        yield
                        s.op("act", lambda e: e.activation(out=eN[g][:], in_=c_[g][:], func=AF.Exp), reads=[b_cc[g], b_eN[g]], writes=[b_eN[g]])
                        yield
                        s.op("pool", lambda e: e.tensor_copy(out=gL[:, g, :], in_=eP[g][:, RL - 1:NT:RL]), reads=[b_eP[g], b_gL], writes=[b_gL])
                        yield
                        yield
                        s.op("pe", lambda e: e.matmul(bank[g][0:64, :], lhsT=aup[:, hc], rhs=adq[:], start=True, stop=True),
                             reads=[b_wup, b_adq, b_bank[g]], writes=[b_bank[g]])
                        yield
                        s.op("act", lambda e: e.activation(out=a_[g][:], in_=bank[g][0:64, :], func=AF.Sigmoid, bias=P["a0"][:, H:H + 1]),
                             reads=[b_bank[g], b_c, b_a[g]], writes=[b_a[g]])
                        yield
                        s.op("dve", lambda e: e.tensor_scalar(out=kk[g][:], in0=k_[g][:], scalar1=P["k_k"][:, H:H + 1], scalar2=None, op0=ALU.mult),
                             reads=[b_k[g], b_c, b_kk[g]], writes=[b_kk[g]])
                        yield
                        s.op("act", lambda e: e.activation(out=R(t3[g][:]), in_=kk[g][:], func=AF.Square), reads=[b_kk[g], b_t3[g]], writes=[b_t3[g]])
                        yield
                        s.op("pe", lambda e: e.matmul(bank[g][0:64, :], lhsT=R(ones64r[:]), rhs=R(t3[g][:]), start=True, stop=True),
                             reads=[b_c, b_t3[g], b_bank[g]], writes=[b_bank[g]])
                        yield
                        s.op("act", lambda e: e.activation(out=t2[g][:], in_=bank[g][0:64, :], func=AF.Sqrt), reads=[b_bank[g], b_t2[g]], writes=[b_t2[g]])
                        yield
                        s.op("dve", lambda e: e.tensor_scalar(out=t2[g][:], in0=t2[g][:], scalar1=1e-12, scalar2=None, op0=ALU.max), reads=[b_t2[g]], writes=[b_t2[g]])
                        yield
                        s.op("dve", lambda e: e.reciprocal(out=t2[g][:], in_=t2[g][:]), reads=[b_t2[g]], writes=[b_t2[g]])
                        yield
                        s.op("dve", lambda e: e.tensor_tensor(out=kkn[g][:], in0=kk[g][:], in1=t2[g][:], op=ALU.mult), reads=[b_kk[g], b_t2[g], b_kkn[g]], writes=[b_kkn[g]])
                        yield
                        s.op("dve", lambda e: e.tensor_scalar(out=t1[g][:], in0=a_[g][:], scalar1=P["k_a"][:, H:H + 1], scalar2=om["k_a"][:, H:H + 1],
                                                              op0=ALU.mult, op1=ALU.add), reads=[b_a[g], b_c, b_t1[g]], writes=[b_t1[g]])
                        yield
                        s.op("dve", lambda e: e.tensor_tensor(out=kmod[g][:], in0=k_[g][:], in1=t1[g][:], op=ALU.mult), reads=[b_k[g], b_t1[g], b_kmod[g]], writes=[b_kmod[g]])
                        yield
                        s.op("dve", lambda e: e.tensor_tensor(out=kka[g][:], in0=kkn[g][:], in1=a_[g][:], op=ALU.mult), reads=[b_kkn[g], b_a[g], b_kka[g]], writes=[b_kka[g]])
                        yield
                        s.op("dve", lambda e: e.scalar_tensor_tensor(out=R(t3[g][:]), in0=r_[g][:], scalar=P["r_k"][:, H:H + 1], in1=kmod[g][:],
                                                                     op0=ALU.mult, op1=ALU.mult), reads=[b_r[g], b_c, b_kmod[g], b_t3[g]], writes=[b_t3[g]])
                        yield
                        s.op("pe", lambda e: e.matmul(bank[g][0:64, :], lhsT=R(ones64r[:]), rhs=R(t3[g][:]), start=True, stop=True),
                             reads=[b_c, b_t3[g], b_bank[g]], writes=[b_bank[g]])
                        yield
                        s.op("dve", lambda e: e.tensor_tensor(out=bonus[:, g, :], in0=bank[g][0:64, :], in1=Vt[:, g, :], op=ALU.mult),
                             reads=[b_bank[g], b_Vt, b_bonus], writes=[b_bonus])
                        yield
                        s.op("dve", lambda e: e.tensor_tensor(out=t1[g][:], in0=e2[g][:], in1=c_[g][:], op=ALU.subtract), reads=[b_e2[g], b_cc[g], b_t1[g]], writes=[b_t1[g]])
                        yield
                        s.op("act", lambda e: e.activation(out=t1[g][:], in_=t1[g][:], func=AF.Exp), reads=[b_t1[g]], writes=[b_t1[g]])
                        yield
                        v3 = lambda ap: ap.rearrange("p (n l) -> p n l", l=RL)
                        s.op("dve", lambda e: e.scalar_tensor_tensor(out=R(AR[:, g, :, 0, :]), in0=v3(kkn[g][:]), scalar=-1.0, in1=v3(t1[g][:]),
                                                                     op0=ALU.mult, op1=ALU.mult), reads=[b_kkn[g], b_t1[g], b_AR], writes=[b_AR])
                        yield
                        s.op("dve", lambda e: e.tensor_tensor(out=R(AR[:, g, :, 1, :]), in0=v3(r_[g][:]), in1=v3(eP[g][:]), op=ALU.mult),
                             reads=[b_r[g], b_eP[g], b_AR], writes=[b_AR])
                        yield
                        s.op("dve", lambda e: e.tensor_tensor(out=R(Bt[:, g, :]), in0=kka[g][:], in1=eN[g][:], op=ALU.mult), reads=[b_kka[g], b_eN[g], b_Bt], writes=[b_Bt])
                        yield
                        s.op("dve", lambda e: e.tensor_tensor(out=R(Kt[:, g, :]), in0=kmod[g][:], in1=eN[g][:], op=ALU.mult), reads=[b_kmod[g], b_eN[g], b_Kt], writes=[b_Kt])
                        yield
                        gbc = gL[:, g, :].unsqueeze(2).broadcast_to([64, NCK, RL])
                        s.op("dve", lambda e: e.tensor_tensor(out=v3(Bh[:, g, :]), in0=v3(Bt[:, g, :]), in1=gbc, op=ALU.mult),
                             reads=[b_Bt, b_gL, b_Bh], writes=[b_Bh])
                        yield
                        s.op("dve", lambda e: e.tensor_tensor(out=v3(Kh[:, g, :]), in0=v3(Kt[:, g, :]), in1=gbc, op=ALU.mult),
                             reads=[b_Kt, b_gL, b_Kh], writes=[b_Kh])
                        yield


                def phase1():
                    for rw, brw, nm, rowk in ((rawd, b_rawd, "mu_wd", "wd"), (rawa, b_rawa, "mu_ad", "ad")):
                        if ti == 0:
                            s.op("pool", lambda e: e.memset(rw[:, 0:1], 0.0), reads=[brw], writes=[brw])
                        else:
                            s.op("pool", lambda e: e.tensor_copy(out=rw[:, 0:1], in_=rw[:, NT:NT + 1]), reads=[brw], writes=[brw])
                        s.dma("sp", rw[:, 1:1 + NT], PT[rows[rowk]:rows[rowk] + 96, t0:t0 + NT], reads=[brw], writes=[brw])
                    dsts = ((twd, b_twd, rawd, b_rawd, "mu_wd"), (adq, b_adq, rawa, b_rawa, "mu_ad"))
                    for dst, bdst, rw, brw, nm in dsts:
                        s.op("dve", lambda e: e.tensor_scalar(out=dst[:], in0=rw[:, 1:1 + NT], scalar1=omd[nm][:], scalar2=None, op0=ALU.mult),
                             reads=[brw, b_c, bdst], writes=[bdst])
                        s.op("dve", lambda e: e.scalar_tensor_tensor(out=dst[:], in0=rw[:, 0:NT], scalar=mud[nm][:], in1=dst[:],
                                                                     op0=ALU.mult, op1=ALU.add),
                             reads=[brw, b_c, bdst], writes=[bdst])
                    s.op("act", lambda e: e.activation(out=twd[:], in_=twd[:], func=AF.Tanh), reads=[b_twd], writes=[b_twd])
                    for nm in ("r", "k", "v"):
                        if ti == 0:
                            s.op("pool", lambda e: e.memset(raw[nm][:, :, 0:1], 0.0), reads=[b_raw[nm]], writes=[b_raw[nm]])
                        else:
                            s.op("pool", lambda e: e.tensor_copy(out=raw[nm][:, :, 0:1], in_=raw[nm][:, :, NT:NT + 1]),
                                 reads=[b_raw[nm]], writes=[b_raw[nm]])
                        r0 = rows[nm] + h0 * 64
                        s.dma("sp", raw[nm][:, :, 1:1 + NT], PT[r0:r0 + G * 64, t0:t0 + NT].rearrange("(h j) t -> j h t", j=64),
                              reads=[b_raw[nm]], writes=[b_raw[nm]])
                    r0 = rows["g"] + h0 * 64
                    s.dma("act", gate[:], PT[r0:r0 + G * 64, t0:t0 + NT].rearrange("(h j) t -> j h t", j=64), reads=[b_gate], writes=[b_gate])
                    s.op("act", lambda e: e.activation(out=gate[:], in_=gate[:], func=AF.Silu), reads=[b_gate], writes=[b_gate])
                    alive = [head_gen(g) for g in range(G)]
                    while alive:
                        for gg in list(alive):
                            try:
                                next(gg)
                            except StopIteration:
                                alive.remove(gg)
                        yield
                def indep_steps(cc):
                    par = cc % 2
                    cs = slice(cc * RL, (cc + 1) * RL)
                    tk, ga_s, gk_s, pn_s, ppl = tokb[par], GAb[par], GKb[par], Pnb[par], PPb[par]
                    b_tk, b_ga, b_gk, b_pn, b_ppl = b_tokb[par], b_GAb[par], b_GKb[par], b_Pnb[par], b_PPb[par]

                    def tr():
                        for g in range(G):
                            s.op("pe", lambda e: e.transpose(ps_x1[:, 0, g, :], Vt[:, g, cs], i64), reads=[b_Vt, b_c, b_bank[2]], writes=[b_bank[2]])
                            s.op("pe", lambda e: e.transpose(ps_x1[:, 1, g, :], Bh[:, g, cs], i64), reads=[b_Bh, b_c, b_bank[2]], writes=[b_bank[2]])
                            s.op("pe", lambda e: e.transpose(ps_x2k[:, g, :], Kh[:, g, cs], i64), reads=[b_Kh, b_c, b_bank[3]], writes=[b_bank[3]])
                        s.op("act", lambda e: e.copy(out=R(tk[:, 0:2]), in_=ps_x1), reads=[b_bank[2], b_tk], writes=[b_tk])
                        s.op("act", lambda e: e.copy(out=R(tk[:, 2]), in_=ps_x2k), reads=[b_bank[3], b_tk], writes=[b_tk])

                    def gm():
                        for g in range(G):
                            arc = AR[:, g, cc].rearrange("p a l -> p (a l)")
                            s.op("pe", lambda e: e.matmul(ps_ga[:, g, :], lhsT=R(Bt[:, g, cs]), rhs=R(arc), start=True, stop=True),
                                 reads=[b_Bt, b_AR, b_bank[0]], writes=[b_bank[0]])
                            s.op("pe", lambda e: e.matmul(ps_gk[:, g, :], lhsT=R(Kt[:, g, cs]), rhs=R(arc), start=True, stop=True),
                                 reads=[b_Kt, b_AR, b_bank[1]], writes=[b_bank[1]])
                            s.op("pe", lambda e: e.matmul(ps_gn[:, g, :], lhsT=R(AR[:, g, cc, 0, :]), rhs=R(Bt[:, g, cs]), start=True, stop=True),
                                 reads=[b_Bt, b_AR, b_bank[3]], writes=[b_bank[3]])
                        s.op("dve", lambda e: e.tensor_tensor(out=R(ga_s[:]), in0=ps_ga, in1=maskA[:], op=ALU.mult),
                             reads=[b_bank[0], b_c, b_ga], writes=[b_ga])
                        s.op("dve", lambda e: e.tensor_tensor(out=R(gk_s[:]), in0=ps_gk, in1=maskA[:], op=ALU.mult),
                             reads=[b_bank[1], b_c, b_gk], writes=[b_gk])
                        s.op("dve", lambda e: e.tensor_tensor(out=R(pn_s[:]), in0=ps_gn, in1=maskN[:], op=ALU.mult),
                             reads=[b_bank[3], b_c, b_pn], writes=[b_pn])

                    def sq(lvl):
                        def f():
                            if lvl == 0:
                                Pl = lambda g: pn_s[:, g, :]
                                PTl = lambda g: ga_s[:, g, 0:RL]
                                rd = [b_pn, b_ga]
                            else:
                                Pl = lambda g: ppl[lvl - 1][:, g, 0, :]
                                PTl = lambda g: ppl[lvl - 1][:, g, 1, :]
                                rd = [b_ppl[lvl - 1]]
                            for g in range(G):
                                s.op("pe", lambda e: e.matmul(ps_pp[:, g, 0, :], lhsT=R(PTl(g)), rhs=R(Pl(g)), start=True, stop=True),
                                     reads=rd + [b_bank[5]], writes=[b_bank[5]])
                                s.op("pe", lambda e: e.matmul(ps_pp[:, g, 1, :], lhsT=R(Pl(g)), rhs=R(PTl(g)), start=True, stop=True),
                                     reads=rd + [b_bank[5]], writes=[b_bank[5]])
                            s.op("act", lambda e: e.copy(out=R(ppl[lvl][:]), in_=ps_pp), reads=[b_bank[5], b_ppl[lvl]], writes=[b_ppl[lvl]])
                        return f
                    return [tr, gm] + [sq(l) for l in range(NLVL - 1)]

                def dep_steps(cc):
                    par = cc % 2
                    cs = slice(cc * RL, (cc + 1) * RL)
                    tk, ga_s, gk_s, pn_s, ppl = tokb[par], GAb[par], GKb[par], Pnb[par], PPb[par]
                    b_tk, b_ga, b_gk, b_pn, b_ppl = b_tokb[par], b_GAb[par], b_GKb[par], b_Pnb[par], b_PPb[par]

                    def zz():
                        for g in range(G):
                            s.op("pe", lambda e: e.matmul(ps_z[:, 0, g, :], lhsT=R(AR[:, g, cc, 0, :]), rhs=R(ST[:, g, :]), start=True, stop=False),
                                 reads=[b_AR, b_ST, b_bank[4]], writes=[b_bank[4]], inc=False)
                            s.op("pe", lambda e: e.matmul(ps_z[:, 0, g, :], lhsT=R(gk_s[:, g, 0:RL]), rhs=R(tk[:, 0, g, :]), start=False, stop=True),
                                 reads=[b_gk, b_tk, b_bank[4]], writes=[b_bank[4]])
                        s.op("act", lambda e: e.copy(out=R(U[:]), in_=ps_z[:, 0]), reads=[b_bank[4], b_U], writes=[b_U])

                    def app(lvl):
                        def f():
                            if lvl == 0:
                                PTl = lambda g: ga_s[:, g, 0:RL]
                                rd = [b_ga]
                            else:
                                PTl = lambda g: ppl[lvl - 1][:, g, 1, :]
                                rd = [b_ppl[lvl - 1]]
                            for g in range(G):
                                s.op("pe", lambda e: e.matmul(ps_z[:, 1, g, :], lhsT=R(PTl(g)), rhs=R(U[:, g, :]), start=True, stop=True),
                                     reads=rd + [b_U, b_bank[4]], writes=[b_bank[4]])
                            s.op("dve", lambda e: e.tensor_tensor(out=R(U[:]), in0=ps_z[:, 1], in1=U[:], op=ALU.add),
                                 reads=[b_bank[4], b_U], writes=[b_U])
                        return f

                    def yy():
                        for g in range(G):
                            s.op("pe", lambda e: e.matmul(ps_y[:, g, :], lhsT=R(AR[:, g, cc, 1, :]), rhs=R(ST[:, g, :]), start=True, stop=False),
                                 reads=[b_AR, b_ST, b_bank[6]], writes=[b_bank[6]], inc=False)
                            s.op("pe", lambda e: e.matmul(ps_y[:, g, :], lhsT=R(ga_s[:, g, RL:2 * RL]), rhs=R(U[:, g, :]), start=False, stop=False),
                                 reads=[b_ga, b_U, b_bank[6]], writes=[b_bank[6]], inc=False)
                            s.op("pe", lambda e: e.matmul(ps_y[:, g, :], lhsT=R(gk_s[:, g, RL:2 * RL]), rhs=R(tk[:, 0, g, :]), start=False, stop=True),
                                 reads=[b_gk, b_tk, b_bank[6]], writes=[b_bank[6]])
                        s.op("act", lambda e: e.copy(out=Ysb[:], in_=ps_y), reads=[b_bank[6], b_Ysb], writes=[b_Ysb])

                    def ss():
                        for g in range(G):
                            s.op("pe", lambda e: e.matmul(ps_s[:, g, :], lhsT=R(tk[:, 1, g, :]), rhs=R(U[:, g, :]), start=True, stop=False),
                                 reads=[b_tk, b_U, b_bank[7]], writes=[b_bank[7]], inc=False)
                            s.op("pe", lambda e: e.matmul(ps_s[:, g, :], lhsT=R(tk[:, 2, g, :]), rhs=R(tk[:, 0, g, :]), start=False, stop=True),
                                 reads=[b_tk, b_bank[7]], writes=[b_bank[7]])
                        for g in range(G):
                            s.op("dve", lambda e: e.scalar_tensor_tensor(out=R(ST[:, g, :]), in0=ST[:, g, :], scalar=gL[:, g, cc:cc + 1], in1=ps_s[:, g, :],
                                                                         op0=ALU.mult, op1=ALU.add),
                                 reads=[b_ST, b_gL, b_bank[7]], writes=[b_ST])

                    def yt():
                        for g in range(G):
                            s.op("pe", lambda e: e.transpose(ps_yT[:, g, :], Ysb[:, g, :], ident[0:RL, 0:RL]), reads=[b_Ysb, b_c, b_bank[6]], writes=[b_bank[6]])
                        s.op("dve", lambda e: e.tensor_copy(out=yfm[:, :, cs], in_=ps_yT), reads=[b_bank[6], b_yfm], writes=[b_yfm])
                    return [zz] + [app(l) for l in range(NLVL)] + [yy, ss, yt]


                def chunks():
                    for f in indep_steps(0):
                        f()
                        yield
                    for cc in range(NCK):
                        dsteps = dep_steps(cc)
                        isteps = indep_steps(cc + 1) if cc + 1 < NCK else []
                        for j in range(max(len(dsteps), len(isteps))):
                            if j < len(dsteps):
                                dsteps[j]()
                            if j < len(isteps):
                                isteps[j]()
                            yield
                def ph4(g):
                    H = h0 + g
                    yi = g % 2
                    s.op("pe", lambda e: e.matmul(bank[(7, 4)[g]][0:64, :], lhsT=ones64[:], rhs=yfm[:, g, :], start=True, stop=True),
                         reads=[b_c, b_yfm, b_bank[(7, 4)[g]]], writes=[b_bank[(7, 4)[g]]])
                    yield
                    s.op("act", lambda e: e.activation(out=t1[g][:], in_=bank[(7, 4)[g]][0:64, :], func=AF.Copy, scale=1.0 / 64), reads=[b_bank[(7, 4)[g]], b_t1[g]], writes=[b_t1[g]])
                    yield
                    s.op("dve", lambda e: e.tensor_tensor(out=t1[g][:], in0=yfm[:, g, :], in1=t1[g][:], op=ALU.subtract), reads=[b_yfm, b_t1[g]], writes=[b_t1[g]])
                    yield
                    s.op("act", lambda e: e.activation(out=t2[g][:], in_=t1[g][:], func=AF.Square), reads=[b_t1[g], b_t2[g]], writes=[b_t2[g]])
                    yield
                    s.op("pe", lambda e: e.matmul(bank[(7, 4)[g]][0:64, :], lhsT=ones64[:], rhs=t2[g][:], start=True, stop=True),
                         reads=[b_c, b_t2[g], b_bank[(7, 4)[g]]], writes=[b_bank[(7, 4)[g]]])
                    yield
                    s.op("act", lambda e: e.activation(out=t2[g][:], in_=bank[(7, 4)[g]][0:64, :], func=AF.Sqrt, scale=1.0 / 64, bias=GN_EPS),
                         reads=[b_bank[(7, 4)[g]], b_t2[g]], writes=[b_t2[g]])
                    yield
                    s.op("dve", lambda e: e.reciprocal(out=t2[g][:], in_=t2[g][:]), reads=[b_t2[g]], writes=[b_t2[g]])
                    yield
                    s.op("dve", lambda e: e.tensor_tensor(out=t1[g][:], in0=t1[g][:], in1=t2[g][:], op=ALU.mult), reads=[b_t1[g], b_t2[g]], writes=[b_t1[g]])
                    yield
                    s.op("dve", lambda e: e.tensor_scalar(out=t1[g][:], in0=t1[g][:], scalar1=P["gn_w"][:, H:H + 1], scalar2=P["gn_b"][:, H:H + 1],
                                                          op0=ALU.mult, op1=ALU.add), reads=[b_t1[g], b_c], writes=[b_t1[g]])
                    yield
                    s.op("dve", lambda e: e.tensor_tensor(out=t1[g][:], in0=t1[g][:], in1=bonus[:, g, :], op=ALU.add), reads=[b_t1[g], b_bonus], writes=[b_t1[g]])
                    yield
                    s.op("dve", lambda e: e.tensor_tensor(out=yo[yi][:], in0=t1[g][:], in1=gate[:, g, :], op=ALU.mult),
                         reads=[b_t1[g], b_gate, b_yo[yi]], writes=[b_yo[yi]])
                    yield
                    r0 = row_y + H * 64
                    s.dma("sp", yT[r0:r0 + 64, t0:t0 + NT], yo[yi][:], reads=[b_yo[yi]])
                    yield

                def phase4():
                    alive = [ph4(g) for g in range(G)]
                    while alive:
                        for gg in list(alive):
                            try:
                                next(gg)
                            except StopIteration:
                                alive.remove(gg)
                        yield
                return phase1, chunks, phase4

            tiles = [make_tile(ti, t0) for ti, t0 in enumerate(range(0, T, NT))]
            for _ in tiles[0][0]():
                pass
            for ti in range(len(tiles)):
                cg = tiles[ti][1]()
                pg = tiles[ti + 1][0]() if ti + 1 < len(tiles) else iter(())
                c_alive, p_alive = True, True
                while c_alive or p_alive:
                    if c_alive:
                        try:
                            next(cg)
                        except StopIteration:
                            c_alive = False
                    for _ in range(P1_PER_STEP):
                        if p_alive:
                            try:
                                next(pg)
                            except StopIteration:
                                p_alive = False
                for _ in tiles[ti][2]():
                    pass
        s.barrier()


class Cfg:
    def __init__(self, D=4096, T=4096, NBLK_A=8, NH_B=32, HC=4, HX=4, M=256, DEPTH=2):
        self.D, self.T, self.NBLK_A, self.NH_B, self.HC, self.HX, self.M, self.DEPTH = D, T, NBLK_A, NH_B, HC, HX, M, DEPTH
        self.KC = D // 128
        self.WA = NBLK_A * 256
        self.WB = NH_B * 64
        self.QKW = HC * 256
        self.WC = HC * 512
        self.WX = HX * 128
        self.in_sizes = (self.WA, self.WA, 3 * self.WB + 192, self.WB, 2 * self.QKW, self.WC, self.WC, self.WC, 2 * HC,
                         self.WX, self.WX, 4 * D)
        self.c_in = sum(self.in_sizes)
        off = np.concatenate([[0], np.cumsum(self.in_sizes)])
        self.col = dict(a_x=off[0], a_g=off[1], b_s=off[2], b_g=off[3], c_qk=off[4], c_v=off[5], c_o=off[6], c_g=off[7],
                        c_if=off[8], x_q=off[9], x_g=off[10], gates=off[11])
        segs = [("a_x", self.col["a_x"], self.WA), ("a_g", self.col["a_g"], self.WA),
                ("r", self.col["b_s"], self.WB), ("k", self.col["b_s"] + self.WB, self.WB), ("v", self.col["b_s"] + 2 * self.WB, self.WB),
                ("wd", self.col["b_s"] + 3 * self.WB, 96), ("ad", self.col["b_s"] + 3 * self.WB + 96, 96),
                ("b_g", self.col["b_g"], self.WB),
                ("c_q", self.col["c_qk"], self.QKW), ("c_k", self.col["c_qk"] + self.QKW, self.QKW),
                ("c_v", self.col["c_v"], self.WC), ("c_o", self.col["c_o"], self.WC), ("c_g", self.col["c_g"], self.WC),
                ("c_if", self.col["c_if"], 2 * HC), ("x_q", self.col["x_q"], self.WX), ("x_g", self.col["x_g"], self.WX)]
        self.segs = segs
        self.row = {}
        r = 0
        for nm, c0, w in segs:
            self.row[nm] = r
            r += ((w + 127) // 128) * 128
        self.NB1 = r // 128
        self.grp_first = {"A": "a_x", "B": "r", "C": "c_q", "X": "x_q"}
        order = ["A", "B", "C", "X"]
        starts = [self.row[self.grp_first[g]] for g in order] + [r]
        self.grp_rows = {g: (starts[i], starts[i + 1]) for i, g in enumerate(order)}
        self.lrow = {}
        for nm, c0, w in segs:
            for g in order:
                lo, hi = self.grp_rows[g]
                if lo <= self.row[nm] < hi:
                    self.lrow[nm] = (g, self.row[nm] - lo)
        self.br_kc = [self.WA // 128, self.WB // 128, self.WC // 128, self.WX // 128]
        self.FY = sum(self.br_kc) * 128


RMS_EPS = 1e-6
RWKV_GN_EPS = 64e-5


def tile_layout(W):
    Kd, M = W.shape
    return np.ascontiguousarray(W.reshape(Kd // 128, 128, M // 128, 128).transpose(2, 1, 0, 3))


def chunk_cols(v):
    return np.ascontiguousarray(v.reshape(-1, 128).T)


def head_cols(v):
    return np.ascontiguousarray(v.reshape(-1, 64).T)


def const_inputs(cfg):
    HC = cfg.HC
    sel = np.zeros((HC, HC, 128), np.float32)
    for h in range(HC):
        sel[h, h, :] = 1
    a_, b_ = np.meshgrid(np.arange(RWKV_L), np.arange(RWKV_L), indexing="ij")
    mA = np.concatenate([(a_ < b_), (a_ <= b_)], 1).astype(np.float32)
    mN = (b_ < a_).astype(np.float32)
    rm = np.ones((64, NT), np.float32)
    rm[:, ::RWKV_L] = 0
    a_, b_ = np.meshgrid(np.arange(MLSTM_L), np.arange(MLSTM_L), indexing="ij")
    return {"c_ident": np.eye(128, dtype=np.float32), "c_sel": sel, "c_i4": np.eye(HC, dtype=np.float32),
            "c_negmask": np.where(a_ <= b_, 0.0, NEG).astype(np.float32), "c_resetmask": rm,
            "c_maskA": np.ascontiguousarray(np.broadcast_to(mA[:, None, :], (RWKV_L, RWKV_G, 2 * RWKV_L))),
            "c_maskN": np.ascontiguousarray(np.broadcast_to(mN[:, None, :], (RWKV_L, RWKV_G, RWKV_L)))}


def layer_inputs(cfg, inp, l):
    D, KC, HC = cfg.D, cfg.KC, cfg.HC
    w_in = inp["w_in"][l]
    W1 = np.zeros((D, cfg.NB1 * 128), np.float32)
    for nm, c0, w in cfg.segs:
        W1[:, cfg.row[nm]:cfg.row[nm] + w] = w_in[:, c0:c0 + w]
    o = {}
    o["W1"] = tile_layout(W1)
    g0 = cfg.col["gates"]
    o["Wg"] = np.stack([tile_layout(w_in[:, g0 + i * D: g0 + (i + 1) * D]) for i in range(4)])
    o["Wbr0"] = tile_layout(inp["w_branch_a"][l])
    o["Wbr1"] = tile_layout(inp["w_branch_b"][l])
    o["Wbr2"] = tile_layout(inp["w_branch_c"][l])
    o["Wbr3"] = tile_layout(inp["w_branch_x"][l])
    o["Wo"] = tile_layout(inp["w_out"][l])
    o["norm_g"] = chunk_cols(inp["norm_g"][l])
    o["mem_norm_g"] = chunk_cols(inp["mem_norm_g"][l])
    NCH = cfg.NBLK_A * 2
    o["lru_conv_w"] = np.ascontiguousarray(inp["lru_conv_w"][l].reshape(4, NCH, 128).transpose(2, 1, 0))
    for nm in ("lru_conv_b", "lru_ba", "lru_bx", "lru_lambda"):
        o[nm] = chunk_cols(inp[nm][l])
    for nm in ("lru_wa", "lru_wx"):
        o[nm] = np.ascontiguousarray(inp[nm][l].reshape(cfg.NBLK_A, 2, 128, 2, 128).transpose(0, 3, 2, 1, 4))
    WB = cfg.WB
    mu = inp["rwkv_mu"][l]
    o["rw_mu_r"], o["rw_mu_k"], o["rw_mu_v"] = head_cols(mu[:WB]), head_cols(mu[WB:2 * WB]), head_cols(mu[2 * WB:3 * WB])
    o["rw_mu_wd"] = np.ascontiguousarray(mu[3 * WB:3 * WB + 96].reshape(96, 1))
    o["rw_mu_ad"] = np.ascontiguousarray(mu[3 * WB + 96:].reshape(96, 1))
    for nm, src in (("w0", "rwkv_w0"), ("a0", "rwkv_a0"), ("k_k", "rwkv_k_k"), ("k_a", "rwkv_k_a"), ("gn_w", "rwkv_gn_w"), ("gn_b", "rwkv_gn_b")):
        o["rw_" + nm] = head_cols(inp[src][l])
    o["rw_r_k"] = head_cols(inp["rwkv_r_k"][l].reshape(-1))
    o["rw_w_up"] = np.ascontiguousarray(inp["rwkv_w_up"][l])
    o["rw_a_up"] = np.ascontiguousarray(inp["rwkv_a_up"][l])
    o["ml_conv_w"] = np.ascontiguousarray(inp["mlstm_conv_w"][l].reshape(4, 4 * HC, 128).transpose(2, 1, 0))
    o["ml_conv_b"] = chunk_cols(inp["mlstm_conv_b"][l])
    o["ml_gn_w"] = chunk_cols(inp["mlstm_gn_w"][l])
    o["ml_b_i"] = np.ascontiguousarray(inp["mlstm_b_i"][l].reshape(HC, 1))
    o["ml_b_f"] = np.ascontiguousarray(inp["mlstm_b_f"][l].reshape(HC, 1))
    wkv = inp["xattn_w_kv"][l]
    o["xa_Wk"] = tile_layout(wkv[:, :cfg.WX])
    o["xa_Wv"] = np.ascontiguousarray(wkv[:, cfg.WX:].reshape(KC, 128, cfg.WX))
    return {f"L{l}_{k_}": np.ascontiguousarray(v, dtype=np.float32) for k_, v in o.items()}


def flat2d(ap, ndim, width):
    names = "abcdefgh"[:ndim]
    f = ap.rearrange(f"{' '.join(names)} -> ({' '.join(names)})")
    return f.rearrange("(r c) -> r c", c=width)


def build_program(cfg, shapes):
    nc = bass.Bass("TRN2", target_bir_lowering=False)
    D, T, KC = cfg.D, cfg.T, cfg.KC
    ins = {nm: nc.dram_tensor(nm, list(sh), F32, kind="ExternalInput").ap() for nm, sh in shapes.items()}
    outT = nc.dram_tensor("outT", [D, T], F32, kind="ExternalOutput").ap()
    with contextlib.ExitStack() as st:
        k = Ctx(nc, st)
        hT = k.dram("hT", [D, T], BF16)
        PTs = {g: k.dram(f"PT{g}", [hi - lo, T], F32) for g, (lo, hi) in cfg.grp_rows.items()}

        def pt_block(b):
            r = b * 128
            for g, (lo, hi) in cfg.grp_rows.items():
                if lo <= r < hi:
                    return PTs[g], r - lo
            raise AssertionError
        lr = lambda nm: cfg.lrow[nm][1]
        yT = k.dram("yT", [cfg.FY, T], BF16)
        memnT = k.dram("memnT", [D, cfg.M], BF16)
        xs = [ins["xT"]] + [k.dram(f"x{l + 1}T", [D, T], F32) for l in range(cfg.DEPTH)]
        cst = {nm[2:]: ins[nm] for nm in ins if nm.startswith("c_")}
        OB = D // 128
        for l in range(cfg.DEPTH):
            L = lambda nm: ins[f"L{l}_{nm}"]
            Wg = k.dram(f"Wg{l}", [4, OB, 128, KC, 128], BF16)
            Wo = k.dram(f"Wo{l}", [OB, 128, KC, 128], BF16)
            Wbr = [k.dram(f"Wbr{l}_{i}", [OB, 128, cfg.br_kc[i], 128], BF16) for i in range(4)]
            for dst, src, nd in [(Wg, L("Wg"), 5), (Wo, L("Wo"), 4)] + [(Wbr[i], L(f"Wbr{i}"), 4) for i in range(4)]:
                n = int(np.prod(dst.shape))
                wdt = 1024 if n % 1024 == 0 else 512
                cast_dram(k, flat2d(dst, nd, wdt), flat2d(src, nd, wdt), n, width=wdt)
            stage_norm(k, xs[l], L("norm_g"), hT, D, T, RMS_EPS, BF16)
            stage_norm(k, ins["memT"], L("mem_norm_g"), memnT, D, cfg.M, RMS_EPS, BF16)
            stage_proj(k, hT, L("W1"), pt_block, D, T, cfg.NB1)
            lru_prm = {"conv_w": L("lru_conv_w"), "conv_b": L("lru_conv_b"), "ba": L("lru_ba"), "bx": L("lru_bx"), "lam": L("lru_lambda"),
                       "wa": L("lru_wa"), "wx": L("lru_wx")}
            stage_lru(k, PTs["A"], lr("a_x"), lr("a_g"), yT, 0, lru_prm, T, cfg.NBLK_A)
            rw_prm = {nm: L("rw_" + nm) for nm in ("mu_r", "mu_k", "mu_v", "w0", "a0", "k_k", "k_a", "r_k", "gn_w", "gn_b", "mu_wd", "mu_ad", "w_up", "a_up")}
            rw_rows = {"r": lr("r"), "k": lr("k"), "v": lr("v"), "wd": lr("wd"), "ad": lr("ad"), "g": lr("b_g")}
            stage_rwkv(k, PTs["B"], rw_rows, yT, cfg.WA, rw_prm, cst, T, cfg.NH_B, RWKV_GN_EPS)
            ml_prm = {"conv_w": L("ml_conv_w"), "conv_b": L("ml_conv_b"), "gn_w": L("ml_gn_w"), "b_i": L("ml_b_i"), "b_f": L("ml_b_f")}
            ml_rows = {"q": lr("c_q"), "k": lr("c_k"), "v": lr("c_v"), "o": lr("c_o"), "g": lr("c_g"), "ifg": lr("c_if")}
            stage_mlstm(k, PTs["C"], ml_rows, yT, cfg.WA + cfg.WB, ml_prm, cst, T, cfg.HC)
            stage_xattn(k, PTs["X"], lr("x_q"), lr("x_g"), yT, cfg.WA + cfg.WB + cfg.WC, memnT, L("xa_Wk"), L("xa_Wv"), cst["ident"],
                        T, D, cfg.M, cfg.HX)
            stage_merge(k, hT, yT, cfg.br_kc, Wg, Wbr, Wo, xs[l], xs[l + 1], D, T)
        stage_norm(k, xs[cfg.DEPTH], ins["final_g"], outT, D, T, RMS_EPS, F32)
        k.s.barrier()
        build_program.ninst = k.s.ninst
        build_program.per_eng = dict(k.s.per_eng)
        build_program.nsem = k.s.nsem
        build_program.nwait = k.s.nwait
    return nc


def run_module(cfg, inputs):
    B = inputs["x"].shape[0]
    shared = const_inputs(cfg)
    for l in range(cfg.DEPTH):
        shared.update(layer_inputs(cfg, inputs, l))
    shared["final_g"] = chunk_cols(np.asarray(inputs["final_norm_g"], dtype=np.float32))
    in_maps = []
    for b in range(B):
        m = dict(shared)
        m["xT"] = np.ascontiguousarray(np.asarray(inputs["x"][b], dtype=np.float32).T)
        m["memT"] = np.ascontiguousarray(np.asarray(inputs["mem"][b], dtype=np.float32).T)
        in_maps.append(m)
    shapes = {nm: v.shape for nm, v in in_maps[0].items()}
    nc = build_program(cfg, shapes)
    res = run_bass_kernel_spmd(nc, in_maps, core_ids=list(range(B)))
    out = np.stack([np.ascontiguousarray(res.results[b]["outT"].T) for b in range(B)])
    return out.astype(np.float32)


def kernel(**inputs):
    inputs = {k_: np.asarray(v) for k_, v in inputs.items()}
    return run_module(Cfg(), inputs)
```

```python
import contextlib
import numpy as np
import concourse.bass as bass
import concourse.mybir as mybir
from concourse.bass_utils import run_bass_kernel_spmd

F32 = mybir.dt.float32
BF16 = mybir.dt.bfloat16
AF = mybir.ActivationFunctionType
ALU = mybir.AluOpType
AX = mybir.AxisListType


class Buf:
    __slots__ = ("name", "w", "r")

    def __init__(self, name=""):
        self.name = name
        self.w = set()
        self.r = set()


SKIP_SAME_ENGINE = False


class Sched:
    EPOCH = 4000
    NDMA = 12

    def __init__(self, nc, stack):
        self.nc = nc
        self.stack = stack
        self.eng = {"pe": nc.tensor, "act": nc.scalar, "dve": nc.vector,
                    "pool": nc.gpsimd, "sp": nc.sync}
        self.sem = {}
        self.cnt = {}
        self.pending = {e: False for e in self.eng}
        self.nsem = 0
        for e in self.eng:
            self._new_sem(e)
        self.dsem = {}
        for q in ("sp", "act", "pool"):
            self.dsem[q] = [[self._alloc(f"d{q}{i}"), 0] for i in range(self.NDMA)]
        self.dnext = {q: 0 for q in self.dsem}
        self.seen = {e: {} for e in self.eng}
        self.all_tokens = {}
        self.ninst = 0
        self.per_eng = {}

    def _alloc(self, name):
        self.nsem += 1
        return self.stack.enter_context(self.nc.semaphore(f"{name}_{self.nsem}"))

    def _new_sem(self, e):
        self.sem[e] = self._alloc(f"s{e}")
        self.cnt[e] = 0

    def _wait(self, e, tok):
        sem, val = tok
        k = id(sem)
        if self.seen[e].get(k, 0) >= val:
            return
        self.eng[e].wait_ge(sem, val)
        self.nwait = getattr(self, "nwait", 0) + 1
        self.seen[e][k] = val

    def _deps(self, e, reads, writes):
        deps = set()
        for b in reads:
            deps |= b.w
        for b in writes:
            deps |= b.w
            deps |= b.r
        best = {}
        for tok in deps:
            if e == "pe" and tok[0] is self.sem["pe"]:
                continue
            if SKIP_SAME_ENGINE and e in ("act", "dve") and tok[0] is self.sem[e]:
                continue
            kk_ = id(tok[0])
            if kk_ not in best or best[kk_][1] < tok[1]:
                best[kk_] = tok
        for tok in best.values():
            self._wait(e, tok)

    def _record(self, tok, reads, writes):
        self.all_tokens[id(tok[0])] = tok
        for b in reads:
            b.r.add(tok)
        for b in writes:
            b.w = {tok}
            b.r = set()

    def op(self, e, fn, reads=(), writes=(), inc=True):
        self._deps(e, reads, writes)
        inst = fn(self.eng[e])
        self.ninst += 1
        self.per_eng[e] = self.per_eng.get(e, 0) + 1
        if inc:
            if self.cnt[e] >= self.EPOCH and not self.pending[e]:
                self._new_sem(e)
            self.cnt[e] += 1
            inst.then_inc(self.sem[e], 1)
            tok = (self.sem[e], self.cnt[e])
            self.pending[e] = False
        else:
            assert e == "pe"
            tok = (self.sem[e], self.cnt[e] + 1)
            self.pending[e] = True
        self._record(tok, reads, writes)
        return inst

    def dma(self, q, out, in_, reads=(), writes=(), **kw):
        slot = self.dsem[q][self.dnext[q]]
        self.dnext[q] = (self.dnext[q] + 1) % self.NDMA
        sem, val = slot
        if val > 0:
            self._wait(q, (sem, val))
        self._deps(q, reads, writes)
        inst = self.eng[q].dma_start(out=out, in_=in_, **kw)
        self.ninst += 1
        self.per_eng['dma_' + q] = self.per_eng.get('dma_' + q, 0) + 1
        slot[1] = val + 16
        inst.then_inc(sem, 16)
        tok = (sem, slot[1])
        self._record(tok, reads, writes)
        return inst

    def barrier(self):
        toks = list(self.all_tokens.values())
        for e in self.eng:
            for tok in toks:
                self._wait(e, tok)


class Ctx:
    def __init__(self, nc, stack):
        self.nc = nc
        self.s = Sched(nc, stack)
        self.n = 0

    def sb(self, st, shape, dt, name="t"):
        self.n += 1
        return st.enter_context(self.nc.sbuf_tensor(f"{name}{self.n}", list(shape), dt))

    def ps(self, st, shape, dt=F32, name="p"):
        self.n += 1
        return st.enter_context(self.nc.psum_tensor(f"{name}{self.n}", list(shape), dt))

    def dram(self, name, shape, dt):
        return self.nc.dram_tensor(name, list(shape), dt, kind="Internal").ap()


def dma_rows(s, q, sb3, dram2, nchunks, reads=(), writes=(), to_dram=False, step=8):
    for c0 in range(0, nchunks, step):
        c1 = min(nchunks, c0 + step)
        d = dram2[c0 * 128:c1 * 128, :].rearrange("(c p) t -> p c t", p=128)
        if to_dram:
            s.dma(q, d, sb3[:, c0:c1, :], reads=reads, writes=writes)
        else:
            s.dma(q, sb3[:, c0:c1, :], d, reads=reads, writes=writes)


def bufs(n):
    return [Buf() for _ in range(n)]


NT = 512


def stage_norm(k, xT, g_dram, out, D, T, eps, out_dt):
    NT = min(512, T)
    s = k.s
    KC = D // 128
    with contextlib.ExitStack() as st:
        ones = k.sb(st, [128, 128], F32)
        gcol = k.sb(st, [128, KC], F32)
        xt = k.sb(st, [128, KC, NT], F32)
        ht = k.sb(st, [128, KC, NT], out_dt)
        sq = [k.sb(st, [128, NT], F32) for _ in range(2)]
        rs = k.sb(st, [128, NT], F32)
        rstd = k.sb(st, [128, NT], F32)
        pss = k.ps(st, [128, NT])
        b_ones, b_g, b_rs, b_rstd, b_ps = bufs(5)
        b_xt, b_ht, b_sq = bufs(KC), bufs(KC), bufs(2)
        ones_f = k.sb(st, [128, 128], F32)
        s.op("pool", lambda e: e.memset(ones_f[:], 1.0), writes=[b_ones])
        s.op("act", lambda e: e.copy(out=ones[:].bitcast(mybir.dt.float32r), in_=ones_f[:]), reads=[b_ones], writes=[b_ones])
        s.dma("sp", gcol[:], g_dram, writes=[b_g])
        for t0 in range(0, T, NT):
            for c in range(KC):
                s.dma("sp", xt[:, c, :], xT[c * 128:(c + 1) * 128, t0:t0 + NT], writes=[b_xt[c]])
                s.op("act", lambda e: e.activation(out=sq[c % 2][:].bitcast(mybir.dt.float32r), in_=xt[:, c, :], func=AF.Square),
                     reads=[b_xt[c]], writes=[b_sq[c % 2]])
                s.op("pe", lambda e: e.matmul(pss[:], lhsT=ones[:].bitcast(mybir.dt.float32r), rhs=sq[c % 2][:].bitcast(mybir.dt.float32r),
                                             start=(c == 0), stop=(c == KC - 1)),
                     reads=[b_ones, b_sq[c % 2]], writes=[b_ps], inc=True)
            s.op("act", lambda e: e.activation(out=rs[:], in_=pss[:], func=AF.Sqrt, scale=1.0 / D, bias=eps_ap(k, eps)),
                 reads=[b_ps], writes=[b_rs])
            s.op("dve", lambda e: e.reciprocal(out=rstd[:], in_=rs[:]), reads=[b_rs], writes=[b_rstd])
            for c in range(KC):
                s.op("dve", lambda e: e.scalar_tensor_tensor(out=ht[:, c, :], in0=xt[:, c, :], scalar=gcol[:, c:c + 1],
                                                             in1=rstd[:], op0=ALU.mult, op1=ALU.mult),
                     reads=[b_xt[c], b_g, b_rstd], writes=[b_ht[c]])
            dma_rows(s, "sp", ht, out[:, t0:t0 + NT], KC, reads=b_ht, to_dram=True)
        s.barrier()


_EPS = {}


def eps_ap(k, val):
    return float(val)


def stage_proj(k, hT, W, PT, D, T, NB):
    NT = min(512, T)
    s = k.s
    KC = D // 128
    GRP = 6
    with contextlib.ExitStack() as st:
        wb = [[k.sb(st, [128, KC, 128], BF16) for _ in range(GRP)] for _ in range(2)]
        ht = [k.sb(st, [128, KC, NT], BF16) for _ in range(2)]
        ot = [k.sb(st, [128, NT], F32) for _ in range(4)]
        ps = [k.ps(st, [128, NT]) for _ in range(8)]
        b_wb, b_ht, b_ot, b_ps = [bufs(GRP), bufs(GRP)], bufs(2), bufs(4), bufs(8)
        groups = list(range(0, NB, GRP))

        def load_group(gi):
            g0 = groups[gi]
            for b in range(min(GRP, NB - g0)):
                s.dma("pool", wb[gi % 2][b][:], W[g0 + b], writes=[b_wb[gi % 2][b]], max_dma_last_dim=4096)

        nht = 0
        no = 0
        npp = 0
        load_group(0)
        for gi, g0 in enumerate(groups):
            nb = min(GRP, NB - g0)
            if gi + 1 < len(groups):
                load_group(gi + 1)
            wset, bset = wb[gi % 2], b_wb[gi % 2]
            for t0 in range(0, T, NT):
                h = nht % 2
                nht += 1
                dma_rows(s, "act", ht[h], hT[:, t0:t0 + NT], KC, writes=[b_ht[h]])
                for b in range(nb):
                    pi = npp % 8
                    npp += 1
                    p = ps[pi]
                    for c in range(KC):
                        s.op("pe", lambda e: e.matmul(p[:], lhsT=wset[b][:, c, :], rhs=ht[h][:, c, :],
                                                     start=(c == 0), stop=(c == KC - 1)),
                             reads=[bset[b], b_ht[h]], writes=[b_ps[pi]], inc=(c == KC - 1))
                    o = no % 4
                    no += 1
                    if no % 2 == 0:
                        s.op("dve", lambda e: e.tensor_copy(out=ot[o][:], in_=p[:]), reads=[b_ps[pi]], writes=[b_ot[o]])
                    else:
                        s.op("act", lambda e: e.copy(out=ot[o][:], in_=p[:]), reads=[b_ps[pi]], writes=[b_ot[o]])
                    pt_t, pt_r = PT(g0 + b)
                    s.dma("sp", pt_t[pt_r:pt_r + 128, t0:t0 + NT], ot[o][:], reads=[b_ot[o]])
        s.barrier()


def cast_dram(k, dst, src, n_elems, q="pool", width=1024):
    s = k.s
    ROW = width
    assert n_elems % ROW == 0
    rows = n_elems // ROW
    CH = 2048
    for r0 in range(0, rows, CH):
        r1 = min(rows, r0 + CH)
        s.dma(q, dst[r0:r1, :], src[r0:r1, :])


def stage_merge(k, hT, yT, br_kc, Wg, Wbr, Wo, xT, xnT, D, T):
    s = k.s
    KC = D // 128
    OB = D // 128
    YC = sum(br_kc)
    yoff = [sum(br_kc[:i]) for i in range(len(br_kc))]
    NBR = len(br_kc)
    with contextlib.ExitStack() as st:
        ht = k.sb(st, [128, KC, NT], BF16)
        yt = k.sb(st, [128, YC, NT], BF16)
        mg = k.sb(st, [128, KC, NT], BF16)
        wg = [k.sb(st, [128, KC, 128], BF16) for _ in range(3)]
        wbr = [k.sb(st, [128, max(br_kc), 128], BF16) for _ in range(3)]
        gs = [k.sb(st, [128, NT], F32) for _ in range(2)]
        acc = [k.sb(st, [128, NT], F32) for _ in range(2)]
        tmp = [k.sb(st, [128, NT], F32) for _ in range(2)]
        xt = [k.sb(st, [128, NT], F32) for _ in range(2)]
        xn = [k.sb(st, [128, NT], F32) for _ in range(2)]
        psg = [k.ps(st, [128, NT]) for _ in range(2)]
        psp = [k.ps(st, [128, NT]) for _ in range(2)]
        pso = [k.ps(st, [128, NT]) for _ in range(2)]
        b_ht, b_yt = Buf(), Buf()
        b_mg = bufs(KC)
        b_wg, b_wbr, b_gs, b_acc, b_tmp, b_xt, b_xn = bufs(3), bufs(3), bufs(2), bufs(2), bufs(2), bufs(2), bufs(2)
        b_psg, b_psp, b_pso = bufs(2), bufs(2), bufs(2)
        nw = 0
        ng = 0
        na = 0
        for t0 in range(0, T, NT):
            dma_rows(s, "act", ht, hT[:, t0:t0 + NT], KC, writes=[b_ht])
            dma_rows(s, "act", yt, yT[:, t0:t0 + NT], YC, writes=[b_yt])
            for ob in range(OB):
                a = na % 2
                na += 1
                for br in range(NBR):
                    w = nw % 3
                    nw += 1
                    g = ng % 2
                    ng += 1
                    kcb = br_kc[br]
                    s.dma("sp", wg[w][:], Wg[br, ob], writes=[b_wg[w]])
                    s.dma("sp", wbr[w][:, 0:kcb, :], Wbr[br][ob], writes=[b_wbr[w]])
                    for c in range(KC):
                        s.op("pe", lambda e: e.matmul(psg[g][:], lhsT=wg[w][:, c, :], rhs=ht[:, c, :],
                                                     start=(c == 0), stop=(c == KC - 1)),
                             reads=[b_wg[w], b_ht], writes=[b_psg[g]], inc=(c == KC - 1))
                    for c in range(kcb):
                        s.op("pe", lambda e: e.matmul(psp[g][:], lhsT=wbr[w][:, c, :], rhs=yt[:, yoff[br] + c, :],
                                                     start=(c == 0), stop=(c == kcb - 1)),
                             reads=[b_wbr[w], b_yt], writes=[b_psp[g]], inc=(c == kcb - 1))
                    s.op("act", lambda e: e.activation(out=gs[g][:], in_=psg[g][:], func=AF.Sigmoid),
                         reads=[b_psg[g]], writes=[b_gs[g]])
                    last = (br == NBR - 1)
                    if br == 0:
                        dst, bdst = (mg[:, ob, :], b_mg[ob]) if last else (acc[a][:], b_acc[a])
                        s.op("dve", lambda e: e.tensor_tensor(out=dst, in0=psp[g][:], in1=gs[g][:], op=ALU.mult),
                             reads=[b_psp[g], b_gs[g]], writes=[bdst])
                    else:
                        s.op("dve", lambda e: e.tensor_tensor(out=tmp[g][:], in0=psp[g][:], in1=gs[g][:], op=ALU.mult),
                             reads=[b_psp[g], b_gs[g]], writes=[b_tmp[g]])
                        if last:
                            s.op("pool", lambda e: e.tensor_tensor(out=mg[:, ob, :], in0=acc[a][:], in1=tmp[g][:], op=ALU.add),
                                 reads=[b_acc[a], b_tmp[g]], writes=[b_mg[ob]])
                        else:
                            s.op("pool", lambda e: e.tensor_tensor(out=acc[a][:], in0=acc[a][:], in1=tmp[g][:], op=ALU.add),
                                 reads=[b_acc[a], b_tmp[g]], writes=[b_acc[a]])
            for ob in range(OB):
                w = nw % 3
                nw += 1
                g = ng % 2
                ng += 1
                s.dma("sp", wg[w][:], Wo[ob], writes=[b_wg[w]])
                s.dma("act", xt[g][:], xT[ob * 128:(ob + 1) * 128, t0:t0 + NT], writes=[b_xt[g]])
                for c in range(KC):
                    s.op("pe", lambda e: e.matmul(pso[g][:], lhsT=wg[w][:, c, :], rhs=mg[:, c, :],
                                                 start=(c == 0), stop=(c == KC - 1)),
                         reads=[b_wg[w], b_mg[c]], writes=[b_pso[g]], inc=(c == KC - 1))
                s.op("dve", lambda e: e.tensor_tensor(out=xn[g][:], in0=pso[g][:], in1=xt[g][:], op=ALU.add),
                     reads=[b_pso[g], b_xt[g]], writes=[b_xn[g]])
                s.dma("sp", xnT[ob * 128:(ob + 1) * 128, t0:t0 + NT], xn[g][:], reads=[b_xn[g]])
        s.barrier()


def stage_lru(k, PT, row_ax, row_ag, yT, row_y, prm, T, NBLK):
    s = k.s
    NCH = NBLK * 2
    with contextlib.ExitStack() as st:
        cw = k.sb(st, [128, NCH, 4], F32)
        cb, ba, bx, lam, c1, c2, tt = [k.sb(st, [128, NCH], F32) for _ in range(7)]
        b_prm = Buf()
        for dst, nm in ((cw, "conv_w"), (cb, "conv_b"), (ba, "ba"), (bx, "bx"), (lam, "lam")):
            s.dma("sp", dst[:], prm[nm], writes=[b_prm])
        s.op("act", lambda e: e.activation(out=tt[:], in_=lam[:], func=AF.Exp, scale=-1.0), reads=[b_prm], writes=[b_prm])
        s.op("act", lambda e: e.activation(out=tt[:], in_=tt[:], func=AF.Ln, bias=1.0), reads=[b_prm], writes=[b_prm])
        s.op("dve", lambda e: e.tensor_scalar(out=c1[:], in0=tt[:], scalar1=-8.0, scalar2=None, op0=ALU.mult), reads=[b_prm], writes=[b_prm])
        s.op("dve", lambda e: e.tensor_scalar(out=c2[:], in0=tt[:], scalar1=-16.0, scalar2=None, op0=ALU.mult), reads=[b_prm], writes=[b_prm])
        waf = k.sb(st, [128, 2, 2, 128], F32)
        wab = [k.sb(st, [128, 2, 2, 128], BF16) for _ in range(2)]
        xin = [k.sb(st, [128, 3 + NT], F32) for _ in range(2)]
        u = [k.sb(st, [128, NT], F32) for _ in range(2)]
        ub = [k.sb(st, [128, NT], BF16) for _ in range(2)]
        rr, ii, aa, a2, mm, bt, gt, sg = [k.sb(st, [128, NT], F32) for _ in range(8)]
        hs = [[k.sb(st, [128, NT], F32) for _ in range(2)] for _ in range(2)]
        yo = [k.sb(st, [128, NT], BF16) for _ in range(2)]
        psr, psi = k.ps(st, [128, NT]), k.ps(st, [128, NT])
        b_waf, b_psr, b_psi, b_rr, b_ii, b_aa, b_a2, b_mm, b_bt, b_gt, b_sg = bufs(11)
        b_wab, b_xin, b_u, b_ub, b_yo = bufs(2), bufs(2), bufs(2), bufs(2), bufs(2)
        b_hs = [bufs(2), bufs(2)]
        for nb in range(NBLK):
            for wi, nm in enumerate(("wa", "wx")):
                s.dma("sp", waf[:], prm[nm][nb].rearrange("o p c m -> p o c m"), writes=[b_waf])
                s.op("act", lambda e: e.copy(out=wab[wi][:], in_=waf[:]), reads=[b_waf], writes=[b_wab[wi]])
            for ti, t0 in enumerate(range(0, T, NT)):
                par = ti % 2
                for kc in range(2):
                    ch = nb * 2 + kc
                    if ti == 0:
                        s.op("pool", lambda e: e.memset(xin[kc][:, 0:3], 0.0), writes=[b_xin[kc]])
                    else:
                        s.op("pool", lambda e: e.tensor_copy(out=xin[kc][:, 0:3], in_=xin[kc][:, NT:NT + 3]),
                             reads=[b_xin[kc]], writes=[b_xin[kc]])
                    s.dma("sp", xin[kc][:, 3:3 + NT], PT[row_ax + ch * 128: row_ax + (ch + 1) * 128, t0:t0 + NT],
                          reads=[b_xin[kc]], writes=[b_xin[kc]])
                    s.op("dve", lambda e: e.tensor_scalar(out=u[kc][:], in0=xin[kc][:, 3:3 + NT], scalar1=cw[:, ch, 3:4],
                                                          scalar2=cb[:, ch:ch + 1], op0=ALU.mult, op1=ALU.add),
                         reads=[b_xin[kc], b_prm], writes=[b_u[kc]])
                    for j in (2, 1, 0):
                        s.op("dve", lambda e: e.scalar_tensor_tensor(out=u[kc][:], in0=xin[kc][:, j:j + NT], scalar=cw[:, ch, j:j + 1],
                                                                     in1=u[kc][:], op0=ALU.mult, op1=ALU.add),
                             reads=[b_xin[kc], b_prm, b_u[kc]], writes=[b_u[kc]])
                    s.op("act", lambda e: e.copy(out=ub[kc][:], in_=u[kc][:]), reads=[b_u[kc]], writes=[b_ub[kc]])
                for oc in range(2):
                    ch = nb * 2 + oc
                    for kc in range(2):
                        s.op("pe", lambda e: e.matmul(psr[:], lhsT=wab[0][:, oc, kc, :], rhs=ub[kc][:], start=(kc == 0), stop=(kc == 1)),
                             reads=[b_wab[0], b_ub[kc]], writes=[b_psr], inc=(kc == 1))
                    for kc in range(2):
                        s.op("pe", lambda e: e.matmul(psi[:], lhsT=wab[1][:, oc, kc, :], rhs=ub[kc][:], start=(kc == 0), stop=(kc == 1)),
                             reads=[b_wab[1], b_ub[kc]], writes=[b_psi], inc=(kc == 1))
                    s.op("act", lambda e: e.activation(out=rr[:], in_=psr[:], func=AF.Sigmoid, bias=ba[:, ch:ch + 1]),
                         reads=[b_psr, b_prm], writes=[b_rr])
                    s.op("act", lambda e: e.activation(out=ii[:], in_=psi[:], func=AF.Sigmoid, bias=bx[:, ch:ch + 1]),
                         reads=[b_psi, b_prm], writes=[b_ii])
                    s.op("act", lambda e: e.activation(out=aa[:], in_=rr[:], func=AF.Exp, scale=c1[:, ch:ch + 1]),
                         reads=[b_rr, b_prm], writes=[b_aa])
                    s.op("act", lambda e: e.activation(out=a2[:], in_=rr[:], func=AF.Exp, scale=c2[:, ch:ch + 1]),
                         reads=[b_rr, b_prm], writes=[b_a2])
                    s.op("act", lambda e: e.activation(out=mm[:], in_=a2[:], func=AF.Sqrt, scale=-1.0, bias=1.0),
                         reads=[b_a2], writes=[b_mm])
                    s.op("dve", lambda e: e.tensor_tensor(out=bt[:], in0=mm[:], in1=ii[:], op=ALU.mult),
                         reads=[b_mm, b_ii], writes=[b_bt])
                    s.op("dve", lambda e: e.tensor_tensor(out=bt[:], in0=bt[:], in1=u[oc][:], op=ALU.mult),
                         reads=[b_bt, b_u[oc]], writes=[b_bt])
                    init = 0.0 if ti == 0 else hs[oc][1 - par][:, NT - 1:NT]
                    s.op("dve", lambda e: e.tensor_tensor_scan(out=hs[oc][par][:], data0=aa[:], data1=bt[:], initial=init,
                                                               op0=ALU.mult, op1=ALU.add),
                         reads=[b_aa, b_bt] + ([] if ti == 0 else [b_hs[oc][1 - par]]), writes=[b_hs[oc][par]])
                    s.dma("act", gt[:], PT[row_ag + ch * 128: row_ag + (ch + 1) * 128, t0:t0 + NT], writes=[b_gt])
                    s.op("act", lambda e: e.activation(out=sg[:], in_=gt[:], func=AF.Silu), reads=[b_gt], writes=[b_sg])
                    s.op("dve", lambda e: e.tensor_tensor(out=yo[oc][:], in0=hs[oc][par][:], in1=sg[:], op=ALU.mult),
                         reads=[b_hs[oc][par], b_sg], writes=[b_yo[oc]])
                    s.dma("sp", yT[row_y + ch * 128: row_y + (ch + 1) * 128, t0:t0 + NT], yo[oc][:], reads=[b_yo[oc]])
        s.barrier()


def stage_xattn(k, PT, row_q, row_g, yT, row_y, memnT, Wk, Wv, ident_d, T, D, M, H):
    s = k.s
    KC = D // 128
    MC = M // 128
    sc = 128 ** -0.5
    with contextlib.ExitStack() as st:
        ident = k.sb(st, [128, 128], F32)
        memn = k.sb(st, [128, KC, M], BF16)
        kT = k.sb(st, [128, H, M], BF16)
        vt = k.sb(st, [128, MC, H * 128], BF16)
        b_id, b_memn, b_kT, b_vt = bufs(4)
        s.dma("sp", ident[:], ident_d, writes=[b_id])
        dma_rows(s, "sp", memn, memnT, KC, writes=[b_memn])
        with contextlib.ExitStack() as st1:
            wf = k.sb(st1, [128, KC, 128], F32)
            wb = k.sb(st1, [128, KC, 128], BF16)
            vf = [k.sb(st1, [128, H * 128], F32) for _ in range(2)]
            vb = [k.sb(st1, [128, H * 128], BF16) for _ in range(2)]
            psk = k.ps(st1, [128, M])
            psv = [k.ps(st1, [128, H * 128]) for _ in range(MC)]
            b_wf, b_wb, b_psk = bufs(3)
            b_vf, b_vb, b_psv = bufs(2), bufs(2), bufs(MC)
            for hd in range(H):
                s.dma("sp", wf[:], Wk[hd], writes=[b_wf])
                s.op("act", lambda e: e.copy(out=wb[:], in_=wf[:]), reads=[b_wf], writes=[b_wb])
                for c in range(KC):
                    s.op("pe", lambda e: e.matmul(psk[:], lhsT=wb[:, c, :], rhs=memn[:, c, :], start=(c == 0), stop=(c == KC - 1)),
                         reads=[b_wb, b_memn], writes=[b_psk], inc=(c == KC - 1))
                s.op("dve", lambda e: e.tensor_copy(out=kT[:, hd, :], in_=psk[:]), reads=[b_psk], writes=[b_kT])
            for c in range(KC):
                i = c % 2
                s.dma("sp", vf[i][:], Wv[c], writes=[b_vf[i]])
                s.op("act", lambda e: e.copy(out=vb[i][:], in_=vf[i][:]), reads=[b_vf[i]], writes=[b_vb[i]])
                for mc in range(MC):
                    s.op("pe", lambda e: e.matmul(psv[mc][:], lhsT=memn[:, c, mc * 128:(mc + 1) * 128], rhs=vb[i][:],
                                                 start=(c == 0), stop=(c == KC - 1)),
                         reads=[b_vb[i], b_memn], writes=[b_psv[mc]], inc=True)
            for mc in range(MC):
                s.op("dve", lambda e: e.tensor_copy(out=vt[:, mc, :], in_=psv[mc][:]), reads=[b_psv[mc]], writes=[b_vt])
            s.barrier()
        qf = k.sb(st, [128, H, NT], F32)
        qb = k.sb(st, [128, H, NT], BF16)
        gf = k.sb(st, [128, H, NT], F32)
        sg = k.sb(st, [128, H, NT], F32)
        pf = [k.sb(st, [128, M], F32) for _ in range(2)]
        pn = [k.sb(st, [128, M], F32) for _ in range(2)]
        pT = [k.sb(st, [128, MC, 128], BF16) for _ in range(2)]
        mx, nmx, rsum, rinv = [[k.sb(st, [128, 1], F32) for _ in range(2)] for _ in range(4)]
        yo = [k.sb(st, [128, NT], BF16) for _ in range(2)]
        pss = [k.ps(st, [128, M]) for _ in range(2)]
        pst = [k.ps(st, [128, MC, 128]) for _ in range(2)]
        pso = [k.ps(st, [128, 128]) for _ in range(2)]
        b_qf, b_qb, b_gf, b_sg = bufs(4)
        b_pf, b_pn, b_pT, b_mx, b_nmx, b_rsum, b_rinv, b_yo, b_pss, b_pst, b_pso = [bufs(2) for _ in range(11)]
        it = 0
        for t0 in range(0, T, NT):
            s.dma("sp", qf[:], PT[row_q:row_q + H * 128, t0:t0 + NT].rearrange("(h p) t -> p h t", p=128), writes=[b_qf])
            s.dma("act", gf[:], PT[row_g:row_g + H * 128, t0:t0 + NT].rearrange("(h p) t -> p h t", p=128), writes=[b_gf])
            s.op("act", lambda e: e.copy(out=qb[:], in_=qf[:]), reads=[b_qf], writes=[b_qb])
            s.op("act", lambda e: e.activation(out=sg[:], in_=gf[:], func=AF.Silu), reads=[b_gf], writes=[b_sg])
            for hd in range(H):
                yi = hd % 2
                for tb in range(NT // 128):
                    i = it % 2
                    it += 1
                    tsl = slice(tb * 128, (tb + 1) * 128)
                    s.op("pe", lambda e: e.matmul(pss[i][:], lhsT=qb[:, hd, tsl], rhs=kT[:, hd, :], start=True, stop=True),
                         reads=[b_qb, b_kT], writes=[b_pss[i]])
                    s.op("dve", lambda e: e.tensor_reduce(out=mx[i][:], in_=pss[i][:], axis=AX.X, op=ALU.max),
                         reads=[b_pss[i]], writes=[b_mx[i]])
                    s.op("dve", lambda e: e.tensor_scalar(out=nmx[i][:], in0=mx[i][:], scalar1=-sc, scalar2=None, op0=ALU.mult),
                         reads=[b_mx[i]], writes=[b_nmx[i]])
                    s.op("act", lambda e: e.activation(out=pf[i][:], in_=pss[i][:], func=AF.Exp, scale=sc, bias=nmx[i][:],
                                                       accum_out=rsum[i][:]),
                         reads=[b_pss[i], b_nmx[i]], writes=[b_pf[i], b_rsum[i]])
                    s.op("dve", lambda e: e.reciprocal(out=rinv[i][:], in_=rsum[i][:]), reads=[b_rsum[i]], writes=[b_rinv[i]])
                    s.op("dve", lambda e: e.tensor_scalar(out=pn[i][:], in0=pf[i][:], scalar1=rinv[i][:], scalar2=None, op0=ALU.mult),
                         reads=[b_pf[i], b_rinv[i]], writes=[b_pn[i]])
                    for mc in range(MC):
                        s.op("pe", lambda e: e.transpose(pst[i][:, mc, :], pn[i][:, mc * 128:(mc + 1) * 128], ident[:]),
                             reads=[b_pn[i], b_id], writes=[b_pst[i]], inc=True)
                    s.op("act", lambda e: e.copy(out=pT[i][:], in_=pst[i][:]), reads=[b_pst[i]], writes=[b_pT[i]])
                    for mc in range(MC):
                        s.op("pe", lambda e: e.matmul(pso[i][:], lhsT=vt[:, mc, hd * 128:(hd + 1) * 128], rhs=pT[i][:, mc, :],
                                                     start=(mc == 0), stop=(mc == MC - 1)),
                             reads=[b_vt, b_pT[i]], writes=[b_pso[i]], inc=(mc == MC - 1))
                    s.op("dve", lambda e: e.tensor_tensor(out=yo[yi][:, tsl], in0=pso[i][:], in1=sg[:, hd, tsl], op=ALU.mult),
                         reads=[b_pso[i], b_sg], writes=[b_yo[yi]])
                s.dma("sp", yT[row_y + hd * 128: row_y + (hd + 1) * 128, t0:t0 + NT], yo[yi][:], reads=[b_yo[yi]])
        s.barrier()


LCH = 64
MLSTM_L = 128
MLSTM_FP32R = True
NEG = -1.0e30
RWKV_G = 2
RWKV_L = 128
RWKV_FP32R = True


def stage_mlstm(k, PT, rows, yT, row_y, prm, cst, T, HC):
    s = k.s
    ML = MLSTM_L
    NCK = T // ML
    QC = 2 * HC
    with contextlib.ExitStack() as st:
        ident = k.sb(st, [128, 128], F32)
        sel = k.sb(st, [HC, HC, 128], F32)
        i4 = k.sb(st, [HC, HC], F32)
        negmask = k.sb(st, [ML, ML], F32)
        ones = k.sb(st, [128, 128], F32)
        cw = k.sb(st, [128, 2 * QC, 4], F32)
        cb = k.sb(st, [128, 2 * QC], F32)
        gnw = k.sb(st, [128, HC * 4], F32)
        bi = k.sb(st, [HC, 1], F32)
        bfn = k.sb(st, [HC, 1], F32)
        b_c = Buf()
        for dst, src in ((ident, cst["ident"]), (sel, cst["sel"]), (i4, cst["i4"]), (negmask, cst["negmask"]),
                         (cw, prm["conv_w"]), (cb, prm["conv_b"]), (gnw, prm["gn_w"]), (bi, prm["b_i"]), (bfn, prm["b_f"])):
            s.dma("sp", dst[:], src, writes=[b_c])
        s.op("pool", lambda e: e.memset(ones[:], 1.0), writes=[b_c])
        ones_r = k.sb(st, [128, 128], F32)
        s.op("act", lambda e: e.copy(out=ones_r[:].bitcast(mybir.dt.float32r), in_=ones[:]), reads=[b_c], writes=[b_c])
        s.op("dve", lambda e: e.tensor_scalar(out=bfn[:], in0=bfn[:], scalar1=-1.0, scalar2=None, op0=ALU.mult), reads=[b_c], writes=[b_c])
        gi = k.sb(st, [HC, T], F32)
        gf = k.sb(st, [HC, T], F32)
        gm = k.sb(st, [HC, T], F32)
        ge = k.sb(st, [HC, T], F32)
        b_g = Buf()
        s.dma("sp", gi[:], PT[rows["ifg"]:rows["ifg"] + HC, :], writes=[b_g])
        s.dma("sp", gf[:], PT[rows["ifg"] + HC:rows["ifg"] + 2 * HC, :], writes=[b_g])
        s.op("act", lambda e: e.activation(out=gf[:], in_=gf[:], func=AF.Exp, scale=-1.0, bias=bfn[:]), reads=[b_g, b_c], writes=[b_g])
        s.op("act", lambda e: e.activation(out=gf[:], in_=gf[:], func=AF.Ln, bias=1.0), reads=[b_g], writes=[b_g])
        s.op("dve", lambda e: e.tensor_scalar(out=gf[:], in0=gf[:], scalar1=-1.0, scalar2=None, op0=ALU.mult), reads=[b_g], writes=[b_g])
        s.op("pool", lambda e: e.memset(gm[:], 1.0), writes=[b_g])
        s.op("dve", lambda e: e.tensor_tensor_scan(out=gf[:], data0=gm[:], data1=gf[:], initial=0.0, op0=ALU.mult, op1=ALU.add),
             reads=[b_g], writes=[b_g])
        s.op("dve", lambda e: e.scalar_tensor_tensor(out=gi[:], in0=gi[:], scalar=bi[:], in1=gf[:], op0=ALU.add, op1=ALU.subtract),
             reads=[b_g, b_c], writes=[b_g])
        s.op("dve", lambda e: e.tensor_tensor_scan(out=gm[:], data0=gi[:], data1=gi[:], initial=NEG, op0=ALU.max, op1=ALU.max),
             reads=[b_g], writes=[b_g])
        s.op("dve", lambda e: e.tensor_tensor(out=ge[:], in0=gf[:], in1=gm[:], op=ALU.add), reads=[b_g], writes=[b_g])
        cols = k.sb(st, [ML, NCK, 2 * HC], F32)
        b_cols = Buf()
        with contextlib.ExitStack() as st1:
            psc = [k.ps(st1, [ML, 8, 2 * HC]) for _ in range(2)]
            b_psc = bufs(2)
            for c8 in range(0, NCK, 8):
                i = (c8 // 8) % 2
                for cc in range(8):
                    c = c8 + cc
                    s.op("pe", lambda e: e.matmul(psc[i][:, cc, 0:HC], lhsT=gi[:, c * ML:(c + 1) * ML], rhs=i4[:, 0:HC], start=True, stop=True),
                         reads=[b_g, b_c], writes=[b_psc[i]])
                    s.op("pe", lambda e: e.matmul(psc[i][:, cc, HC:2 * HC], lhsT=ge[:, c * ML:(c + 1) * ML], rhs=i4[:, 0:HC], start=True, stop=True),
                         reads=[b_g, b_c], writes=[b_psc[i]])
                s.op("dve", lambda e: e.tensor_copy(out=cols[:, c8:c8 + 8, :], in_=psc[i][:]), reads=[b_psc[i]], writes=[b_cols])
            s.op("act", lambda e: e.activation(out=cols[:, :, HC:2 * HC], in_=cols[:, :, HC:2 * HC], func=AF.Exp, scale=-1.0),
                 reads=[b_cols], writes=[b_cols])
            s.barrier()
        xin = k.sb(st, [128, 4, 3 + NT], F32)
        qk = k.sb(st, [128, 4, NT], F32)
        vT = k.sb(st, [128, 4, NT], F32)
        oT = k.sb(st, [128, 4, NT], F32)
        gT = k.sb(st, [128, 4, NT], F32)
        negMb = k.sb(st, [128, NT], F32)
        ms = k.sb(st, [128, NT // ML + 1], F32)
        keep2 = [k.sb(st, [128, 1], F32) for _ in range(2)]
        b_keep2 = bufs(2)
        ho = k.sb(st, [128, 4, NT], F32)
        Cst = k.sb(st, [128, 2, 512], F32)
        Cr = k.sb(st, [128, 2, 512], F32)
        b_Cr = Buf()
        R = (lambda ap: ap.bitcast(mybir.dt.float32r)) if MLSTM_FP32R else (lambda ap: ap)
        nst = k.sb(st, [128, 2], F32)
        ktok2 = [k.sb(st, [ML, 256], F32) for _ in range(2)]
        b_ktok2 = bufs(2)
        khat2 = [k.sb(st, [ML, 256], F32) for _ in range(2)]
        b_khat2 = bufs(2)
        vtok2 = [k.sb(st, [ML, 512], F32) for _ in range(2)]
        b_vtok2 = bufs(2)
        wk2 = [k.sb(st, [ML, 1], F32) for _ in range(2)]
        b_wk2 = bufs(2)
        argE = k.sb(st, [ML, ML], F32)
        ET = k.sb(st, [ML, ML], F32)
        scT2 = [k.sb(st, [ML, ML], F32) for _ in range(2)]
        b_scT2 = bufs(2)
        wint = k.sb(st, [128, ML], F32)
        qt2 = [k.sb(st, [128, 2, ML], F32) for _ in range(2)]
        b_qt2 = bufs(2)
        den = k.sb(st, [ML, 1], F32)
        rden = k.sb(st, [ML, 1], F32)
        htok = k.sb(st, [ML, 512], F32)
        mean = k.sb(st, [128, NT], F32)
        yc = k.sb(st, [128, 4, NT], F32)
        ysq = k.sb(st, [128, NT], F32)
        rstd = k.sb(st, [128, NT], F32)
        yo = [k.sb(st, [128, NT], BF16) for _ in range(2)]
        ps_b = k.ps(st, [128, NT])
        ps_t = k.ps(st, [ML, 512])
        ps_s = k.ps(st, [ML, ML])
        ps_n = k.ps(st, [ML, 512])
        ps_d = k.ps(st, [ML, 8])
        ps_c = k.ps(st, [128, 512])
        ps_h = k.ps(st, [128, 4, ML])
        ps_m = k.ps(st, [128, 8])
        (b_xin, b_qk, b_vT, b_oT, b_gT, b_negMb, b_ms, b_keep, b_ho, b_C, b_n, b_ktok, b_khat, b_vtok, b_wk, b_argE, b_ET, b_scT,
         b_wint, b_qt, b_den, b_rden, b_htok, b_mean, b_yc, b_ysq, b_rstd, b_psb, b_pst, b_pss, b_psn, b_psd, b_psc2, b_psh, b_psm) = bufs(35)
        b_yo = bufs(2)
        for h in range(HC):
            qrows = [rows["q"] + h * 256, rows["q"] + h * 256 + 128, rows["k"] + h * 256, rows["k"] + h * 256 + 128]
            cidx = [h * 2, h * 2 + 1, QC + h * 2, QC + h * 2 + 1]
            s.op("pool", lambda e: e.memset(Cst[:], 0.0), reads=[b_C], writes=[b_C])
            s.op("act", lambda e: e.copy(out=R(Cr[:]), in_=Cst[:]), reads=[b_C, b_Cr], writes=[b_Cr])
            s.op("pool", lambda e: e.memset(nst[:], 0.0), reads=[b_n], writes=[b_n])
            s.op("pool", lambda e: e.memset(ms[:, 0:1], NEG), reads=[b_ms], writes=[b_ms])
            for ti, t0 in enumerate(range(0, T, NT)):
                for j in range(4):
                    if ti == 0:
                        s.op("pool", lambda e: e.memset(xin[:, j, 0:3], 0.0), reads=[b_xin], writes=[b_xin])
                    else:
                        s.op("pool", lambda e: e.tensor_copy(out=xin[:, j, 0:3], in_=xin[:, j, NT:NT + 3]), reads=[b_xin], writes=[b_xin])
                for j in range(4):
                    s.dma("sp", xin[:, j, 3:3 + NT], PT[qrows[j]:qrows[j] + 128, t0:t0 + NT], reads=[b_xin], writes=[b_xin])
                s.dma("act", vT[:], PT[rows["v"] + h * 512: rows["v"] + (h + 1) * 512, t0:t0 + NT].rearrange("(c p) t -> p c t", p=128), writes=[b_vT])
                s.dma("act", oT[:], PT[rows["o"] + h * 512: rows["o"] + (h + 1) * 512, t0:t0 + NT].rearrange("(c p) t -> p c t", p=128), writes=[b_oT])
                s.dma("act", gT[:], PT[rows["g"] + h * 512: rows["g"] + (h + 1) * 512, t0:t0 + NT].rearrange("(c p) t -> p c t", p=128), writes=[b_gT])
                for j in range(4):
                    ci = cidx[j]
                    s.op("dve", lambda e: e.tensor_scalar(out=qk[:, j, :], in0=xin[:, j, 3:3 + NT], scalar1=cw[:, ci, 3:4],
                                                          scalar2=cb[:, ci:ci + 1], op0=ALU.mult, op1=ALU.add),
                         reads=[b_xin, b_c, b_qk], writes=[b_qk])
                    for jj in (2, 1, 0):
                        s.op("dve", lambda e: e.scalar_tensor_tensor(out=qk[:, j, :], in0=xin[:, j, jj:jj + NT], scalar=cw[:, ci, jj:jj + 1],
                                                                     in1=qk[:, j, :], op0=ALU.mult, op1=ALU.add),
                             reads=[b_xin, b_c, b_qk], writes=[b_qk])
                s.op("act", lambda e: e.activation(out=qk[:], in_=qk[:], func=AF.Silu), reads=[b_qk], writes=[b_qk])
                s.op("dve", lambda e: e.tensor_scalar(out=qk[:, 2:4, :], in0=qk[:, 2:4, :], scalar1=256 ** -0.5, scalar2=None, op0=ALU.mult),
                     reads=[b_qk], writes=[b_qk])
                s.op("act", lambda e: e.activation(out=oT[:], in_=oT[:], func=AF.Sigmoid), reads=[b_oT], writes=[b_oT])
                s.op("act", lambda e: e.activation(out=gT[:], in_=gT[:], func=AF.Silu), reads=[b_gT], writes=[b_gT])
                s.op("pe", lambda e: e.matmul(ps_b[:], lhsT=sel[:, h, :], rhs=gm[:, t0:t0 + NT], start=True, stop=True),
                     reads=[b_c, b_g, b_psb], writes=[b_psb])
                if ti > 0:
                    s.op("dve", lambda e: e.tensor_scalar(out=ms[:, 0:1], in0=negMb[:, NT - 1:NT], scalar1=-1.0, scalar2=None, op0=ALU.mult),
                         reads=[b_negMb, b_ms], writes=[b_ms])
                s.op("act", lambda e: e.activation(out=negMb[:], in_=ps_b[:], func=AF.Copy, scale=-1.0), reads=[b_psb, b_negMb], writes=[b_negMb])
                nck = NT // ML
                s.op("dve", lambda e: e.tensor_scalar(out=ms[:, 1:nck + 1], in0=negMb[:, ML - 1:NT:ML], scalar1=-1.0, scalar2=None, op0=ALU.mult),
                     reads=[b_negMb, b_ms], writes=[b_ms])
                def m_indep(cc):
                    c = ti * nck + cc
                    cp = c % 2
                    csl = slice(cc * ML, (cc + 1) * ML)
                    ktok, khat, vtok, scT, qt, keep, wk = ktok2[cp], khat2[cp], vtok2[cp], scT2[cp], qt2[cp], keep2[cp], wk2[cp]
                    b_ktok, b_khat, b_vtok, b_scT, b_qt, b_keep, b_wk = (b_ktok2[cp], b_khat2[cp], b_vtok2[cp], b_scT2[cp], b_qt2[cp],
                                                                          b_keep2[cp], b_wk2[cp])
                    def t1():
                        for j in range(2):
                            s.op("pe", lambda e: e.transpose(ps_t[:, j * 128:(j + 1) * 128], qk[:, 2 + j, csl], ident[:]),
                                 reads=[b_qk, b_c, b_pst], writes=[b_pst])
                        s.op("act", lambda e: e.copy(out=ktok[:], in_=ps_t[:, 0:256]), reads=[b_pst, b_ktok], writes=[b_ktok])
                        for j in range(4):
                            s.op("pe", lambda e: e.transpose(ps_t[:, j * 128:(j + 1) * 128], vT[:, j, csl], ident[:]),
                                 reads=[b_vT, b_c, b_pst], writes=[b_pst])
                        s.op("act", lambda e: e.copy(out=R(vtok[:]), in_=ps_t[:]), reads=[b_pst, b_vtok], writes=[b_vtok])
                    def t2():
                        for j in range(2):
                            s.op("pe", lambda e: e.matmul(ps_s[:], lhsT=qk[:, 2 + j, csl], rhs=qk[:, j, csl], start=(j == 0), stop=(j == 1)),
                                 reads=[b_qk, b_pss], writes=[b_pss])
                        s.op("dve", lambda e: e.tensor_tensor(out=argE[:], in0=negMb[0:ML, csl], in1=negmask[:], op=ALU.add),
                             reads=[b_negMb, b_c, b_argE], writes=[b_argE])
                        s.op("act", lambda e: e.activation(out=ET[:], in_=argE[:], func=AF.Exp, bias=cols[:, c, h:h + 1]),
                             reads=[b_argE, b_cols, b_ET], writes=[b_ET])
                        s.op("dve", lambda e: e.tensor_tensor(out=R(scT[:]), in0=ps_s[:], in1=ET[:], op=ALU.mult),
                             reads=[b_pss, b_ET, b_scT], writes=[b_scT])
                    def t3():
                        s.op("act", lambda e: e.activation(out=wint[:], in_=negMb[:, csl], func=AF.Exp, bias=ms[:, cc:cc + 1]),
                             reads=[b_negMb, b_ms, b_wint], writes=[b_wint])
                        for j in range(2):
                            s.op("dve", lambda e: e.tensor_tensor(out=R(qt[:, j, :]), in0=qk[:, j, csl], in1=wint[:], op=ALU.mult),
                                 reads=[b_qk, b_wint, b_qt], writes=[b_qt])
                    def t4():
                        s.op("act", lambda e: e.activation(out=keep[:], in_=negMb[:, (cc + 1) * ML - 1:(cc + 1) * ML], func=AF.Exp, bias=ms[:, cc:cc + 1]),
                             reads=[b_negMb, b_ms, b_keep], writes=[b_keep])
                        s.op("act", lambda e: e.activation(out=wk[:], in_=negMb[0:ML, (cc + 1) * ML - 1:(cc + 1) * ML], func=AF.Exp, bias=cols[:, c, h:h + 1]),
                             reads=[b_negMb, b_cols, b_wk], writes=[b_wk])
                        s.op("dve", lambda e: e.tensor_scalar(out=R(khat[:]), in0=ktok[:], scalar1=wk[:], scalar2=None, op0=ALU.mult),
                             reads=[b_ktok, b_wk, b_khat], writes=[b_khat])
                    return [t1, t2, t3, t4]

                def m_dep(cc):
                    c = ti * nck + cc
                    cp = c % 2
                    csl = slice(cc * ML, (cc + 1) * ML)
                    ktok, khat, vtok, scT, qt, keep, wk = ktok2[cp], khat2[cp], vtok2[cp], scT2[cp], qt2[cp], keep2[cp], wk2[cp]
                    b_ktok, b_khat, b_vtok, b_scT, b_qt, b_keep, b_wk = (b_ktok2[cp], b_khat2[cp], b_vtok2[cp], b_scT2[cp], b_qt2[cp],
                                                                          b_keep2[cp], b_wk2[cp])
                    def d1():
                        for j in range(2):
                            s.op("pe", lambda e: e.matmul(ps_n[:], lhsT=R(qt[:, j, :]), rhs=R(Cr[:, j, :]), start=(j == 0), stop=False),
                                 reads=[b_qt, b_Cr, b_psn], writes=[b_psn], inc=False)
                        s.op("pe", lambda e: e.matmul(ps_n[:], lhsT=R(scT[:]), rhs=R(vtok[:]), start=False, stop=True),
                             reads=[b_scT, b_vtok, b_psn], writes=[b_psn])
                        for j in range(2):
                            s.op("pe", lambda e: e.matmul(ps_d[:, 0:1], lhsT=qt[:, j, :], rhs=nst[:, j:j + 1], start=(j == 0), stop=False),
                                 reads=[b_qt, b_n, b_psd], writes=[b_psd], inc=False)
                        s.op("pe", lambda e: e.matmul(ps_d[:, 0:1], lhsT=scT[:], rhs=ones[0:ML, 0:1], start=False, stop=True),
                             reads=[b_scT, b_c, b_psd], writes=[b_psd])
                    def d2():
                        s.op("act", lambda e: e.activation(out=den[:], in_=ps_d[:, 0:1], func=AF.Abs),
                             reads=[b_psd, b_den], writes=[b_den])
                        s.op("dve", lambda e: e.tensor_tensor(out=den[:], in0=den[:], in1=cols[:, c, HC + h:HC + h + 1], op=ALU.max),
                             reads=[b_den, b_cols], writes=[b_den])
                        s.op("dve", lambda e: e.reciprocal(out=rden[:], in_=den[:]), reads=[b_den, b_rden], writes=[b_rden])
                        s.op("act", lambda e: e.activation(out=htok[:], in_=ps_n[:], func=AF.Copy, scale=rden[:]),
                             reads=[b_psn, b_rden, b_htok], writes=[b_htok])
                    def d3():
                        for j in range(4):
                            s.op("pe", lambda e: e.transpose(ps_h[:, j, :], htok[:, j * 128:(j + 1) * 128], ident[0:ML, 0:ML]),
                                 reads=[b_htok, b_c, b_psh], writes=[b_psh])
                        s.op("dve", lambda e: e.tensor_tensor(out=ho[:, :, csl].bitcast(mybir.dt.float32r), in0=ps_h[:], in1=oT[:, :, csl], op=ALU.mult),
                             reads=[b_psh, b_oT, b_ho], writes=[b_ho])
                    def d4():
                        for j in range(2):
                            s.op("pe", lambda e: e.matmul(ps_c[:], lhsT=R(khat[:, j * 128:(j + 1) * 128]), rhs=R(vtok[:]), start=True, stop=True),
                                 reads=[b_khat, b_vtok, b_psc2], writes=[b_psc2])
                            s.op("dve", lambda e: e.scalar_tensor_tensor(out=Cst[:, j, :], in0=Cst[:, j, :], scalar=keep[:], in1=ps_c[:],
                                                                         op0=ALU.mult, op1=ALU.add),
                                 reads=[b_C, b_keep, b_psc2], writes=[b_C])
                            s.op("act", lambda e: e.copy(out=R(Cr[:, j, :]), in_=Cst[:, j, :]), reads=[b_C, b_Cr], writes=[b_Cr])
                            s.op("pe", lambda e: e.matmul(ps_m[:, 0:1], lhsT=khat[:, j * 128:(j + 1) * 128], rhs=ones[0:ML, 0:1], start=True, stop=True),
                                 reads=[b_khat, b_c, b_psm], writes=[b_psm])
                            s.op("dve", lambda e: e.scalar_tensor_tensor(out=nst[:, j:j + 1], in0=nst[:, j:j + 1], scalar=keep[:], in1=ps_m[:, 0:1],
                                                                         op0=ALU.mult, op1=ALU.add),
                                 reads=[b_n, b_keep, b_psm], writes=[b_n])
                    return [d1, d2, d3, d4]

                for f_ in m_indep(0):
                    f_()
                for cc in range(nck):
                    dst_ = m_dep(cc)
                    ist_ = m_indep(cc + 1) if cc + 1 < nck else []
                    for j_ in range(max(len(dst_), len(ist_))):
                        if j_ < len(dst_):
                            dst_[j_]()
                        if j_ < len(ist_):
                            ist_[j_]()
                for j in range(4):
                    s.op("pe", lambda e: e.matmul(ps_b[:], lhsT=ones_r[:].bitcast(mybir.dt.float32r), rhs=ho[:, j, :].bitcast(mybir.dt.float32r), start=(j == 0), stop=(j == 3)),
                         reads=[b_c, b_ho, b_psb], writes=[b_psb], inc=(j == 3))
                s.op("act", lambda e: e.activation(out=mean[:], in_=ps_b[:], func=AF.Copy, scale=1.0 / 512), reads=[b_psb, b_mean], writes=[b_mean])
                for j in range(4):
                    s.op("dve", lambda e: e.tensor_tensor(out=yc[:, j, :], in0=ho[:, j, :], in1=mean[:], op=ALU.subtract),
                         reads=[b_ho, b_mean, b_yc], writes=[b_yc])
                for j in range(4):
                    s.op("act", lambda e: e.activation(out=ysq[:].bitcast(mybir.dt.float32r), in_=yc[:, j, :], func=AF.Square), reads=[b_yc, b_ysq], writes=[b_ysq])
                    s.op("pe", lambda e: e.matmul(ps_b[:], lhsT=ones_r[:].bitcast(mybir.dt.float32r), rhs=ysq[:].bitcast(mybir.dt.float32r), start=(j == 0), stop=(j == 3)),
                         reads=[b_c, b_ysq, b_psb], writes=[b_psb])
                s.op("act", lambda e: e.activation(out=rstd[:], in_=ps_b[:], func=AF.Sqrt, scale=1.0 / 512, bias=1e-6),
                     reads=[b_psb, b_rstd], writes=[b_rstd])
                s.op("dve", lambda e: e.reciprocal(out=rstd[:], in_=rstd[:]), reads=[b_rstd], writes=[b_rstd])
                for j in range(4):
                    yi = j % 2
                    s.op("dve", lambda e: e.scalar_tensor_tensor(out=yc[:, j, :], in0=yc[:, j, :], scalar=gnw[:, h * 4 + j:h * 4 + j + 1],
                                                                 in1=rstd[:], op0=ALU.mult, op1=ALU.mult),
                         reads=[b_yc, b_c, b_rstd], writes=[b_yc])
                    s.op("dve", lambda e: e.tensor_tensor(out=yo[yi][:], in0=yc[:, j, :], in1=gT[:, j, :], op=ALU.mult),
                         reads=[b_yc, b_gT, b_yo[yi]], writes=[b_yo[yi]])
                    r0 = row_y + h * 512 + j * 128
                    s.dma("sp", yT[r0:r0 + 128, t0:t0 + NT], yo[yi][:], reads=[b_yo[yi]])
        s.barrier()


def stage_rwkv(k, PT, rows, yT, row_y, prm, cst, T, NH, GN_EPS):
    s = k.s
    G = RWKV_G
    assert G == 2
    NLVL = {64: 6, 128: 7}[RWKV_L]
    RL = RWKV_L
    NCK = NT // RL
    assert NH % G == 0 and T % NT == 0
    with contextlib.ExitStack() as st:
        ident = k.sb(st, [128, 128], F32)
        ones64 = k.sb(st, [64, 64], F32)
        rmask = k.sb(st, [64, NT], F32)
        maskA = k.sb(st, [RL, G, 2 * RL], F32)
        maskN = k.sb(st, [RL, G, RL], F32)
        P = {}
        for nm in ("mu_r", "mu_k", "mu_v", "w0", "a0", "k_k", "k_a", "r_k", "gn_w", "gn_b"):
            P[nm] = k.sb(st, [64, NH], F32)
        om = {nm: k.sb(st, [64, NH], F32) for nm in ("mu_r", "mu_k", "mu_v", "k_a")}
        nw0 = k.sb(st, [64, NH], F32)
        mud = {nm: k.sb(st, [96, 1], F32) for nm in ("mu_wd", "mu_ad")}
        omd = {nm: k.sb(st, [96, 1], F32) for nm in ("mu_wd", "mu_ad")}
        b_c = Buf()
        for dst, src in ((ident, cst["ident"]), (rmask, cst["resetmask"]), (maskA, cst["maskA"]), (maskN, cst["maskN"])):
            s.dma("sp", dst[:], src, writes=[b_c])
        for nm in P:
            s.dma("sp", P[nm][:], prm[nm], writes=[b_c])
        for nm in mud:
            s.dma("sp", mud[nm][:], prm[nm], writes=[b_c])
        s.op("pool", lambda e: e.memset(ones64[:], 1.0), writes=[b_c])
        ones64r = k.sb(st, [64, 64], F32)
        s.op("act", lambda e: e.copy(out=ones64r[:].bitcast(mybir.dt.float32r), in_=ones64[:]), reads=[b_c], writes=[b_c])
        t3 = [k.sb(st, [64, NT], F32) for _ in range(G)]
        b_t3 = bufs(G)
        zer = k.sb(st, [64, G, 64], F32)
        s.op("pool", lambda e: e.memset(zer[:], 0.0), writes=[b_c])
        for nm in om:
            s.op("dve", lambda e: e.tensor_scalar(out=om[nm][:], in0=P[nm][:], scalar1=-1.0, scalar2=1.0, op0=ALU.mult, op1=ALU.add),
                 reads=[b_c], writes=[b_c])
        for nm in omd:
            s.op("dve", lambda e: e.tensor_scalar(out=omd[nm][:], in0=mud[nm][:], scalar1=-1.0, scalar2=1.0, op0=ALU.mult, op1=ALU.add),
                 reads=[b_c], writes=[b_c])
        s.op("dve", lambda e: e.tensor_scalar(out=nw0[:], in0=P["w0"][:], scalar1=-1.0, scalar2=None, op0=ALU.mult), reads=[b_c], writes=[b_c])

        wup = k.sb(st, [96, G * 64], F32)
        aup = k.sb(st, [96, G * 64], F32)
        rawd = k.sb(st, [96, 1 + NT], F32)
        rawa = k.sb(st, [96, 1 + NT], F32)
        twd = k.sb(st, [96, NT], F32)
        adq = k.sb(st, [96, NT], F32)
        raw = {nm: k.sb(st, [64, G, 1 + NT], F32) for nm in ("r", "k", "v")}
        gate = None
        (r_, k_, e2, c_, eP, eN, a_, kk, kkn, kmod, kka, t1, t2) = [[k.sb(st, [64, NT], F32) for _ in range(G)] for _ in range(13)]
        yfm = k.sb(st, [64, G, NT], F32)
        dbl_tiles = [[k.sb(st, shp, F32) for _ in range(2)] for shp in
                     ([64, G, NT], [64, G, NCK, 2, RL], [64, G, NT], [64, G, NT], [64, G, NT], [64, G, NT], [64, G, NT], [64, G, NT], [64, G, NCK])]
        dbl_bufs = [bufs(2) for _ in range(9)]
        P1_PER_STEP = 2
        tokb = [k.sb(st, [RL, 3, G, 64], F32) for _ in range(2)]
        GAb = [k.sb(st, [RL, G, 2 * RL], F32) for _ in range(2)]
        GKb = [k.sb(st, [RL, G, 2 * RL], F32) for _ in range(2)]
        Pnb = [k.sb(st, [RL, G, RL], F32) for _ in range(2)]
        PPb = [[k.sb(st, [RL, G, 2, RL], F32) for _ in range(NLVL - 1)] for _ in range(2)]
        b_tokb, b_GAb, b_GKb, b_Pnb = bufs(2), bufs(2), bufs(2), bufs(2)
        b_PPb = [bufs(NLVL - 1), bufs(NLVL - 1)]
        U = k.sb(st, [RL, G, 64], F32)
        Ysb = k.sb(st, [RL, G, 64], F32)
        ST = k.sb(st, [64, G, 64], F32)
        yo = [k.sb(st, [64, NT], BF16) for _ in range(2)]
        bank = [k.ps(st, [128, 512]) for _ in range(8)]
        b_bank = bufs(8)
        ps_A, ps_B = bank[0], bank[1]
        ps_ga = bank[0][0:RL, 0:G * 2 * RL].rearrange("p (g x) -> p g x", g=G)
        ps_gk = bank[1][0:RL, 0:G * 2 * RL].rearrange("p (g x) -> p g x", g=G)
        ps_x1 = bank[2][0:RL, 0:2 * G * 64].rearrange("p (a g x) -> p a g x", a=2, g=G)
        ps_x2k = bank[3][0:RL, 0:G * 64].rearrange("p (g x) -> p g x", g=G)
        ps_gn = bank[3][0:RL, G * 64:G * 64 + G * RL].rearrange("p (g x) -> p g x", g=G)
        ps_z = bank[4][0:RL, 0:2 * G * 64].rearrange("p (a g x) -> p a g x", a=2, g=G)
        ps_pp = bank[5][0:RL, 0:G * 2 * RL].rearrange("p (g a x) -> p g a x", g=G, a=2)
        ps_y = bank[6][0:RL, 0:G * 64].rearrange("p (g x) -> p g x", g=G)
        ps_yT = bank[6][0:64, G * 64:G * 64 + G * RL].rearrange("p (g x) -> p g x", g=G)
        ps_s = bank[7][0:64, 0:G * 64].rearrange("p (g x) -> p g x", g=G)
        ps_ln = bank[7]
        (b_wup, b_rawd, b_rawa, b_twd, b_adq, b_gate,
         b_Vt, b_AR, b_Bt, b_Kt, b_Bh, b_Kh, b_bonus, b_yfm, b_gL, b_tok, b_GAs, b_GKs, b_Pn, b_U, b_Ysb, b_ST) = bufs(22)
        (b_r, b_k, b_e2, b_cc, b_eP, b_eN, b_a, b_kk, b_kkn, b_kmod, b_kka, b_t1, b_t2) = [bufs(G) for _ in range(13)]
        b_raw = {nm: Buf() for nm in raw}
        b_PP = bufs(2)
        b_yo = bufs(2)
        i64 = ident[0:64, 0:64]
        R = (lambda ap: ap.bitcast(mybir.dt.float32r)) if RWKV_FP32R else (lambda ap: ap)

        def shift(dst, src3, g, mu_t, om_t, H, reads, bdst):
            s.op("dve", lambda e: e.tensor_scalar(out=dst, in0=src3[:, g, 1:1 + NT], scalar1=om_t[:, H:H + 1], scalar2=None, op0=ALU.mult),
                 reads=reads + [b_c, bdst], writes=[bdst])
            s.op("dve", lambda e: e.scalar_tensor_tensor(out=dst, in0=src3[:, g, 0:NT], scalar=mu_t[:, H:H + 1], in1=dst,
                                                         op0=ALU.mult, op1=ALU.add),
                 reads=reads + [b_c, bdst], writes=[bdst])

        for h0 in range(0, NH, G):
            s.dma("sp", wup[:], prm["w_up"][:, h0 * 64:(h0 + G) * 64], reads=[b_wup], writes=[b_wup])
            s.dma("sp", aup[:], prm["a_up"][:, h0 * 64:(h0 + G) * 64], reads=[b_wup], writes=[b_wup])
            s.op("dve", lambda e: e.tensor_copy(out=R(ST[:]), in_=zer[:]), reads=[b_ST, b_c], writes=[b_ST])
            def make_tile(ti, t0):
                tp = ti % 2
                Vt, AR, Bt, Kt, Bh, Kh, bonus, gate, gL = (x[tp] for x in dbl_tiles)
                b_Vt, b_AR, b_Bt, b_Kt, b_Bh, b_Kh, b_bonus, b_gate, b_gL = (x[tp] for x in dbl_bufs)
                def head_gen(g):
                        H = h0 + g
                        hc = slice(g * 64, (g + 1) * 64)
                        shift(r_[g][:], raw["r"], g, P["mu_r"], om["mu_r"], H, [b_raw["r"]], b_r[g])
                        yield
                        shift(k_[g][:], raw["k"], g, P["mu_k"], om["mu_k"], H, [b_raw["k"]], b_k[g])
                        yield
                        shift(Vt[:, g, :], raw["v"], g, P["mu_v"], om["mu_v"], H, [b_raw["v"]], b_Vt)
                        yield
                        s.op("pe", lambda e: e.matmul(bank[g][0:64, :], lhsT=wup[:, hc], rhs=twd[:], start=True, stop=True),
                             reads=[b_wup, b_twd, b_bank[g]], writes=[b_bank[g]])
                        yield
                        s.op("act", lambda e: e.activation(out=t1[g][:], in_=bank[g][0:64, :], func=AF.Exp, scale=-1.0, bias=nw0[:, H:H + 1]),
                             reads=[b_bank[g], b_c, b_t1[g]], writes=[b_t1[g]])
                        yield
                        s.op("act", lambda e: e.activation(out=t1[g][:], in_=t1[g][:], func=AF.Ln, bias=1.0), reads=[b_t1[g]], writes=[b_t1[g]])
                        yield
                        s.op("act", lambda e: e.activation(out=e2[g][:], in_=t1[g][:], func=AF.Exp, scale=-1.0, bias=-0.5), reads=[b_t1[g], b_e2[g]], writes=[b_e2[g]])
                        yield
                        s.op("dve", lambda e: e.tensor_tensor_scan(out=c_[g][:], data0=rmask[:], data1=e2[g][:], initial=0.0, op0=ALU.mult, op1=ALU.add),
                             reads=[b_c, b_e2[g], b_cc[g]], writes=[b_cc[g]])
                        yield
                        s.op("act", lambda e: e.activation(out=eP[g][:], in_=c_[g][:], func=AF.Exp, scale=-1.0), reads=[b_cc[g], b_eP[g]], writes=[b_eP[g]])
                        yield
                        s.op("act", lambda e: e.activation(out=eN[g][:], in_=c_[g][:], func=AF.Exp), reads=[b_cc[g], b_eN[g]], writes=[b_eN[g]])
                        yield
                        s.op("pool", lambda e: e.tensor_copy(out=gL[:, g, :], in_=eP[g][:, RL - 1:NT:RL]), reads=[b_eP[g], b_gL], writes=[b_gL])
                        yield
                        yield
                        s.op("pe", lambda e: e.matmul(bank[g][0:64, :], lhsT=aup[:, hc], rhs=adq[:], start=True, stop=True),
                             reads=[b_wup, b_adq, b_bank[g]], writes=[b_bank[g]])
                        yield
                        s.op("act", lambda e: e.activation(out=a_[g][:], in_=bank[g][0:64, :], func=AF.Sigmoid, bias=P["a0"][:, H:H + 1]),
                             reads=[b_bank[g], b_c, b_a[g]], writes=[b_a[g]])
                        yield
                        s.op("dve", lambda e: e.tensor_scalar(out=kk[g][:], in0=k_[g][:], scalar1=P["k_k"][:, H:H + 1], scalar2=None, op0=ALU.mult),
                             reads=[b_k[g], b_c, b_kk[g]], writes=[b_kk[g]])
                        yield
                        s.op("act", lambda e: e.activation(out=R(t3[g][:]), in_=kk[g][:], func=AF.Square), reads=[b_kk[g], b_t3[g]], writes=[b_t3[g]])
                        yield
                        s.op("pe", lambda e: e.matmul(bank[g][0:64, :], lhsT=R(ones64r[:]), rhs=R(t3[g][:]), start=True, stop=True),
                             reads=[b_c, b_t3[g], b_bank[g]], writes=[b_bank[g]])
                        yield
                        s.op("act", lambda e: e.activation(out=t2[g][:], in_=bank[g][0:64, :], func=AF.Sqrt), reads=[b_bank[g], b_t2[g]], writes=[b_t2[g]])
                        yield
                        s.op("dve", lambda e: e.tensor_scalar(out=t2[g][:], in0=t2[g][:], scalar1=1e-12, scalar2=None, op0=ALU.max), reads=[b_t2[g]], writes=[b_t2[g]])
                        yield
                        s.op("dve", lambda e: e.reciprocal(out=t2[g][:], in_=t2[g][:]), reads=[b_t2[g]], writes=[b_t2[g]])
                        yield
                        s.op("dve", lambda e: e.tensor_tensor(out=kkn[g][:], in0=kk[g][:], in1=t2[g][:], op=ALU.mult), reads=[b_kk[g], b_t2[g], b_kkn[g]], writes=[b_kkn[g]])
                        yield
                        s.op("dve", lambda e: e.tensor_scalar(out=t1[g][:], in0=a_[g][:], scalar1=P["k_a"][:, H:H + 1], scalar2=om["k_a"][:, H:H + 1],
                                                              op0=ALU.mult, op1=ALU.add), reads=[b_a[g], b_c, b_t1[g]], writes=[b_t1[g]])
                        yield
                        s.op("dve", lambda e: e.tensor_tensor(out=kmod[g][:], in0=k_[g][:], in1=t1[g][:], op=ALU.mult), reads=[b_k[g], b_t1[g], b_kmod[g]], writes=[b_kmod[g]])
                        yield
                        s.op("dve", lambda e: e.tensor_tensor(out=kka[g][:], in0=kkn[g][:], in1=a_[g][:], op=ALU.mult), reads=[b_kkn[g], b_a[g], b_kka[g]], writes=[b_kka[g]])
                        yield
                        s.op("dve", lambda e: e.scalar_tensor_tensor(out=R(t3[g][:]), in0=r_[g][:], scalar=P["r_k"][:, H:H + 1], in1=kmod[g][:],
                                                                     op0=ALU.mult, op1=ALU.mult), reads=[b_r[g], b_c, b_kmod[g], b_t3[g]], writes=[b_t3[g]])
                        yield
                        s.op("pe", lambda e: e.matmul(bank[g][0:64, :], lhsT=R(ones64r[:]), rhs=R(t3[g][:]), start=True, stop=True),
                             reads=[b_c, b_t3[g], b_bank[g]], writes=[b_bank[g]])
                        yield
                        s.op("dve", lambda e: e.tensor_tensor(out=bonus[:, g, :], in0=bank[g][0:64, :], in1=Vt[:, g, :], op=ALU.mult),
                             reads=[b_bank[g], b_Vt, b_bonus], writes=[b_bonus])
                        yield
                        s.op("dve", lambda e: e.tensor_tensor(out=t1[g][:], in0=e2[g][:], in1=c_[g][:], op=ALU.subtract), reads=[b_e2[g], b_cc[g], b_t1[g]], writes=[b_t1[g]])
                        yield
                        s.op("act", lambda e: e.activation(out=t1[g][:], in_=t1[g][:], func=AF.Exp), reads=[b_t1[g]], writes=[b_t1[g]])
                        yield
                        v3 = lambda ap: ap.rearrange("p (n l) -> p n l", l=RL)
                        s.op("dve", lambda e: e.scalar_tensor_tensor(out=R(AR[:, g, :, 0, :]), in0=v3(kkn[g][:]), scalar=-1.0, in1=v3(t1[g][:]),
                                                                     op0=ALU.mult, op1=ALU.mult), reads=[b_kkn[g], b_t1[g], b_AR], writes=[b_AR])
                        yield
                        s.op("dve", lambda e: e.tensor_tensor(out=R(AR[:, g, :, 1, :]), in0=v3(r_[g][:]), in1=v3(eP[g][:]), op=ALU.mult),
                             reads=[b_r[g], b_eP[g], b_AR], writes=[b_AR])
                        yield
                        s.op("dve", lambda e: e.tensor_tensor(out=R(Bt[:, g, :]), in0=kka[g][:], in1=eN[g][:], op=ALU.mult), reads=[b_kka[g], b_eN[g], b_Bt], writes=[b_Bt])
                        yield
                        s.op("dve", lambda e: e.tensor_tensor(out=R(Kt[:, g, :]), in0=kmod[g][:], in1=eN[g][:], op=ALU.mult), reads=[b_kmod[g], b_eN[g], b_Kt], writes=[b_Kt])
                        yield
                        gbc = gL[:, g, :].unsqueeze(2).broadcast_to([64, NCK, RL])
                        s.op("dve", lambda e: e.tensor_tensor(out=v3(Bh[:, g, :]), in0=v3(Bt[:, g, :]), in1=gbc, op=ALU.mult),
                             reads=[b_Bt, b_gL, b_Bh], writes=[b_Bh])
                        yield
                        s.op("dve", lambda e: e.tensor_tensor(out=v3(Kh[:, g, :]), in0=v3(Kt[:, g, :]), in1=gbc, op=ALU.mult),
                             reads=[b_Kt, b_gL, b_Kh], writes=[b_Kh])
                        yield


                def phase1():
                    for rw, brw, nm, rowk in ((rawd, b_rawd, "mu_wd", "wd"), (rawa, b_rawa, "mu_ad", "ad")):
                        if ti == 0:
                            s.op("pool", lambda e: e.memset(rw[:, 0:1], 0.0), reads=[brw], writes=[brw])
                        else:
                            s.op("pool", lambda e: e.tensor_copy(out=rw[:, 0:1], in_=rw[:, NT:NT + 1]), reads=[brw], writes=[brw])
                        s.dma("sp", rw[:, 1:1 + NT], PT[rows[rowk]:rows[rowk] + 96, t0:t0 + NT], reads=[brw], writes=[brw])
                    dsts = ((twd, b_twd, rawd, b_rawd, "mu_wd"), (adq, b_adq, rawa, b_rawa, "mu_ad"))
                    for dst, bdst, rw, brw, nm in dsts:
                        s.op("dve", lambda e: e.tensor_scalar(out=dst[:], in0=rw[:, 1:1 + NT], scalar1=omd[nm][:], scalar2=None, op0=ALU.mult),
                             reads=[brw, b_c, bdst], writes=[bdst])
                        s.op("dve", lambda e: e.scalar_tensor_tensor(out=dst[:], in0=rw[:, 0:NT], scalar=mud[nm][:], in1=dst[:],
                                                                     op0=ALU.mult, op1=ALU.add),
                             reads=[brw, b_c, bdst], writes=[bdst])
                    s.op("act", lambda e: e.activation(out=twd[:], in_=twd[:], func=AF.Tanh), reads=[b_twd], writes=[b_twd])
                    for nm in ("r", "k", "v"):
                        if ti == 0:
                            s.op("pool", lambda e: e.memset(raw[nm][:, :, 0:1], 0.0), reads=[b_raw[nm]], writes=[b_raw[nm]])
                        else:
                            s.op("pool", lambda e: e.tensor_copy(out=raw[nm][:, :, 0:1], in_=raw[nm][:, :, NT:NT + 1]),
                                 reads=[b_raw[nm]], writes=[b_raw[nm]])
                        r0 = rows[nm] + h0 * 64
                        s.dma("sp", raw[nm][:, :, 1:1 + NT], PT[r0:r0 + G * 64, t0:t0 + NT].rearrange("(h j) t -> j h t", j=64),
                              reads=[b_raw[nm]], writes=[b_raw[nm]])
                    r0 = rows["g"] + h0 * 64
                    s.dma("act", gate[:], PT[r0:r0 + G * 64, t0:t0 + NT].rearrange("(h j) t -> j h t", j=64), reads=[b_gate], writes=[b_gate])
                    s.op("act", lambda e: e.activation(out=gate[:], in_=gate[:], func=AF.Silu), reads=[b_gate], writes=[b_gate])
                    alive = [head_gen(g) for g in range(G)]
                    while alive:
                        for gg in list(alive):
                            try:
                                next(gg)
                            except StopIteration:
                                alive.remove(gg)
                        yield
                def indep_steps(cc):
                    par = cc % 2
                    cs = slice(cc * RL, (cc + 1) * RL)
                    tk, ga_s, gk_s, pn_s, ppl = tokb[par], GAb[par], GKb[par], Pnb[par], PPb[par]
                    b_tk, b_ga, b_gk, b_pn, b_ppl = b_tokb[par], b_GAb[par], b_GKb[par], b_Pnb[par], b_PPb[par]

                    def tr():
                        for g in range(G):
                            s.op("pe", lambda e: e.transpose(ps_x1[:, 0, g, :], Vt[:, g, cs], i64), reads=[b_Vt, b_c, b_bank[2]], writes=[b_bank[2]])
                            s.op("pe", lambda e: e.transpose(ps_x1[:, 1, g, :], Bh[:, g, cs], i64), reads=[b_Bh, b_c, b_bank[2]], writes=[b_bank[2]])
                            s.op("pe", lambda e: e.transpose(ps_x2k[:, g, :], Kh[:, g, cs], i64), reads=[b_Kh, b_c, b_bank[3]], writes=[b_bank[3]])
                        s.op("act", lambda e: e.copy(out=R(tk[:, 0:2]), in_=ps_x1), reads=[b_bank[2], b_tk], writes=[b_tk])
                        s.op("act", lambda e: e.copy(out=R(tk[:, 2]), in_=ps_x2k), reads=[b_bank[3], b_tk], writes=[b_tk])

                    def gm():
                        for g in range(G):
                            arc = AR[:, g, cc].rearrange("p a l -> p (a l)")
                            s.op("pe", lambda e: e.matmul(ps_ga[:, g, :], lhsT=R(Bt[:, g, cs]), rhs=R(arc), start=True, stop=True),
                                 reads=[b_Bt, b_AR, b_bank[0]], writes=[b_bank[0]])
                            s.op("pe", lambda e: e.matmul(ps_gk[:, g, :], lhsT=R(Kt[:, g, cs]), rhs=R(arc), start=True, stop=True),
                                 reads=[b_Kt, b_AR, b_bank[1]], writes=[b_bank[1]])
                            s.op("pe", lambda e: e.matmul(ps_gn[:, g, :], lhsT=R(AR[:, g, cc, 0, :]), rhs=R(Bt[:, g, cs]), start=True, stop=True),
                                 reads=[b_Bt, b_AR, b_bank[3]], writes=[b_bank[3]])
                        s.op("dve", lambda e: e.tensor_tensor(out=R(ga_s[:]), in0=ps_ga, in1=maskA[:], op=ALU.mult),
                             reads=[b_bank[0], b_c, b_ga], writes=[b_ga])
                        s.op("dve", lambda e: e.tensor_tensor(out=R(gk_s[:]), in0=ps_gk, in1=maskA[:], op=ALU.mult),
                             reads=[b_bank[1], b_c, b_gk], writes=[b_gk])
                        s.op("dve", lambda e: e.tensor_tensor(out=R(pn_s[:]), in0=ps_gn, in1=maskN[:], op=ALU.mult),
                             reads=[b_bank[3], b_c, b_pn], writes=[b_pn])

                    def sq(lvl):
                        def f():
                            if lvl == 0:
                                Pl = lambda g: pn_s[:, g, :]
                                PTl = lambda g: ga_s[:, g, 0:RL]
                                rd = [b_pn, b_ga]
                            else:
                                Pl = lambda g: ppl[lvl - 1][:, g, 0, :]
                                PTl = lambda g: ppl[lvl - 1][:, g, 1, :]
                                rd = [b_ppl[lvl - 1]]
                            for g in range(G):
                                s.op("pe", lambda e: e.matmul(ps_pp[:, g, 0, :], lhsT=R(PTl(g)), rhs=R(Pl(g)), start=True, stop=True),
                                     reads=rd + [b_bank[5]], writes=[b_bank[5]])
                                s.op("pe", lambda e: e.matmul(ps_pp[:, g, 1, :], lhsT=R(Pl(g)), rhs=R(PTl(g)), start=True, stop=True),
                                     reads=rd + [b_bank[5]], writes=[b_bank[5]])
                            s.op("act", lambda e: e.copy(out=R(ppl[lvl][:]), in_=ps_pp), reads=[b_bank[5], b_ppl[lvl]], writes=[b_ppl[lvl]])
                        return f
                    return [tr, gm] + [sq(l) for l in range(NLVL - 1)]

                def dep_steps(cc):
                    par = cc % 2
                    cs = slice(cc * RL, (cc + 1) * RL)
                    tk, ga_s, gk_s, pn_s, ppl = tokb[par], GAb[par], GKb[par], Pnb[par], PPb[par]
                    b_tk, b_ga, b_gk, b_pn, b_ppl = b_tokb[par], b_GAb[par], b_GKb[par], b_Pnb[par], b_PPb[par]

                    def zz():
                        for g in range(G):
                            s.op("pe", lambda e: e.matmul(ps_z[:, 0, g, :], lhsT=R(AR[:, g, cc, 0, :]), rhs=R(ST[:, g, :]), start=True, stop=False),
                                 reads=[b_AR, b_ST, b_bank[4]], writes=[b_bank[4]], inc=False)
                            s.op("pe", lambda e: e.matmul(ps_z[:, 0, g, :], lhsT=R(gk_s[:, g, 0:RL]), rhs=R(tk[:, 0, g, :]), start=False, stop=True),
                                 reads=[b_gk, b_tk, b_bank[4]], writes=[b_bank[4]])
                        s.op("act", lambda e: e.copy(out=R(U[:]), in_=ps_z[:, 0]), reads=[b_bank[4], b_U], writes=[b_U])

                    def app(lvl):
                        def f():
                            if lvl == 0:
                                PTl = lambda g: ga_s[:, g, 0:RL]
                                rd = [b_ga]
                            else:
                                PTl = lambda g: ppl[lvl - 1][:, g, 1, :]
                                rd = [b_ppl[lvl - 1]]
                            for g in range(G):
                                s.op("pe", lambda e: e.matmul(ps_z[:, 1, g, :], lhsT=R(PTl(g)), rhs=R(U[:, g, :]), start=True, stop=True),
                                     reads=rd + [b_U, b_bank[4]], writes=[b_bank[4]])
                            s.op("dve", lambda e: e.tensor_tensor(out=R(U[:]), in0=ps_z[:, 1], in1=U[:], op=ALU.add),
                                 reads=[b_bank[4], b_U], writes=[b_U])
                        return f

                    def yy():
                        for g in range(G):
                            s.op("pe", lambda e: e.matmul(ps_y[:, g, :], lhsT=R(AR[:, g, cc, 1, :]), rhs=R(ST[:, g, :]), start=True, stop=False),
                                 reads=[b_AR, b_ST, b_bank[6]], writes=[b_bank[6]], inc=False)
                            s.op("pe", lambda e: e.matmul(ps_y[:, g, :], lhsT=R(ga_s[:, g, RL:2 * RL]), rhs=R(U[:, g, :]), start=False, stop=False),
                                 reads=[b_ga, b_U, b_bank[6]], writes=[b_bank[6]], inc=False)
                            s.op("pe", lambda e: e.matmul(ps_y[:, g, :], lhsT=R(gk_s[:, g, RL:2 * RL]), rhs=R(tk[:, 0, g, :]), start=False, stop=True),
                                 reads=[b_gk, b_tk, b_bank[6]], writes=[b_bank[6]])
                        s.op("act", lambda e: e.copy(out=Ysb[:], in_=ps_y), reads=[b_bank[6], b_Ysb], writes=[b_Ysb])

                    def ss():
                        for g in range(G):
                            s.op("pe", lambda e: e.matmul(ps_s[:, g, :], lhsT=R(tk[:, 1, g, :]), rhs=R(U[:, g, :]), start=True, stop=False),
                                 reads=[b_tk, b_U, b_bank[7]], writes=[b_bank[7]], inc=False)
                            s.op("pe", lambda e: e.matmul(ps_s[:, g, :], lhsT=R(tk[:, 2, g, :]), rhs=R(tk[:, 0, g, :]), start=False, stop=True),
                                 reads=[b_tk, b_bank[7]], writes=[b_bank[7]])
                        for g in range(G):
                            s.op("dve", lambda e: e.scalar_tensor_tensor(out=R(ST[:, g, :]), in0=ST[:, g, :], scalar=gL[:, g, cc:cc + 1], in1=ps_s[:, g, :],
                                                                         op0=ALU.mult, op1=ALU.add),
                                 reads=[b_ST, b_gL, b_bank[7]], writes=[b_ST])

                    def yt():
                        for g in range(G):
                            s.op("pe", lambda e: e.transpose(ps_yT[:, g, :], Ysb[:, g, :], ident[0:RL, 0:RL]), reads=[b_Ysb, b_c, b_bank[6]], writes=[b_bank[6]])
                        s.op("dve", lambda e: e.tensor_copy(out=R(yfm[:, :, cs]), in_=ps_yT), reads=[b_bank[6], b_yfm], writes=[b_yfm])
                    return [zz] + [app(l) for l in range(NLVL)] + [yy, ss, yt]


                def chunks():
                    for f in indep_steps(0):
                        f()
                        yield
                    for cc in range(NCK):
                        dsteps = dep_steps(cc)
                        isteps = indep_steps(cc + 1) if cc + 1 < NCK else []
                        for j in range(max(len(dsteps), len(isteps))):
                            if j < len(dsteps):
                                dsteps[j]()
                            if j < len(isteps):
                                isteps[j]()
                            yield
                def ph4(g):
                    H = h0 + g
                    yi = g % 2
                    s.op("pe", lambda e: e.matmul(bank[(7, 4)[g]][0:64, :], lhsT=R(ones64r[:]), rhs=R(yfm[:, g, :]), start=True, stop=True),
                         reads=[b_c, b_yfm, b_bank[(7, 4)[g]]], writes=[b_bank[(7, 4)[g]]])
                    yield
                    s.op("act", lambda e: e.activation(out=t1[g][:], in_=bank[(7, 4)[g]][0:64, :], func=AF.Copy, scale=1.0 / 64), reads=[b_bank[(7, 4)[g]], b_t1[g]], writes=[b_t1[g]])
                    yield
                    s.op("dve", lambda e: e.tensor_tensor(out=t1[g][:], in0=yfm[:, g, :], in1=t1[g][:], op=ALU.subtract), reads=[b_yfm, b_t1[g]], writes=[b_t1[g]])
                    yield
                    s.op("act", lambda e: e.activation(out=R(t3[g][:]), in_=t1[g][:], func=AF.Square), reads=[b_t1[g], b_t3[g]], writes=[b_t3[g]])
                    yield
                    s.op("pe", lambda e: e.matmul(bank[(7, 4)[g]][0:64, :], lhsT=R(ones64r[:]), rhs=R(t3[g][:]), start=True, stop=True),
                         reads=[b_c, b_t3[g], b_bank[(7, 4)[g]]], writes=[b_bank[(7, 4)[g]]])
                    yield
                    s.op("act", lambda e: e.activation(out=t2[g][:], in_=bank[(7, 4)[g]][0:64, :], func=AF.Sqrt, scale=1.0 / 64, bias=GN_EPS),
                         reads=[b_bank[(7, 4)[g]], b_t2[g]], writes=[b_t2[g]])
                    yield
                    s.op("dve", lambda e: e.reciprocal(out=t2[g][:], in_=t2[g][:]), reads=[b_t2[g]], writes=[b_t2[g]])
                    yield
                    s.op("dve", lambda e: e.tensor_tensor(out=t1[g][:], in0=t1[g][:], in1=t2[g][:], op=ALU.mult), reads=[b_t1[g], b_t2[g]], writes=[b_t1[g]])
                    yield
                    s.op("dve", lambda e: e.tensor_scalar(out=t1[g][:], in0=t1[g][:], scalar1=P["gn_w"][:, H:H + 1], scalar2=P["gn_b"][:, H:H + 1],
                                                          op0=ALU.mult, op1=ALU.add), reads=[b_t1[g], b_c], writes=[b_t1[g]])
                    yield
                    s.op("dve", lambda e: e.tensor_tensor(out=t1[g][:], in0=t1[g][:], in1=bonus[:, g, :], op=ALU.add), reads=[b_t1[g], b_bonus], writes=[b_t1[g]])
                    yield
                    s.op("dve", lambda e: e.tensor_tensor(out=yo[yi][:], in0=t1[g][:], in1=gate[:, g, :], op=ALU.mult),
                         reads=[b_t1[g], b_gate, b_yo[yi]], writes=[b_yo[yi]])
                    yield
                    r0 = row_y + H * 64
                    s.dma("sp", yT[r0:r0 + 64, t0:t0 + NT], yo[yi][:], reads=[b_yo[yi]])
                    yield

                def phase4():
                    alive = [ph4(g) for g in range(G)]
                    while alive:
                        for gg in list(alive):
                            try:
                                next(gg)
                            except StopIteration:
                                alive.remove(gg)
                        yield
                return phase1, chunks, phase4

            tiles = [make_tile(ti, t0) for ti, t0 in enumerate(range(0, T, NT))]
            for _ in tiles[0][0]():
                pass
            for ti in range(len(tiles)):
                cg = tiles[ti][1]()
                pg = tiles[ti + 1][0]() if ti + 1 < len(tiles) else iter(())
                c_alive, p_alive = True, True
                while c_alive or p_alive:
                    if c_alive:
                        try:
                            next(cg)
                        except StopIteration:
                            c_alive = False
                    for _ in range(P1_PER_STEP):
                        if p_alive:
                            try:
                                next(pg)
                            except StopIteration:
                                p_alive = False
                for _ in tiles[ti][2]():
                    pass
        s.barrier()


class Cfg:
    def __init__(self, D=4096, T=4096, NBLK_A=8, NH_B=32, HC=4, HX=4, M=256, DEPTH=2):
        self.D, self.T, self.NBLK_A, self.NH_B, self.HC, self.HX, self.M, self.DEPTH = D, T, NBLK_A, NH_B, HC, HX, M, DEPTH
        self.KC = D // 128
        self.WA = NBLK_A * 256
        self.WB = NH_B * 64
        self.QKW = HC * 256
        self.WC = HC * 512
        self.WX = HX * 128
        self.in_sizes = (self.WA, self.WA, 3 * self.WB + 192, self.WB, 2 * self.QKW, self.WC, self.WC, self.WC, 2 * HC,
                         self.WX, self.WX, 4 * D)
        self.c_in = sum(self.in_sizes)
        off = np.concatenate([[0], np.cumsum(self.in_sizes)])
        self.col = dict(a_x=off[0], a_g=off[1], b_s=off[2], b_g=off[3], c_qk=off[4], c_v=off[5], c_o=off[6], c_g=off[7],
                        c_if=off[8], x_q=off[9], x_g=off[10], gates=off[11])
        segs = [("a_x", self.col["a_x"], self.WA), ("a_g", self.col["a_g"], self.WA),
                ("r", self.col["b_s"], self.WB), ("k", self.col["b_s"] + self.WB, self.WB), ("v", self.col["b_s"] + 2 * self.WB, self.WB),
                ("wd", self.col["b_s"] + 3 * self.WB, 96), ("ad", self.col["b_s"] + 3 * self.WB + 96, 96),
                ("b_g", self.col["b_g"], self.WB),
                ("c_q", self.col["c_qk"], self.QKW), ("c_k", self.col["c_qk"] + self.QKW, self.QKW),
                ("c_v", self.col["c_v"], self.WC), ("c_o", self.col["c_o"], self.WC), ("c_g", self.col["c_g"], self.WC),
                ("c_if", self.col["c_if"], 2 * HC), ("x_q", self.col["x_q"], self.WX), ("x_g", self.col["x_g"], self.WX)]
        self.segs = segs
        self.row = {}
        r = 0
        for nm, c0, w in segs:
            self.row[nm] = r
            r += ((w + 127) // 128) * 128
        self.NB1 = r // 128
        self.grp_first = {"A": "a_x", "B": "r", "C": "c_q", "X": "x_q"}
        order = ["A", "B", "C", "X"]
        starts = [self.row[self.grp_first[g]] for g in order] + [r]
        self.grp_rows = {g: (starts[i], starts[i + 1]) for i, g in enumerate(order)}
        self.lrow = {}
        for nm, c0, w in segs:
            for g in order:
                lo, hi = self.grp_rows[g]
                if lo <= self.row[nm] < hi:
                    self.lrow[nm] = (g, self.row[nm] - lo)
        self.br_kc = [self.WA // 128, self.WB // 128, self.WC // 128, self.WX // 128]
        self.FY = sum(self.br_kc) * 128


RMS_EPS = 1e-6
RWKV_GN_EPS = 64e-5


def tile_layout(W):
    Kd, M = W.shape
    return np.ascontiguousarray(W.reshape(Kd // 128, 128, M // 128, 128).transpose(2, 1, 0, 3))


def chunk_cols(v):
    return np.ascontiguousarray(v.reshape(-1, 128).T)


def head_cols(v):
    return np.ascontiguousarray(v.reshape(-1, 64).T)


def const_inputs(cfg):
    HC = cfg.HC
    sel = np.zeros((HC, HC, 128), np.float32)
    for h in range(HC):
        sel[h, h, :] = 1
    a_, b_ = np.meshgrid(np.arange(RWKV_L), np.arange(RWKV_L), indexing="ij")
    mA = np.concatenate([(a_ < b_), (a_ <= b_)], 1).astype(np.float32)
    mN = (b_ < a_).astype(np.float32)
    rm = np.ones((64, NT), np.float32)
    rm[:, ::RWKV_L] = 0
    a_, b_ = np.meshgrid(np.arange(MLSTM_L), np.arange(MLSTM_L), indexing="ij")
    return {"c_ident": np.eye(128, dtype=np.float32), "c_sel": sel, "c_i4": np.eye(HC, dtype=np.float32),
            "c_negmask": np.where(a_ <= b_, 0.0, NEG).astype(np.float32), "c_resetmask": rm,
            "c_maskA": np.ascontiguousarray(np.broadcast_to(mA[:, None, :], (RWKV_L, RWKV_G, 2 * RWKV_L))),
            "c_maskN": np.ascontiguousarray(np.broadcast_to(mN[:, None, :], (RWKV_L, RWKV_G, RWKV_L)))}


def layer_inputs(cfg, inp, l):
    D, KC, HC = cfg.D, cfg.KC, cfg.HC
    w_in = inp["w_in"][l]
    W1 = np.zeros((D, cfg.NB1 * 128), np.float32)
    for nm, c0, w in cfg.segs:
        W1[:, cfg.row[nm]:cfg.row[nm] + w] = w_in[:, c0:c0 + w]
    o = {}
    o["W1"] = tile_layout(W1)
    g0 = cfg.col["gates"]
    o["Wg"] = np.stack([tile_layout(w_in[:, g0 + i * D: g0 + (i + 1) * D]) for i in range(4)])
    o["Wbr0"] = tile_layout(inp["w_branch_a"][l])
    o["Wbr1"] = tile_layout(inp["w_branch_b"][l])
    o["Wbr2"] = tile_layout(inp["w_branch_c"][l])
    o["Wbr3"] = tile_layout(inp["w_branch_x"][l])
    o["Wo"] = tile_layout(inp["w_out"][l])
    o["norm_g"] = chunk_cols(inp["norm_g"][l])
    o["mem_norm_g"] = chunk_cols(inp["mem_norm_g"][l])
    NCH = cfg.NBLK_A * 2
    o["lru_conv_w"] = np.ascontiguousarray(inp["lru_conv_w"][l].reshape(4, NCH, 128).transpose(2, 1, 0))
    for nm in ("lru_conv_b", "lru_ba", "lru_bx", "lru_lambda"):
        o[nm] = chunk_cols(inp[nm][l])
    for nm in ("lru_wa", "lru_wx"):
        o[nm] = np.ascontiguousarray(inp[nm][l].reshape(cfg.NBLK_A, 2, 128, 2, 128).transpose(0, 3, 2, 1, 4))
    WB = cfg.WB
    mu = inp["rwkv_mu"][l]
    o["rw_mu_r"], o["rw_mu_k"], o["rw_mu_v"] = head_cols(mu[:WB]), head_cols(mu[WB:2 * WB]), head_cols(mu[2 * WB:3 * WB])
    o["rw_mu_wd"] = np.ascontiguousarray(mu[3 * WB:3 * WB + 96].reshape(96, 1))
    o["rw_mu_ad"] = np.ascontiguousarray(mu[3 * WB + 96:].reshape(96, 1))
    for nm, src in (("w0", "rwkv_w0"), ("a0", "rwkv_a0"), ("k_k", "rwkv_k_k"), ("k_a", "rwkv_k_a"), ("gn_w", "rwkv_gn_w"), ("gn_b", "rwkv_gn_b")):
        o["rw_" + nm] = head_cols(inp[src][l])
    o["rw_r_k"] = head_cols(inp["rwkv_r_k"][l].reshape(-1))
    o["rw_w_up"] = np.ascontiguousarray(inp["rwkv_w_up"][l])
    o["rw_a_up"] = np.ascontiguousarray(inp["rwkv_a_up"][l])
    o["ml_conv_w"] = np.ascontiguousarray(inp["mlstm_conv_w"][l].reshape(4, 4 * HC, 128).transpose(2, 1, 0))
    o["ml_conv_b"] = chunk_cols(inp["mlstm_conv_b"][l])
    o["ml_gn_w"] = chunk_cols(inp["mlstm_gn_w"][l])
    o["ml_b_i"] = np.ascontiguousarray(inp["mlstm_b_i"][l].reshape(HC, 1))
    o["ml_b_f"] = np.ascontiguousarray(inp["mlstm_b_f"][l].reshape(HC, 1))
    wkv = inp["xattn_w_kv"][l]
    o["xa_Wk"] = tile_layout(wkv[:, :cfg.WX])
    o["xa_Wv"] = np.ascontiguousarray(wkv[:, cfg.WX:].reshape(KC, 128, cfg.WX))
    return {f"L{l}_{k_}": np.ascontiguousarray(v, dtype=np.float32) for k_, v in o.items()}


def flat2d(ap, ndim, width):
    names = "abcdefgh"[:ndim]
    f = ap.rearrange(f"{' '.join(names)} -> ({' '.join(names)})")
    return f.rearrange("(r c) -> r c", c=width)


def build_program(cfg, shapes):
    nc = bass.Bass("TRN2", target_bir_lowering=False)
    D, T, KC = cfg.D, cfg.T, cfg.KC
    ins = {nm: nc.dram_tensor(nm, list(sh), F32, kind="ExternalInput").ap() for nm, sh in shapes.items()}
    outT = nc.dram_tensor("outT", [D, T], F32, kind="ExternalOutput").ap()
    with contextlib.ExitStack() as st:
        k = Ctx(nc, st)
        hT = k.dram("hT", [D, T], BF16)
        PTs = {g: k.dram(f"PT{g}", [hi - lo, T], F32) for g, (lo, hi) in cfg.grp_rows.items()}

        def pt_block(b):
            r = b * 128
            for g, (lo, hi) in cfg.grp_rows.items():
                if lo <= r < hi:
                    return PTs[g], r - lo
            raise AssertionError
        lr = lambda nm: cfg.lrow[nm][1]
        yT = k.dram("yT", [cfg.FY, T], BF16)
        memnT = k.dram("memnT", [D, cfg.M], BF16)
        xs = [ins["xT"]] + [k.dram(f"x{l + 1}T", [D, T], F32) for l in range(cfg.DEPTH)]
        cst = {nm[2:]: ins[nm] for nm in ins if nm.startswith("c_")}
        OB = D // 128
        for l in range(cfg.DEPTH):
            L = lambda nm: ins[f"L{l}_{nm}"]
            Wg = k.dram(f"Wg{l}", [4, OB, 128, KC, 128], BF16)
            Wo = k.dram(f"Wo{l}", [OB, 128, KC, 128], BF16)
            Wbr = [k.dram(f"Wbr{l}_{i}", [OB, 128, cfg.br_kc[i], 128], BF16) for i in range(4)]
            for dst, src, nd in [(Wg, L("Wg"), 5), (Wo, L("Wo"), 4)] + [(Wbr[i], L(f"Wbr{i}"), 4) for i in range(4)]:
                n = int(np.prod(dst.shape))
                wdt = 1024 if n % 1024 == 0 else 512
                cast_dram(k, flat2d(dst, nd, wdt), flat2d(src, nd, wdt), n, width=wdt)
            stage_norm(k, xs[l], L("norm_g"), hT, D, T, RMS_EPS, BF16)
            stage_norm(k, ins["memT"], L("mem_norm_g"), memnT, D, cfg.M, RMS_EPS, BF16)
            stage_proj(k, hT, L("W1"), pt_block, D, T, cfg.NB1)
            lru_prm = {"conv_w": L("lru_conv_w"), "conv_b": L("lru_conv_b"), "ba": L("lru_ba"), "bx": L("lru_bx"), "lam": L("lru_lambda"),
                       "wa": L("lru_wa"), "wx": L("lru_wx")}
            stage_lru(k, PTs["A"], lr("a_x"), lr("a_g"), yT, 0, lru_prm, T, cfg.NBLK_A)
            rw_prm = {nm: L("rw_" + nm) for nm in ("mu_r", "mu_k", "mu_v", "w0", "a0", "k_k", "k_a", "r_k", "gn_w", "gn_b", "mu_wd", "mu_ad", "w_up", "a_up")}
            rw_rows = {"r": lr("r"), "k": lr("k"), "v": lr("v"), "wd": lr("wd"), "ad": lr("ad"), "g": lr("b_g")}
            stage_rwkv(k, PTs["B"], rw_rows, yT, cfg.WA, rw_prm, cst, T, cfg.NH_B, RWKV_GN_EPS)
            ml_prm = {"conv_w": L("ml_conv_w"), "conv_b": L("ml_conv_b"), "gn_w": L("ml_gn_w"), "b_i": L("ml_b_i"), "b_f": L("ml_b_f")}
            ml_rows = {"q": lr("c_q"), "k": lr("c_k"), "v": lr("c_v"), "o": lr("c_o"), "g": lr("c_g"), "ifg": lr("c_if")}
            stage_mlstm(k, PTs["C"], ml_rows, yT, cfg.WA + cfg.WB, ml_prm, cst, T, cfg.HC)
            stage_xattn(k, PTs["X"], lr("x_q"), lr("x_g"), yT, cfg.WA + cfg.WB + cfg.WC, memnT, L("xa_Wk"), L("xa_Wv"), cst["ident"],
                        T, D, cfg.M, cfg.HX)
            stage_merge(k, hT, yT, cfg.br_kc, Wg, Wbr, Wo, xs[l], xs[l + 1], D, T)
        stage_norm(k, xs[cfg.DEPTH], ins["final_g"], outT, D, T, RMS_EPS, F32)
        k.s.barrier()
        build_program.ninst = k.s.ninst
        build_program.per_eng = dict(k.s.per_eng)
        build_program.nsem = k.s.nsem
        build_program.nwait = k.s.nwait
    return nc


def run_module(cfg, inputs):
    B = inputs["x"].shape[0]
    shared = const_inputs(cfg)
    for l in range(cfg.DEPTH):
        shared.update(layer_inputs(cfg, inputs, l))
    shared["final_g"] = chunk_cols(np.asarray(inputs["final_norm_g"], dtype=np.float32))
    in_maps = []
    for b in range(B):
        m = dict(shared)
        m["xT"] = np.ascontiguousarray(np.asarray(inputs["x"][b], dtype=np.float32).T)
        m["memT"] = np.ascontiguousarray(np.asarray(inputs["mem"][b], dtype=np.float32).T)
        in_maps.append(m)
    shapes = {nm: v.shape for nm, v in in_maps[0].items()}
    nc = build_program(cfg, shapes)
    res = run_bass_kernel_spmd(nc, in_maps, core_ids=list(range(B)))
    out = np.stack([np.ascontiguousarray(res.results[b]["outT"].T) for b in range(B)])
    return out.astype(np.float32)


def kernel(**inputs):
    inputs = {k_: np.asarray(v) for k_, v in inputs.items()}
    return run_module(Cfg(), inputs)
```
